# Optimizing a Trainium2 kernel written in Bass

```python
import math
import jax
import jax.numpy as jnp
from jax import lax
import numpy as np

D_MODEL = 2048
BATCH = 2
SEQ = 4096
DEPTH = 1

GRID_W = 64
CTX_LEN = 256
EPS = 1e-6
N_MOD = 6

RW_HEAD_DIM = 64
RW_WIDTH = D_MODEL // 2
RW_HEADS = RW_WIDTH // RW_HEAD_DIM
DECAY_LORA = 64
AAA_LORA = 64
GATE_LORA = 160
LNX_EPS = 64e-5
RW_STATE_COLS = 2 * RW_WIDTH + 2 * DECAY_LORA + 2 * AAA_LORA
RW_COLS = RW_STATE_COLS + RW_WIDTH + GATE_LORA

RET_WIDTH = D_MODEL - RW_WIDTH
RET_HEADS = 8
RET_HEAD_DIM = RET_WIDTH // RET_HEADS
RET_CHUNK = 128
ROPE_BASE = 10000.0
RET_STATE_COLS = 2 * RET_WIDTH
RET_COLS = 4 * RET_WIDTH

P_IN = RW_COLS + RET_COLS
D_MIX = RW_WIDTH + RET_WIDTH
D_FF = 5632

kernel_name = 'hybrid_rwkv7_retention_convglu_dit'


def rmsnorm(x, g):
    xf = x.astype(jnp.float32)
    y = xf * lax.rsqrt(jnp.mean(xf * xf, axis=-1, keepdims=True) + EPS)
    return (y * g.astype(jnp.float32)).astype(x.dtype)


def dwconv_seq(x, w):
    return lax.conv_general_dilated(x, w[:, None, :].astype(x.dtype), (1,), ((1, 1),),
                                    dimension_numbers=('NWC', 'WIO', 'NWC'),
                                    feature_group_count=x.shape[-1])


def dwconv_grid(x, w, b, rows):
    B, L, C = x.shape
    y = lax.conv_general_dilated(x.reshape(B, rows, GRID_W, C), w[:, :, None, :].astype(x.dtype),
                                 (1, 1), ((1, 1), (1, 1)),
                                 dimension_numbers=('NHWC', 'HWIO', 'NHWC'),
                                 feature_group_count=C)
    return y.reshape(B, L, C) + b


def rope_grid(x):
    L, d = x.shape[2], x.shape[3]
    n = d // 4
    t = jnp.arange(L)
    inv = ROPE_BASE ** (-jnp.arange(n, dtype=jnp.float32) / n)
    ang_row = (t // GRID_W).astype(jnp.float32)[:, None] * inv
    ang_col = (t % GRID_W).astype(jnp.float32)[:, None] * inv

    def rot(u, ang):
        u1, u2 = jnp.split(u, 2, axis=-1)
        cos, sin = jnp.cos(ang).astype(u.dtype), jnp.sin(ang).astype(u.dtype)
        return jnp.concatenate([u1 * cos - u2 * sin, u1 * sin + u2 * cos], axis=-1)

    x_row, x_col = jnp.split(x, 2, axis=-1)
    return jnp.concatenate([rot(x_row, ang_row), rot(x_col, ang_col)], axis=-1)


def rw_heads(t):
    B, L, _ = t.shape
    return t.reshape(B, L, RW_HEADS, RW_HEAD_DIM)


def rwkv_state_inputs(rw, w0, w_up, a0, a_up, k_k, k_a):
    f32 = jnp.float32
    W = RW_WIDTH
    k = rw[..., :W].astype(f32)
    v = rw[..., W:2 * W].astype(f32)
    lo = rw[..., 2 * W:RW_STATE_COLS].astype(f32)
    wd = (lo[..., :DECAY_LORA], lo[..., DECAY_LORA:2 * DECAY_LORA])
    ad = (lo[..., 2 * DECAY_LORA:2 * DECAY_LORA + AAA_LORA], lo[..., 2 * DECAY_LORA + AAA_LORA:])
    kk = rw_heads(k * k_k)
    kk = kk * lax.rsqrt(jnp.sum(kk * kk, axis=-1, keepdims=True) + 1e-12)
    dirs = []
    for d in range(2):
        w_log = -jax.nn.softplus(-(w0[d] + jnp.tanh(wd[d]) @ w_up[d])) - 0.5
        decay = jnp.exp(-jnp.exp(w_log))
        a = jax.nn.sigmoid(a0[d] + ad[d] @ a_up[d])
        k_d = k * (1.0 + (a - 1.0) * k_a)
        dirs.append((rw_heads(decay), rw_heads(k_d), rw_heads(a)))
    return rw_heads(v), kk, dirs


def rwkv_dir(r, decay, k, v, kk, a, s0, reverse):
    readout = r is not None
    xs = [jnp.moveaxis(t, 1, 0) for t in (decay, k, v, -kk, kk * a)]
    if readout:
        xs.append(jnp.moveaxis(r, 1, 0))

    def step(s, inp):
        w_t, k_t, v_t, a_t, b_t = inp[:5]
        s = (s * w_t[:, :, None, :]
             + jnp.einsum('bhij,bhj->bhi', s, a_t)[..., None] * b_t[:, :, None, :]
             + v_t[..., None] * k_t[:, :, None, :])
        y = jnp.einsum('bhij,bhj->bhi', s, inp[5]) if readout else None
        return s, y

    s_fin, ys = lax.scan(step, s0, tuple(xs), reverse=reverse)
    return (jnp.moveaxis(ys, 0, 1) if readout else None), s_fin


def rwkv_group(rw, s0, w0, w_up, a0, a_up, g_up, k_k, k_a, r_k, lnx_w, lnx_b, readout):
    v, kk, dirs = rwkv_state_inputs(rw, w0, w_up, a0, a_up, k_k, k_a)
    r = rw_heads(rw[..., RW_STATE_COLS:RW_STATE_COLS + RW_WIDTH].astype(jnp.float32)) if readout else None
    ys, states = [], []
    for d in range(2):
        decay, k_d, a = dirs[d]
        y, s = rwkv_dir(r, decay, k_d, v, kk, a, s0[d], reverse=(d == 1))
        ys.append(y)
        states.append(s)
    if not readout:
        return None, states
    B, L = rw.shape[0], rw.shape[1]
    y = ys[0] + ys[1]
    mu = jnp.mean(y, axis=-1, keepdims=True)
    var = jnp.mean(jnp.square(y - mu), axis=-1, keepdims=True)
    y = ((y - mu) * lax.rsqrt(var + LNX_EPS)).reshape(B, L, RW_WIDTH) * lnx_w + lnx_b
    bonus = (jnp.sum(r * dirs[0][1] * r_k, axis=-1, keepdims=True)
             + jnp.sum(r * dirs[1][1] * r_k, axis=-1, keepdims=True)) * v
    g = jax.nn.sigmoid(rw[..., RW_STATE_COLS + RW_WIDTH:].astype(jnp.float32)) @ g_up
    return ((y + bonus.reshape(B, L, RW_WIDTH)) * g).astype(rw.dtype), states


def ret_heads(t):
    B, L, _ = t.shape
    return t.reshape(B, L, RET_HEADS, RET_HEAD_DIM).transpose(0, 2, 1, 3)


def retention_dir(q, k, v, log_g, s0, strict):
    B, H, L, d = k.shape
    n = L // RET_CHUNK
    pos = jnp.arange(RET_CHUNK, dtype=jnp.float32)
    kc = k.reshape(B, H, n, RET_CHUNK, d)
    vc = v.reshape(B, H, n, RET_CHUNK, v.shape[-1])
    to_end = jnp.exp((RET_CHUNK - 1.0 - pos)[None, :] * log_g[:, None])
    u = jnp.einsum('bhncd,bhnce->nbhde', kc * to_end[None, :, None, :, None], vc)
    g_chunk = jnp.exp(RET_CHUNK * log_g)[None, :, None, None]

    def step(s, u_n):
        return g_chunk * s + u_n, s

    s_fin, s_prev = lax.scan(step, s0, u)
    if q is None:
        return None, s_fin
    qc = q.reshape(B, H, n, RET_CHUNK, d)
    diff = pos[:, None] - pos[None, :]
    mask = (diff > 0) if strict else (diff >= 0)
    decay = jnp.where(mask, jnp.exp(jnp.where(mask, diff, 0.0)[None] * log_g[:, None, None]), 0.0)
    scores = jnp.einsum('bhnid,bhnjd->bhnij', qc, kc) * decay[None, :, None]
    intra = jnp.einsum('bhnij,bhnje->bhnie', scores, vc)
    from_start = jnp.exp((pos + 1.0)[None, :] * log_g[:, None])
    cross = jnp.einsum('bhnid,nbhde->bhnie', qc * from_start[None, :, None, :, None], s_prev)
    return (intra + cross).reshape(B, H, L, -1), s_fin


def retention_group(ret, s0, rope, readout):
    f32 = jnp.float32
    W = RET_WIDTH
    log_g = jnp.log(1.0 - 2.0 ** (-5.0 - jnp.arange(RET_HEADS, dtype=f32)))
    k = ret_heads(ret[..., :W].astype(f32))
    v = ret_heads(ret[..., W:2 * W].astype(f32))
    q = ret_heads(ret[..., 2 * W:3 * W].astype(f32)) if readout else None
    if rope:
        q, k = rope_grid(q), rope_grid(k)
    k = k * (RET_HEAD_DIM ** -0.5)
    flip = lambda t: jnp.flip(t, axis=2)
    y_f, s_f = retention_dir(q, k, v, log_g, s0[0], strict=False)
    y_b, s_b = retention_dir(None if q is None else flip(q), flip(k), flip(v), log_g, s0[1], strict=True)
    if not readout:
        return None, [s_f, s_b]
    B, L = ret.shape[0], ret.shape[1]
    y = (y_f + flip(y_b)).transpose(0, 2, 1, 3)
    y = y * lax.rsqrt(jnp.mean(y * y, axis=-1, keepdims=True) + EPS)
    g = jax.nn.silu(ret[..., 3 * W:].astype(f32))
    return (y.reshape(B, L, W) * g).astype(ret.dtype), [s_f, s_b]


def token_mixer(xm, xm_ctx, w_in, rw_conv, w0, w_up, a0, a_up, g_up, k_k, k_a, r_k, lnx_w, lnx_b,
                w_out, ctx_out):
    B = xm.shape[0]
    rw_params = (w0, w_up, a0, a_up, g_up, k_k, k_a, r_k, lnx_w, lnx_b)
    if ctx_out:
        pc = xm_ctx @ w_in
        rw_c = dwconv_seq(pc[..., :RW_COLS], rw_conv)
        ret_c = pc[..., RW_COLS:]
    else:
        rw_c = dwconv_seq(xm_ctx @ w_in[:, :RW_STATE_COLS], rw_conv[:, :RW_STATE_COLS])
        ret_c = xm_ctx @ w_in[:, RW_COLS:RW_COLS + RET_STATE_COLS]
    z_rw = jnp.zeros((B, RW_HEADS, RW_HEAD_DIM, RW_HEAD_DIM), jnp.float32)
    z_ret = jnp.zeros((B, RET_HEADS, RET_HEAD_DIM, RET_HEAD_DIM), jnp.float32)
    o_rw_c, s_rw_c = rwkv_group(rw_c, (z_rw, z_rw), *rw_params, readout=ctx_out)
    o_ret_c, s_ret_c = retention_group(ret_c, (z_ret, z_ret), rope=False, readout=ctx_out)
    p = xm @ w_in
    o_rw, _ = rwkv_group(dwconv_seq(p[..., :RW_COLS], rw_conv), s_rw_c, *rw_params, readout=True)
    o_ret, _ = retention_group(p[..., RW_COLS:], s_ret_c, rope=True, readout=True)
    out = jnp.concatenate([o_rw, o_ret], axis=-1) @ w_out
    out_c = (jnp.concatenate([o_rw_c, o_ret_c], axis=-1) @ w_out) if ctx_out else None
    return out, out_c


def conv_ffn(h, w_gate, w_up, conv_w, conv_b, w_down, rows):
    gt = h @ w_gate
    gt = dwconv_grid(gt, conv_w, conv_b, rows) if rows is not None else dwconv_seq(gt, conv_w[1]) + conv_b
    return (jax.nn.silu(gt) * (h @ w_up)) @ w_down


def setup_inputs(seed: int = 0) -> dict:
    key = jax.random.key(seed)
    ks = jax.random.split(key, 28)
    f32 = jnp.float32
    D = D_MODEL

    def nrm(k, shape, scale):
        return jax.random.normal(k, shape, f32) * scale

    ratio = jnp.arange(RW_WIDTH, dtype=f32) / (RW_WIDTH - 1)
    w0_base = -6.0 + 5.0 * ratio
    return {
        'x': nrm(ks[0], (BATCH, SEQ, D), 1.0),
        'c': nrm(ks[1], (BATCH, D), 1.0),
        'ctx': nrm(ks[2], (BATCH, CTX_LEN, D), 1.0),
        'c_ctx': nrm(ks[3], (D,), 1.0),
        'w_ada': nrm(ks[4], (DEPTH, D, N_MOD * D), 0.5 * D ** -0.5),
        'b_ada': nrm(ks[5], (DEPTH, N_MOD * D), 0.02),
        'norm_pre_mix': 1.0 + nrm(ks[6], (DEPTH, D), 0.02),
        'norm_post_mix': 1.0 + nrm(ks[7], (DEPTH, D), 0.02),
        'norm_pre_ffn': 1.0 + nrm(ks[8], (DEPTH, D), 0.02),
        'norm_post_ffn': 1.0 + nrm(ks[9], (DEPTH, D), 0.02),
        'w_in': nrm(ks[10], (DEPTH, D, P_IN), D ** -0.5),
        'rw_conv': nrm(ks[11], (DEPTH, 3, RW_COLS), 3 ** -0.5),
        'rw_w0': w0_base + nrm(ks[12], (DEPTH, 2, RW_WIDTH), 0.1),
        'rw_w_up': nrm(ks[13], (DEPTH, 2, DECAY_LORA, RW_WIDTH), 0.1 * DECAY_LORA ** -0.5),
        'rw_a0': nrm(ks[14], (DEPTH, 2, RW_WIDTH), 0.1),
        'rw_a_up': nrm(ks[15], (DEPTH, 2, AAA_LORA, RW_WIDTH), AAA_LORA ** -0.5),
        'rw_g_up': nrm(ks[16], (DEPTH, GATE_LORA, RW_WIDTH), GATE_LORA ** -0.5),
        'rw_k_k': 0.85 + nrm(ks[17], (DEPTH, RW_WIDTH), 0.02),
        'rw_k_a': 1.0 + nrm(ks[18], (DEPTH, RW_WIDTH), 0.02),
        'rw_r_k': nrm(ks[19], (DEPTH, RW_HEADS, RW_HEAD_DIM), 0.1),
        'rw_lnx_w': 1.0 + nrm(ks[20], (DEPTH, RW_WIDTH), 0.02),
        'rw_lnx_b': nrm(ks[21], (DEPTH, RW_WIDTH), 0.02),
        'w_out': nrm(ks[22], (DEPTH, D_MIX, D), D_MIX ** -0.5),
        'ffn_w_gate': nrm(ks[23], (DEPTH, D, D_FF), D ** -0.5),
        'ffn_w_up': nrm(ks[24], (DEPTH, D, D_FF), D ** -0.5),
        'ffn_conv': nrm(ks[25], (DEPTH, 3, 3, D_FF), 1.0 / 3.0),
        'ffn_conv_b': nrm(ks[26], (DEPTH, D_FF), 0.02),
        'ffn_w_down': nrm(ks[27], (DEPTH, D_FF, D), D_FF ** -0.5),
    }


def reference(x, c, ctx, c_ctx, w_ada, b_ada, norm_pre_mix, norm_post_mix, norm_pre_ffn, norm_post_ffn,
              w_in, rw_conv, rw_w0, rw_w_up, rw_a0, rw_a_up, rw_g_up, rw_k_k, rw_k_a, rw_r_k,
              rw_lnx_w, rw_lnx_b, w_out, ffn_w_gate, ffn_w_up, ffn_conv, ffn_conv_b, ffn_w_down):
    rows = x.shape[1] // GRID_W
    for l in range(DEPTH):
        ctx_out = l < DEPTH - 1
        mod = jax.nn.silu(c) @ w_ada[l] + b_ada[l]
        sh1, sc1, g1, sh2, sc2, g2 = jnp.split(mod[:, None, :], N_MOD, axis=-1)
        n_c = N_MOD if ctx_out else 2
        mod_c = jnp.split(jax.nn.silu(c_ctx) @ w_ada[l][:, :n_c * D_MODEL] + b_ada[l][:n_c * D_MODEL], n_c)

        xm = rmsnorm(x, norm_pre_mix[l]) * (1.0 + sc1) + sh1
        xm_c = rmsnorm(ctx, norm_pre_mix[l]) * (1.0 + mod_c[1]) + mod_c[0]
        out, out_c = token_mixer(xm, xm_c, w_in[l], rw_conv[l], rw_w0[l], rw_w_up[l], rw_a0[l], rw_a_up[l],
                                 rw_g_up[l], rw_k_k[l], rw_k_a[l], rw_r_k[l], rw_lnx_w[l], rw_lnx_b[l],
                                 w_out[l], ctx_out)
        x = x + g1 * rmsnorm(out, norm_post_mix[l])
        h = rmsnorm(x, norm_pre_ffn[l]) * (1.0 + sc2) + sh2
        y = conv_ffn(h, ffn_w_gate[l], ffn_w_up[l], ffn_conv[l], ffn_conv_b[l], ffn_w_down[l], rows)
        x = x + g2 * rmsnorm(y, norm_post_ffn[l])
        if ctx_out:
            ctx = ctx + mod_c[2] * rmsnorm(out_c, norm_post_mix[l])
            hc = rmsnorm(ctx, norm_pre_ffn[l]) * (1.0 + mod_c[4]) + mod_c[3]
            yc = conv_ffn(hc, ffn_w_gate[l], ffn_w_up[l], ffn_conv[l], ffn_conv_b[l], ffn_w_down[l], None)
            ctx = ctx + mod_c[5] * rmsnorm(yc, norm_post_ffn[l])
    return x
```

```python
import numpy as np
from contextlib import ExitStack
import concourse.bass as bass
import concourse.mybir as mybir
from concourse.bass_utils import run_bass_kernel_spmd

F32 = mybir.dt.float32
BF16 = mybir.dt.bfloat16
AF = mybir.ActivationFunctionType
ALU = mybir.AluOpType
AX = mybir.AxisListType

D = 2048
SEQ = 4096
CTX = 256
DFF = 5632
NFT = 44
EPS = 1e-6
LNX_EPS = 64e-5
BLK = 256
NWIN = 258
C = 64
NCOLS = 2208
DECAY_C = float(np.exp(-0.5))

DBG = {}


class Ev:
    __slots__ = ("kind", "eng", "idx", "key", "val", "needed")

    def __init__(self, kind, eng=None, idx=0, key=None, val=0):
        self.kind, self.eng, self.idx, self.key, self.val, self.needed = kind, eng, idx, key, val, False


class Prog:
    ENGS = ["pe", "act", "dve", "pool", "sp"]

    def __init__(self, nc):
        self.nc = nc
        self.stream = {e: [] for e in self.ENGS}
        self.last_write = {}
        self.readers = {}
        self.waited = {e: {} for e in self.ENGS}
        self.dma_count = {}
        self.dma_keys = []
        self.last_ev = {}
        self.trace_lines = False
        self.lines = {}
        self.imap = {}

    def _deps(self, eng, reads, writes):
        evs = []
        for r in reads:
            w = self.last_write.get(r)
            if w is not None:
                evs.append(w)
        for r in writes:
            w = self.last_write.get(r)
            if w is not None:
                evs.append(w)
            evs.extend(self.readers.get(r, ()))
        best = {}
        for ev in evs:
            if ev.kind == "eng":
                if ev.eng == eng and eng == "pe":
                    continue
                k = ("eng", ev.eng)
                if k not in best or best[k].idx < ev.idx:
                    best[k] = ev
            else:
                k = ("dma", ev.key)
                if k not in best or best[k].val < ev.val:
                    best[k] = ev
        out = []
        for k, ev in best.items():
            cur = self.waited[eng].get(k, -1)
            v = ev.idx if ev.kind == "eng" else ev.val
            if cur >= v:
                continue
            self.waited[eng][k] = v
            ev.needed = True
            out.append(ev)
        return out

    def _commit(self, ev, reads, writes):
        for r in reads:
            self.readers.setdefault(r, []).append(ev)
        for r in writes:
            self.last_write[r] = ev
            self.readers[r] = []

    def op(self, eng, fn, reads=(), writes=()):
        waits = self._deps(eng, reads, writes)
        ev = Ev("eng", eng=eng, idx=len(self.stream[eng]))
        if self.trace_lines:
            import sys as _s
            fr = _s._getframe(2)
            self.lines[(eng, len(self.stream[eng]))] = (fr.f_lineno, fr.f_back.f_lineno if fr.f_back else 0)
        self.stream[eng].append((fn, waits, ev))
        self._commit(ev, reads, writes)
        self.last_ev[eng] = ev
        return ev

    def dma(self, eng, out, in_, key, reads=(), writes=(), **kw):
        waits = self._deps(eng, reads, writes)
        if key not in self.dma_count:
            self.dma_count[key] = 0
            self.dma_keys.append(key)
        self.dma_count[key] += 16
        ev = Ev("dma", eng=eng, idx=len(self.stream[eng]), key=key, val=self.dma_count[key])
        self.stream[eng].append((lambda e: e.dma_start(out=out, in_=in_, **kw), waits, ev))
        self._commit(ev, reads, writes)
        return ev

    def custom(self, eng, fn, key, reads=(), writes=(), inc=1):
        waits = self._deps(eng, reads, writes)
        if key not in self.dma_count:
            self.dma_count[key] = 0
            self.dma_keys.append(key)
        self.dma_count[key] += inc
        ev = Ev("dma", eng=eng, idx=len(self.stream[eng]), key=key, val=self.dma_count[key])
        self.stream[eng].append((fn, waits, ev))
        self._commit(ev, reads, writes)
        return ev

    def alias(self, new, olds):
        evs = []
        for o in olds:
            w = self.last_write.get(o)
            if w is not None:
                evs.append(w)
            evs.extend(self.readers.get(o, ()))
        self.readers.setdefault(new, []).extend(evs)

    def all_res(self):
        return list(set(list(self.last_write.keys()) + list(self.readers.keys())))

    def emit(self, es, final_waits):
        nc = self.nc
        LIM = 30000
        engobj = {"pe": nc.tensor, "act": nc.scalar, "dve": nc.vector, "pool": nc.gpsimd, "sp": nc.sync}
        esems = {}
        for e in self.ENGS:
            cnt = 0
            for (fn, waits, ev) in self.stream[e]:
                if ev.kind == "eng" and ev.needed:
                    cnt += 1
                    ev.val = cnt
            nsem = cnt // LIM + 1
            import os as _os
            if _os.environ.get("KCNT"):
                print("SEMCNT", e, cnt, "ninstr", len(self.stream[e]), "nwaits", sum(len(w) for (_, w, _) in self.stream[e]))
            esems[e] = [es.enter_context(nc.semaphore("s_%s_%d" % (e, i))) for i in range(nsem)]
        dsems = {k: es.enter_context(nc.semaphore("d_%s" % str(k))) for k in self.dma_keys}

        def semval(ev):
            if ev.kind == "eng":
                i = (ev.val - 1) // LIM
                return esems[ev.eng][i], ev.val - i * LIM
            if ev.key in ("const", "constp"):
                return dsems[ev.key], self.dma_count[ev.key]
            return dsems[ev.key], ev.val

        block = es.enter_context(nc.Block())
        streams = self.stream

        def run(e, eng):
            for ii, (fn, waits, ev) in enumerate(streams[e]):
                for w in waits:
                    s, v = semval(w)
                    eng.wait_ge(s, v)
                ins = fn(eng)
                if self.trace_lines:
                    try:
                        self.imap[str(ins.ins.name)] = self.lines.get((e, ii))
                    except Exception:
                        pass
                if ev.kind == "dma":
                    if ev.eng is not None and ev.key is not None:
                        inc = 16 if not str(ev.key).startswith("cc") else 1
                        ins.then_inc(dsems[ev.key], inc)
                elif ev.needed:
                    s, v = semval(ev)
                    ins.then_inc(s, 1)
            for (k, v) in final_waits.get(e, []):
                eng.wait_ge(dsems[k], v)

        @block.tensor
        def _(eng):
            run("pe", eng)

        @block.scalar
        def _(eng):
            run("act", eng)

        @block.vector
        def _(eng):
            run("dve", eng)

        @block.gpsimd
        def _(eng):
            run("pool", eng)

        @block.sync
        def _(eng):
            run("sp", eng)


def _consts(g):
    cst = {}
    cst["ident"] = np.eye(128, dtype=np.float32)
    bo = np.zeros((128, 128), np.float32)
    bo[:64, :64] = 1.0
    bo[64:, 64:] = 1.0
    cst["bones"] = bo
    cst["ones"] = np.ones((128, 128), np.float32)
    s = np.arange(128)[:, None]
    t = np.arange(128)[None, :]
    same = (s // 64) == (t // 64)
    tri = np.zeros((128, 4, 128), np.float32)
    tri[:, 0, :] = same & (s <= t)
    tri[:, 1, :] = same & (s < t)
    tri[:, 2, :] = same & (s >= t)
    tri[:, 3, :] = same & (s > t)
    cst["tri"] = tri
    s = np.arange(64)[:, None]
    t = np.arange(64)[None, :]
    mst = np.zeros((64, 2, 128), np.float32)
    mst[:, 0, :64] = s < t
    mst[:, 0, 64:] = s <= t
    mst[:, 1, :64] = s > t
    mst[:, 1, 64:] = s >= t
    cst["mst"] = mst
    m1 = np.zeros((64, 2, 64), np.float32)
    m1[:, 0, :] = (t < s).T.T
    tt = np.arange(64)[:, None]
    ss = np.arange(64)[None, :]
    m1[:, 0, :] = ss < tt
    m1[:, 1, :] = ss > tt
    cst["m1"] = m1
    j = np.arange(128)[:, None].astype(np.float64)
    i = np.arange(128)[None, :].astype(np.float64)
    dmask = np.zeros((128, 4, 128), np.float32)
    qsc = np.zeros((128, 4, 128), np.float32)
    ksc = np.zeros((128, 4), np.float32)
    gc = np.zeros((128, 2), np.float32)
    sc = 128.0 ** -0.5
    for hr in range(2):
        gam = 1.0 - 2.0 ** (-5.0 - (2 * g + hr))
        lg = np.log(gam)
        dmask[:, hr * 2 + 0, :] = np.where(j <= i, np.exp((i - j) * lg), 0.0) * sc
        dmask[:, hr * 2 + 1, :] = np.where(j > i, np.exp((j - i) * lg), 0.0) * sc
        qsc[:, hr * 2 + 0, :] = np.exp((i + 1.0) * lg)
        qsc[:, hr * 2 + 1, :] = np.exp((128.0 - i) * lg)
        ksc[:, hr * 2 + 0] = (np.exp((127.0 - j) * lg) * sc)[:, 0]
        ksc[:, hr * 2 + 1] = (np.exp(j * lg) * sc)[:, 0]
        gc[:, hr] = np.exp(128.0 * lg)
    cst["dmask"], cst["qsc"], cst["ksc"], cst["gc"] = dmask, qsc, ksc, gc
    n = 32
    inv = 10000.0 ** (-np.arange(n, dtype=np.float64) / n)
    tpos = np.arange(SEQ)
    ang = np.zeros((128, SEQ), np.float64)
    for d in range(128):
        if d < 64:
            ang[d] = (tpos // 64) * inv[d % 32]
        else:
            ang[d] = (tpos % 64) * inv[d % 32]
    cst["ropec"] = np.cos(ang).astype(np.float32)
    cst["ropes"] = np.sin(ang).astype(np.float32)
    pm = np.zeros((128, 128), np.float32)
    for dp in range(128):
        if (dp % 64) < 32:
            pm[dp + 32, dp] = -1.0
        else:
            pm[dp - 32, dp] = 1.0
    cst["pm"] = pm
    sel = np.zeros((2, 130), np.float32)
    sel[0, 0] = 1.0
    sel[1, 1] = 1.0
    sel[0, 2:] = 1.0
    cst["sel"] = sel
    return cst


def _ktile(w):
    K, N = w.shape
    return np.ascontiguousarray(w.reshape(K // 128, 128, N).transpose(1, 0, 2))


def _col(v):
    return np.ascontiguousarray(v.reshape(-1, 128).T)


def build(stop_after=None, dbg=False):
    nc = bass.Bass("TRN2", target_bir_lowering=False)
    P = Prog(nc)
    P.trace_lines = dbg
    DBG["P"] = P
    es = ExitStack()

    def din(name, shape, dt=F32):
        return nc.dram_tensor(name, list(shape), dt, kind="ExternalInput").ap()

    xa = din("xa", [SEQ + CTX + 4, D])
    xb = din("xb", [1152, D])
    cvt = din("cvt", [128, 16, 2])
    wada = din("wada", [24, 128, 16, 512])
    bada = din("bada", [1, 12288])
    nrm_col = din("nrm_col", [128, 2, 16])
    nrm_row = din("nrm_row", [2, D])
    win = din("win", [128, 16, NCOLS])
    rwconv = din("rwconv", [128, 10, 3])
    colv = din("colv", [128, 2, 9])
    w0row = din("w0row", [1, 2 * 256])
    wup = din("wup", [128, 256])
    aup = din("aup", [128, 256])
    gup = din("gup", [160, 256])
    wout = din("wout", [128, 16, D])
    wgate = din("wgate", [NFT, 128, 16, 128])
    wupf = din("wupf", [NFT, 128, 16, 128])
    wdown = din("wdown", [16, 128, NFT, 128])
    fconv = din("fconv", [128, NFT, 10])
    hmask = din("hmask", [128, 2])
    tsel = din("tsel", [128, 8])
    c_ident = din("c_ident", [128, 128])
    c_bones = din("c_bones", [128, 128])
    c_ones = din("c_ones", [128, 128])
    c_tri = din("c_tri", [128, 4, 128])
    c_mst = din("c_mst", [64, 2, 128])
    c_m1 = din("c_m1", [64, 2, 64])
    c_dmask = din("c_dmask", [128, 4, 128])
    c_qsc = din("c_qsc", [128, 4, 128])
    c_ksc = din("c_ksc", [128, 4])
    c_gc = din("c_gc", [128, 2])
    c_ropec = din("c_ropec", [128, SEQ])
    c_ropes = din("c_ropes", [128, SEQ])
    c_pm = din("c_pm", [128, 128])
    c_sel = din("c_sel", [2, 130])
    out_d = None
    yspill = nc.dram_tensor("yspill", [4, 128, SEQ], F32).ap()
    TWO_A = stop_after == "A2L"
    TWO_B = stop_after == "B2L"
    osend = nc.dram_tensor("osend", [512, SEQ], BF16, kind=("ExternalOutput" if TWO_A else "Internal"))
    if not TWO_A:
        out_d = nc.dram_tensor("out", [1024, D], F32, kind="ExternalOutput").ap()
    if TWO_A:
        modcol_o = nc.dram_tensor("modcol_o", [128, 96], F32, kind="ExternalOutput").ap()
        a12_o = nc.dram_tensor("a12_o", [1, 2 * D], F32, kind="ExternalOutput").ap()
    if TWO_B:
        oin = din("oin", [128, 16, 1152], BF16)
        modcol_i = din("modcol_i", [128, 96])
        a12_i = din("a12_i", [1, 2 * D])
    ogath = nc.dram_tensor("ogath", [8 * 512, SEQ], BF16)
    x1sp = nc.dram_tensor("x1sp", [1024, D], F32).ap()
    dbg_d = None
    if dbg:
        dbg_d = nc.dram_tensor("dbg", [128, 4096], F32, kind="ExternalOutput").ap()

    def sb(name, shape, dt=F32):
        return es.enter_context(nc.sbuf_tensor(name, list(shape), dt))

    ident = sb("ident", [128, 128])
    bones = sb("bones", [128, 128])
    ones = sb("ones", [128, 128])
    tri = sb("tri", [128, 4, 128])
    mst = sb("mst", [64, 2, 128])
    m1m = sb("m1m", [64, 2, 64])
    dmask = sb("dmask", [128, 4, 128])
    qsc = sb("qsc", [128, 4, 128])
    ksc = sb("ksc", [128, 4])
    gcs = sb("gcs", [128, 2])
    pm = sb("pm", [128, 128])
    sel = sb("sel", [2, 130])
    nrmc = sb("nrmc", [128, 2, 16])
    rwcv = sb("rwcv", [128, 10, 3])
    colvs = sb("colvs", [128, 2, 9])
    w0bc = sb("w0bc", [128, 512])
    wup_s = sb("wup_s", [128, 256], BF16)
    aup_s = sb("aup_s", [128, 256], BF16)
    gup_a = sb("gup_a", [128, 256], BF16)
    gup_b = sb("gup_b", [32, 256], BF16)
    hmask_s = sb("hmask_s", [128, 2])
    tsel_s = sb("tsel_s", [128, 8])
    fconv_s = sb("fconv_s", [128, NFT, 10])
    modcol = sb("modcol", [128, 6, 16])
    ARENA_W = 48800
    arena = sb("arena", [128, ARENA_W])
    psb = [es.enter_context(nc.psum_tensor("psb%d" % i, [128, 512], F32)) for i in range(8)]

    class Arena:
        def __init__(self):
            self.off = 0

        def f32(self, n):
            o = self.off
            self.off += n
            assert self.off <= ARENA_W, self.off
            return arena[:, o:o + n]

        def bf16(self, n):
            w = (n + 1) // 2
            o = self.off
            self.off += w
            assert self.off <= ARENA_W, self.off
            return arena[:, o:o + w].bitcast(BF16)

    V, S, T, G, PE = "dve", "act", "pe", "pool", "pe"

    def tt(out, a, b, op, R, W, eng="dve"):
        P.op(eng, lambda e: e.tensor_tensor(out=out, in0=a, in1=b, op=op), R, W)

    def ts(out, a, s1, s2, op0, op1, R, W, eng="dve"):
        if s2 is None:
            P.op(eng, lambda e: e.tensor_scalar(out=out, in0=a, scalar1=s1, scalar2=None, op0=op0), R, W)
        else:
            P.op(eng, lambda e: e.tensor_scalar(out=out, in0=a, scalar1=s1, scalar2=s2, op0=op0, op1=op1), R, W)

    def stt(out, a, s, b, op0, op1, R, W, eng="dve"):
        P.op(eng, lambda e: e.scalar_tensor_tensor(out=out, in0=a, scalar=s, in1=b, op0=op0, op1=op1), R, W)

    def act(out, a, func, R, W, bias=None, scale=None, accum=None):
        kw = {}
        if bias is not None:
            kw["bias"] = bias
        if scale is not None:
            kw["scale"] = scale
        if accum is not None:
            kw["accum_out"] = accum
        P.op("act", lambda e: e.activation(out=out, in_=a, func=func, **kw), R, W)

    epsc = sb("epsc", [128, 4])
    identh = sb("identh", [128, 128], BF16)

    def rsq(out, a, scale, biasv, R, W):
        bi = {EPS: 0, LNX_EPS: 1, 1e-12: 2}[biasv]
        P.op("act", lambda e: e.activation(out=out, in_=a, func=AF.Sqrt, bias=epsc[0:out.shape[0], bi:bi + 1], scale=scale), list(R) + ["epsc"], W)
        P.op("dve", lambda e: e.reciprocal(out=out, in_=out), W, W)

    def cp(out, a, R, W, eng="act"):
        if eng == "act":
            P.op("act", lambda e: e.copy(out=out, in_=a), R, W)
        else:
            P.op(eng, lambda e: e.tensor_copy(out=out, in_=a), R, W)

    def mm(out, lhsT, rhs, R, W, start=True, stop=True):
        P.op("pe", lambda e: e.matmul(out, lhsT, rhs, start=start, stop=stop), R, W)

    def trp(out, in_, idn, R, W):
        P.op("pe", lambda e: e.transpose(out, in_, idn), R, W)

    def memset(ap, val, W, eng="dve"):
        P.op(eng, lambda e: e.memset(ap, val), (), W)

    dq = ["sp", "act"]
    dqi = [0]

    def load(out, in_, key, W, R=(), eng=None):
        if eng is None:
            eng = "sp"
        return P.dma(eng, out, in_, key, reads=R, writes=W)

    memset(epsc[:, 0:1], EPS, ["epsc"])
    memset(epsc[:, 1:2], LNX_EPS, ["epsc"])
    memset(epsc[:, 2:3], 1e-12, ["epsc"])
    CK = "const"
    for (dst, src, nm) in [(ident, c_ident, "ident"), (bones, c_bones, "bones"), (ones, c_ones, "ones"),
                           (tri, c_tri, "tri"), (mst, c_mst, "mst"), (m1m, c_m1, "m1m"), (dmask, c_dmask, "dmask"),
                           (qsc, c_qsc, "qsc"), (ksc, c_ksc, "ksc"), (gcs, c_gc, "gcs"), (pm, c_pm, "pm"),
                           (sel, c_sel, "sel"), (nrmc, nrm_col, "nrmc"), (rwcv, rwconv, "rwcv"),
                           (colvs, colv, "colvs"), (hmask_s, hmask, "hmask"), (tsel_s, tsel, "tsel"),
                           (fconv_s, fconv, "fconv")]:
        load(dst[:], src, CK, [nm])
    load(w0bc[:], w0row.partition_broadcast(128), CK, ["w0bc"])
    P.op("act", lambda e: e.copy(out=identh[:], in_=ident[:]), ["ident"], ["identh"])
    P.dma("pool", wup_s[:], wup, "constp", writes=["wup"])
    P.dma("pool", aup_s[:], aup, "constp", writes=["aup"])
    P.dma("pool", gup_a[:], gup[0:128, :], "constp", writes=["gupa"])
    P.dma("pool", gup_b[:], gup[128:160, :], "constp", writes=["gupb"])

    A0 = Arena()
    wa_buf = [A0.f32(16 * 512).rearrange("p (k n) -> p k n", k=16) for _ in range(2)]
    modrow = A0.f32(12288)
    badab = [A0.f32(512) for _ in range(2)]
    npost = A0.f32(2 * D).rearrange("p (a n) -> p a n", a=2)
    A12 = A0.f32(2 * D).rearrange("p (a n) -> p a n", a=2)
    a12sp = nc.dram_tensor("a12sp", [1, 2 * D], F32).ap()
    scT = sb("scT", [128, 16, 2])
    load(scT[:], cvt, CK, ["scT"])
    load(npost[:], nrm_row.rearrange("a n -> (a n)").partition_broadcast(128).rearrange("p (a n) -> p a n", a=2)
         if False else nrm_row.unsqueeze(0).to_broadcast([128, 2, D]), CK, ["npost"])
    act(scT[:], scT[:], AF.Silu, ["scT"], ["scT"])
    for n in range(24):
        wb = wa_buf[n % 2]
        load(wb, wada[n], "wada%d" % (n % 2), ["wab%d" % (n % 2)])
        load(badab[n % 2][0:2, :], bada[:, n * 512:(n + 1) * 512].partition_broadcast(2), "wada%d" % (n % 2), ["wab%d" % (n % 2)])
        ps = psb[n % 2]
        for kt in range(16):
            mm(ps[0:2, :], scT[:, kt, :], wb[:, kt, :], ["scT", "wab%d" % (n % 2)], ["psb%d" % (n % 2)],
               start=(kt == 0), stop=(kt == 15))
        tt(modrow[0:2, n * 512:(n + 1) * 512], ps[0:2, :], badab[n % 2][0:2, :], ALU.add,
           ["psb%d" % (n % 2), "wab%d" % (n % 2)], ["modrow"])
    segs = [(0, 0), (0, 1), (0, 3), (0, 4), (1, 0), (1, 1)]
    psc = psb[2]
    for i, (r, sg) in enumerate(segs):
        for kt in range(16):
            mm(psc[:, i * 16 + kt:i * 16 + kt + 1], modrow[0:2, sg * D + kt * 128: sg * D + (kt + 1) * 128],
               sel[0:2, r:r + 1], ["modrow", "sel"], ["psb2"])
    cp(modcol[:].rearrange("p a k -> p (a k)"), psc[:, 0:96], ["psb2"], ["modcol"], eng="dve")
    for (i, nidx) in [(1, 0), (3, 1), (5, 0)]:
        stt(modcol[:, i, :], modcol[:, i, :], 1.0, nrmc[:, nidx, :], ALU.add, ALU.mult, ["modcol", "nrmc"], ["modcol"])
    for a, sg in enumerate([2, 5]):
        for j in range(4):
            ps = psb[3 + (j % 2)]
            mm(ps[:, :], sel[0:2, 2:130], modrow[0:2, sg * D + j * 512: sg * D + (j + 1) * 512], ["modrow", "sel"],
               ["psb%d" % (3 + j % 2)])
            tt(A12[:, a, j * 512:(j + 1) * 512], ps[:, :], npost[:, a, j * 512:(j + 1) * 512], ALU.mult,
               ["psb%d" % (3 + j % 2), "npost"], ["A12"])
    P.dma("sp", a12sp, A12[0:1].rearrange("p a n -> p (a n)"), "a12st", reads=["A12"], writes=["a12sp"])
    if TWO_A:
        P.dma("sp", a12_o, A12[0:1].rearrange("p a n -> p (a n)"), "a12o", reads=["A12"], writes=["a12_o"])
    ph0_res = ["wab0", "wab1", "modrow", "badab", "npost", "A12"]

    if dbg and stop_after == 0:
        dtile = A0.f32(4096)
        memset(dtile[:], 0.0, ["dtile"])
        cp(dtile[:, 0:96], modcol[:].rearrange("p a k -> p (a k)"), ["modcol"], ["dtile"], eng="dve")
        cp(dtile[:, 128:128 + 2048], A12[:, 0, :], ["A12"], ["dtile"], eng="dve")
        ev = P.dma("sp", dbg_d, dtile[:], "dbgout", reads=["dtile"], writes=["dbg_d"])
        P.emit(es, {"sp": [("dbgout", 16)]})
        es.close()
        return nc

    AA = Arena()
    for nm in ["winb", "xt0", "xt1", "sqj", "xmT", "rwT", "retT"]:
        pass
    vtm = AA.bf16(2 * 2 * 128).rearrange("p (c h n) -> p c h n", c=2, h=2)
    winb = AA.bf16(16 * NCOLS).rearrange("p (k n) -> p k n", k=16)
    xts = [AA.f32(D) for _ in range(2)]
    sqj = AA.bf16(D)
    xmT = AA.bf16(16 * NWIN).rearrange("p (k n) -> p k n", k=16)
    rwT = AA.f32(10 * BLK).rearrange("p (k n) -> p k n", k=10)
    retT = AA.f32(8 * BLK).rearrange("p (k n) -> p k n", k=8)
    stat = AA.f32(8)

    def t2(dt=F32):
        if dt == F32:
            return AA.f32(2 * BLK).rearrange("p (k n) -> p k n", k=2)
        return AA.bf16(2 * BLK).rearrange("p (k n) -> p k n", k=2)

    thb = AA.bf16(BLK)
    adb = AA.bf16(BLK)
    gsb = AA.bf16(2 * BLK).rearrange("p (k n) -> p k n", k=2)
    sigtm = t2()
    cumT, cumpT, aT, kkT, tmpA, tmpB, Ep, Em, Epv, kdT, BhT, KhT, yT = [t2() for _ in range(13)]
    aT2, kd2T = tmpB, BhT
    ART = AA.bf16(2 * 4 * 128).rearrange("p (k c n) -> p k c n", k=2, c=4)
    BKT = AA.bf16(2 * 4 * 128).rearrange("p (k c n) -> p k c n", k=2, c=4)
    BKV = AA.bf16(4 * 3 * 2 * 2 * 128).rearrange("p (c m k h n) -> p c m k h n", c=4, m=3, k=2, h=2)
    Ms = AA.bf16(4 * 2 * 128).rearrange("p (h m n) -> p h m n", h=4, m=2)
    M1s = AA.bf16(4 * 64).rearrange("p (h n) -> p h n", h=4)
    Pa = [AA.bf16(4 * 2 * 64).rearrange("p (h m n) -> p h m n", h=4, m=2) for _ in range(2)]
    Qs = [AA.bf16(4 * 64).rearrange("p (h n) -> p h n", h=4) for _ in range(2)]
    Wsb = AA.bf16(2 * 128).rearrange("p (k n) -> p k n", k=2)
    Usb = AA.bf16(2 * 2 * 128).rearrange("p (k h n) -> p k h n", k=2, h=2)
    Hst = AA.f32(2 * 128).rearrange("p (k n) -> p k n", k=2)
    Hb = AA.bf16(2 * 128).rearrange("p (k n) -> p k n", k=2)
    qr = t2(BF16)
    kr = t2(BF16)
    krf = t2()
    qtl = t2(BF16)
    rc_t = AA.f32(BLK)
    rs_t = AA.f32(BLK)
    ktm = AA.bf16(2 * 2 * 128).rearrange("p (c h n) -> p c h n", c=2, h=2)
    Ssb = AA.bf16(2 * 128).rearrange("p (h n) -> p h n", h=2)
    Sst = AA.f32(2 * 128).rearrange("p (h n) -> p h n", h=2)
    Sbf = AA.bf16(2 * 128).rearrange("p (h n) -> p h n", h=2)
    yrT = t2()
    vrb = t2(BF16)
    ysp = AA.f32(4 * BLK).rearrange("p (k n) -> p k n", k=4)
    print("AA.off", AA.off)
    oT = AA.bf16(4 * BLK).rearrange("p (k n) -> p k n", k=4)
    Aall = ["winb", "xt0", "xt1", "sqj", "xmT", "rwT", "retT", "stat", "thb", "adb", "gsb", "sigtm", "cumT", "cumpT",
            "aT", "aT2", "kkT", "tmpA", "tmpB", "Ep", "Em", "Epv", "kdT", "kd2T", "BhT", "KhT", "yT", "ART", "BKT",
            "BKV", "Ms", "M1s", "Pa0", "Pa1", "Qs0", "Qs1", "Wsb", "Usb", "Hst", "Hb", "qr", "kr", "krf", "qtl",
            "rc_t", "rs_t", "vtm", "vrb", "ktm", "Ssb", "Sst", "Sbf", "yrT", "ysp", "oT"]
    for nm in Aall:
        P.alias(nm, ph0_res)
    P.dma("pool", winb, win, "constp", writes=["winb"])
    memset(BKV[0:64], 0.0, ["BKV"])
    memset(Usb[0:64], 0.0, ["Usb"])

    PSN = ["psb%d" % i for i in range(8)]

    def block(sw, seq, bi, nblk):
        lat = seq == "l"
        ro = lat
        row0 = (0 if not lat else 258) + bi * BLK
        s1i, shi_ = (5, 4) if not lat else (1, 0)
        for j in range(3):
            nr = 128 if j < 2 else 2
            xt = xts[j % 2]
            xn = "xt%d" % (j % 2)
            load(xt[0:nr, :], xa[row0 + j * 128: row0 + j * 128 + nr, :], "xk%d" % (j % 2), [xn])
            act(sqj[0:nr, :], xt[0:nr, :], AF.Square, [xn], ["sqj", "stat"], accum=stat[0:nr, 0:1])
            rsq(stat[0:nr, 1:2], stat[0:nr, 0:1], 1.0 / D, EPS, ["stat"], ["stat"])
            ts(xt[0:nr, :], xt[0:nr, :], stat[0:nr, 1:2], None, ALU.mult, None, [xn, "stat"], [xn])
            for k4 in range(4):
                ps = psb[k4 % 2]
                pn = PSN[k4 % 2]
                for kk_ in range(4):
                    kt = k4 * 4 + kk_
                    trp(ps[:, kk_ * 128: kk_ * 128 + nr], xt[0:nr, kt * 128:(kt + 1) * 128], ident[0:nr, 0:nr], [xn, "ident"], [pn])
                for kk_ in range(4):
                    kt = k4 * 4 + kk_
                    if kk_ % 2 == 0:
                        act(xmT[:, kt, j * 128: j * 128 + nr], ps[:, kk_ * 128: kk_ * 128 + nr], AF.Identity, [pn, "modcol"], ["xmT"],
                            bias=modcol[:, shi_, kt:kt + 1], scale=modcol[:, s1i, kt:kt + 1])
                    else:
                        ts(xmT[:, kt, j * 128: j * 128 + nr], ps[:, kk_ * 128: kk_ * 128 + nr], modcol[:, s1i, kt:kt + 1],
                           modcol[:, shi_, kt:kt + 1], ALU.mult, ALU.add, [pn, "modcol"], ["xmT"])
        if bi == 0:
            memset(xmT[:, :, 0:1], 0.0, ["xmT"])
        if bi == nblk - 1:
            memset(xmT[:, :, 257:258], 0.0, ["xmT"])
        tiles = [0, 1, 2, 3, 4, 5, 6, 7, 8, 9, 10, 11, 12, 13, 14, 15, 16, 17] if ro else [0, 1, 2, 3, 4, 5, 6, 7, 10, 11, 12, 13]
        for ti, mt in enumerate(tiles):
            ps = psb[ti % 2]
            pn = PSN[ti % 2]
            if mt < 9:
                c0, mw = mt * 128, 128
            elif mt == 9:
                c0, mw = 1152, 32
            else:
                c0, mw = 1184 + (mt - 10) * 128, 128
            for kt in range(16):
                mm(ps[0:mw, 0:NWIN], winb[:, kt, c0:c0 + mw], xmT[:, kt, :], ["winb", "xmT"], [pn], start=(kt == 0), stop=(kt == 15))
            if mt < 10:
                ts(rwT[0:mw, mt, :], ps[0:mw, 1:257], rwcv[0:mw, mt, 1:2], None, ALU.mult, None, [pn, "rwcv"], ["rwT"])
                stt(rwT[0:mw, mt, :], ps[0:mw, 0:256], rwcv[0:mw, mt, 0:1], rwT[0:mw, mt, :], ALU.mult, ALU.add, [pn, "rwcv", "rwT"], ["rwT"])
                stt(rwT[0:mw, mt, :], ps[0:mw, 2:258], rwcv[0:mw, mt, 2:3], rwT[0:mw, mt, :], ALU.mult, ALU.add, [pn, "rwcv", "rwT"], ["rwT"])
            else:
                cp(retT[:, mt - 10, :], ps[:, 1:257], [pn], ["retT"])
        if dbg and stop_after == 1 and seq == "l" and bi == 0 and sw == 0:
            return "dbg1"
        dp = slice(0, 64) if sw == 0 else slice(64, 128)
        v4 = lambda ap: ap.rearrange("p (c n) -> p c n", c=4)
        cend = 63 if sw == 0 else 0
        act(thb[dp, :], rwT[dp, 6, :], AF.Tanh, ["rwT"], ["thb"])
        cp(adb[:, :], rwT[:, 7, :], ["rwT"], ["adb"])
        for t_ in range(2):
            ps = psb[2 + t_]
            pn = PSN[2 + t_]
            mm(ps[:, 0:256], thb[dp, t_ * 128:(t_ + 1) * 128], wup_s[dp, :], ["thb", "wup"], [pn])
            tt(sigtm[:, t_, :], ps[:, 0:256], w0bc[:, sw * 256:(sw + 1) * 256], ALU.add, [pn, "w0bc"], ["sigtm"])
        act(sigtm[:], sigtm[:], AF.Sigmoid, ["sigtm"], ["sigtm"])
        for which, dst, dn in [(0, cumT, "cumT"), (1, cumpT, "cumpT")]:
            ps = psb[2 + which]
            pn = PSN[2 + which]
            for ct in range(2):
                for t_ in range(2):
                    mm(ps[:, ct * 256 + t_ * 128: ct * 256 + (t_ + 1) * 128], sigtm[:, t_, ct * 128:(ct + 1) * 128],
                       tri[:, 2 * sw + which, :], ["sigtm", "tri"], [pn])
            cp(dst[:].rearrange("p k n -> p (k n)"), ps[:, :], [pn], [dn])
        act(Ep[:], cumT[:], AF.Exp, ["cumT"], ["Ep"], scale=-DECAY_C)
        act(Em[:], cumT[:], AF.Exp, ["cumT"], ["Em"], scale=DECAY_C)
        act(Epv[:], cumpT[:], AF.Exp, ["cumpT"], ["Epv"], scale=-DECAY_C)
        for ct in range(2):
            cs_ = slice(ct * 128, (ct + 1) * 128)
            ps = psb[2 + ct]
            pn = PSN[2 + ct]
            mm(ps[:, 0:256], aup_s[dp, cs_], adb[dp, :], ["aup", "adb"], [pn])
            act(aT[:, ct, :], ps[:, 0:256], AF.Sigmoid, [pn, "colvs"], ["aT"], bias=colvs[:, ct, 5 + sw:6 + sw])
            ts(tmpA[:, ct, :], rwT[:, ct, :], colvs[:, ct, 0:1], None, ALU.mult, None, ["rwT", "colvs"], ["tmpA"])
            tt(tmpB[:, ct, :], tmpA[:, ct, :], tmpA[:, ct, :], ALU.mult, ["tmpA"], ["tmpB"])
            mm(ps[:, 256:512], bones[:, :], tmpB[:, ct, :], ["bones", "tmpB"], [pn])
            rsq(tmpB[:, ct, :], ps[:, 256:512], 1.0, 1e-12, [pn], ["tmpB"])
            tt(kkT[:, ct, :], tmpA[:, ct, :], tmpB[:, ct, :], ALU.mult, ["tmpA", "tmpB"], ["kkT"])
            stt(ART[:, ct, :, 0:64], v4(kkT[:, ct, :]), -1.0, v4(Epv[:, ct, :]), ALU.mult, ALU.mult, ["kkT", "Epv"], ["ART"])
            tt(ART[:, ct, :, 64:128], v4(rwT[:, 4 + ct, :]), v4(Ep[:, ct, :]), ALU.mult, ["rwT", "Ep"], ["ART"])
            gcb = v4(Ep[:, ct, :])[:, :, cend:cend + 1].to_broadcast([128, 4, 64])
            tt(tmpB[:, ct, :], kkT[:, ct, :], aT[:, ct, :], ALU.mult, ["kkT", "aT"], ["tmpB"])
            tt(BhT[:, ct, :], tmpB[:, ct, :], Em[:, ct, :], ALU.mult, ["tmpB", "Em"], ["BhT"])
            cp(BKT[:, ct, :, 0:64], v4(BhT[:, ct, :]), ["BhT"], ["BKT"])
            tt(v4(BhT[:, ct, :]), v4(BhT[:, ct, :]), gcb, ALU.mult, ["BhT", "Ep"], ["BhT"])
            ts(tmpA[:, ct, :], aT[:, ct, :], colvs[:, ct, 1:2], colvs[:, ct, 7:8], ALU.mult, ALU.add, ["aT", "colvs"], ["tmpA"])
            tt(kdT[:, ct, :], rwT[:, ct, :], tmpA[:, ct, :], ALU.mult, ["rwT", "tmpA"], ["kdT"])
            tt(KhT[:, ct, :], kdT[:, ct, :], Em[:, ct, :], ALU.mult, ["kdT", "Em"], ["KhT"])
            cp(BKT[:, ct, :, 64:128], v4(KhT[:, ct, :]), ["KhT"], ["BKT"])
            tt(v4(KhT[:, ct, :]), v4(KhT[:, ct, :]), gcb, ALU.mult, ["KhT", "Ep"], ["KhT"])
        if dbg and stop_after == 2.1:
            return "dbg2"
        for c in range(4):
            for ct in range(2):
                ps = psb[2 + ct]
                pn = PSN[2 + ct]
                for m_, (src, sn) in enumerate([(BhT[:, ct, :], "BhT"), (KhT[:, ct, :], "KhT"), (rwT[:, 2 + ct, :], "rwT")]):
                    trp(ps[0:64, m_ * 128:(m_ + 1) * 128], src[:, c * 64:(c + 1) * 64], ident[:, :], [sn, "ident"], [pn])
                pv = ps[0:64, 0:384].rearrange("p (m n) -> p m n", m=3)
                for hh in range(2):
                    cp(BKV[0:64, c, :, ct, hh, hh * 64:(hh + 1) * 64], pv[:, :, hh * 64:(hh + 1) * 64], [pn], ["BKV"],
                       eng=("act" if hh == 0 else "dve"))
        if dbg and stop_after == 2.2:
            return "dbg2"
        identb = ident[0:64, 0:64].unsqueeze(1).to_broadcast([64, 4, 64])
        for c in (range(4) if sw == 0 else range(3, -1, -1)):
            for h in range(4):
                ct, hh = h // 2, h % 2
                hp = slice(hh * 64, hh * 64 + 64)
                pb = psb[4 + hh]
                mm(pb[0:64, (ct * 2) * 128:(ct * 2 + 1) * 128], BKT[hp, ct, c, 0:64], ART[hp, ct, c, :], ["BKT", "ART"], [PSN[4 + hh]])
                mm(pb[0:64, (ct * 2 + 1) * 128:(ct * 2 + 2) * 128], BKT[hp, ct, c, 64:128], ART[hp, ct, c, :], ["BKT", "ART"], [PSN[4 + hh]])
                mm(psb[6 + hh][0:64, ct * 64:(ct + 1) * 64], ART[hp, ct, c, 0:64], BKT[hp, ct, c, 0:64], ["BKT", "ART"], [("psb6a" if hh == 0 else "psb7")])
            if dbg and stop_after == 2.21:
                return "dbg2"
            for hh in range(2):
                tt(Ms[0:64, 2 * hh:2 * hh + 2, :, :].rearrange("p h m n -> p (h m) n"),
                   psb[4 + hh][0:64, :].rearrange("p (a n) -> p a n", a=4),
                   mst[:, sw, :].unsqueeze(1).to_broadcast([64, 4, 128]), ALU.mult, [PSN[4 + hh], "mst"], ["Ms"])
                tt(M1s[0:64, 2 * hh:2 * hh + 2, :], psb[6 + hh][0:64, 0:128].rearrange("p (a n) -> p a n", a=2),
                   m1m[:, sw, :].unsqueeze(1).to_broadcast([64, 2, 64]), ALU.mult, [("psb6a" if hh == 0 else "psb7"), "m1m"], ["M1s"])
            if dbg and stop_after == 2.22:
                return "dbg2"
            tt(Qs[0][0:64, :, :], Ms[0:64, :, 0, 0:64], identb, ALU.add, ["Ms", "ident"], ["Qs0"])
            if dbg and stop_after == 2.23:
                return "dbg2"
            pN = psb[7][0:64, :].rearrange("p (h m n) -> p h m n", h=4, m=2)
            pQ = psb[6][0:64, 256:512].rearrange("p (h n) -> p h n", h=4)
            for k in range(1, 6):
                cur, prv = k % 2, (k - 1) % 2
                for h in range(4):
                    if k == 1:
                        Pp, Ppp = Ms[0:64, h, 0, 0:64], M1s[0:64, h, :]
                        rn_ = ["Ms", "M1s"]
                    else:
                        Pp, Ppp = Pa[prv][0:64, h, 0, :], Pa[prv][0:64, h, 1, :]
                        rn_ = ["Pa%d" % prv]
                    mm(pN[:, h, 0, :], Ppp, Pp, rn_, ["psb7"])
                    mm(pN[:, h, 1, :], Pp, Ppp, rn_, ["psb7"])
                cp(Pa[cur][0:64].rearrange("p h m n -> p (h m n)"), psb[7][0:64, :], ["psb7"], ["Pa%d" % cur])
                for h in range(4):
                    mm(pQ[:, h, :], Pa[cur][0:64, h, 1, :], Qs[prv][0:64, h, :], ["Pa%d" % cur, "Qs%d" % prv], ["psb6b"])
                tt(Qs[cur][0:64, :, :], pQ, Qs[prv][0:64, :, :], ALU.add, ["psb6b", "Qs%d" % prv], ["Qs%d" % cur])
            TT = Qs[1]
            if dbg and stop_after == 2.3:
                return "dbg2"
            pW = psb[2][0:64, 0:256].rearrange("p (k n) -> p k n", k=2)
            for ct in range(2):
                mm(pW[:, ct, :], ART[:, ct, c, 0:64], Hb[:, ct, :], ["ART", "Hb"], ["psb2"], start=True, stop=False)
                for hh in range(2):
                    mm(pW[:, ct, :], Ms[0:64, hh * 2 + ct, 1, 0:64], BKV[0:64, c, 2, ct, hh, :], ["Ms", "BKV"], ["psb2"],
                       start=False, stop=(hh == 1))
            cp(Wsb[0:64, :, :], pW, ["psb2"], ["Wsb"])
            pU = psb[3][0:64, 0:256].rearrange("p (k n) -> p k n", k=2)
            for ct in range(2):
                for hh in range(2):
                    mm(pU[:, ct, hh * 64:(hh + 1) * 64], TT[0:64, hh * 2 + ct, :], Wsb[0:64, ct, hh * 64:(hh + 1) * 64],
                       ["Qs1", "Wsb"], ["psb3"])
            for hh in range(2):
                cp(Usb[0:64, :, hh, hh * 64:(hh + 1) * 64], pU[:, :, hh * 64:(hh + 1) * 64], ["psb3"], ["Usb"],
                   eng=("act" if hh == 0 else "dve"))
            if ro:
                pY = psb[0][:, 0:128].rearrange("p (k n) -> p k n", k=2)
                for ct in range(2):
                    mm(pY[:, ct, :], Hb[:, ct, :], ART[:, ct, c, 64:128], ["Hb", "ART"], ["psb0"], start=True, stop=False)
                    for hh in range(2):
                        mm(pY[:, ct, :], Usb[0:64, ct, hh, :], Ms[0:64, hh * 2 + ct, 0, 64:128], ["Usb", "Ms"], ["psb0"],
                           start=False, stop=False)
                        mm(pY[:, ct, :], BKV[0:64, c, 2, ct, hh, :], Ms[0:64, hh * 2 + ct, 1, 64:128], ["BKV", "Ms"], ["psb0"],
                           start=False, stop=(hh == 1))
                cp(yT[:, :, c * 64:(c + 1) * 64], pY, ["psb0"], ["yT"])
            pH = psb[1][:, 0:256].rearrange("p (k n) -> p k n", k=2)
            for ct in range(2):
                for hh in range(2):
                    mm(pH[:, ct, :], BKV[0:64, c, 0, ct, hh, :], Usb[0:64, ct, hh, :], ["BKV", "Usb"], ["psb1"],
                       start=(hh == 0), stop=False)
                    mm(pH[:, ct, :], BKV[0:64, c, 1, ct, hh, :], BKV[0:64, c, 2, ct, hh, :], ["BKV"], ["psb1"],
                       start=False, stop=(hh == 1))
                gcol = Ep[:, ct, c * 64 + cend: c * 64 + cend + 1]
                stt(Hst[:, ct, :], Hst[:, ct, :], gcol, pH[:, ct, :], ALU.mult, ALU.add, ["Hst", "Ep", "psb1"], ["Hst"])
            cp(Hb[:], Hst[:], ["Hst"], ["Hb"])
        if dbg and stop_after == 2 and seq == "l" and bi == 0 and sw == 0:
            return "dbg2"
        if dbg and stop_after == 2.5:
            return "dbg2"
        cols = slice(bi * BLK, (bi + 1) * BLK)
        if lat:
            load(rc_t, c_ropec[:, cols], "ropek", ["rope"])
            load(rs_t, c_ropes[:, cols], "ropek", ["rope"])
            for hr in range(2):
                for isk, src_t in [(True, retT[:, hr, :]), (False, retT[:, 4 + hr, :])]:
                    ps = psb[2 + (0 if isk else 1)]
                    pn = PSN[2 + (0 if isk else 1)]
                    mm(ps[:, 0:256], pm[:, :], src_t, ["pm", "retT"], [pn])
                    tt(tmpA[:, 0, :], ps[:, 0:256], rs_t, ALU.mult, [pn, "rope"], ["tmpA"])
                    tt(tmpB[:, 0, :], src_t, rc_t, ALU.mult, ["retT", "rope"], ["tmpB"])
                    if isk:
                        tt(krf[:, hr, :], tmpA[:, 0, :], tmpB[:, 0, :], ALU.add, ["tmpA", "tmpB"], ["krf"])
                        cp(kr[:, hr, :], krf[:, hr, :], ["krf"], ["kr"])
                    else:
                        tt(qr[:, hr, :], tmpA[:, 0, :], tmpB[:, 0, :], ALU.add, ["tmpA", "tmpB"], ["qr"])
        else:
            for hr in range(2):
                cp(krf[:, hr, :], retT[:, hr, :], ["retT"], ["krf"], eng="dve")
                cp(kr[:, hr, :], retT[:, hr, :], ["retT"], ["kr"])
        if dbg and stop_after == 2.61:
            return "dbg2"
        for c2 in range(2):
            cc = slice(c2 * 128, (c2 + 1) * 128)
            for hr in range(2):
                ps = psb[2 + hr]
                pn = PSN[2 + hr]
                cp(vrb[:, hr, cc], retT[:, 2 + hr, cc], ["retT"], ["vrb"])
                mm(ps[:, 0:128], vrb[:, hr, cc], identh[:, :], ["vrb", "identh"], [pn])
                mm(ps[:, 128:256], kr[:, hr, cc], identh[:, :], ["kr", "identh"], [pn])
                import os as _os
                cp(vtm[:, c2, hr, :], ps[:, 0:128], [pn], ["vtm"], eng="dve")
                if True:
                    ts(ktm[:, c2, hr, :], ps[:, 128:256], ksc[:, hr * 2 + sw:hr * 2 + sw + 1], None, ALU.mult, None, [pn, "ksc"], ["ktm"])
        if dbg and stop_after == 2.62:
            return "dbg2"
        for c2 in (range(2) if sw == 0 else (1, 0)):
            cc = slice(c2 * 128, (c2 + 1) * 128)
            for hr in range(2):
                hc = slice(hr * 128, (hr + 1) * 128)
                if ro:
                    mm(psb[2][:, hc], kr[:, hr, cc], qr[:, hr, cc], ["kr", "qr"], ["psb2"])
                    tt(Ssb[:, hr, :], psb[2][:, hc], dmask[:, hr * 2 + sw, :], ALU.mult, ["psb2", "dmask"], ["Ssb"])
                    tt(qtl[:, hr, cc], qr[:, hr, cc], qsc[:, hr * 2 + sw, :], ALU.mult, ["qr", "qsc"], ["qtl"])
                    mm(psb[3][:, hc], vtm[:, c2, hr, :], Ssb[:, hr, :], ["vtm", "Ssb"], ["psb3"], start=True, stop=False)
                    mm(psb[3][:, hc], Sbf[:, hr, :], qtl[:, hr, cc], ["Sbf", "qtl"], ["psb3"], start=False, stop=True)
                    cp(yrT[:, hr, cc], psb[3][:, hc], ["psb3"], ["yrT"])
                mm(psb[0][:, hc], ktm[:, c2, hr, :], vtm[:, c2, hr, :], ["ktm", "vtm"], ["psb0"])
                stt(Sst[:, hr, :], Sst[:, hr, :], gcs[:, hr:hr + 1], psb[0][:, hc], ALU.mult, ALU.add, ["Sst", "gcs", "psb0"], ["Sst"])
                cp(Sbf[:, hr, :], Sst[:, hr, :], ["Sst"], ["Sbf"])
        if dbg and stop_after == 2.6:
            return "dbg2"
        if not lat:
            return None
        ysl = yspill[:, :, cols].rearrange("k p n -> p k n")
        if sw == 0:
            P.dma("sp", ysl[:, 0:2, :], yT[:], "yst0", reads=["yT"], writes=["yspill%d" % bi])
            P.dma("sp", ysl[:, 2:4, :], yrT[:], "yst1", reads=["yrT"], writes=["yspillr%d" % bi])
            return None
        load(ysp[:], ysl, "yld", ["ysp"], R=["yspill%d" % bi, "yspillr%d" % bi])
        act(gsb[:, 0, :], rwT[:, 8, :], AF.Sigmoid, ["rwT"], ["gsb"])
        act(gsb[0:32, 1, :], rwT[0:32, 9, :], AF.Sigmoid, ["rwT"], ["gsb"])
        od = slice(0, 64)
        for ct in range(2):
            cs_ = slice(ct * 128, (ct + 1) * 128)
            ps = psb[2 + ct]
            pn = PSN[2 + ct]
            y = cumT[:, ct, :]
            sq = cumpT[:, ct, :]
            bn = Ep[:, ct, :]
            tt(y, yT[:, ct, :], ysp[:, ct, :], ALU.add, ["yT", "ysp"], ["cumT"])
            mm(ps[:, 0:256], bones[:, :], y, ["bones", "cumT"], [pn])
            stt(y, ps[:, 0:256], -1.0 / 64.0, y, ALU.mult, ALU.add, [pn, "cumT"], ["cumT"])
            tt(sq, y, y, ALU.mult, ["cumT"], ["cumpT"])
            mm(ps[:, 256:512], bones[:, :], sq, ["bones", "cumpT"], [pn])
            rsq(sq, ps[:, 256:512], 1.0 / 64.0, LNX_EPS, [pn], ["cumpT"])
            tt(y, y, sq, ALU.mult, ["cumT", "cumpT"], ["cumT"])
            ts(y, y, colvs[:, ct, 3:4], colvs[:, ct, 4:5], ALU.mult, ALU.add, ["cumT", "colvs"], ["cumT"])
            mm(ps[:, 0:256], aup_s[od, cs_], adb[od, :], ["aup", "adb"], [pn])
            act(bn, ps[:, 0:256], AF.Sigmoid, [pn, "colvs"], ["Ep"], bias=colvs[:, ct, 5:6])
            ts(bn, bn, colvs[:, ct, 1:2], colvs[:, ct, 7:8], ALU.mult, ALU.add, ["Ep", "colvs"], ["Ep"])
            tt(bn, rwT[:, ct, :], bn, ALU.mult, ["rwT", "Ep"], ["Ep"])
            tt(bn, bn, kdT[:, ct, :], ALU.add, ["Ep", "kdT"], ["Ep"])
            stt(bn, rwT[:, 4 + ct, :], colvs[:, ct, 2:3], bn, ALU.mult, ALU.mult, ["rwT", "colvs", "Ep"], ["Ep"])
            mm(ps[:, 256:512], bones[:, :], bn, ["bones", "Ep"], [pn])
            tt(bn, ps[:, 256:512], rwT[:, 2 + ct, :], ALU.mult, [pn, "rwT"], ["Ep"])
            tt(y, y, bn, ALU.add, ["cumT", "Ep"], ["cumT"])
            mm(ps[:, 0:256], gup_a[:, cs_], gsb[:, 0, :], ["gupa", "gsb"], [pn], start=True, stop=False)
            mm(ps[:, 0:256], gup_b[0:32, cs_], gsb[0:32, 1, :], ["gupb", "gsb"], [pn], start=False, stop=True)
            tt(oT[:, ct, :], y, ps[:, 0:256], ALU.mult, ["cumT", pn], ["oT"])
        for hr in range(2):
            ps = psb[2 + hr]
            pn = PSN[2 + hr]
            y = Em[:, hr, :]
            sq = Epv[:, hr, :]
            tt(y, yrT[:, hr, :], ysp[:, 2 + hr, :], ALU.add, ["yrT", "ysp"], ["Em"])
            tt(sq, y, y, ALU.mult, ["Em"], ["Epv"])
            mm(ps[:, 0:256], ones[:, :], sq, ["ones", "Epv"], [pn])
            rsq(sq, ps[:, 0:256], 1.0 / 128.0, EPS, [pn], ["Epv"])
            tt(y, y, sq, ALU.mult, ["Em", "Epv"], ["Em"])
            act(sq, retT[:, 6 + hr, :], AF.Silu, ["retT"], ["Epv"])
            tt(oT[:, 2 + hr, :], y, sq, ALU.mult, ["Em", "Epv"], ["oT"])
        if dbg and stop_after == 3:
            return "dbg3"
        P.dma("sp", osend.ap()[:, cols].rearrange("(k p) n -> p k n", p=128), oT[:], "ost", reads=["oT"], writes=["osend"])
        return None

    ts(colvs[:, :, 7], colvs[:, :, 1], -1.0, 1.0, ALU.mult, ALU.add, ["colvs"], ["colvs"])

    def run_sweep(sw):
        memset(Hst[:], 0.0, ["Hst"])
        memset(Hb[:], 0.0, ["Hb"])
        memset(Sst[:], 0.0, ["Sst"])
        memset(Sbf[:], 0.0, ["Sbf"])
        order = [("c", 0, 1)] + [("l", bi, 16) for bi in (range(16) if sw == 0 else range(15, -1, -1))]
        for (seq, bi, nb) in order:
            r = block(sw, seq, bi, nb)
            if r is not None:
                return r
        return None

    r = run_sweep(0)
    if r == "dbg2":
        dtile = arena[:, 0:4096]
        memset(dtile[:], 0.0, ["dtile", "winb"])
        cp(dtile[:, 0:512], yT[:].rearrange("p k n -> p (k n)"), ["yT"], ["dtile", "winb"], eng="dve")
        cp(dtile[:, 512:768], Hst[:].rearrange("p k n -> p (k n)"), ["Hst"], ["dtile", "winb"], eng="dve")
        cp(dtile[:, 1024:1536], kkT[:].rearrange("p k n -> p (k n)"), ["kkT"], ["dtile", "winb"], eng="dve")
        cp(dtile[:, 1536:2048], Ep[:].rearrange("p k n -> p (k n)"), ["Ep"], ["dtile", "winb"], eng="dve")
        cp(dtile[:, 2048:2560], aT[:].rearrange("p k n -> p (k n)"), ["aT"], ["dtile", "winb"], eng="dve")
        cp(dtile[:, 2560:3072], kdT[:].rearrange("p k n -> p (k n)"), ["kdT"], ["dtile", "winb"], eng="dve")
        P.dma("sp", dbg_d, dtile[:], "dbgout", reads=["dtile"], writes=["dbg_d"])
        P.emit(es, {"sp": [("dbgout", 16)]})
        es.close()
        return nc
    if r is None:
        r = run_sweep(1)
    if r == "dbg3":
        dtile = arena[:, 0:4096]
        memset(dtile[:], 0.0, ["dtile", "winb"])
        cp(dtile[:, 0:1024], oT[:].rearrange("p k n -> p (k n)"), ["oT"], ["dtile", "winb"], eng="dve")
        cp(dtile[:, 1024:1536], yT[:].rearrange("p k n -> p (k n)"), ["yT"], ["dtile", "winb"], eng="dve")
        cp(dtile[:, 1536:2048], yrT[:].rearrange("p k n -> p (k n)"), ["yrT"], ["dtile", "winb"], eng="dve")
        cp(dtile[:, 2048:3072], ysp[:].rearrange("p k n -> p (k n)"), ["ysp"], ["dtile", "winb"], eng="dve")
        P.dma("sp", dbg_d, dtile[:], "dbgout", reads=["dtile"], writes=["dbg_d"])
        P.emit(es, {"sp": [("dbgout", 16)]})
        es.close()
        return nc
    if r == "dbg1":
        dtile = arena[:, 0:4096]
        cp(dtile[:, 0:2560], rwT[:].rearrange("p k n -> p (k n)"), ["rwT"], ["dtile", "winb"], eng="dve")
        cp(dtile[:, 2560:4096], retT[:, 0:6, :].rearrange("p k n -> p (k n)"), ["retT"], ["dtile", "winb"], eng="dve")
        P.dma("sp", dbg_d, dtile[:], "dbgout", reads=["dtile"], writes=["dbg_d"])
        P.emit(es, {"sp": [("dbgout", 16)]})
        es.close()
        return nc

    if stop_after == "noB":
        P.emit(es, {})
        es.close()
        return nc
    if TWO_A:
        P.dma("sp", modcol_o, modcol[:].rearrange("p a k -> p (a k)"), "mco", reads=["modcol"], writes=["modcol_o"])
        P.emit(es, {"sp": [("ost", P.dma_count["ost"]), ("mco", 16), ("a12o", 16)]})
        es.close()
        return nc
    if TWO_B:
        P = Prog(nc)
        DBG["P"] = P
        memset(epsc[:, 0:1], EPS, ["epsc"])
        load(ident[:], c_ident, CK, ["ident"])
        load(hmask_s[:], hmask, CK, ["hmask"])
        load(fconv_s[:], fconv, CK, ["fconv"])
        load(modcol[:].rearrange("p a k -> p (a k)"), modcol_i, CK, ["modcol"])
        a12sp = a12_i
    if stop_after != "nocc" and not TWO_B:
      P.custom("pool", lambda e: e.collective_compute("AllGather", ALU.bypass, replica_groups=[[0, 1, 2, 3, 4, 5, 6, 7]],
                                                    ins=[osend.ap().opt()], outs=[ogath.ap().opt()]),
               "cc0", reads=["osend"], writes=["ogath"])
    og = ogath.ap() if stop_after != "nocc" else osend.ap()

    prevA = P.all_res()
    HT_W = 9216
    hT = arena[:, 0:HT_W].bitcast(BF16).rearrange("p (k n) -> p k n", k=16)
    B2 = Arena()
    B2.off = HT_W
    oTb = B2.bf16(16 * 1152).rearrange("p (k n) -> p k n", k=16)
    cand = [B2.bf16(8 * 1152).rearrange("p (k n) -> p k n", k=8)]
    woutb = B2.bf16(16 * D).rearrange("p (k n) -> p k n", k=16)
    mixrow = B2.f32(D)
    xtb = B2.f32(D)
    A1row = B2.f32(D)
    sqb = B2.bf16(D)
    statb = B2.f32(16)
    for nm in ["hT", "oTb", "cand0", "woutb", "mixrow", "xtb", "A1row", "sqb", "statb"]:
        P.alias(nm, prevA)
    P.dma("pool", woutb, wout, "woutk", writes=["woutb"])
    load(A1row, a12sp[:, 0:D].partition_broadcast(128), "a1k", ["A1row"], R=["a12sp"])
    NB_ = 2 if stop_after != "nocc" else 1
    if TWO_B:
        load(oTb, oin, "oink", ["oTb"])
    if NB_ == 1:
        memset(cand[0][:, 4:8, :], 0.0, ["cand0"])
    for bq in range(NB_):
        memset(cand[0][:, bq * 4 + 0, 0:64], 0.0, ["cand0"])
        memset(cand[0][:, bq * 4 + 3, 1088:1152], 0.0, ["cand0"])
    for kt in (range(16) if not TWO_B else []):
        cb = cand[0]
        cn = "cand0"
        for bq in range(NB_):
            r0 = (bq * 4 + kt // 4) * 512 + (kt % 4) * 128
            if stop_after == "nocc":
                r0 = (kt % 4) * 128
            load(cb[:, bq * 4 + 0, 64:1152], og[r0:r0 + 128, 0:1088], "candk0", [cn], R=["ogath", "osend"])
            load(cb[:, bq * 4 + 1, :], og[r0:r0 + 128, 960:2112], "candk0", [cn], R=["ogath", "osend"])
            load(cb[:, bq * 4 + 2, :], og[r0:r0 + 128, 1984:3136], "candk0", [cn], R=["ogath", "osend"])
            load(cb[:, bq * 4 + 3, 0:1088], og[r0:r0 + 128, 3008:4096], "candk0", [cn], R=["ogath", "osend"])
        ts(oTb[:, kt, :], cb[:, 0, :], tsel_s[:, 0:1], None, ALU.mult, None, [cn, "tsel"], ["oTb"])
        for g_ in range(1, 8):
            stt(oTb[:, kt, :], cb[:, g_, :], tsel_s[:, g_:g_ + 1], oTb[:, kt, :], ALU.mult, ALU.add, [cn, "tsel", "oTb"], ["oTb"])
    for t_ in range(9):
        tk = slice(t_ * 128, (t_ + 1) * 128)
        load(xtb, xb[tk, :], "xbk", ["xtb"])
        for cch in range(4):
            ps = psb[cch % 2]
            pn = PSN[cch % 2]
            for kt in range(16):
                mm(ps[:, :], oTb[:, kt, tk], woutb[:, kt, cch * 512:(cch + 1) * 512], ["oTb", "woutb"], [pn], start=(kt == 0), stop=(kt == 15))
            cp(mixrow[:, cch * 512:(cch + 1) * 512], ps[:, :], [pn], ["mixrow"])
        act(sqb, mixrow, AF.Square, ["mixrow"], ["sqb", "statb"], accum=statb[:, 0:1])
        rsq(statb[:, 1:2], statb[:, 0:1], 1.0 / D, EPS, ["statb"], ["statb"])
        stt(mixrow, mixrow, statb[:, 1:2], A1row, ALU.mult, ALU.mult, ["mixrow", "statb", "A1row"], ["mixrow"])
        tt(xtb, xtb, mixrow, ALU.add, ["xtb", "mixrow"], ["xtb"])
        if t_ == 0:
            P.dma("sp", x1sp[0:64, :], xtb[64:128, :], "x1st", reads=["xtb"], writes=["x1sp"])
        elif t_ == 8:
            P.dma("sp", x1sp[960:1024, :], xtb[0:64, :], "x1st", reads=["xtb"], writes=["x1sp"])
        else:
            P.dma("sp", x1sp[t_ * 128 - 64:t_ * 128 + 64, :], xtb, "x1st", reads=["xtb"], writes=["x1sp"])
        act(sqb, xtb, AF.Square, ["xtb"], ["sqb", "statb"], accum=statb[:, 2:3])
        rsq(statb[:, 3:4], statb[:, 2:3], 1.0 / D, EPS, ["statb"], ["statb"])
        ts(mixrow, xtb, statb[:, 3:4], None, ALU.mult, None, ["xtb", "statb"], ["mixrow"])
        for k4 in range(4):
            ps = psb[2 + k4 % 2]
            pn = PSN[2 + k4 % 2]
            for kk_ in range(4):
                kt = k4 * 4 + kk_
                trp(ps[:, kk_ * 128:(kk_ + 1) * 128], mixrow[:, kt * 128:(kt + 1) * 128], ident[:, :], ["mixrow", "ident"], [pn])
            for kk_ in range(4):
                kt = k4 * 4 + kk_
                if kk_ % 2 == 0:
                    act(hT[:, kt, tk], ps[:, kk_ * 128:(kk_ + 1) * 128], AF.Identity, [pn, "modcol"], ["hT"],
                        bias=modcol[:, 2, kt:kt + 1], scale=modcol[:, 3, kt:kt + 1])
                else:
                    ts(hT[:, kt, tk], ps[:, kk_ * 128:(kk_ + 1) * 128], modcol[:, 3, kt:kt + 1], modcol[:, 2, kt:kt + 1],
                       ALU.mult, ALU.add, [pn, "modcol"], ["hT"])
    B2res = ["oTb", "cand0", "woutb", "mixrow", "xtb", "A1row", "sqb", "statb"]
    B3 = Arena()
    B3.off = HT_W
    wgb = [B3.bf16(16 * 128).rearrange("p (k n) -> p k n", k=16) for _ in range(2)]
    wub = [B3.bf16(16 * 128).rearrange("p (k n) -> p k n", k=16) for _ in range(2)]
    gts = B3.f32(1152)
    cacc = B3.f32(1024)
    HID_OFF = ARENA_W - 22528
    assert B3.off <= HID_OFF
    hid = arena[:, HID_OFF:ARENA_W].bitcast(BF16).rearrange("p (k n) -> p k n", k=NFT)
    for nm in ["wgb0", "wgb1", "wub0", "wub1", "gts", "cacc", "hid"]:
        P.alias(nm, B2res)
    g3 = gts.rearrange("p (r c) -> p r c", c=64)
    a3 = cacc.rearrange("p (r c) -> p r c", c=64)
    for ft in range(NFT):
        wg, wu = wgb[ft % 2], wub[ft % 2]
        P.dma("pool", wg, wgate[ft], "wgk%d" % (ft % 2), writes=["wgb%d" % (ft % 2)])
        P.dma("pool", wu, wupf[ft], "wuk%d" % (ft % 2), writes=["wub%d" % (ft % 2)])
        for ch in range(3):
            ps = psb[ch]
            for kt in range(16):
                mm(ps[:, 0:384], wg[:, kt, :], hT[:, kt, ch * 384:(ch + 1) * 384], ["wgb%d" % (ft % 2), "hT"], [PSN[ch]],
                   start=(kt == 0), stop=(kt == 15))
            cp(gts[:, ch * 384:(ch + 1) * 384], ps[:, 0:384], [PSN[ch]], ["gts"])
        ts(gts[:, 0:64], gts[:, 0:64], hmask_s[:, 0:1], None, ALU.mult, None, ["gts", "hmask"], ["gts"])
        ts(gts[:, 1088:1152], gts[:, 1088:1152], hmask_s[:, 1:2], None, ALU.mult, None, ["gts", "hmask"], ["gts"])
        ts(a3, g3[:, 1:17, :], fconv_s[:, ft, 4:5], fconv_s[:, ft, 9:10], ALU.mult, ALU.add, ["gts", "fconv"], ["cacc"])
        for dr in range(3):
            for dc in range(3):
                if dr == 1 and dc == 1:
                    continue
                wcol = fconv_s[:, ft, dr * 3 + dc:dr * 3 + dc + 1]
                if dc == 1:
                    o_, i_ = a3, g3[:, dr:dr + 16, :]
                elif dc == 0:
                    o_, i_ = a3[:, :, 1:64], g3[:, dr:dr + 16, 0:63]
                else:
                    o_, i_ = a3[:, :, 0:63], g3[:, dr:dr + 16, 1:64]
                stt(o_, i_, wcol, o_, ALU.mult, ALU.add, ["gts", "fconv", "cacc"], ["cacc"])
        act(cacc, cacc, AF.Silu, ["cacc"], ["cacc"])
        for ch in range(2):
            ps = psb[3 + ch]
            for kt in range(16):
                mm(ps[:, :], wu[:, kt, :], hT[:, kt, 64 + ch * 512:64 + (ch + 1) * 512], ["wub%d" % (ft % 2), "hT"], [PSN[3 + ch]],
                   start=(kt == 0), stop=(kt == 15))
            tt(hid[:, ft, ch * 512:(ch + 1) * 512], ps[:, :], cacc[:, ch * 512:(ch + 1) * 512], ALU.mult, [PSN[3 + ch], "cacc"], ["hid"])
    yTf = arena[:, 0:16384].rearrange("p (k n) -> p k n", k=16)
    B4 = Arena()
    B4.off = 16384
    wdb = [B4.bf16(NFT * 128).rearrange("p (k n) -> p k n", k=NFT) for _ in range(2)]
    assert B4.off <= HID_OFF
    B3res = ["hT", "wgb0", "wgb1", "wub0", "wub1", "gts", "cacc"]
    for nm in ["yTf", "wdb0", "wdb1"]:
        P.alias(nm, B3res)
    for mt in range(16):
        wd = wdb[mt % 2]
        P.dma("pool", wd, wdown[mt], "wdk%d" % (mt % 2), writes=["wdb%d" % (mt % 2)])
        for ch in range(2):
            ps = psb[(mt * 2 + ch) % 4]
            pn = PSN[(mt * 2 + ch) % 4]
            for ft in range(NFT):
                mm(ps[:, :], wd[:, ft, :], hid[:, ft, ch * 512:(ch + 1) * 512], ["wdb%d" % (mt % 2), "hid"], [pn],
                   start=(ft == 0), stop=(ft == NFT - 1))
            cp(yTf[:, mt, ch * 512:(ch + 1) * 512], ps[:, :], [pn], ["yTf"])
    B5 = Arena()
    B5.off = 16384
    ytm = B5.f32(D)
    x1t = [B5.f32(D) for _ in range(2)]
    A2row = B5.f32(D)
    sq5 = B5.bf16(D)
    st5 = B5.f32(8)
    assert B5.off <= HID_OFF
    for nm in ["ytm", "x1t0", "x1t1", "A2row", "sq5", "st5"]:
        P.alias(nm, ["wdb0", "wdb1"])
    load(A2row, a12sp[:, D:2 * D].partition_broadcast(128), "a2k", ["A2row"], R=["a12sp"])
    for t_ in range(8):
        tk = slice(t_ * 128, (t_ + 1) * 128)
        xt_ = x1t[t_ % 2]
        xn = "x1t%d" % (t_ % 2)
        load(xt_, x1sp[tk, :], "x1ld%d" % (t_ % 2), [xn], R=["x1sp"])
        for k4 in range(4):
            ps = psb[4 + k4 % 2]
            pn = PSN[4 + k4 % 2]
            for kk_ in range(4):
                mt = k4 * 4 + kk_
                trp(ps[:, kk_ * 128:(kk_ + 1) * 128], yTf[:, mt, tk], ident[:, :], ["yTf", "ident"], [pn])
            cp(ytm[:, k4 * 512:(k4 + 1) * 512], ps[:, :], [pn], ["ytm"], eng=("act" if k4 % 2 == 0 else "dve"))
        act(sq5, ytm, AF.Square, ["ytm"], ["sq5", "st5"], accum=st5[:, 0:1])
        rsq(st5[:, 1:2], st5[:, 0:1], 1.0 / D, EPS, ["st5"], ["st5"])
        stt(ytm, ytm, st5[:, 1:2], A2row, ALU.mult, ALU.mult, ["ytm", "st5", "A2row"], ["ytm"])
        tt(xt_, xt_, ytm, ALU.add, [xn, "ytm"], [xn])
        P.dma("sp", out_d[tk, :], xt_, "outk%d" % (t_ % 2), reads=[xn], writes=["out_d"])
    P.emit(es, {"sp": [("outk0", P.dma_count["outk0"]), ("outk1", P.dma_count["outk1"])]})
    es.close()
    return nc


def _prep(inp):
    f = lambda a: np.ascontiguousarray(np.asarray(a, dtype=np.float32))
    x, c, ctx, c_ctx = f(inp["x"]), f(inp["c"]), f(inp["ctx"]), f(inp["c_ctx"])
    w_in = f(inp["w_in"][0])
    RW = 1024
    shared = {}
    shared["wada"] = np.ascontiguousarray(
        f(inp["w_ada"][0]).reshape(16, 128, 24, 512).transpose(2, 1, 0, 3))
    shared["bada"] = f(inp["b_ada"][0]).reshape(1, 12288)
    shared["nrm_col"] = np.ascontiguousarray(
        np.stack([_col(f(inp["norm_pre_mix"][0])), _col(f(inp["norm_pre_ffn"][0]))], axis=1))
    shared["nrm_row"] = np.stack([f(inp["norm_post_mix"][0]), f(inp["norm_post_ffn"][0])], axis=0)
    shared["wgate"] = np.ascontiguousarray(
        f(inp["ffn_w_gate"][0]).reshape(16, 128, NFT, 128).transpose(2, 1, 0, 3))
    shared["wupf"] = np.ascontiguousarray(
        f(inp["ffn_w_up"][0]).reshape(16, 128, NFT, 128).transpose(2, 1, 0, 3))
    shared["wdown"] = np.ascontiguousarray(
        f(inp["ffn_w_down"][0]).reshape(NFT, 128, 16, 128).transpose(2, 1, 0, 3))
    fc = np.concatenate([f(inp["ffn_conv"][0]).reshape(9, DFF), f(inp["ffn_conv_b"][0]).reshape(1, DFF)], axis=0)
    shared["fconv"] = np.ascontiguousarray(fc.reshape(10, NFT, 128).transpose(2, 1, 0))
    w_out = f(inp["w_out"][0])
    maps = []
    for core in range(8):
        b, g = core // 4, core % 4
        m = dict(shared)
        xa = np.zeros((SEQ + CTX + 4, D), np.float32)
        xa[1:257] = ctx[b]
        xa[259:259 + SEQ] = x[b]
        m["xa"] = xa
        xbm = np.zeros((1152, D), np.float32)
        lo, hi = g * 1024 - 64, g * 1024 + 1088
        slo, shi = max(lo, 0), min(hi, SEQ)
        xbm[slo - lo: shi - lo] = x[b, slo:shi]
        m["xb"] = xbm
        cv = np.stack([c[b], c_ctx], axis=1)
        m["cvt"] = np.ascontiguousarray(cv.reshape(16, 128, 2).transpose(1, 0, 2))
        cs = slice(g * 256, (g + 1) * 256)
        cols = np.concatenate([
            np.arange(0, RW)[cs], np.arange(RW, 2 * RW)[cs],
            np.arange(2304, 2304 + RW)[cs],
            np.arange(2048, 2304),
            np.arange(3328, 3488),
            3488 + np.arange(0, RW)[cs], 3488 + np.arange(RW, 2 * RW)[cs],
            3488 + np.arange(2 * RW, 3 * RW)[cs], 3488 + np.arange(3 * RW, 4 * RW)[cs]])
        assert cols.size == NCOLS
        m["win"] = _ktile(w_in[:, cols])
        rc = f(inp["rw_conv"][0])[:, cols[:1184]]
        rcp = np.zeros((3, 1280), np.float32)
        rcp[:, :1184] = rc
        m["rwconv"] = np.ascontiguousarray(rcp.reshape(3, 10, 128).transpose(2, 1, 0))
        cvs = np.zeros((128, 2, 9), np.float32)
        hs = slice(g * 256, (g + 1) * 256)
        vecs = [f(inp["rw_k_k"][0])[hs], f(inp["rw_k_a"][0])[hs], f(inp["rw_r_k"][0]).reshape(-1)[hs],
                f(inp["rw_lnx_w"][0])[hs], f(inp["rw_lnx_b"][0])[hs],
                f(inp["rw_a0"][0])[0, hs], f(inp["rw_a0"][0])[1, hs]]
        for vi, v in enumerate(vecs):
            cvs[:, :, vi] = v.reshape(2, 128).T
        m["colv"] = cvs
        m["w0row"] = np.concatenate([f(inp["rw_w0"][0])[0, hs], f(inp["rw_w0"][0])[1, hs]]).reshape(1, 512)
        m["wup"] = np.concatenate([f(inp["rw_w_up"][0])[0][:, hs], f(inp["rw_w_up"][0])[1][:, hs]], axis=0)
        m["aup"] = np.concatenate([f(inp["rw_a_up"][0])[0][:, hs], f(inp["rw_a_up"][0])[1][:, hs]], axis=0)
        m["gup"] = np.ascontiguousarray(f(inp["rw_g_up"][0])[:, hs])
        rows = np.concatenate([np.concatenate([np.arange(gp * 256, (gp + 1) * 256),
                                               1024 + np.arange(gp * 256, (gp + 1) * 256)]) for gp in range(4)])
        m["wout"] = _ktile(w_out[rows])
        hm = np.zeros((128, 2), np.float32)
        hm[:, 0] = 1.0 if g > 0 else 0.0
        hm[:, 1] = 1.0 if g < 3 else 0.0
        m["hmask"] = hm
        tsl = np.zeros((128, 8), np.float32)
        tsl[:, b * 4 + g] = 1.0
        m["tsel"] = tsl
        for k, v in _consts(g).items():
            m["c_" + k] = v
        maps.append(m)
    return maps


_NC_CACHE = {}
FUSED = False


def kernel(**inputs):
    import ml_dtypes
    maps = _prep(inputs)
    out = np.zeros((2, SEQ, D), np.float32)
    if FUSED:
        if "nc" not in _NC_CACHE:
            _NC_CACHE["nc"] = build()
        res = run_bass_kernel_spmd(_NC_CACHE["nc"], maps, core_ids=list(range(8)))
        for core in range(8):
            b, g = core // 4, core % 4
            out[b, g * 1024:(g + 1) * 1024] = np.asarray(res.results[core]["out"], dtype=np.float32)
        return out
    if "ncA" not in _NC_CACHE:
        _NC_CACHE["ncA"] = build(stop_after="A2L")
        _NC_CACHE["ncB"] = build(stop_after="B2L")
    dummy = {"oin": np.zeros((128, 16, 1152), ml_dtypes.bfloat16), "modcol_i": np.zeros((128, 96), np.float32),
             "a12_i": np.zeros((1, 2 * D), np.float32)}
    resA = run_bass_kernel_spmd(_NC_CACHE["ncA"], maps, core_ids=list(range(8)))
    osend = [np.asarray(resA.results[c]["osend"]) for c in range(8)]
    mapsB = []
    for core in range(8):
        b, g = core // 4, core % 4
        m = dict(maps[core])
        oin = np.zeros((128, 16, 1152), osend[0].dtype)
        lo, hi = g * 1024 - 64, g * 1024 + 1088
        slo, shi = max(lo, 0), min(hi, SEQ)
        for kt in range(16):
            src = osend[4 * b + kt // 4]
            j = kt % 4
            oin[:, kt, slo - lo:shi - lo] = src[j * 128:(j + 1) * 128, slo:shi]
        m["oin"] = oin
        m["modcol_i"] = np.asarray(resA.results[core]["modcol_o"])
        m["a12_i"] = np.asarray(resA.results[core]["a12_o"])
        mapsB.append(m)
    resB = run_bass_kernel_spmd(_NC_CACHE["ncB"], mapsB, core_ids=list(range(8)))
    for core in range(8):
        b, g = core // 4, core % 4
        out[b, g * 1024:(g + 1) * 1024] = np.asarray(resB.results[core]["out"], dtype=np.float32)
    return out
```

```python
import numpy as np
from contextlib import ExitStack
import concourse.bass as bass
import concourse.mybir as mybir
from concourse.bass_utils import run_bass_kernel_spmd

F32 = mybir.dt.float32
BF16 = mybir.dt.bfloat16
AF = mybir.ActivationFunctionType
ALU = mybir.AluOpType
AX = mybir.AxisListType

D = 2048
SEQ = 4096
CTX = 256
DFF = 5632
NFT = 44
EPS = 1e-6
LNX_EPS = 64e-5
BLK = 256
NWIN = 258
C = 64
NCOLS = 2208
DECAY_C = float(np.exp(-0.5))

DBG = {}


class Ev:
    __slots__ = ("kind", "eng", "idx", "key", "val", "needed")

    def __init__(self, kind, eng=None, idx=0, key=None, val=0):
        self.kind, self.eng, self.idx, self.key, self.val, self.needed = kind, eng, idx, key, val, False


class Prog:
    ENGS = ["pe", "act", "dve", "pool", "sp"]

    def __init__(self, nc):
        self.nc = nc
        self.stream = {e: [] for e in self.ENGS}
        self.last_write = {}
        self.readers = {}
        self.waited = {e: {} for e in self.ENGS}
        self.dma_count = {}
        self.dma_keys = []
        self.last_ev = {}
        self.trace_lines = False
        self.lines = {}
        self.imap = {}

    def _deps(self, eng, reads, writes):
        evs = []
        for r in reads:
            w = self.last_write.get(r)
            if w is not None:
                evs.append(w)
        for r in writes:
            w = self.last_write.get(r)
            if w is not None:
                evs.append(w)
            evs.extend(self.readers.get(r, ()))
        best = {}
        for ev in evs:
            if ev.kind == "eng":
                if ev.eng == eng and eng == "pe":
                    continue
                k = ("eng", ev.eng)
                if k not in best or best[k].idx < ev.idx:
                    best[k] = ev
            else:
                k = ("dma", ev.key)
                if k not in best or best[k].val < ev.val:
                    best[k] = ev
        out = []
        for k, ev in best.items():
            cur = self.waited[eng].get(k, -1)
            v = ev.idx if ev.kind == "eng" else ev.val
            if cur >= v:
                continue
            self.waited[eng][k] = v
            ev.needed = True
            out.append(ev)
        return out

    def _commit(self, ev, reads, writes):
        for r in reads:
            self.readers.setdefault(r, []).append(ev)
        for r in writes:
            self.last_write[r] = ev
            self.readers[r] = []

    def op(self, eng, fn, reads=(), writes=()):
        waits = self._deps(eng, reads, writes)
        ev = Ev("eng", eng=eng, idx=len(self.stream[eng]))
        if self.trace_lines:
            import sys as _s
            fr = _s._getframe(2)
            self.lines[(eng, len(self.stream[eng]))] = (fr.f_lineno, fr.f_back.f_lineno if fr.f_back else 0)
        self.stream[eng].append((fn, waits, ev))
        self._commit(ev, reads, writes)
        self.last_ev[eng] = ev
        return ev

    def dma(self, eng, out, in_, key, reads=(), writes=(), **kw):
        waits = self._deps(eng, reads, writes)
        if key not in self.dma_count:
            self.dma_count[key] = 0
            self.dma_keys.append(key)
        self.dma_count[key] += 16
        ev = Ev("dma", eng=eng, idx=len(self.stream[eng]), key=key, val=self.dma_count[key])
        self.stream[eng].append((lambda e: e.dma_start(out=out, in_=in_, **kw), waits, ev))
        self._commit(ev, reads, writes)
        return ev

    def custom(self, eng, fn, key, reads=(), writes=(), inc=1):
        waits = self._deps(eng, reads, writes)
        if key not in self.dma_count:
            self.dma_count[key] = 0
            self.dma_keys.append(key)
        self.dma_count[key] += inc
        ev = Ev("dma", eng=eng, idx=len(self.stream[eng]), key=key, val=self.dma_count[key])
        self.stream[eng].append((fn, waits, ev))
        self._commit(ev, reads, writes)
        return ev

    def alias(self, new, olds):
        evs = []
        for o in olds:
            w = self.last_write.get(o)
            if w is not None:
                evs.append(w)
            evs.extend(self.readers.get(o, ()))
        self.readers.setdefault(new, []).extend(evs)

    def all_res(self):
        return list(set(list(self.last_write.keys()) + list(self.readers.keys())))

    def emit(self, es, final_waits):
        nc = self.nc
        LIM = 30000
        engobj = {"pe": nc.tensor, "act": nc.scalar, "dve": nc.vector, "pool": nc.gpsimd, "sp": nc.sync}
        esems = {}
        for e in self.ENGS:
            cnt = 0
            for (fn, waits, ev) in self.stream[e]:
                if ev.kind == "eng" and ev.needed:
                    cnt += 1
                    ev.val = cnt
            nsem = cnt // LIM + 1
            import os as _os
            if _os.environ.get("KCNT"):
                print("SEMCNT", e, cnt, "ninstr", len(self.stream[e]), "nwaits", sum(len(w) for (_, w, _) in self.stream[e]))
            esems[e] = [es.enter_context(nc.semaphore("s_%s_%d" % (e, i))) for i in range(nsem)]
        dsems = {k: es.enter_context(nc.semaphore("d_%s" % str(k))) for k in self.dma_keys}

        def semval(ev):
            if ev.kind == "eng":
                i = (ev.val - 1) // LIM
                return esems[ev.eng][i], ev.val - i * LIM
            if ev.key in ("const", "constp"):
                return dsems[ev.key], self.dma_count[ev.key]
            return dsems[ev.key], ev.val

        block = es.enter_context(nc.Block())
        streams = self.stream

        def run(e, eng):
            for ii, (fn, waits, ev) in enumerate(streams[e]):
                for w in waits:
                    s, v = semval(w)
                    eng.wait_ge(s, v)
                ins = fn(eng)
                if self.trace_lines:
                    try:
                        self.imap[str(ins.ins.name)] = self.lines.get((e, ii))
                    except Exception:
                        pass
                if ev.kind == "dma":
                    if ev.eng is not None and ev.key is not None:
                        inc = 16 if not str(ev.key).startswith("cc") else 1
                        ins.then_inc(dsems[ev.key], inc)
                elif ev.needed:
                    s, v = semval(ev)
                    ins.then_inc(s, 1)
            for (k, v) in final_waits.get(e, []):
                eng.wait_ge(dsems[k], v)

        @block.tensor
        def _(eng):
            run("pe", eng)

        @block.scalar
        def _(eng):
            run("act", eng)

        @block.vector
        def _(eng):
            run("dve", eng)

        @block.gpsimd
        def _(eng):
            run("pool", eng)

        @block.sync
        def _(eng):
            run("sp", eng)


def _consts(g):
    cst = {}
    cst["ident"] = np.eye(128, dtype=np.float32)
    bo = np.zeros((128, 128), np.float32)
    bo[:64, :64] = 1.0
    bo[64:, 64:] = 1.0
    cst["bones"] = bo
    cst["ones"] = np.ones((128, 128), np.float32)
    s = np.arange(128)[:, None]
    t = np.arange(128)[None, :]
    same = (s // 64) == (t // 64)
    tri = np.zeros((128, 4, 128), np.float32)
    tri[:, 0, :] = same & (s <= t)
    tri[:, 1, :] = same & (s < t)
    tri[:, 2, :] = same & (s >= t)
    tri[:, 3, :] = same & (s > t)
    cst["tri"] = tri
    s = np.arange(64)[:, None]
    t = np.arange(64)[None, :]
    mst = np.zeros((64, 2, 128), np.float32)
    mst[:, 0, :64] = s < t
    mst[:, 0, 64:] = s <= t
    mst[:, 1, :64] = s > t
    mst[:, 1, 64:] = s >= t
    cst["mst"] = mst
    m1 = np.zeros((64, 2, 64), np.float32)
    m1[:, 0, :] = (t < s).T.T
    tt = np.arange(64)[:, None]
    ss = np.arange(64)[None, :]
    m1[:, 0, :] = ss < tt
    m1[:, 1, :] = ss > tt
    cst["m1"] = m1
    j = np.arange(128)[:, None].astype(np.float64)
    i = np.arange(128)[None, :].astype(np.float64)
    dmask = np.zeros((128, 4, 128), np.float32)
    qsc = np.zeros((128, 4, 128), np.float32)
    ksc = np.zeros((128, 4), np.float32)
    gc = np.zeros((128, 2), np.float32)
    sc = 128.0 ** -0.5
    for hr in range(2):
        gam = 1.0 - 2.0 ** (-5.0 - (2 * g + hr))
        lg = np.log(gam)
        dmask[:, hr * 2 + 0, :] = np.where(j <= i, np.exp((i - j) * lg), 0.0) * sc
        dmask[:, hr * 2 + 1, :] = np.where(j > i, np.exp((j - i) * lg), 0.0) * sc
        qsc[:, hr * 2 + 0, :] = np.exp((i + 1.0) * lg)
        qsc[:, hr * 2 + 1, :] = np.exp((128.0 - i) * lg)
        ksc[:, hr * 2 + 0] = (np.exp((127.0 - j) * lg) * sc)[:, 0]
        ksc[:, hr * 2 + 1] = (np.exp(j * lg) * sc)[:, 0]
        gc[:, hr] = np.exp(128.0 * lg)
    cst["dmask"], cst["qsc"], cst["ksc"], cst["gc"] = dmask, qsc, ksc, gc
    n = 32
    inv = 10000.0 ** (-np.arange(n, dtype=np.float64) / n)
    tpos = np.arange(SEQ)
    ang = np.zeros((128, SEQ), np.float64)
    for d in range(128):
        if d < 64:
            ang[d] = (tpos // 64) * inv[d % 32]
        else:
            ang[d] = (tpos % 64) * inv[d % 32]
    cst["ropec"] = np.cos(ang).astype(np.float32)
    cst["ropes"] = np.sin(ang).astype(np.float32)
    pm = np.zeros((128, 128), np.float32)
    for dp in range(128):
        if (dp % 64) < 32:
            pm[dp + 32, dp] = -1.0
        else:
            pm[dp - 32, dp] = 1.0
    cst["pm"] = pm
    sel = np.zeros((2, 130), np.float32)
    sel[0, 0] = 1.0
    sel[1, 1] = 1.0
    sel[0, 2:] = 1.0
    cst["sel"] = sel
    return cst


def _ktile(w):
    K, N = w.shape
    return np.ascontiguousarray(w.reshape(K // 128, 128, N).transpose(1, 0, 2))


def _col(v):
    return np.ascontiguousarray(v.reshape(-1, 128).T)


def build(stop_after=None, dbg=False):
    nc = bass.Bass("TRN2", target_bir_lowering=False)
    P = Prog(nc)
    P.trace_lines = dbg
    DBG["P"] = P
    es = ExitStack()

    def din(name, shape, dt=F32):
        return nc.dram_tensor(name, list(shape), dt, kind="ExternalInput").ap()

    xa = din("xa", [SEQ + CTX + 4, D])
    xb = din("xb", [1152, D])
    cvt = din("cvt", [128, 16, 2])
    wada = din("wada", [24, 128, 16, 512])
    bada = din("bada", [1, 12288])
    nrm_col = din("nrm_col", [128, 2, 16])
    nrm_row = din("nrm_row", [2, D])
    win = din("win", [128, 16, NCOLS])
    rwconv = din("rwconv", [128, 10, 3])
    colv = din("colv", [128, 2, 9])
    w0row = din("w0row", [1, 2 * 256])
    wup = din("wup", [128, 256])
    aup = din("aup", [128, 256])
    gup = din("gup", [160, 256])
    wout = din("wout", [128, 16, D])
    wgate = din("wgate", [NFT, 128, 16, 128])
    wupf = din("wupf", [NFT, 128, 16, 128])
    wdown = din("wdown", [16, 128, NFT, 128])
    fconv = din("fconv", [128, NFT, 10])
    hmask = din("hmask", [128, 2])
    tsel = din("tsel", [128, 8])
    c_ident = din("c_ident", [128, 128])
    c_bones = din("c_bones", [128, 128])
    c_ones = din("c_ones", [128, 128])
    c_tri = din("c_tri", [128, 4, 128])
    c_mst = din("c_mst", [64, 2, 128])
    c_m1 = din("c_m1", [64, 2, 64])
    c_dmask = din("c_dmask", [128, 4, 128])
    c_qsc = din("c_qsc", [128, 4, 128])
    c_ksc = din("c_ksc", [128, 4])
    c_gc = din("c_gc", [128, 2])
    c_ropec = din("c_ropec", [128, SEQ])
    c_ropes = din("c_ropes", [128, SEQ])
    c_pm = din("c_pm", [128, 128])
    c_sel = din("c_sel", [2, 130])
    out_d = None
    yspill = nc.dram_tensor("yspill", [4, 128, SEQ], F32).ap()
    pspill = nc.dram_tensor("pspill", [17, 18, 128, BLK], F32).ap()
    TWO_A = stop_after == "A2L"
    TWO_B = stop_after == "B2L"
    osend = nc.dram_tensor("osend", [512, SEQ], BF16, kind=("ExternalOutput" if TWO_A else "Internal"))
    if not TWO_A:
        out_d = nc.dram_tensor("out", [1024, D], F32, kind="ExternalOutput").ap()
    if TWO_A:
        modcol_o = nc.dram_tensor("modcol_o", [128, 96], F32, kind="ExternalOutput").ap()
        a12_o = nc.dram_tensor("a12_o", [1, 2 * D], F32, kind="ExternalOutput").ap()
    if TWO_B:
        oin = din("oin", [128, 16, 1152], BF16)
        modcol_i = din("modcol_i", [128, 96])
        a12_i = din("a12_i", [1, 2 * D])
    ogath = nc.dram_tensor("ogath", [8 * 512, SEQ], BF16)
    x1sp = nc.dram_tensor("x1sp", [1024, D], F32).ap()
    dbg_d = None
    if dbg:
        dbg_d = nc.dram_tensor("dbg", [128, 4096], F32, kind="ExternalOutput").ap()

    def sb(name, shape, dt=F32):
        return es.enter_context(nc.sbuf_tensor(name, list(shape), dt))

    ident = sb("ident", [128, 128])
    bones = sb("bones", [128, 128])
    ones = sb("ones", [128, 128])
    tri = sb("tri", [128, 4, 128])
    mst = sb("mst", [64, 2, 128])
    m1m = sb("m1m", [64, 2, 64])
    dmask = sb("dmask", [128, 4, 128])
    qsc = sb("qsc", [128, 4, 128])
    ksc = sb("ksc", [128, 4])
    gcs = sb("gcs", [128, 2])
    pm = sb("pm", [128, 128])
    sel = sb("sel", [2, 130])
    nrmc = sb("nrmc", [128, 2, 16])
    rwcv = sb("rwcv", [128, 10, 3])
    colvs = sb("colvs", [128, 2, 9])
    w0bc = sb("w0bc", [128, 512])
    wup_s = sb("wup_s", [128, 256], BF16)
    aup_s = sb("aup_s", [128, 256], BF16)
    gup_a = sb("gup_a", [128, 256], BF16)
    gup_b = sb("gup_b", [32, 256], BF16)
    hmask_s = sb("hmask_s", [128, 2])
    tsel_s = sb("tsel_s", [128, 8])
    fconv_s = sb("fconv_s", [128, NFT, 10])
    modcol = sb("modcol", [128, 6, 16])
    ARENA_W = 48800
    arena = sb("arena", [128, ARENA_W])
    psb = [es.enter_context(nc.psum_tensor("psb%d" % i, [128, 512], F32)) for i in range(8)]

    class Arena:
        def __init__(self):
            self.off = 0

        def f32(self, n):
            o = self.off
            self.off += n
            assert self.off <= ARENA_W, self.off
            return arena[:, o:o + n]

        def bf16(self, n):
            w = (n + 1) // 2
            o = self.off
            self.off += w
            assert self.off <= ARENA_W, self.off
            return arena[:, o:o + w].bitcast(BF16)

    V, S, T, G, PE = "dve", "act", "pe", "pool", "pe"

    def tt(out, a, b, op, R, W, eng="dve"):
        P.op(eng, lambda e: e.tensor_tensor(out=out, in0=a, in1=b, op=op), R, W)

    def ts(out, a, s1, s2, op0, op1, R, W, eng="dve"):
        if s2 is None:
            P.op(eng, lambda e: e.tensor_scalar(out=out, in0=a, scalar1=s1, scalar2=None, op0=op0), R, W)
        else:
            P.op(eng, lambda e: e.tensor_scalar(out=out, in0=a, scalar1=s1, scalar2=s2, op0=op0, op1=op1), R, W)

    def stt(out, a, s, b, op0, op1, R, W, eng="dve"):
        P.op(eng, lambda e: e.scalar_tensor_tensor(out=out, in0=a, scalar=s, in1=b, op0=op0, op1=op1), R, W)

    def act(out, a, func, R, W, bias=None, scale=None, accum=None):
        kw = {}
        if bias is not None:
            kw["bias"] = bias
        if scale is not None:
            kw["scale"] = scale
        if accum is not None:
            kw["accum_out"] = accum
        P.op("act", lambda e: e.activation(out=out, in_=a, func=func, **kw), R, W)

    epsc = sb("epsc", [128, 4])
    identh = sb("identh", [128, 128], BF16)

    def rsq(out, a, scale, biasv, R, W):
        bi = {EPS: 0, LNX_EPS: 1, 1e-12: 2}[biasv]
        P.op("act", lambda e: e.activation(out=out, in_=a, func=AF.Sqrt, bias=epsc[0:out.shape[0], bi:bi + 1], scale=scale), list(R) + ["epsc"], W)
        P.op("dve", lambda e: e.reciprocal(out=out, in_=out), W, W)

    def cp(out, a, R, W, eng="act"):
        if eng == "act":
            P.op("act", lambda e: e.copy(out=out, in_=a), R, W)
        else:
            P.op(eng, lambda e: e.tensor_copy(out=out, in_=a), R, W)

    def mm(out, lhsT, rhs, R, W, start=True, stop=True):
        P.op("pe", lambda e: e.matmul(out, lhsT, rhs, start=start, stop=stop), R, W)

    def trp(out, in_, idn, R, W):
        P.op("pe", lambda e: e.transpose(out, in_, idn), R, W)

    def memset(ap, val, W, eng="dve"):
        P.op(eng, lambda e: e.memset(ap, val), (), W)

    dq = ["sp", "act"]
    dqi = [0]

    def load(out, in_, key, W, R=(), eng=None):
        if eng is None:
            eng = "sp"
        return P.dma(eng, out, in_, key, reads=R, writes=W)

    memset(epsc[:, 0:1], EPS, ["epsc"])
    memset(epsc[:, 1:2], LNX_EPS, ["epsc"])
    memset(epsc[:, 2:3], 1e-12, ["epsc"])
    CK = "const"
    for (dst, src, nm) in [(ident, c_ident, "ident"), (bones, c_bones, "bones"), (ones, c_ones, "ones"),
                           (tri, c_tri, "tri"), (mst, c_mst, "mst"), (m1m, c_m1, "m1m"), (dmask, c_dmask, "dmask"),
                           (qsc, c_qsc, "qsc"), (ksc, c_ksc, "ksc"), (gcs, c_gc, "gcs"), (pm, c_pm, "pm"),
                           (sel, c_sel, "sel"), (nrmc, nrm_col, "nrmc"), (rwcv, rwconv, "rwcv"),
                           (colvs, colv, "colvs"), (hmask_s, hmask, "hmask"), (tsel_s, tsel, "tsel"),
                           (fconv_s, fconv, "fconv")]:
        load(dst[:], src, CK, [nm])
    load(w0bc[:], w0row.partition_broadcast(128), CK, ["w0bc"])
    P.op("act", lambda e: e.copy(out=identh[:], in_=ident[:]), ["ident"], ["identh"])
    P.dma("pool", wup_s[:], wup, "constp", writes=["wup"])
    P.dma("pool", aup_s[:], aup, "constp", writes=["aup"])
    P.dma("pool", gup_a[:], gup[0:128, :], "constp", writes=["gupa"])
    P.dma("pool", gup_b[:], gup[128:160, :], "constp", writes=["gupb"])

    A0 = Arena()
    wa_buf = [A0.f32(16 * 512).rearrange("p (k n) -> p k n", k=16) for _ in range(2)]
    modrow = A0.f32(12288)
    badab = [A0.f32(512) for _ in range(2)]
    npost = A0.f32(2 * D).rearrange("p (a n) -> p a n", a=2)
    A12 = A0.f32(2 * D).rearrange("p (a n) -> p a n", a=2)
    a12sp = nc.dram_tensor("a12sp", [1, 2 * D], F32).ap()
    scT = sb("scT", [128, 16, 2])
    load(scT[:], cvt, CK, ["scT"])
    load(npost[:], nrm_row.rearrange("a n -> (a n)").partition_broadcast(128).rearrange("p (a n) -> p a n", a=2)
         if False else nrm_row.unsqueeze(0).to_broadcast([128, 2, D]), CK, ["npost"])
    act(scT[:], scT[:], AF.Silu, ["scT"], ["scT"])
    for n in range(24):
        wb = wa_buf[n % 2]
        load(wb, wada[n], "wada%d" % (n % 2), ["wab%d" % (n % 2)])
        load(badab[n % 2][0:2, :], bada[:, n * 512:(n + 1) * 512].partition_broadcast(2), "wada%d" % (n % 2), ["wab%d" % (n % 2)])
        ps = psb[n % 2]
        for kt in range(16):
            mm(ps[0:2, :], scT[:, kt, :], wb[:, kt, :], ["scT", "wab%d" % (n % 2)], ["psb%d" % (n % 2)],
               start=(kt == 0), stop=(kt == 15))
        tt(modrow[0:2, n * 512:(n + 1) * 512], ps[0:2, :], badab[n % 2][0:2, :], ALU.add,
           ["psb%d" % (n % 2), "wab%d" % (n % 2)], ["modrow"])
    segs = [(0, 0), (0, 1), (0, 3), (0, 4), (1, 0), (1, 1)]
    psc = psb[2]
    for i, (r, sg) in enumerate(segs):
        for kt in range(16):
            mm(psc[:, i * 16 + kt:i * 16 + kt + 1], modrow[0:2, sg * D + kt * 128: sg * D + (kt + 1) * 128],
               sel[0:2, r:r + 1], ["modrow", "sel"], ["psb2"])
    cp(modcol[:].rearrange("p a k -> p (a k)"), psc[:, 0:96], ["psb2"], ["modcol"], eng="dve")
    for (i, nidx) in [(1, 0), (3, 1), (5, 0)]:
        stt(modcol[:, i, :], modcol[:, i, :], 1.0, nrmc[:, nidx, :], ALU.add, ALU.mult, ["modcol", "nrmc"], ["modcol"])
    for a, sg in enumerate([2, 5]):
        for j in range(4):
            ps = psb[3 + (j % 2)]
            mm(ps[:, :], sel[0:2, 2:130], modrow[0:2, sg * D + j * 512: sg * D + (j + 1) * 512], ["modrow", "sel"],
               ["psb%d" % (3 + j % 2)])
            tt(A12[:, a, j * 512:(j + 1) * 512], ps[:, :], npost[:, a, j * 512:(j + 1) * 512], ALU.mult,
               ["psb%d" % (3 + j % 2), "npost"], ["A12"])
    P.dma("sp", a12sp, A12[0:1].rearrange("p a n -> p (a n)"), "a12st", reads=["A12"], writes=["a12sp"])
    if TWO_A:
        P.dma("sp", a12_o, A12[0:1].rearrange("p a n -> p (a n)"), "a12o", reads=["A12"], writes=["a12_o"])
    ph0_res = ["wab0", "wab1", "modrow", "badab", "npost", "A12"]

    if dbg and stop_after == 0:
        dtile = A0.f32(4096)
        memset(dtile[:], 0.0, ["dtile"])
        cp(dtile[:, 0:96], modcol[:].rearrange("p a k -> p (a k)"), ["modcol"], ["dtile"], eng="dve")
        cp(dtile[:, 128:128 + 2048], A12[:, 0, :], ["A12"], ["dtile"], eng="dve")
        ev = P.dma("sp", dbg_d, dtile[:], "dbgout", reads=["dtile"], writes=["dbg_d"])
        P.emit(es, {"sp": [("dbgout", 16)]})
        es.close()
        return nc

    AA = Arena()
    for nm in ["winb", "xt0", "xt1", "sqj", "xmT", "rwT", "retT"]:
        pass
    vtm = AA.bf16(2 * 2 * 128).rearrange("p (c h n) -> p c h n", c=2, h=2)
    winb = AA.bf16(16 * NCOLS).rearrange("p (k n) -> p k n", k=16)
    xts = [AA.f32(D)] * 2
    sqj = AA.bf16(D)
    xmT = AA.bf16(16 * NWIN).rearrange("p (k n) -> p k n", k=16)
    rwT = AA.f32(10 * BLK).rearrange("p (k n) -> p k n", k=10)
    retT = AA.f32(8 * BLK).rearrange("p (k n) -> p k n", k=8)
    stat = AA.f32(8)

    def t2(dt=F32):
        if dt == F32:
            return AA.f32(2 * BLK).rearrange("p (k n) -> p k n", k=2)
        return AA.bf16(2 * BLK).rearrange("p (k n) -> p k n", k=2)

    thb = AA.bf16(BLK)
    adb = AA.bf16(BLK)
    gsb = AA.bf16(2 * BLK).rearrange("p (k n) -> p k n", k=2)
    sigtm = t2()
    cumT, cumpT, aT, kkT, tmpA, tmpB, Ep, Em, Epv, kdT, BhT, KhT, yT = [t2() for _ in range(13)]
    aT2, kd2T = tmpB, BhT
    ART = AA.bf16(2 * 4 * 128).rearrange("p (k c n) -> p k c n", k=2, c=4)
    BKT = AA.bf16(2 * 4 * 128).rearrange("p (k c n) -> p k c n", k=2, c=4)
    BKV = AA.bf16(4 * 3 * 2 * 2 * 128).rearrange("p (c m k h n) -> p c m k h n", c=4, m=3, k=2, h=2)
    Ms2 = AA.bf16(2 * 4 * 2 * 128).rearrange("p (c h m n) -> p c h m n", c=2, h=4, m=2)
    M1s2 = AA.bf16(2 * 4 * 64).rearrange("p (c h n) -> p c h n", c=2, h=4)
    Pa = [AA.bf16(2 * 4 * 2 * 64).rearrange("p (c h m n) -> p c h m n", c=2, h=4, m=2) for _ in range(2)]
    Qs = [AA.bf16(2 * 4 * 64).rearrange("p (c h n) -> p c h n", c=2, h=4) for _ in range(2)]
    Wsb = AA.bf16(2 * 128).rearrange("p (k n) -> p k n", k=2)
    Usb = AA.bf16(2 * 2 * 128).rearrange("p (k h n) -> p k h n", k=2, h=2)
    Hst = AA.f32(2 * 128).rearrange("p (k n) -> p k n", k=2)
    Hb = AA.bf16(2 * 128).rearrange("p (k n) -> p k n", k=2)
    qr = t2(BF16)
    kr = t2(BF16)
    krf = t2()
    qtl = t2(BF16)
    rc_t = AA.f32(BLK)
    rs_t = AA.f32(BLK)
    ktm = AA.bf16(2 * 2 * 128).rearrange("p (c h n) -> p c h n", c=2, h=2)
    Ssb = AA.bf16(2 * 128).rearrange("p (h n) -> p h n", h=2)
    Sst = AA.f32(2 * 128).rearrange("p (h n) -> p h n", h=2)
    Sbf = AA.bf16(2 * 128).rearrange("p (h n) -> p h n", h=2)
    yrT = t2()
    vrb = t2(BF16)
    ysp = AA.f32(4 * BLK).rearrange("p (k n) -> p k n", k=4)
    print("AA.off", AA.off)
    oT = AA.bf16(4 * BLK).rearrange("p (k n) -> p k n", k=4)
    Aall = ["winb", "xt0", "xt1", "sqj", "xmT", "rwT", "retT", "stat", "thb", "adb", "gsb", "sigtm", "cumT", "cumpT",
            "aT", "aT2", "kkT", "tmpA", "tmpB", "Ep", "Em", "Epv", "kdT", "kd2T", "BhT", "KhT", "yT", "ART", "BKT",
            "BKV", "Ms", "M1s", "Pa0", "Pa1", "Qs0", "Qs1", "Wsb", "Usb", "Hst", "Hb", "qr", "kr", "krf", "qtl",
            "rc_t", "rs_t", "vtm", "vrb", "ktm", "Ssb", "Sst", "Sbf", "yrT", "ysp", "oT"]
    for nm in Aall:
        P.alias(nm, ph0_res)
    P.dma("pool", winb, win, "constp", writes=["winb"])
    memset(BKV[0:64], 0.0, ["BKV"])
    memset(Usb[0:64], 0.0, ["Usb"])

    PSN = ["psb%d" % i for i in range(8)]

    def block(sw, seq, bi, nblk):
        lat = seq == "l"
        ro = lat
        row0 = (0 if not lat else 258) + bi * BLK
        s1i, shi_ = (5, 4) if not lat else (1, 0)
        blk_i = 0 if not lat else bi + 1
        nrw = 9 if ro else 8
        nrt = 8 if ro else 4
        if sw == 0:
            for j in range(3):
                nr = 128 if j < 2 else 2
                xt = xts[j % 2]
                xn = "xt0"
                load(xt[0:nr, :], xa[row0 + j * 128: row0 + j * 128 + nr, :], "xk0", [xn])
                act(sqj[0:nr, :], xt[0:nr, :], AF.Square, [xn], ["sqj", "stat"], accum=stat[0:nr, 0:1])
                rsq(stat[0:nr, 1:2], stat[0:nr, 0:1], 1.0 / D, EPS, ["stat"], ["stat"])
                ts(xt[0:nr, :], xt[0:nr, :], stat[0:nr, 1:2], None, ALU.mult, None, [xn, "stat"], [xn])
                for k4 in range(4):
                    ps = psb[k4 % 2]
                    pn = PSN[k4 % 2]
                    for kk_ in range(4):
                        kt = k4 * 4 + kk_
                        trp(ps[:, kk_ * 128: kk_ * 128 + nr], xt[0:nr, kt * 128:(kt + 1) * 128], ident[0:nr, 0:nr], [xn, "ident"], [pn])
                    for kk_ in range(4):
                        kt = k4 * 4 + kk_
                        if kk_ % 2 == 0:
                            act(xmT[:, kt, j * 128: j * 128 + nr], ps[:, kk_ * 128: kk_ * 128 + nr], AF.Identity, [pn, "modcol"], ["xmT"],
                                bias=modcol[:, shi_, kt:kt + 1], scale=modcol[:, s1i, kt:kt + 1])
                        else:
                            ts(xmT[:, kt, j * 128: j * 128 + nr], ps[:, kk_ * 128: kk_ * 128 + nr], modcol[:, s1i, kt:kt + 1],
                               modcol[:, shi_, kt:kt + 1], ALU.mult, ALU.add, [pn, "modcol"], ["xmT"])
            if bi == 0:
                memset(xmT[:, :, 0:1], 0.0, ["xmT"])
            if bi == nblk - 1:
                memset(xmT[:, :, 257:258], 0.0, ["xmT"])
            tiles = [0, 1, 2, 3, 4, 5, 6, 7, 8, 9, 10, 11, 12, 13, 14, 15, 16, 17] if ro else [0, 1, 2, 3, 4, 5, 6, 7, 10, 11, 12, 13]
            for ti, mt in enumerate(tiles):
                ps = psb[ti % 2]
                pn = PSN[ti % 2]
                if mt < 9:
                    c0, mw = mt * 128, 128
                elif mt == 9:
                    c0, mw = 1152, 32
                else:
                    c0, mw = 1184 + (mt - 10) * 128, 128
                for kt in range(16):
                    mm(ps[0:mw, 0:NWIN], winb[:, kt, c0:c0 + mw], xmT[:, kt, :], ["winb", "xmT"], [pn], start=(kt == 0), stop=(kt == 15))
                if mt < 10:
                    ts(rwT[0:mw, mt, :], ps[0:mw, 1:257], rwcv[0:mw, mt, 1:2], None, ALU.mult, None, [pn, "rwcv"], ["rwT"])
                    stt(rwT[0:mw, mt, :], ps[0:mw, 0:256], rwcv[0:mw, mt, 0:1], rwT[0:mw, mt, :], ALU.mult, ALU.add, [pn, "rwcv", "rwT"], ["rwT"])
                    stt(rwT[0:mw, mt, :], ps[0:mw, 2:258], rwcv[0:mw, mt, 2:3], rwT[0:mw, mt, :], ALU.mult, ALU.add, [pn, "rwcv", "rwT"], ["rwT"])
                else:
                    cp(retT[:, mt - 10, :], ps[:, 1:257], [pn], ["retT"])

            P.dma("sp", pspill[blk_i, 0:nrw].rearrange("k p n -> p k n"), rwT[:, 0:nrw, :], "psa", reads=["rwT"], writes=["pspA%d" % blk_i])
            if ro:
                P.dma("sp", pspill[blk_i, 9, 0:32, :], rwT[0:32, 9, :], "psb_", reads=["rwT"], writes=["pspB%d" % blk_i])
            P.dma("sp", pspill[blk_i, 10:10 + nrt].rearrange("k p n -> p k n"), retT[:, 0:nrt, :], "psc", reads=["retT"], writes=["pspC%d" % blk_i])
        else:
            load(rwT[:, 0:nrw, :], pspill[blk_i, 0:nrw].rearrange("k p n -> p k n"), "pla", ["rwT"], R=["pspA%d" % blk_i])
            if ro:
                load(rwT[0:32, 9, :], pspill[blk_i, 9, 0:32, :], "plb", ["rwT"], R=["pspB%d" % blk_i])
            load(retT[:, 0:nrt, :], pspill[blk_i, 10:10 + nrt].rearrange("k p n -> p k n"), "plc", ["retT"], R=["pspC%d" % blk_i])
        if dbg and stop_after == 1 and seq == "l" and bi == 0 and sw == 0:
            return "dbg1"
        dp = slice(0, 64) if sw == 0 else slice(64, 128)
        v4 = lambda ap: ap.rearrange("p (c n) -> p c n", c=4)
        cend = 63 if sw == 0 else 0
        act(thb[dp, :], rwT[dp, 6, :], AF.Tanh, ["rwT"], ["thb"])
        cp(adb[:, :], rwT[:, 7, :], ["rwT"], ["adb"])
        for t_ in range(2):
            ps = psb[2 + t_]
            pn = PSN[2 + t_]
            mm(ps[:, 0:256], thb[dp, t_ * 128:(t_ + 1) * 128], wup_s[dp, :], ["thb", "wup"], [pn])
            tt(sigtm[:, t_, :], ps[:, 0:256], w0bc[:, sw * 256:(sw + 1) * 256], ALU.add, [pn, "w0bc"], ["sigtm"])
        act(sigtm[:], sigtm[:], AF.Sigmoid, ["sigtm"], ["sigtm"])
        for which, dst, dn in [(0, cumT, "cumT"), (1, cumpT, "cumpT")]:
            ps = psb[2 + which]
            pn = PSN[2 + which]
            for ct in range(2):
                for t_ in range(2):
                    mm(ps[:, ct * 256 + t_ * 128: ct * 256 + (t_ + 1) * 128], sigtm[:, t_, ct * 128:(ct + 1) * 128],
                       tri[:, 2 * sw + which, :], ["sigtm", "tri"], [pn])
            cp(dst[:].rearrange("p k n -> p (k n)"), ps[:, :], [pn], [dn])
        act(Ep[:], cumT[:], AF.Exp, ["cumT"], ["Ep"], scale=-DECAY_C)
        act(Em[:], cumT[:], AF.Exp, ["cumT"], ["Em"], scale=DECAY_C)
        act(Epv[:], cumpT[:], AF.Exp, ["cumpT"], ["Epv"], scale=-DECAY_C)
        for ct in range(2):
            cs_ = slice(ct * 128, (ct + 1) * 128)
            ps = psb[2 + ct]
            pn = PSN[2 + ct]
            mm(ps[:, 0:256], aup_s[dp, cs_], adb[dp, :], ["aup", "adb"], [pn])
            act(aT[:, ct, :], ps[:, 0:256], AF.Sigmoid, [pn, "colvs"], ["aT"], bias=colvs[:, ct, 5 + sw:6 + sw])
            ts(tmpA[:, ct, :], rwT[:, ct, :], colvs[:, ct, 0:1], None, ALU.mult, None, ["rwT", "colvs"], ["tmpA"])
            tt(tmpB[:, ct, :], tmpA[:, ct, :], tmpA[:, ct, :], ALU.mult, ["tmpA"], ["tmpB"])
            mm(ps[:, 256:512], bones[:, :], tmpB[:, ct, :], ["bones", "tmpB"], [pn])
            rsq(tmpB[:, ct, :], ps[:, 256:512], 1.0, 1e-12, [pn], ["tmpB"])
            tt(kkT[:, ct, :], tmpA[:, ct, :], tmpB[:, ct, :], ALU.mult, ["tmpA", "tmpB"], ["kkT"])
            stt(ART[:, ct, :, 0:64], v4(kkT[:, ct, :]), -1.0, v4(Epv[:, ct, :]), ALU.mult, ALU.mult, ["kkT", "Epv"], ["ART"])
            tt(ART[:, ct, :, 64:128], v4(rwT[:, 4 + ct, :]), v4(Ep[:, ct, :]), ALU.mult, ["rwT", "Ep"], ["ART"])
            gcb = v4(Ep[:, ct, :])[:, :, cend:cend + 1].to_broadcast([128, 4, 64])
            tt(tmpB[:, ct, :], kkT[:, ct, :], aT[:, ct, :], ALU.mult, ["kkT", "aT"], ["tmpB"])
            tt(BhT[:, ct, :], tmpB[:, ct, :], Em[:, ct, :], ALU.mult, ["tmpB", "Em"], ["BhT"])
            cp(BKT[:, ct, :, 0:64], v4(BhT[:, ct, :]), ["BhT"], ["BKT"])
            tt(v4(BhT[:, ct, :]), v4(BhT[:, ct, :]), gcb, ALU.mult, ["BhT", "Ep"], ["BhT"])
            ts(tmpA[:, ct, :], aT[:, ct, :], colvs[:, ct, 1:2], colvs[:, ct, 7:8], ALU.mult, ALU.add, ["aT", "colvs"], ["tmpA"])
            tt(kdT[:, ct, :], rwT[:, ct, :], tmpA[:, ct, :], ALU.mult, ["rwT", "tmpA"], ["kdT"])
            tt(KhT[:, ct, :], kdT[:, ct, :], Em[:, ct, :], ALU.mult, ["kdT", "Em"], ["KhT"])
            cp(BKT[:, ct, :, 64:128], v4(KhT[:, ct, :]), ["KhT"], ["BKT"])
            tt(v4(KhT[:, ct, :]), v4(KhT[:, ct, :]), gcb, ALU.mult, ["KhT", "Ep"], ["KhT"])
        if dbg and stop_after == 2.1:
            return "dbg2"
        for c in range(4):
            for ct in range(2):
                ps = psb[2 + ct]
                pn = PSN[2 + ct]
                for m_, (src, sn) in enumerate([(BhT[:, ct, :], "BhT"), (KhT[:, ct, :], "KhT"), (rwT[:, 2 + ct, :], "rwT")]):
                    trp(ps[0:64, m_ * 128:(m_ + 1) * 128], src[:, c * 64:(c + 1) * 64], ident[:, :], [sn, "ident"], [pn])
                pv = ps[0:64, 0:384].rearrange("p (m n) -> p m n", m=3)
                for hh in range(2):
                    cp(BKV[0:64, c, :, ct, hh, hh * 64:(hh + 1) * 64], pv[:, :, hh * 64:(hh + 1) * 64], [pn], ["BKV"],
                       eng=("act" if hh == 0 else "dve"))
        if dbg and stop_after == 2.2:
            return "dbg2"
        cols = slice(bi * BLK, (bi + 1) * BLK)

        def ret_steps():
            cols = slice(bi * BLK, (bi + 1) * BLK)
            if lat:
                load(rc_t, c_ropec[:, cols], "ropek", ["rope"])
                load(rs_t, c_ropes[:, cols], "ropek", ["rope"])
                for hr in range(2):
                    for isk, src_t in [(True, retT[:, hr, :]), (False, retT[:, 4 + hr, :])]:
                        ps = psb[2 + (0 if isk else 1)]
                        pn = PSN[2 + (0 if isk else 1)]
                        mm(ps[:, 0:256], pm[:, :], src_t, ["pm", "retT"], [pn])
                        tt(tmpA[:, 0, :], ps[:, 0:256], rs_t, ALU.mult, [pn, "rope"], ["tmpA"])
                        tt(tmpB[:, 0, :], src_t, rc_t, ALU.mult, ["retT", "rope"], ["tmpB"])
                        if isk:
                            tt(krf[:, hr, :], tmpA[:, 0, :], tmpB[:, 0, :], ALU.add, ["tmpA", "tmpB"], ["krf"])
                            cp(kr[:, hr, :], krf[:, hr, :], ["krf"], ["kr"])
                        else:
                            tt(qr[:, hr, :], tmpA[:, 0, :], tmpB[:, 0, :], ALU.add, ["tmpA", "tmpB"], ["qr"])
                    yield
            else:
                for hr in range(2):
                    cp(krf[:, hr, :], retT[:, hr, :], ["retT"], ["krf"], eng="dve")
                    cp(kr[:, hr, :], retT[:, hr, :], ["retT"], ["kr"])
            if dbg and stop_after == 2.61:
                return "dbg2"
            for c2 in range(2):
                cc = slice(c2 * 128, (c2 + 1) * 128)
                for hr in range(2):
                    ps = psb[2 + hr]
                    pn = PSN[2 + hr]
                    cp(vrb[:, hr, cc], retT[:, 2 + hr, cc], ["retT"], ["vrb"])
                    mm(ps[:, 0:128], vrb[:, hr, cc], identh[:, :], ["vrb", "identh"], [pn])
                    mm(ps[:, 128:256], kr[:, hr, cc], identh[:, :], ["kr", "identh"], [pn])
                    import os as _os
                    cp(vtm[:, c2, hr, :], ps[:, 0:128], [pn], ["vtm"], eng="dve")
                    if True:
                        ts(ktm[:, c2, hr, :], ps[:, 128:256], ksc[:, hr * 2 + sw:hr * 2 + sw + 1], None, ALU.mult, None, [pn, "ksc"], ["ktm"])
                    yield
            if dbg and stop_after == 2.62:
                return "dbg2"
            for c2 in (range(2) if sw == 0 else (1, 0)):
                cc = slice(c2 * 128, (c2 + 1) * 128)
                for hr in range(2):
                    hc = slice(hr * 128, (hr + 1) * 128)
                    if ro:
                        mm(psb[2][:, hc], kr[:, hr, cc], qr[:, hr, cc], ["kr", "qr"], ["psb2"])
                        tt(Ssb[:, hr, :], psb[2][:, hc], dmask[:, hr * 2 + sw, :], ALU.mult, ["psb2", "dmask"], ["Ssb"])
                        tt(qtl[:, hr, cc], qr[:, hr, cc], qsc[:, hr * 2 + sw, :], ALU.mult, ["qr", "qsc"], ["qtl"])
                        yield
                        mm(psb[3][:, hc], vtm[:, c2, hr, :], Ssb[:, hr, :], ["vtm", "Ssb"], ["psb3"], start=True, stop=False)
                        mm(psb[3][:, hc], Sbf[:, hr, :], qtl[:, hr, cc], ["Sbf", "qtl"], ["psb3"], start=False, stop=True)
                        cp(yrT[:, hr, cc], psb[3][:, hc], ["psb3"], ["yrT"])
                        yield
                    mm(psb[0][:, hc], ktm[:, c2, hr, :], vtm[:, c2, hr, :], ["ktm", "vtm"], ["psb0"])
                    stt(Sst[:, hr, :], Sst[:, hr, :], gcs[:, hr:hr + 1], psb[0][:, hc], ALU.mult, ALU.add, ["Sst", "gcs", "psb0"], ["Sst"])
                    cp(Sbf[:, hr, :], Sst[:, hr, :], ["Sst"], ["Sbf"])
                    yield
            yield

        rg_ = ret_steps()

        def rstep():
            try:
                next(rg_)
            except StopIteration:
                pass
        chunk_order = list(range(4)) if sw == 0 else [3, 2, 1, 0]
        for pr in range(2):
          pair = chunk_order[2 * pr:2 * pr + 2]
          for ci, c in enumerate(pair):
            for h in range(4):
                ct, hh = h // 2, h % 2
                hp = slice(hh * 64, hh * 64 + 64)
                pb = psb[4 + hh]
                mm(pb[0:64, (ct * 2) * 128:(ct * 2 + 1) * 128], BKT[hp, ct, c, 0:64], ART[hp, ct, c, :], ["BKT", "ART"], [PSN[4 + hh]])
                mm(pb[0:64, (ct * 2 + 1) * 128:(ct * 2 + 2) * 128], BKT[hp, ct, c, 64:128], ART[hp, ct, c, :], ["BKT", "ART"], [PSN[4 + hh]])
                mm(psb[6 + hh][0:64, ci * 128 + ct * 64:ci * 128 + (ct + 1) * 64], ART[hp, ct, c, 0:64], BKT[hp, ct, c, 0:64], ["BKT", "ART"], [PSN[6 + hh]])
            for hh in range(2):
                tt(Ms2[0:64, ci, 2 * hh:2 * hh + 2, :, :].rearrange("p h m n -> p (h m) n"),
                   psb[4 + hh][0:64, :].rearrange("p (a n) -> p a n", a=4),
                   mst[:, sw, :].unsqueeze(1).to_broadcast([64, 4, 128]), ALU.mult, [PSN[4 + hh], "mst"], ["Ms"])
          for hh in range(2):
            tt(M1s2[0:64, :, 2 * hh:2 * hh + 2, :], psb[6 + hh][0:64, 0:256].rearrange("p (c a n) -> p c a n", c=2, a=2),
               m1m[:, sw, :].unsqueeze(1).unsqueeze(1).to_broadcast([64, 2, 2, 64]), ALU.mult, [PSN[6 + hh], "m1m"], ["M1s"])
          identb = ident[0:64, 0:64].unsqueeze(1).unsqueeze(1).to_broadcast([64, 2, 4, 64])
          tt(Qs[0][0:64], Ms2[0:64, :, :, 0, 0:64], identb, ALU.add, ["Ms", "ident"], ["Qs0"])
          pN = [psb[6 + ci][0:64, :].rearrange("p (h m n) -> p h m n", h=4, m=2) for ci in range(2)]
          pQ = psb[5][0:64, :].rearrange("p (c h n) -> p c h n", c=2, h=4)
          for k in range(1, 6):
            cur, prv = k % 2, (k - 1) % 2
            for ci in range(2):
                for h in range(4):
                    if k == 1:
                        Pp, Ppp = Ms2[0:64, ci, h, 0, 0:64], M1s2[0:64, ci, h, :]
                        rn_ = ["Ms", "M1s"]
                    else:
                        Pp, Ppp = Pa[prv][0:64, ci, h, 0, :], Pa[prv][0:64, ci, h, 1, :]
                        rn_ = ["Pa%d" % prv]
                    mm(pN[ci][:, h, 0, :], Ppp, Pp, rn_, [PSN[6 + ci]])
                    mm(pN[ci][:, h, 1, :], Pp, Ppp, rn_, [PSN[6 + ci]])
            for ci in range(2):
                cp(Pa[cur][0:64, ci].rearrange("p h m n -> p (h m n)"), psb[6 + ci][0:64, :], [PSN[6 + ci]], ["Pa%d" % cur],
                   eng=("act" if ci == 0 else "dve"))
            for ci in range(2):
                for h in range(4):
                    mm(pQ[:, ci, h, :], Pa[cur][0:64, ci, h, 1, :], Qs[prv][0:64, ci, h, :], ["Pa%d" % cur, "Qs%d" % prv], ["psb5"])
            tt(Qs[cur][0:64], pQ, Qs[prv][0:64], ALU.add, ["psb5", "Qs%d" % prv], ["Qs%d" % cur])
            rstep()
          for ci, c in enumerate(pair):
            Ms = Ms2[:, ci]
            TT = Qs[1][:, ci]
            pW = psb[2][0:64, 0:256].rearrange("p (k n) -> p k n", k=2)
            for ct in range(2):
                mm(pW[:, ct, :], ART[:, ct, c, 0:64], Hb[:, ct, :], ["ART", "Hb"], ["psb2"], start=True, stop=False)
                for hh in range(2):
                    mm(pW[:, ct, :], Ms[0:64, hh * 2 + ct, 1, 0:64], BKV[0:64, c, 2, ct, hh, :], ["Ms", "BKV"], ["psb2"],
                       start=False, stop=(hh == 1))
            cp(Wsb[0:64, :, :], pW, ["psb2"], ["Wsb"])
            rstep()
            pU = psb[3][0:64, 0:256].rearrange("p (k n) -> p k n", k=2)
            for ct in range(2):
                for hh in range(2):
                    mm(pU[:, ct, hh * 64:(hh + 1) * 64], TT[0:64, hh * 2 + ct, :], Wsb[0:64, ct, hh * 64:(hh + 1) * 64],
                       ["Qs1", "Wsb"], ["psb3"])
            for hh in range(2):
                cp(Usb[0:64, :, hh, hh * 64:(hh + 1) * 64], pU[:, :, hh * 64:(hh + 1) * 64], ["psb3"], ["Usb"],
                   eng=("act" if hh == 0 else "dve"))
            if ro:
                pY = psb[0][:, 0:128].rearrange("p (k n) -> p k n", k=2)
                for ct in range(2):
                    mm(pY[:, ct, :], Hb[:, ct, :], ART[:, ct, c, 64:128], ["Hb", "ART"], ["psb0"], start=True, stop=False)
                    for hh in range(2):
                        mm(pY[:, ct, :], Usb[0:64, ct, hh, :], Ms[0:64, hh * 2 + ct, 0, 64:128], ["Usb", "Ms"], ["psb0"],
                           start=False, stop=False)
                        mm(pY[:, ct, :], BKV[0:64, c, 2, ct, hh, :], Ms[0:64, hh * 2 + ct, 1, 64:128], ["BKV", "Ms"], ["psb0"],
                           start=False, stop=(hh == 1))
                cp(yT[:, :, c * 64:(c + 1) * 64], pY, ["psb0"], ["yT"])
            pH = psb[1][:, 0:256].rearrange("p (k n) -> p k n", k=2)
            for ct in range(2):
                for hh in range(2):
                    mm(pH[:, ct, :], BKV[0:64, c, 0, ct, hh, :], Usb[0:64, ct, hh, :], ["BKV", "Usb"], ["psb1"],
                       start=(hh == 0), stop=False)
                    mm(pH[:, ct, :], BKV[0:64, c, 1, ct, hh, :], BKV[0:64, c, 2, ct, hh, :], ["BKV"], ["psb1"],
                       start=False, stop=(hh == 1))
                gcol = Ep[:, ct, c * 64 + cend: c * 64 + cend + 1]
                stt(Hst[:, ct, :], Hst[:, ct, :], gcol, pH[:, ct, :], ALU.mult, ALU.add, ["Hst", "Ep", "psb1"], ["Hst"])
            cp(Hb[:], Hst[:], ["Hst"], ["Hb"])
            rstep()
        if dbg and stop_after == 2 and seq == "l" and bi == 0 and sw == 0:
            return "dbg2"
        if dbg and stop_after == 2.5:
            return "dbg2"
        for _ in rg_:
            pass
        if dbg and stop_after == 2.6:
            return "dbg2"
        if not lat:
            return None
        ysl = yspill[:, :, cols].rearrange("k p n -> p k n")
        if sw == 0:
            P.dma("sp", ysl[:, 0:2, :], yT[:], "yst0", reads=["yT"], writes=["yspill%d" % bi])
            P.dma("sp", ysl[:, 2:4, :], yrT[:], "yst1", reads=["yrT"], writes=["yspillr%d" % bi])
            return None
        load(ysp[:], ysl, "yld", ["ysp"], R=["yspill%d" % bi, "yspillr%d" % bi])
        act(gsb[:, 0, :], rwT[:, 8, :], AF.Sigmoid, ["rwT"], ["gsb"])
        act(gsb[0:32, 1, :], rwT[0:32, 9, :], AF.Sigmoid, ["rwT"], ["gsb"])
        od = slice(0, 64)
        for ct in range(2):
            cs_ = slice(ct * 128, (ct + 1) * 128)
            ps = psb[2 + ct]
            pn = PSN[2 + ct]
            y = cumT[:, ct, :]
            sq = cumpT[:, ct, :]
            bn = Ep[:, ct, :]
            tt(y, yT[:, ct, :], ysp[:, ct, :], ALU.add, ["yT", "ysp"], ["cumT"])
            mm(ps[:, 0:256], bones[:, :], y, ["bones", "cumT"], [pn])
            stt(y, ps[:, 0:256], -1.0 / 64.0, y, ALU.mult, ALU.add, [pn, "cumT"], ["cumT"])
            tt(sq, y, y, ALU.mult, ["cumT"], ["cumpT"])
            mm(ps[:, 256:512], bones[:, :], sq, ["bones", "cumpT"], [pn])
            rsq(sq, ps[:, 256:512], 1.0 / 64.0, LNX_EPS, [pn], ["cumpT"])
            tt(y, y, sq, ALU.mult, ["cumT", "cumpT"], ["cumT"])
            ts(y, y, colvs[:, ct, 3:4], colvs[:, ct, 4:5], ALU.mult, ALU.add, ["cumT", "colvs"], ["cumT"])
            mm(ps[:, 0:256], aup_s[od, cs_], adb[od, :], ["aup", "adb"], [pn])
            act(bn, ps[:, 0:256], AF.Sigmoid, [pn, "colvs"], ["Ep"], bias=colvs[:, ct, 5:6])
            ts(bn, bn, colvs[:, ct, 1:2], colvs[:, ct, 7:8], ALU.mult, ALU.add, ["Ep", "colvs"], ["Ep"])
            tt(bn, rwT[:, ct, :], bn, ALU.mult, ["rwT", "Ep"], ["Ep"])
            tt(bn, bn, kdT[:, ct, :], ALU.add, ["Ep", "kdT"], ["Ep"])
            stt(bn, rwT[:, 4 + ct, :], colvs[:, ct, 2:3], bn, ALU.mult, ALU.mult, ["rwT", "colvs", "Ep"], ["Ep"])
            mm(ps[:, 256:512], bones[:, :], bn, ["bones", "Ep"], [pn])
            tt(bn, ps[:, 256:512], rwT[:, 2 + ct, :], ALU.mult, [pn, "rwT"], ["Ep"])
            tt(y, y, bn, ALU.add, ["cumT", "Ep"], ["cumT"])
            mm(ps[:, 0:256], gup_a[:, cs_], gsb[:, 0, :], ["gupa", "gsb"], [pn], start=True, stop=False)
            mm(ps[:, 0:256], gup_b[0:32, cs_], gsb[0:32, 1, :], ["gupb", "gsb"], [pn], start=False, stop=True)
            tt(oT[:, ct, :], y, ps[:, 0:256], ALU.mult, ["cumT", pn], ["oT"])
        for hr in range(2):
            ps = psb[2 + hr]
            pn = PSN[2 + hr]
            y = Em[:, hr, :]
            sq = Epv[:, hr, :]
            tt(y, yrT[:, hr, :], ysp[:, 2 + hr, :], ALU.add, ["yrT", "ysp"], ["Em"])
            tt(sq, y, y, ALU.mult, ["Em"], ["Epv"])
            mm(ps[:, 0:256], ones[:, :], sq, ["ones", "Epv"], [pn])
            rsq(sq, ps[:, 0:256], 1.0 / 128.0, EPS, [pn], ["Epv"])
            tt(y, y, sq, ALU.mult, ["Em", "Epv"], ["Em"])
            act(sq, retT[:, 6 + hr, :], AF.Silu, ["retT"], ["Epv"])
            tt(oT[:, 2 + hr, :], y, sq, ALU.mult, ["Em", "Epv"], ["oT"])
        if dbg and stop_after == 3:
            return "dbg3"
        P.dma("sp", osend.ap()[:, cols].rearrange("(k p) n -> p k n", p=128), oT[:], "ost", reads=["oT"], writes=["osend"])
        return None

    ts(colvs[:, :, 7], colvs[:, :, 1], -1.0, 1.0, ALU.mult, ALU.add, ["colvs"], ["colvs"])

    def run_sweep(sw):
        memset(Hst[:], 0.0, ["Hst"])
        memset(Hb[:], 0.0, ["Hb"])
        memset(Sst[:], 0.0, ["Sst"])
        memset(Sbf[:], 0.0, ["Sbf"])
        order = [("c", 0, 1)] + [("l", bi, 16) for bi in (range(16) if sw == 0 else range(15, -1, -1))]
        for (seq, bi, nb) in order:
            r = block(sw, seq, bi, nb)
            if r is not None:
                return r
        return None

    r = run_sweep(0)
    if r == "dbg2":
        dtile = arena[:, 0:4096]
        memset(dtile[:], 0.0, ["dtile", "winb"])
        cp(dtile[:, 0:512], yT[:].rearrange("p k n -> p (k n)"), ["yT"], ["dtile", "winb"], eng="dve")
        cp(dtile[:, 512:768], Hst[:].rearrange("p k n -> p (k n)"), ["Hst"], ["dtile", "winb"], eng="dve")
        cp(dtile[:, 1024:1536], kkT[:].rearrange("p k n -> p (k n)"), ["kkT"], ["dtile", "winb"], eng="dve")
        cp(dtile[:, 1536:2048], Ep[:].rearrange("p k n -> p (k n)"), ["Ep"], ["dtile", "winb"], eng="dve")
        cp(dtile[:, 2048:2560], aT[:].rearrange("p k n -> p (k n)"), ["aT"], ["dtile", "winb"], eng="dve")
        cp(dtile[:, 2560:3072], kdT[:].rearrange("p k n -> p (k n)"), ["kdT"], ["dtile", "winb"], eng="dve")
        P.dma("sp", dbg_d, dtile[:], "dbgout", reads=["dtile"], writes=["dbg_d"])
        P.emit(es, {"sp": [("dbgout", 16)]})
        es.close()
        return nc
    if r is None:
        r = run_sweep(1)
    if r == "dbg3":
        dtile = arena[:, 0:4096]
        memset(dtile[:], 0.0, ["dtile", "winb"])
        cp(dtile[:, 0:1024], oT[:].rearrange("p k n -> p (k n)"), ["oT"], ["dtile", "winb"], eng="dve")
        cp(dtile[:, 1024:1536], yT[:].rearrange("p k n -> p (k n)"), ["yT"], ["dtile", "winb"], eng="dve")
        cp(dtile[:, 1536:2048], yrT[:].rearrange("p k n -> p (k n)"), ["yrT"], ["dtile", "winb"], eng="dve")
        cp(dtile[:, 2048:3072], ysp[:].rearrange("p k n -> p (k n)"), ["ysp"], ["dtile", "winb"], eng="dve")
        P.dma("sp", dbg_d, dtile[:], "dbgout", reads=["dtile"], writes=["dbg_d"])
        P.emit(es, {"sp": [("dbgout", 16)]})
        es.close()
        return nc
    if r == "dbg1":
        dtile = arena[:, 0:4096]
        cp(dtile[:, 0:2560], rwT[:].rearrange("p k n -> p (k n)"), ["rwT"], ["dtile", "winb"], eng="dve")
        cp(dtile[:, 2560:4096], retT[:, 0:6, :].rearrange("p k n -> p (k n)"), ["retT"], ["dtile", "winb"], eng="dve")
        P.dma("sp", dbg_d, dtile[:], "dbgout", reads=["dtile"], writes=["dbg_d"])
        P.emit(es, {"sp": [("dbgout", 16)]})
        es.close()
        return nc

    if stop_after == "noB":
        P.emit(es, {})
        es.close()
        return nc
    if TWO_A:
        P.dma("sp", modcol_o, modcol[:].rearrange("p a k -> p (a k)"), "mco", reads=["modcol"], writes=["modcol_o"])
        P.emit(es, {"sp": [("ost", P.dma_count["ost"]), ("mco", 16), ("a12o", 16)]})
        es.close()
        return nc
    if TWO_B:
        P = Prog(nc)
        DBG["P"] = P
        memset(epsc[:, 0:1], EPS, ["epsc"])
        load(ident[:], c_ident, CK, ["ident"])
        load(hmask_s[:], hmask, CK, ["hmask"])
        load(fconv_s[:], fconv, CK, ["fconv"])
        load(modcol[:].rearrange("p a k -> p (a k)"), modcol_i, CK, ["modcol"])
        a12sp = a12_i
    if stop_after != "nocc" and not TWO_B:
      P.custom("pool", lambda e: e.collective_compute("AllGather", ALU.bypass, replica_groups=[[0, 1, 2, 3, 4, 5, 6, 7]],
                                                    ins=[osend.ap().opt()], outs=[ogath.ap().opt()]),
               "cc0", reads=["osend"], writes=["ogath"])
    og = ogath.ap() if stop_after != "nocc" else osend.ap()

    prevA = P.all_res()
    HT_W = 9216
    hT = arena[:, 0:HT_W].bitcast(BF16).rearrange("p (k n) -> p k n", k=16)
    B2 = Arena()
    B2.off = HT_W
    oTb = B2.bf16(16 * 1152).rearrange("p (k n) -> p k n", k=16)
    cand = [B2.bf16(8 * 1152).rearrange("p (k n) -> p k n", k=8)]
    woutb = B2.bf16(16 * D).rearrange("p (k n) -> p k n", k=16)
    mixrow = B2.f32(D)
    xtb = B2.f32(D)
    A1row = B2.f32(D)
    sqb = B2.bf16(D)
    statb = B2.f32(16)
    for nm in ["hT", "oTb", "cand0", "woutb", "mixrow", "xtb", "A1row", "sqb", "statb"]:
        P.alias(nm, prevA)
    P.dma("pool", woutb, wout, "woutk", writes=["woutb"])
    load(A1row, a12sp[:, 0:D].partition_broadcast(128), "a1k", ["A1row"], R=["a12sp"])
    NB_ = 2 if stop_after != "nocc" else 1
    if TWO_B:
        load(oTb, oin, "oink", ["oTb"])
    if NB_ == 1:
        memset(cand[0][:, 4:8, :], 0.0, ["cand0"])
    for bq in range(NB_):
        memset(cand[0][:, bq * 4 + 0, 0:64], 0.0, ["cand0"])
        memset(cand[0][:, bq * 4 + 3, 1088:1152], 0.0, ["cand0"])
    for kt in (range(16) if not TWO_B else []):
        cb = cand[0]
        cn = "cand0"
        for bq in range(NB_):
            r0 = (bq * 4 + kt // 4) * 512 + (kt % 4) * 128
            if stop_after == "nocc":
                r0 = (kt % 4) * 128
            load(cb[:, bq * 4 + 0, 64:1152], og[r0:r0 + 128, 0:1088], "candk0", [cn], R=["ogath", "osend"])
            load(cb[:, bq * 4 + 1, :], og[r0:r0 + 128, 960:2112], "candk0", [cn], R=["ogath", "osend"])
            load(cb[:, bq * 4 + 2, :], og[r0:r0 + 128, 1984:3136], "candk0", [cn], R=["ogath", "osend"])
            load(cb[:, bq * 4 + 3, 0:1088], og[r0:r0 + 128, 3008:4096], "candk0", [cn], R=["ogath", "osend"])
        ts(oTb[:, kt, :], cb[:, 0, :], tsel_s[:, 0:1], None, ALU.mult, None, [cn, "tsel"], ["oTb"])
        for g_ in range(1, 8):
            stt(oTb[:, kt, :], cb[:, g_, :], tsel_s[:, g_:g_ + 1], oTb[:, kt, :], ALU.mult, ALU.add, [cn, "tsel", "oTb"], ["oTb"])
    for t_ in range(9):
        tk = slice(t_ * 128, (t_ + 1) * 128)
        load(xtb, xb[tk, :], "xbk", ["xtb"])
        for cch in range(4):
            ps = psb[cch % 2]
            pn = PSN[cch % 2]
            for kt in range(16):
                mm(ps[:, :], oTb[:, kt, tk], woutb[:, kt, cch * 512:(cch + 1) * 512], ["oTb", "woutb"], [pn], start=(kt == 0), stop=(kt == 15))
            cp(mixrow[:, cch * 512:(cch + 1) * 512], ps[:, :], [pn], ["mixrow"])
        act(sqb, mixrow, AF.Square, ["mixrow"], ["sqb", "statb"], accum=statb[:, 0:1])
        rsq(statb[:, 1:2], statb[:, 0:1], 1.0 / D, EPS, ["statb"], ["statb"])
        stt(mixrow, mixrow, statb[:, 1:2], A1row, ALU.mult, ALU.mult, ["mixrow", "statb", "A1row"], ["mixrow"])
        tt(xtb, xtb, mixrow, ALU.add, ["xtb", "mixrow"], ["xtb"])
        if t_ == 0:
            P.dma("sp", x1sp[0:64, :], xtb[64:128, :], "x1st", reads=["xtb"], writes=["x1sp"])
        elif t_ == 8:
            P.dma("sp", x1sp[960:1024, :], xtb[0:64, :], "x1st", reads=["xtb"], writes=["x1sp"])
        else:
            P.dma("sp", x1sp[t_ * 128 - 64:t_ * 128 + 64, :], xtb, "x1st", reads=["xtb"], writes=["x1sp"])
        act(sqb, xtb, AF.Square, ["xtb"], ["sqb", "statb"], accum=statb[:, 2:3])
        rsq(statb[:, 3:4], statb[:, 2:3], 1.0 / D, EPS, ["statb"], ["statb"])
        ts(mixrow, xtb, statb[:, 3:4], None, ALU.mult, None, ["xtb", "statb"], ["mixrow"])
        for k4 in range(4):
            ps = psb[2 + k4 % 2]
            pn = PSN[2 + k4 % 2]
            for kk_ in range(4):
                kt = k4 * 4 + kk_
                trp(ps[:, kk_ * 128:(kk_ + 1) * 128], mixrow[:, kt * 128:(kt + 1) * 128], ident[:, :], ["mixrow", "ident"], [pn])
            for kk_ in range(4):
                kt = k4 * 4 + kk_
                if kk_ % 2 == 0:
                    act(hT[:, kt, tk], ps[:, kk_ * 128:(kk_ + 1) * 128], AF.Identity, [pn, "modcol"], ["hT"],
                        bias=modcol[:, 2, kt:kt + 1], scale=modcol[:, 3, kt:kt + 1])
                else:
                    ts(hT[:, kt, tk], ps[:, kk_ * 128:(kk_ + 1) * 128], modcol[:, 3, kt:kt + 1], modcol[:, 2, kt:kt + 1],
                       ALU.mult, ALU.add, [pn, "modcol"], ["hT"])
    B2res = ["oTb", "cand0", "woutb", "mixrow", "xtb", "A1row", "sqb", "statb"]
    B3 = Arena()
    B3.off = HT_W
    wgb = [B3.bf16(16 * 128).rearrange("p (k n) -> p k n", k=16) for _ in range(2)]
    wub = [B3.bf16(16 * 128).rearrange("p (k n) -> p k n", k=16) for _ in range(2)]
    gts = B3.f32(1152)
    cacc = B3.f32(1024)
    HID_OFF = ARENA_W - 22528
    assert B3.off <= HID_OFF
    hid = arena[:, HID_OFF:ARENA_W].bitcast(BF16).rearrange("p (k n) -> p k n", k=NFT)
    for nm in ["wgb0", "wgb1", "wub0", "wub1", "gts", "cacc", "hid"]:
        P.alias(nm, B2res)
    g3 = gts.rearrange("p (r c) -> p r c", c=64)
    a3 = cacc.rearrange("p (r c) -> p r c", c=64)
    for ft in range(NFT):
        wg, wu = wgb[ft % 2], wub[ft % 2]
        P.dma("pool", wg, wgate[ft], "wgk%d" % (ft % 2), writes=["wgb%d" % (ft % 2)])
        P.dma("pool", wu, wupf[ft], "wuk%d" % (ft % 2), writes=["wub%d" % (ft % 2)])
        for ch in range(3):
            ps = psb[ch]
            for kt in range(16):
                mm(ps[:, 0:384], wg[:, kt, :], hT[:, kt, ch * 384:(ch + 1) * 384], ["wgb%d" % (ft % 2), "hT"], [PSN[ch]],
                   start=(kt == 0), stop=(kt == 15))
            cp(gts[:, ch * 384:(ch + 1) * 384], ps[:, 0:384], [PSN[ch]], ["gts"])
        ts(gts[:, 0:64], gts[:, 0:64], hmask_s[:, 0:1], None, ALU.mult, None, ["gts", "hmask"], ["gts"])
        ts(gts[:, 1088:1152], gts[:, 1088:1152], hmask_s[:, 1:2], None, ALU.mult, None, ["gts", "hmask"], ["gts"])
        ts(a3, g3[:, 1:17, :], fconv_s[:, ft, 4:5], fconv_s[:, ft, 9:10], ALU.mult, ALU.add, ["gts", "fconv"], ["cacc"])
        for dr in range(3):
            for dc in range(3):
                if dr == 1 and dc == 1:
                    continue
                wcol = fconv_s[:, ft, dr * 3 + dc:dr * 3 + dc + 1]
                if dc == 1:
                    o_, i_ = a3, g3[:, dr:dr + 16, :]
                elif dc == 0:
                    o_, i_ = a3[:, :, 1:64], g3[:, dr:dr + 16, 0:63]
                else:
                    o_, i_ = a3[:, :, 0:63], g3[:, dr:dr + 16, 1:64]
                stt(o_, i_, wcol, o_, ALU.mult, ALU.add, ["gts", "fconv", "cacc"], ["cacc"])
        act(cacc, cacc, AF.Silu, ["cacc"], ["cacc"])
        for ch in range(2):
            ps = psb[3 + ch]
            for kt in range(16):
                mm(ps[:, :], wu[:, kt, :], hT[:, kt, 64 + ch * 512:64 + (ch + 1) * 512], ["wub%d" % (ft % 2), "hT"], [PSN[3 + ch]],
                   start=(kt == 0), stop=(kt == 15))
            tt(hid[:, ft, ch * 512:(ch + 1) * 512], ps[:, :], cacc[:, ch * 512:(ch + 1) * 512], ALU.mult, [PSN[3 + ch], "cacc"], ["hid"])
    yTf = arena[:, 0:16384].rearrange("p (k n) -> p k n", k=16)
    B4 = Arena()
    B4.off = 16384
    wdb = [B4.bf16(NFT * 128).rearrange("p (k n) -> p k n", k=NFT) for _ in range(2)]
    assert B4.off <= HID_OFF
    B3res = ["hT", "wgb0", "wgb1", "wub0", "wub1", "gts", "cacc"]
    for nm in ["yTf", "wdb0", "wdb1"]:
        P.alias(nm, B3res)
    for mt in range(16):
        wd = wdb[mt % 2]
        P.dma("pool", wd, wdown[mt], "wdk%d" % (mt % 2), writes=["wdb%d" % (mt % 2)])
        for ch in range(2):
            ps = psb[(mt * 2 + ch) % 4]
            pn = PSN[(mt * 2 + ch) % 4]
            for ft in range(NFT):
                mm(ps[:, :], wd[:, ft, :], hid[:, ft, ch * 512:(ch + 1) * 512], ["wdb%d" % (mt % 2), "hid"], [pn],
                   start=(ft == 0), stop=(ft == NFT - 1))
            cp(yTf[:, mt, ch * 512:(ch + 1) * 512], ps[:, :], [pn], ["yTf"])
    B5 = Arena()
    B5.off = 16384
    ytm = B5.f32(D)
    x1t = [B5.f32(D) for _ in range(2)]
    A2row = B5.f32(D)
    sq5 = B5.bf16(D)
    st5 = B5.f32(8)
    assert B5.off <= HID_OFF
    for nm in ["ytm", "x1t0", "x1t1", "A2row", "sq5", "st5"]:
        P.alias(nm, ["wdb0", "wdb1"])
    load(A2row, a12sp[:, D:2 * D].partition_broadcast(128), "a2k", ["A2row"], R=["a12sp"])
    for t_ in range(8):
        tk = slice(t_ * 128, (t_ + 1) * 128)
        xt_ = x1t[t_ % 2]
        xn = "x1t%d" % (t_ % 2)
        load(xt_, x1sp[tk, :], "x1ld%d" % (t_ % 2), [xn], R=["x1sp"])
        for k4 in range(4):
            ps = psb[4 + k4 % 2]
            pn = PSN[4 + k4 % 2]
            for kk_ in range(4):
                mt = k4 * 4 + kk_
                trp(ps[:, kk_ * 128:(kk_ + 1) * 128], yTf[:, mt, tk], ident[:, :], ["yTf", "ident"], [pn])
            cp(ytm[:, k4 * 512:(k4 + 1) * 512], ps[:, :], [pn], ["ytm"], eng=("act" if k4 % 2 == 0 else "dve"))
        act(sq5, ytm, AF.Square, ["ytm"], ["sq5", "st5"], accum=st5[:, 0:1])
        rsq(st5[:, 1:2], st5[:, 0:1], 1.0 / D, EPS, ["st5"], ["st5"])
        stt(ytm, ytm, st5[:, 1:2], A2row, ALU.mult, ALU.mult, ["ytm", "st5", "A2row"], ["ytm"])
        tt(xt_, xt_, ytm, ALU.add, [xn, "ytm"], [xn])
        P.dma("sp", out_d[tk, :], xt_, "outk%d" % (t_ % 2), reads=[xn], writes=["out_d"])
    P.emit(es, {"sp": [("outk0", P.dma_count["outk0"]), ("outk1", P.dma_count["outk1"])]})
    es.close()
    return nc


def _prep(inp):
    f = lambda a: np.ascontiguousarray(np.asarray(a, dtype=np.float32))
    x, c, ctx, c_ctx = f(inp["x"]), f(inp["c"]), f(inp["ctx"]), f(inp["c_ctx"])
    w_in = f(inp["w_in"][0])
    RW = 1024
    shared = {}
    shared["wada"] = np.ascontiguousarray(
        f(inp["w_ada"][0]).reshape(16, 128, 24, 512).transpose(2, 1, 0, 3))
    shared["bada"] = f(inp["b_ada"][0]).reshape(1, 12288)
    shared["nrm_col"] = np.ascontiguousarray(
        np.stack([_col(f(inp["norm_pre_mix"][0])), _col(f(inp["norm_pre_ffn"][0]))], axis=1))
    shared["nrm_row"] = np.stack([f(inp["norm_post_mix"][0]), f(inp["norm_post_ffn"][0])], axis=0)
    shared["wgate"] = np.ascontiguousarray(
        f(inp["ffn_w_gate"][0]).reshape(16, 128, NFT, 128).transpose(2, 1, 0, 3))
    shared["wupf"] = np.ascontiguousarray(
        f(inp["ffn_w_up"][0]).reshape(16, 128, NFT, 128).transpose(2, 1, 0, 3))
    shared["wdown"] = np.ascontiguousarray(
        f(inp["ffn_w_down"][0]).reshape(NFT, 128, 16, 128).transpose(2, 1, 0, 3))
    fc = np.concatenate([f(inp["ffn_conv"][0]).reshape(9, DFF), f(inp["ffn_conv_b"][0]).reshape(1, DFF)], axis=0)
    shared["fconv"] = np.ascontiguousarray(fc.reshape(10, NFT, 128).transpose(2, 1, 0))
    w_out = f(inp["w_out"][0])
    maps = []
    for core in range(8):
        b, g = core // 4, core % 4
        m = dict(shared)
        xa = np.zeros((SEQ + CTX + 4, D), np.float32)
        xa[1:257] = ctx[b]
        xa[259:259 + SEQ] = x[b]
        m["xa"] = xa
        xbm = np.zeros((1152, D), np.float32)
        lo, hi = g * 1024 - 64, g * 1024 + 1088
        slo, shi = max(lo, 0), min(hi, SEQ)
        xbm[slo - lo: shi - lo] = x[b, slo:shi]
        m["xb"] = xbm
        cv = np.stack([c[b], c_ctx], axis=1)
        m["cvt"] = np.ascontiguousarray(cv.reshape(16, 128, 2).transpose(1, 0, 2))
        cs = slice(g * 256, (g + 1) * 256)
        cols = np.concatenate([
            np.arange(0, RW)[cs], np.arange(RW, 2 * RW)[cs],
            np.arange(2304, 2304 + RW)[cs],
            np.arange(2048, 2304),
            np.arange(3328, 3488),
            3488 + np.arange(0, RW)[cs], 3488 + np.arange(RW, 2 * RW)[cs],
            3488 + np.arange(2 * RW, 3 * RW)[cs], 3488 + np.arange(3 * RW, 4 * RW)[cs]])
        assert cols.size == NCOLS
        m["win"] = _ktile(w_in[:, cols])
        rc = f(inp["rw_conv"][0])[:, cols[:1184]]
        rcp = np.zeros((3, 1280), np.float32)
        rcp[:, :1184] = rc
        m["rwconv"] = np.ascontiguousarray(rcp.reshape(3, 10, 128).transpose(2, 1, 0))
        cvs = np.zeros((128, 2, 9), np.float32)
        hs = slice(g * 256, (g + 1) * 256)
        vecs = [f(inp["rw_k_k"][0])[hs], f(inp["rw_k_a"][0])[hs], f(inp["rw_r_k"][0]).reshape(-1)[hs],
                f(inp["rw_lnx_w"][0])[hs], f(inp["rw_lnx_b"][0])[hs],
                f(inp["rw_a0"][0])[0, hs], f(inp["rw_a0"][0])[1, hs]]
        for vi, v in enumerate(vecs):
            cvs[:, :, vi] = v.reshape(2, 128).T
        m["colv"] = cvs
        m["w0row"] = np.concatenate([f(inp["rw_w0"][0])[0, hs], f(inp["rw_w0"][0])[1, hs]]).reshape(1, 512)
        m["wup"] = np.concatenate([f(inp["rw_w_up"][0])[0][:, hs], f(inp["rw_w_up"][0])[1][:, hs]], axis=0)
        m["aup"] = np.concatenate([f(inp["rw_a_up"][0])[0][:, hs], f(inp["rw_a_up"][0])[1][:, hs]], axis=0)
        m["gup"] = np.ascontiguousarray(f(inp["rw_g_up"][0])[:, hs])
        rows = np.concatenate([np.concatenate([np.arange(gp * 256, (gp + 1) * 256),
                                               1024 + np.arange(gp * 256, (gp + 1) * 256)]) for gp in range(4)])
        m["wout"] = _ktile(w_out[rows])
        hm = np.zeros((128, 2), np.float32)
        hm[:, 0] = 1.0 if g > 0 else 0.0
        hm[:, 1] = 1.0 if g < 3 else 0.0
        m["hmask"] = hm
        tsl = np.zeros((128, 8), np.float32)
        tsl[:, b * 4 + g] = 1.0
        m["tsel"] = tsl
        for k, v in _consts(g).items():
            m["c_" + k] = v
        maps.append(m)
    return maps


_NC_CACHE = {}
FUSED = False


def kernel(**inputs):
    import ml_dtypes
    maps = _prep(inputs)
    out = np.zeros((2, SEQ, D), np.float32)
    if FUSED:
        if "nc" not in _NC_CACHE:
            _NC_CACHE["nc"] = build()
        res = run_bass_kernel_spmd(_NC_CACHE["nc"], maps, core_ids=list(range(8)))
        for core in range(8):
            b, g = core // 4, core % 4
            out[b, g * 1024:(g + 1) * 1024] = np.asarray(res.results[core]["out"], dtype=np.float32)
        return out
    if "ncA" not in _NC_CACHE:
        _NC_CACHE["ncA"] = build(stop_after="A2L")
        _NC_CACHE["ncB"] = build(stop_after="B2L")
    dummy = {"oin": np.zeros((128, 16, 1152), ml_dtypes.bfloat16), "modcol_i": np.zeros((128, 96), np.float32),
             "a12_i": np.zeros((1, 2 * D), np.float32)}
    resA = run_bass_kernel_spmd(_NC_CACHE["ncA"], maps, core_ids=list(range(8)))
    osend = [np.asarray(resA.results[c]["osend"]) for c in range(8)]
    mapsB = []
    for core in range(8):
        b, g = core // 4, core % 4
        m = dict(maps[core])
        oin = np.zeros((128, 16, 1152), osend[0].dtype)
        lo, hi = g * 1024 - 64, g * 1024 + 1088
        slo, shi = max(lo, 0), min(hi, SEQ)
        for kt in range(16):
            src = osend[4 * b + kt // 4]
            j = kt % 4
            oin[:, kt, slo - lo:shi - lo] = src[j * 128:(j + 1) * 128, slo:shi]
        m["oin"] = oin
        m["modcol_i"] = np.asarray(resA.results[core]["modcol_o"])
        m["a12_i"] = np.asarray(resA.results[core]["a12_o"])
        mapsB.append(m)
    resB = run_bass_kernel_spmd(_NC_CACHE["ncB"], mapsB, core_ids=list(range(8)))
    for core in range(8):
        b, g = core // 4, core % 4
        out[b, g * 1024:(g + 1) * 1024] = np.asarray(resB.results[core]["out"], dtype=np.float32)
    return out
```

```python
import numpy as np
from contextlib import ExitStack
import concourse.bass as bass
import concourse.mybir as mybir
from concourse.bass_utils import run_bass_kernel_spmd

F32 = mybir.dt.float32
BF16 = mybir.dt.bfloat16
AF = mybir.ActivationFunctionType
ALU = mybir.AluOpType
AX = mybir.AxisListType

D = 2048
SEQ = 4096
CTX = 256
DFF = 5632
NFT = 44
EPS = 1e-6
LNX_EPS = 64e-5
BLK = 256
NWIN = 258
C = 64
NCOLS = 2208
DECAY_C = float(np.exp(-0.5))

DBG = {}


class Ev:
    __slots__ = ("kind", "eng", "idx", "key", "val", "needed")

    def __init__(self, kind, eng=None, idx=0, key=None, val=0):
        self.kind, self.eng, self.idx, self.key, self.val, self.needed = kind, eng, idx, key, val, False


class Prog:
    ENGS = ["pe", "act", "dve", "pool", "sp"]

    def __init__(self, nc):
        self.nc = nc
        self.stream = {e: [] for e in self.ENGS}
        self.last_write = {}
        self.readers = {}
        self.waited = {e: {} for e in self.ENGS}
        self.dma_count = {}
        self.dma_keys = []
        self.last_ev = {}
        self.trace_lines = False
        self.lines = {}
        self.imap = {}

    def _deps(self, eng, reads, writes):
        evs = []
        for r in reads:
            w = self.last_write.get(r)
            if w is not None:
                evs.append(w)
        for r in writes:
            w = self.last_write.get(r)
            if w is not None:
                evs.append(w)
            evs.extend(self.readers.get(r, ()))
        best = {}
        for ev in evs:
            if ev.kind == "eng":
                if ev.eng == eng and eng == "pe":
                    continue
                k = ("eng", ev.eng)
                if k not in best or best[k].idx < ev.idx:
                    best[k] = ev
            else:
                k = ("dma", ev.key)
                if k not in best or best[k].val < ev.val:
                    best[k] = ev
        out = []
        for k, ev in best.items():
            cur = self.waited[eng].get(k, -1)
            v = ev.idx if ev.kind == "eng" else ev.val
            if cur >= v:
                continue
            self.waited[eng][k] = v
            ev.needed = True
            out.append(ev)
        return out

    def _commit(self, ev, reads, writes):
        for r in reads:
            self.readers.setdefault(r, []).append(ev)
        for r in writes:
            self.last_write[r] = ev
            self.readers[r] = []

    def op(self, eng, fn, reads=(), writes=()):
        waits = self._deps(eng, reads, writes)
        ev = Ev("eng", eng=eng, idx=len(self.stream[eng]))
        if self.trace_lines:
            import sys as _s
            fr = _s._getframe(2)
            self.lines[(eng, len(self.stream[eng]))] = (fr.f_lineno, fr.f_back.f_lineno if fr.f_back else 0)
        self.stream[eng].append((fn, waits, ev))
        self._commit(ev, reads, writes)
        self.last_ev[eng] = ev
        return ev

    def dma(self, eng, out, in_, key, reads=(), writes=(), **kw):
        waits = self._deps(eng, reads, writes)
        if key not in self.dma_count:
            self.dma_count[key] = 0
            self.dma_keys.append(key)
        self.dma_count[key] += 16
        ev = Ev("dma", eng=eng, idx=len(self.stream[eng]), key=key, val=self.dma_count[key])
        self.stream[eng].append((lambda e: e.dma_start(out=out, in_=in_, **kw), waits, ev))
        self._commit(ev, reads, writes)
        return ev

    def custom(self, eng, fn, key, reads=(), writes=(), inc=1):
        waits = self._deps(eng, reads, writes)
        if key not in self.dma_count:
            self.dma_count[key] = 0
            self.dma_keys.append(key)
        self.dma_count[key] += inc
        ev = Ev("dma", eng=eng, idx=len(self.stream[eng]), key=key, val=self.dma_count[key])
        self.stream[eng].append((fn, waits, ev))
        self._commit(ev, reads, writes)
        return ev

    def alias(self, new, olds):
        evs = []
        for o in olds:
            w = self.last_write.get(o)
            if w is not None:
                evs.append(w)
            evs.extend(self.readers.get(o, ()))
        self.readers.setdefault(new, []).extend(evs)

    def all_res(self):
        return list(set(list(self.last_write.keys()) + list(self.readers.keys())))

    def emit(self, es, final_waits):
        nc = self.nc
        LIM = 30000
        engobj = {"pe": nc.tensor, "act": nc.scalar, "dve": nc.vector, "pool": nc.gpsimd, "sp": nc.sync}
        esems = {}
        for e in self.ENGS:
            cnt = 0
            for (fn, waits, ev) in self.stream[e]:
                if ev.kind == "eng" and ev.needed:
                    cnt += 1
                    ev.val = cnt
            nsem = cnt // LIM + 1
            import os as _os
            if _os.environ.get("KCNT"):
                print("SEMCNT", e, cnt, "ninstr", len(self.stream[e]), "nwaits", sum(len(w) for (_, w, _) in self.stream[e]))
            esems[e] = [es.enter_context(nc.semaphore("s_%s_%d" % (e, i))) for i in range(nsem)]
        dsems = {k: es.enter_context(nc.semaphore("d_%s" % str(k))) for k in self.dma_keys}

        def semval(ev):
            if ev.kind == "eng":
                i = (ev.val - 1) // LIM
                return esems[ev.eng][i], ev.val - i * LIM
            if ev.key in ("const", "constp"):
                return dsems[ev.key], self.dma_count[ev.key]
            return dsems[ev.key], ev.val

        block = es.enter_context(nc.Block())
        streams = self.stream

        def run(e, eng):
            for ii, (fn, waits, ev) in enumerate(streams[e]):
                for w in waits:
                    s, v = semval(w)
                    eng.wait_ge(s, v)
                ins = fn(eng)
                if self.trace_lines:
                    try:
                        self.imap[str(ins.ins.name)] = self.lines.get((e, ii))
                    except Exception:
                        pass
                if ev.kind == "dma":
                    if ev.eng is not None and ev.key is not None:
                        inc = 16 if not str(ev.key).startswith("cc") else 1
                        ins.then_inc(dsems[ev.key], inc)
                elif ev.needed:
                    s, v = semval(ev)
                    ins.then_inc(s, 1)
            for (k, v) in final_waits.get(e, []):
                eng.wait_ge(dsems[k], v)

        @block.tensor
        def _(eng):
            run("pe", eng)

        @block.scalar
        def _(eng):
            run("act", eng)

        @block.vector
        def _(eng):
            run("dve", eng)

        @block.gpsimd
        def _(eng):
            run("pool", eng)

        @block.sync
        def _(eng):
            run("sp", eng)


def _consts(g):
    cst = {}
    cst["ident"] = np.eye(128, dtype=np.float32)
    bo = np.zeros((128, 128), np.float32)
    bo[:64, :64] = 1.0
    bo[64:, 64:] = 1.0
    cst["bones"] = bo
    cst["ones"] = np.ones((128, 128), np.float32)
    s = np.arange(128)[:, None]
    t = np.arange(128)[None, :]
    same = (s // 64) == (t // 64)
    tri = np.zeros((128, 4, 128), np.float32)
    tri[:, 0, :] = same & (s <= t)
    tri[:, 1, :] = same & (s < t)
    tri[:, 2, :] = same & (s >= t)
    tri[:, 3, :] = same & (s > t)
    cst["tri"] = tri
    s = np.arange(64)[:, None]
    t = np.arange(64)[None, :]
    mst = np.zeros((64, 2, 128), np.float32)
    mst[:, 0, :64] = s < t
    mst[:, 0, 64:] = s <= t
    mst[:, 1, :64] = s > t
    mst[:, 1, 64:] = s >= t
    cst["mst"] = mst
    m1 = np.zeros((64, 2, 64), np.float32)
    m1[:, 0, :] = (t < s).T.T
    tt = np.arange(64)[:, None]
    ss = np.arange(64)[None, :]
    m1[:, 0, :] = ss < tt
    m1[:, 1, :] = ss > tt
    cst["m1"] = m1
    j = np.arange(128)[:, None].astype(np.float64)
    i = np.arange(128)[None, :].astype(np.float64)
    dmask = np.zeros((128, 4, 128), np.float32)
    qsc = np.zeros((128, 4, 128), np.float32)
    ksc = np.zeros((128, 4), np.float32)
    gc = np.zeros((128, 2), np.float32)
    sc = 128.0 ** -0.5
    for hr in range(2):
        gam = 1.0 - 2.0 ** (-5.0 - (2 * g + hr))
        lg = np.log(gam)
        dmask[:, hr * 2 + 0, :] = np.where(j <= i, np.exp((i - j) * lg), 0.0) * sc
        dmask[:, hr * 2 + 1, :] = np.where(j > i, np.exp((j - i) * lg), 0.0) * sc
        qsc[:, hr * 2 + 0, :] = np.exp((i + 1.0) * lg)
        qsc[:, hr * 2 + 1, :] = np.exp((128.0 - i) * lg)
        ksc[:, hr * 2 + 0] = (np.exp((127.0 - j) * lg) * sc)[:, 0]
        ksc[:, hr * 2 + 1] = (np.exp(j * lg) * sc)[:, 0]
        gc[:, hr] = np.exp(128.0 * lg)
    cst["dmask"], cst["qsc"], cst["ksc"], cst["gc"] = dmask, qsc, ksc, gc
    n = 32
    inv = 10000.0 ** (-np.arange(n, dtype=np.float64) / n)
    tpos = np.arange(SEQ)
    ang = np.zeros((128, SEQ), np.float64)
    for d in range(128):
        if d < 64:
            ang[d] = (tpos // 64) * inv[d % 32]
        else:
            ang[d] = (tpos % 64) * inv[d % 32]
    cst["ropec"] = np.cos(ang).astype(np.float32)
    cst["ropes"] = np.sin(ang).astype(np.float32)
    pm = np.zeros((128, 128), np.float32)
    for dp in range(128):
        if (dp % 64) < 32:
            pm[dp + 32, dp] = -1.0
        else:
            pm[dp - 32, dp] = 1.0
    cst["pm"] = pm
    sel = np.zeros((2, 130), np.float32)
    sel[0, 0] = 1.0
    sel[1, 1] = 1.0
    sel[0, 2:] = 1.0
    cst["sel"] = sel
    return cst


def _ktile(w):
    K, N = w.shape
    return np.ascontiguousarray(w.reshape(K // 128, 128, N).transpose(1, 0, 2))


def _col(v):
    return np.ascontiguousarray(v.reshape(-1, 128).T)


def build(stop_after=None, dbg=False):
    nc = bass.Bass("TRN2", target_bir_lowering=False)
    P = Prog(nc)
    P.trace_lines = dbg
    DBG["P"] = P
    es = ExitStack()

    def din(name, shape, dt=F32):
        return nc.dram_tensor(name, list(shape), dt, kind="ExternalInput").ap()

    xa = din("xa", [SEQ + CTX + 4, D])
    xb = din("xb", [1152, D])
    cvt = din("cvt", [128, 16, 2])
    wada = din("wada", [24, 128, 16, 512])
    bada = din("bada", [1, 12288])
    nrm_col = din("nrm_col", [128, 2, 16])
    nrm_row = din("nrm_row", [2, D])
    win = din("win", [128, 16, NCOLS])
    rwconv = din("rwconv", [128, 10, 3])
    colv = din("colv", [128, 2, 9])
    w0row = din("w0row", [1, 2 * 256])
    wup = din("wup", [128, 256])
    aup = din("aup", [128, 256])
    gup = din("gup", [160, 256])
    wout = din("wout", [128, 16, D])
    wgate = din("wgate", [NFT, 128, 16, 128])
    wupf = din("wupf", [NFT, 128, 16, 128])
    wdown = din("wdown", [16, 128, NFT, 128])
    fconv = din("fconv", [128, NFT, 10])
    hmask = din("hmask", [128, 2])
    tsel = din("tsel", [128, 8])
    c_ident = din("c_ident", [128, 128])
    c_bones = din("c_bones", [128, 128])
    c_ones = din("c_ones", [128, 128])
    c_tri = din("c_tri", [128, 4, 128])
    c_mst = din("c_mst", [64, 2, 128])
    c_m1 = din("c_m1", [64, 2, 64])
    c_dmask = din("c_dmask", [128, 4, 128])
    c_qsc = din("c_qsc", [128, 4, 128])
    c_ksc = din("c_ksc", [128, 4])
    c_gc = din("c_gc", [128, 2])
    c_ropec = din("c_ropec", [128, SEQ])
    c_ropes = din("c_ropes", [128, SEQ])
    c_pm = din("c_pm", [128, 128])
    c_sel = din("c_sel", [2, 130])
    out_d = None
    yspill = nc.dram_tensor("yspill", [4, 128, SEQ], F32).ap()
    pspill = nc.dram_tensor("pspill", [17, 18, 128, BLK], F32).ap()
    TWO_A = stop_after == "A2L"
    TWO_B = stop_after == "B2L"
    osend = nc.dram_tensor("osend", [512, SEQ], BF16, kind=("ExternalOutput" if TWO_A else "Internal"))
    if not TWO_A:
        out_d = nc.dram_tensor("out", [1024, D], F32, kind="ExternalOutput").ap()
    if TWO_A:
        modcol_o = nc.dram_tensor("modcol_o", [128, 96], F32, kind="ExternalOutput").ap()
        a12_o = nc.dram_tensor("a12_o", [1, 2 * D], F32, kind="ExternalOutput").ap()
    if TWO_B:
        oin = din("oin", [128, 16, 1152], BF16)
        modcol_i = din("modcol_i", [128, 96])
        a12_i = din("a12_i", [1, 2 * D])
    ogath = nc.dram_tensor("ogath", [8 * 512, SEQ], BF16)
    x1sp = nc.dram_tensor("x1sp", [1024, D], F32).ap()
    dbg_d = None
    if dbg:
        dbg_d = nc.dram_tensor("dbg", [128, 4096], F32, kind="ExternalOutput").ap()

    def sb(name, shape, dt=F32):
        return es.enter_context(nc.sbuf_tensor(name, list(shape), dt))

    ident = sb("ident", [128, 128])
    bones = sb("bones", [128, 128])
    ones = sb("ones", [128, 128])
    tri = sb("tri", [128, 4, 128])
    mst = sb("mst", [64, 2, 128])
    m1m = sb("m1m", [64, 2, 64])
    dmask = sb("dmask", [128, 4, 128])
    qsc = sb("qsc", [128, 4, 128])
    ksc = sb("ksc", [128, 4])
    gcs = sb("gcs", [128, 2])
    pm = sb("pm", [128, 128])
    sel = sb("sel", [2, 130])
    nrmc = sb("nrmc", [128, 2, 16])
    rwcv = sb("rwcv", [128, 10, 3])
    colvs = sb("colvs", [128, 2, 9])
    w0bc = sb("w0bc", [128, 512])
    wup_s = sb("wup_s", [128, 256], BF16)
    aup_s = sb("aup_s", [128, 256], BF16)
    gup_a = sb("gup_a", [128, 256], BF16)
    gup_b = sb("gup_b", [32, 256], BF16)
    hmask_s = sb("hmask_s", [128, 2])
    tsel_s = sb("tsel_s", [128, 8])
    fconv_s = sb("fconv_s", [128, NFT, 10])
    modcol = sb("modcol", [128, 6, 16])
    ARENA_W = 48800
    arena = sb("arena", [128, ARENA_W])
    psb = [es.enter_context(nc.psum_tensor("psb%d" % i, [128, 512], F32)) for i in range(8)]

    class Arena:
        def __init__(self):
            self.off = 0

        def f32(self, n):
            o = self.off
            self.off += n
            assert self.off <= ARENA_W, self.off
            return arena[:, o:o + n]

        def bf16(self, n):
            w = (n + 1) // 2
            o = self.off
            self.off += w
            assert self.off <= ARENA_W, self.off
            return arena[:, o:o + w].bitcast(BF16)

    V, S, T, G, PE = "dve", "act", "pe", "pool", "pe"

    def tt(out, a, b, op, R, W, eng="dve"):
        P.op(eng, lambda e: e.tensor_tensor(out=out, in0=a, in1=b, op=op), R, W)

    def ts(out, a, s1, s2, op0, op1, R, W, eng="dve"):
        if s2 is None:
            P.op(eng, lambda e: e.tensor_scalar(out=out, in0=a, scalar1=s1, scalar2=None, op0=op0), R, W)
        else:
            P.op(eng, lambda e: e.tensor_scalar(out=out, in0=a, scalar1=s1, scalar2=s2, op0=op0, op1=op1), R, W)

    def stt(out, a, s, b, op0, op1, R, W, eng="dve"):
        P.op(eng, lambda e: e.scalar_tensor_tensor(out=out, in0=a, scalar=s, in1=b, op0=op0, op1=op1), R, W)

    def act(out, a, func, R, W, bias=None, scale=None, accum=None):
        kw = {}
        if bias is not None:
            kw["bias"] = bias
        if scale is not None:
            kw["scale"] = scale
        if accum is not None:
            kw["accum_out"] = accum
        P.op("act", lambda e: e.activation(out=out, in_=a, func=func, **kw), R, W)

    epsc = sb("epsc", [128, 4])
    identh = sb("identh", [128, 128], BF16)

    def rsq(out, a, scale, biasv, R, W):
        bi = {EPS: 0, LNX_EPS: 1, 1e-12: 2}[biasv]
        P.op("act", lambda e: e.activation(out=out, in_=a, func=AF.Sqrt, bias=epsc[0:out.shape[0], bi:bi + 1], scale=scale), list(R) + ["epsc"], W)
        P.op("dve", lambda e: e.reciprocal(out=out, in_=out), W, W)

    def cp(out, a, R, W, eng="act"):
        if eng == "act":
            P.op("act", lambda e: e.copy(out=out, in_=a), R, W)
        else:
            P.op(eng, lambda e: e.tensor_copy(out=out, in_=a), R, W)

    def mm(out, lhsT, rhs, R, W, start=True, stop=True):
        P.op("pe", lambda e: e.matmul(out, lhsT, rhs, start=start, stop=stop), R, W)

    def trp(out, in_, idn, R, W):
        P.op("pe", lambda e: e.transpose(out, in_, idn), R, W)

    def memset(ap, val, W, eng="dve"):
        P.op(eng, lambda e: e.memset(ap, val), (), W)

    dq = ["sp", "act"]
    dqi = [0]

    def load(out, in_, key, W, R=(), eng=None):
        if eng is None:
            eng = "sp"
        return P.dma(eng, out, in_, key, reads=R, writes=W)

    memset(epsc[:, 0:1], EPS, ["epsc"])
    memset(epsc[:, 1:2], LNX_EPS, ["epsc"])
    memset(epsc[:, 2:3], 1e-12, ["epsc"])
    CK = "const"
    for (dst, src, nm) in [(ident, c_ident, "ident"), (bones, c_bones, "bones"), (ones, c_ones, "ones"),
                           (tri, c_tri, "tri"), (mst, c_mst, "mst"), (m1m, c_m1, "m1m"), (dmask, c_dmask, "dmask"),
                           (qsc, c_qsc, "qsc"), (ksc, c_ksc, "ksc"), (gcs, c_gc, "gcs"), (pm, c_pm, "pm"),
                           (sel, c_sel, "sel"), (nrmc, nrm_col, "nrmc"), (rwcv, rwconv, "rwcv"),
                           (colvs, colv, "colvs"), (hmask_s, hmask, "hmask"), (tsel_s, tsel, "tsel"),
                           (fconv_s, fconv, "fconv")]:
        load(dst[:], src, CK, [nm])
    load(w0bc[:], w0row.partition_broadcast(128), CK, ["w0bc"])
    P.op("act", lambda e: e.copy(out=identh[:], in_=ident[:]), ["ident"], ["identh"])
    P.dma("pool", wup_s[:], wup, "constp", writes=["wup"])
    P.dma("pool", aup_s[:], aup, "constp", writes=["aup"])
    P.dma("pool", gup_a[:], gup[0:128, :], "constp", writes=["gupa"])
    P.dma("pool", gup_b[:], gup[128:160, :], "constp", writes=["gupb"])

    A0 = Arena()
    wa_buf = [A0.f32(16 * 512).rearrange("p (k n) -> p k n", k=16) for _ in range(2)]
    modrow = A0.f32(12288)
    badab = [A0.f32(512) for _ in range(2)]
    npost = A0.f32(2 * D).rearrange("p (a n) -> p a n", a=2)
    A12 = A0.f32(2 * D).rearrange("p (a n) -> p a n", a=2)
    a12sp = nc.dram_tensor("a12sp", [1, 2 * D], F32).ap()
    scT = sb("scT", [128, 16, 2])
    load(scT[:], cvt, CK, ["scT"])
    load(npost[:], nrm_row.rearrange("a n -> (a n)").partition_broadcast(128).rearrange("p (a n) -> p a n", a=2)
         if False else nrm_row.unsqueeze(0).to_broadcast([128, 2, D]), CK, ["npost"])
    act(scT[:], scT[:], AF.Silu, ["scT"], ["scT"])
    for n in range(24):
        wb = wa_buf[n % 2]
        load(wb, wada[n], "wada%d" % (n % 2), ["wab%d" % (n % 2)])
        load(badab[n % 2][0:2, :], bada[:, n * 512:(n + 1) * 512].partition_broadcast(2), "wada%d" % (n % 2), ["wab%d" % (n % 2)])
        ps = psb[n % 2]
        for kt in range(16):
            mm(ps[0:2, :], scT[:, kt, :], wb[:, kt, :], ["scT", "wab%d" % (n % 2)], ["psb%d" % (n % 2)],
               start=(kt == 0), stop=(kt == 15))
        tt(modrow[0:2, n * 512:(n + 1) * 512], ps[0:2, :], badab[n % 2][0:2, :], ALU.add,
           ["psb%d" % (n % 2), "wab%d" % (n % 2)], ["modrow"])
    segs = [(0, 0), (0, 1), (0, 3), (0, 4), (1, 0), (1, 1)]
    psc = psb[2]
    for i, (r, sg) in enumerate(segs):
        for kt in range(16):
            mm(psc[:, i * 16 + kt:i * 16 + kt + 1], modrow[0:2, sg * D + kt * 128: sg * D + (kt + 1) * 128],
               sel[0:2, r:r + 1], ["modrow", "sel"], ["psb2"])
    cp(modcol[:].rearrange("p a k -> p (a k)"), psc[:, 0:96], ["psb2"], ["modcol"], eng="dve")
    for (i, nidx) in [(1, 0), (3, 1), (5, 0)]:
        stt(modcol[:, i, :], modcol[:, i, :], 1.0, nrmc[:, nidx, :], ALU.add, ALU.mult, ["modcol", "nrmc"], ["modcol"])
    for a, sg in enumerate([2, 5]):
        for j in range(4):
            ps = psb[3 + (j % 2)]
            mm(ps[:, :], sel[0:2, 2:130], modrow[0:2, sg * D + j * 512: sg * D + (j + 1) * 512], ["modrow", "sel"],
               ["psb%d" % (3 + j % 2)])
            tt(A12[:, a, j * 512:(j + 1) * 512], ps[:, :], npost[:, a, j * 512:(j + 1) * 512], ALU.mult,
               ["psb%d" % (3 + j % 2), "npost"], ["A12"])
    P.dma("sp", a12sp, A12[0:1].rearrange("p a n -> p (a n)"), "a12st", reads=["A12"], writes=["a12sp"])
    if TWO_A:
        P.dma("sp", a12_o, A12[0:1].rearrange("p a n -> p (a n)"), "a12o", reads=["A12"], writes=["a12_o"])
    ph0_res = ["wab0", "wab1", "modrow", "badab", "npost", "A12"]

    if dbg and stop_after == 0:
        dtile = A0.f32(4096)
        memset(dtile[:], 0.0, ["dtile"])
        cp(dtile[:, 0:96], modcol[:].rearrange("p a k -> p (a k)"), ["modcol"], ["dtile"], eng="dve")
        cp(dtile[:, 128:128 + 2048], A12[:, 0, :], ["A12"], ["dtile"], eng="dve")
        ev = P.dma("sp", dbg_d, dtile[:], "dbgout", reads=["dtile"], writes=["dbg_d"])
        P.emit(es, {"sp": [("dbgout", 16)]})
        es.close()
        return nc

    AA = Arena()
    for nm in ["winb", "xt0", "xt1", "sqj", "xmT", "rwT", "retT"]:
        pass
    vtm = AA.bf16(2 * 2 * 128).rearrange("p (c h n) -> p c h n", c=2, h=2)
    winb = AA.bf16(16 * NCOLS).rearrange("p (k n) -> p k n", k=16)
    xts = [AA.f32(D) for _ in range(2)]
    xmT = AA.bf16(16 * NWIN).rearrange("p (k n) -> p k n", k=16)
    rwT = AA.f32(10 * BLK).rearrange("p (k n) -> p k n", k=10)
    retT = AA.f32(8 * BLK).rearrange("p (k n) -> p k n", k=8)
    stat = AA.f32(8)

    def t2(dt=F32):
        if dt == F32:
            return AA.f32(2 * BLK).rearrange("p (k n) -> p k n", k=2)
        return AA.bf16(2 * BLK).rearrange("p (k n) -> p k n", k=2)

    thb = AA.bf16(BLK)
    adb = AA.bf16(BLK)
    gsb = AA.bf16(2 * BLK).rearrange("p (k n) -> p k n", k=2)
    sigtm = t2()
    cumT, cumpT, aT, kkT, tmpA, tmpB, Ep, Em, Epv, kdT, BhT, KhT, yT = [t2() for _ in range(13)]
    aT2, kd2T = tmpB, BhT
    ART = AA.bf16(2 * 4 * 128).rearrange("p (k c n) -> p k c n", k=2, c=4)
    BKT = AA.bf16(2 * 4 * 128).rearrange("p (k c n) -> p k c n", k=2, c=4)
    BKV = AA.bf16(4 * 3 * 2 * 2 * 128).rearrange("p (c m k h n) -> p c m k h n", c=4, m=3, k=2, h=2)
    Ms2 = AA.bf16(2 * 4 * 2 * 128).rearrange("p (c h m n) -> p c h m n", c=2, h=4, m=2)
    M1s2 = AA.bf16(2 * 4 * 64).rearrange("p (c h n) -> p c h n", c=2, h=4)
    Pa = [AA.bf16(2 * 4 * 2 * 64).rearrange("p (c h m n) -> p c h m n", c=2, h=4, m=2) for _ in range(2)]
    Qs = [AA.bf16(2 * 4 * 64).rearrange("p (c h n) -> p c h n", c=2, h=4) for _ in range(2)]
    Wsb = AA.bf16(2 * 128).rearrange("p (k n) -> p k n", k=2)
    Usb = AA.bf16(2 * 2 * 128).rearrange("p (k h n) -> p k h n", k=2, h=2)
    Hst = AA.f32(2 * 128).rearrange("p (k n) -> p k n", k=2)
    Hb = AA.bf16(2 * 128).rearrange("p (k n) -> p k n", k=2)
    qr = t2(BF16)
    kr = t2(BF16)
    krf = t2()
    qtl = t2(BF16)
    rc_t = AA.f32(BLK)
    rs_t = AA.f32(BLK)
    ktm = AA.bf16(2 * 2 * 128).rearrange("p (c h n) -> p c h n", c=2, h=2)
    Ssb = AA.bf16(2 * 128).rearrange("p (h n) -> p h n", h=2)
    Sst = AA.f32(2 * 128).rearrange("p (h n) -> p h n", h=2)
    Sbf = AA.bf16(2 * 128).rearrange("p (h n) -> p h n", h=2)
    yrT = t2()
    vrb = t2(BF16)
    ysp_flat = AA.f32(4 * BLK)
    ysp = ysp_flat.rearrange("p (k n) -> p k n", k=4)
    sqj = ysp_flat.bitcast(BF16)
    print("AA.off", AA.off)
    oT = BhT.rearrange("p k n -> p (k n)").bitcast(BF16).rearrange("p (k n) -> p k n", k=4)
    Aall = ["winb", "xt0", "xt1", "sqj", "xmT", "rwT", "retT", "stat", "thb", "adb", "gsb", "sigtm", "cumT", "cumpT",
            "aT", "aT2", "kkT", "tmpA", "tmpB", "Ep", "Em", "Epv", "kdT", "kd2T", "BhT", "KhT", "yT", "ART", "BKT",
            "BKV", "Ms", "M1s", "Pa0", "Pa1", "Qs0", "Qs1", "Wsb", "Usb", "Hst", "Hb", "qr", "kr", "krf", "qtl",
            "rc_t", "rs_t", "vtm", "vrb", "ktm", "Ssb", "Sst", "Sbf", "yrT", "ysp", "oT"]
    for nm in Aall:
        P.alias(nm, ph0_res)
    P.dma("pool", winb, win, "constp", writes=["winb"])
    memset(BKV[0:64], 0.0, ["BKV"])
    memset(Usb[0:64], 0.0, ["Usb"])

    PSN = ["psb%d" % i for i in range(8)]

    def block(sw, seq, bi, nblk):
        lat = seq == "l"
        ro = lat
        row0 = (0 if not lat else 258) + bi * BLK
        s1i, shi_ = (5, 4) if not lat else (1, 0)
        blk_i = 0 if not lat else bi + 1
        nrw = 9 if ro else 8
        nrt = 8 if ro else 4
        if sw == 0:
            for j in range(3):
                nr = 128 if j < 2 else 2
                xt = xts[j % 2]
                xn = "xt%d" % (j % 2)
                load(xt[0:nr, :], xa[row0 + j * 128: row0 + j * 128 + nr, :], "xk%d" % (j % 2), [xn])
                act(sqj[0:nr, :], xt[0:nr, :], AF.Square, [xn], ["ysp", "stat"], accum=stat[0:nr, 0:1])
                rsq(stat[0:nr, 1:2], stat[0:nr, 0:1], 1.0 / D, EPS, ["stat"], ["stat"])
                ts(xt[0:nr, :], xt[0:nr, :], stat[0:nr, 1:2], None, ALU.mult, None, [xn, "stat"], [xn])
                for k4 in range(4):
                    ps = psb[k4 % 2]
                    pn = PSN[k4 % 2]
                    for kk_ in range(4):
                        kt = k4 * 4 + kk_
                        trp(ps[:, kk_ * 128: kk_ * 128 + nr], xt[0:nr, kt * 128:(kt + 1) * 128], ident[0:nr, 0:nr], [xn, "ident"], [pn])
                    for kk_ in range(4):
                        kt = k4 * 4 + kk_
                        if kk_ % 2 == 0:
                            act(xmT[:, kt, j * 128: j * 128 + nr], ps[:, kk_ * 128: kk_ * 128 + nr], AF.Identity, [pn, "modcol"], ["xmT"],
                                bias=modcol[:, shi_, kt:kt + 1], scale=modcol[:, s1i, kt:kt + 1])
                        else:
                            ts(xmT[:, kt, j * 128: j * 128 + nr], ps[:, kk_ * 128: kk_ * 128 + nr], modcol[:, s1i, kt:kt + 1],
                               modcol[:, shi_, kt:kt + 1], ALU.mult, ALU.add, [pn, "modcol"], ["xmT"])
            if bi == 0:
                memset(xmT[:, :, 0:1], 0.0, ["xmT"])
            if bi == nblk - 1:
                memset(xmT[:, :, 257:258], 0.0, ["xmT"])
            tiles = [0, 1, 2, 3, 4, 5, 6, 7, 8, 9, 10, 11, 12, 13, 14, 15, 16, 17] if ro else [0, 1, 2, 3, 4, 5, 6, 7, 10, 11, 12, 13]
            for ti, mt in enumerate(tiles):
                ps = psb[ti % 2]
                pn = PSN[ti % 2]
                if mt < 9:
                    c0, mw = mt * 128, 128
                elif mt == 9:
                    c0, mw = 1152, 32
                else:
                    c0, mw = 1184 + (mt - 10) * 128, 128
                for kt in range(16):
                    mm(ps[0:mw, 0:NWIN], winb[:, kt, c0:c0 + mw], xmT[:, kt, :], ["winb", "xmT"], [pn], start=(kt == 0), stop=(kt == 15))
                if mt < 10:
                    ts(rwT[0:mw, mt, :], ps[0:mw, 1:257], rwcv[0:mw, mt, 1:2], None, ALU.mult, None, [pn, "rwcv"], ["rwT"])
                    stt(rwT[0:mw, mt, :], ps[0:mw, 0:256], rwcv[0:mw, mt, 0:1], rwT[0:mw, mt, :], ALU.mult, ALU.add, [pn, "rwcv", "rwT"], ["rwT"])
                    stt(rwT[0:mw, mt, :], ps[0:mw, 2:258], rwcv[0:mw, mt, 2:3], rwT[0:mw, mt, :], ALU.mult, ALU.add, [pn, "rwcv", "rwT"], ["rwT"])
                else:
                    cp(retT[:, mt - 10, :], ps[:, 1:257], [pn], ["retT"])

            P.dma("sp", pspill[blk_i, 0:nrw].rearrange("k p n -> p k n"), rwT[:, 0:nrw, :], "psa", reads=["rwT"], writes=["pspA%d" % blk_i])
            if ro:
                P.dma("sp", pspill[blk_i, 9, 0:32, :], rwT[0:32, 9, :], "psb_", reads=["rwT"], writes=["pspB%d" % blk_i])
            P.dma("sp", pspill[blk_i, 10:10 + nrt].rearrange("k p n -> p k n"), retT[:, 0:nrt, :], "psc", reads=["retT"], writes=["pspC%d" % blk_i])
        else:
            load(rwT[:, 0:nrw, :], pspill[blk_i, 0:nrw].rearrange("k p n -> p k n"), "pla", ["rwT"], R=["pspA%d" % blk_i])
            if ro:
                load(rwT[0:32, 9, :], pspill[blk_i, 9, 0:32, :], "plb", ["rwT"], R=["pspB%d" % blk_i])
            load(retT[:, 0:nrt, :], pspill[blk_i, 10:10 + nrt].rearrange("k p n -> p k n"), "plc", ["retT"], R=["pspC%d" % blk_i])
        if dbg and stop_after == 1 and seq == "l" and bi == 0 and sw == 0:
            return "dbg1"
        dp = slice(0, 64) if sw == 0 else slice(64, 128)
        v4 = lambda ap: ap.rearrange("p (c n) -> p c n", c=4)
        cend = 63 if sw == 0 else 0
        act(thb[dp, :], rwT[dp, 6, :], AF.Tanh, ["rwT"], ["thb"])
        cp(adb[:, :], rwT[:, 7, :], ["rwT"], ["adb"])
        for t_ in range(2):
            ps = psb[2 + t_]
            pn = PSN[2 + t_]
            mm(ps[:, 0:256], thb[dp, t_ * 128:(t_ + 1) * 128], wup_s[dp, :], ["thb", "wup"], [pn])
            tt(sigtm[:, t_, :], ps[:, 0:256], w0bc[:, sw * 256:(sw + 1) * 256], ALU.add, [pn, "w0bc"], ["sigtm"])
        act(sigtm[:], sigtm[:], AF.Sigmoid, ["sigtm"], ["sigtm"])
        for which, dst, dn in [(0, cumT, "cumT"), (1, cumpT, "cumpT")]:
            ps = psb[2 + which]
            pn = PSN[2 + which]
            for ct in range(2):
                for t_ in range(2):
                    mm(ps[:, ct * 256 + t_ * 128: ct * 256 + (t_ + 1) * 128], sigtm[:, t_, ct * 128:(ct + 1) * 128],
                       tri[:, 2 * sw + which, :], ["sigtm", "tri"], [pn])
            cp(dst[:].rearrange("p k n -> p (k n)"), ps[:, :], [pn], [dn])
        act(Ep[:], cumT[:], AF.Exp, ["cumT"], ["Ep"], scale=-DECAY_C)
        act(Em[:], cumT[:], AF.Exp, ["cumT"], ["Em"], scale=DECAY_C)
        act(Epv[:], cumpT[:], AF.Exp, ["cumpT"], ["Epv"], scale=-DECAY_C)
        for ct in range(2):
            cs_ = slice(ct * 128, (ct + 1) * 128)
            ps = psb[2 + ct]
            pn = PSN[2 + ct]
            mm(ps[:, 0:256], aup_s[dp, cs_], adb[dp, :], ["aup", "adb"], [pn])
            act(aT[:, ct, :], ps[:, 0:256], AF.Sigmoid, [pn, "colvs"], ["aT"], bias=colvs[:, ct, 5 + sw:6 + sw])
            ts(tmpA[:, ct, :], rwT[:, ct, :], colvs[:, ct, 0:1], None, ALU.mult, None, ["rwT", "colvs"], ["tmpA"])
            tt(tmpB[:, ct, :], tmpA[:, ct, :], tmpA[:, ct, :], ALU.mult, ["tmpA"], ["tmpB"])
            mm(ps[:, 256:512], bones[:, :], tmpB[:, ct, :], ["bones", "tmpB"], [pn])
            rsq(tmpB[:, ct, :], ps[:, 256:512], 1.0, 1e-12, [pn], ["tmpB"])
            tt(kkT[:, ct, :], tmpA[:, ct, :], tmpB[:, ct, :], ALU.mult, ["tmpA", "tmpB"], ["kkT"])
            stt(ART[:, ct, :, 0:64], v4(kkT[:, ct, :]), -1.0, v4(Epv[:, ct, :]), ALU.mult, ALU.mult, ["kkT", "Epv"], ["ART"])
            tt(ART[:, ct, :, 64:128], v4(rwT[:, 4 + ct, :]), v4(Ep[:, ct, :]), ALU.mult, ["rwT", "Ep"], ["ART"])
            gcb = v4(Ep[:, ct, :])[:, :, cend:cend + 1].to_broadcast([128, 4, 64])
            tt(tmpB[:, ct, :], kkT[:, ct, :], aT[:, ct, :], ALU.mult, ["kkT", "aT"], ["tmpB"])
            tt(BhT[:, ct, :], tmpB[:, ct, :], Em[:, ct, :], ALU.mult, ["tmpB", "Em"], ["BhT"])
            cp(BKT[:, ct, :, 0:64], v4(BhT[:, ct, :]), ["BhT"], ["BKT"])
            tt(v4(BhT[:, ct, :]), v4(BhT[:, ct, :]), gcb, ALU.mult, ["BhT", "Ep"], ["BhT"])
            ts(tmpA[:, ct, :], aT[:, ct, :], colvs[:, ct, 1:2], colvs[:, ct, 7:8], ALU.mult, ALU.add, ["aT", "colvs"], ["tmpA"])
            tt(kdT[:, ct, :], rwT[:, ct, :], tmpA[:, ct, :], ALU.mult, ["rwT", "tmpA"], ["kdT"])
            tt(KhT[:, ct, :], kdT[:, ct, :], Em[:, ct, :], ALU.mult, ["kdT", "Em"], ["KhT"])
            cp(BKT[:, ct, :, 64:128], v4(KhT[:, ct, :]), ["KhT"], ["BKT"])
            tt(v4(KhT[:, ct, :]), v4(KhT[:, ct, :]), gcb, ALU.mult, ["KhT", "Ep"], ["KhT"])
        if dbg and stop_after == 2.1:
            return "dbg2"
        for c in range(4):
            for ct in range(2):
                ps = psb[2 + ct]
                pn = PSN[2 + ct]
                for m_, (src, sn) in enumerate([(BhT[:, ct, :], "BhT"), (KhT[:, ct, :], "KhT"), (rwT[:, 2 + ct, :], "rwT")]):
                    trp(ps[0:64, m_ * 128:(m_ + 1) * 128], src[:, c * 64:(c + 1) * 64], ident[:, :], [sn, "ident"], [pn])
                pv = ps[0:64, 0:384].rearrange("p (m n) -> p m n", m=3)
                for hh in range(2):
                    cp(BKV[0:64, c, :, ct, hh, hh * 64:(hh + 1) * 64], pv[:, :, hh * 64:(hh + 1) * 64], [pn], ["BKV"],
                       eng=("act" if hh == 0 else "dve"))
        if dbg and stop_after == 2.2:
            return "dbg2"
        cols = slice(bi * BLK, (bi + 1) * BLK)

        def ret_steps():
            cols = slice(bi * BLK, (bi + 1) * BLK)
            if lat:
                load(rc_t, c_ropec[:, cols], "ropek", ["rope"])
                load(rs_t, c_ropes[:, cols], "ropek", ["rope"])
                for hr in range(2):
                    for isk, src_t in [(True, retT[:, hr, :]), (False, retT[:, 4 + hr, :])]:
                        ps = psb[2 + (0 if isk else 1)]
                        pn = PSN[2 + (0 if isk else 1)]
                        mm(ps[:, 0:256], pm[:, :], src_t, ["pm", "retT"], [pn])
                        tt(tmpA[:, 0, :], ps[:, 0:256], rs_t, ALU.mult, [pn, "rope"], ["tmpA"])
                        tt(tmpB[:, 0, :], src_t, rc_t, ALU.mult, ["retT", "rope"], ["tmpB"])
                        if isk:
                            tt(krf[:, hr, :], tmpA[:, 0, :], tmpB[:, 0, :], ALU.add, ["tmpA", "tmpB"], ["krf"])
                            cp(kr[:, hr, :], krf[:, hr, :], ["krf"], ["kr"])
                        else:
                            tt(qr[:, hr, :], tmpA[:, 0, :], tmpB[:, 0, :], ALU.add, ["tmpA", "tmpB"], ["qr"])
                    yield
            else:
                for hr in range(2):
                    cp(krf[:, hr, :], retT[:, hr, :], ["retT"], ["krf"], eng="dve")
                    cp(kr[:, hr, :], retT[:, hr, :], ["retT"], ["kr"])
            if dbg and stop_after == 2.61:
                return "dbg2"
            for c2 in range(2):
                cc = slice(c2 * 128, (c2 + 1) * 128)
                for hr in range(2):
                    ps = psb[2 + hr]
                    pn = PSN[2 + hr]
                    cp(vrb[:, hr, cc], retT[:, 2 + hr, cc], ["retT"], ["vrb"])
                    mm(ps[:, 0:128], vrb[:, hr, cc], identh[:, :], ["vrb", "identh"], [pn])
                    mm(ps[:, 128:256], kr[:, hr, cc], identh[:, :], ["kr", "identh"], [pn])
                    import os as _os
                    cp(vtm[:, c2, hr, :], ps[:, 0:128], [pn], ["vtm"], eng="dve")
                    if True:
                        ts(ktm[:, c2, hr, :], ps[:, 128:256], ksc[:, hr * 2 + sw:hr * 2 + sw + 1], None, ALU.mult, None, [pn, "ksc"], ["ktm"])
                    yield
            if dbg and stop_after == 2.62:
                return "dbg2"
            for c2 in (range(2) if sw == 0 else (1, 0)):
                cc = slice(c2 * 128, (c2 + 1) * 128)
                for hr in range(2):
                    hc = slice(hr * 128, (hr + 1) * 128)
                    if ro:
                        mm(psb[2][:, hc], kr[:, hr, cc], qr[:, hr, cc], ["kr", "qr"], ["psb2"])
                        tt(Ssb[:, hr, :], psb[2][:, hc], dmask[:, hr * 2 + sw, :], ALU.mult, ["psb2", "dmask"], ["Ssb"])
                        tt(qtl[:, hr, cc], qr[:, hr, cc], qsc[:, hr * 2 + sw, :], ALU.mult, ["qr", "qsc"], ["qtl"])
                        yield
                        mm(psb[3][:, hc], vtm[:, c2, hr, :], Ssb[:, hr, :], ["vtm", "Ssb"], ["psb3"], start=True, stop=False)
                        mm(psb[3][:, hc], Sbf[:, hr, :], qtl[:, hr, cc], ["Sbf", "qtl"], ["psb3"], start=False, stop=True)
                        cp(yrT[:, hr, cc], psb[3][:, hc], ["psb3"], ["yrT"])
                        yield
                    mm(psb[0][:, hc], ktm[:, c2, hr, :], vtm[:, c2, hr, :], ["ktm", "vtm"], ["psb0"])
                    stt(Sst[:, hr, :], Sst[:, hr, :], gcs[:, hr:hr + 1], psb[0][:, hc], ALU.mult, ALU.add, ["Sst", "gcs", "psb0"], ["Sst"])
                    cp(Sbf[:, hr, :], Sst[:, hr, :], ["Sst"], ["Sbf"])
                    yield
            yield

        rg_ = ret_steps()

        def rstep():
            try:
                next(rg_)
            except StopIteration:
                pass
        chunk_order = list(range(4)) if sw == 0 else [3, 2, 1, 0]
        for pr in range(2):
          pair = chunk_order[2 * pr:2 * pr + 2]
          for ci, c in enumerate(pair):
            for h in range(4):
                ct, hh = h // 2, h % 2
                hp = slice(hh * 64, hh * 64 + 64)
                pb = psb[4 + hh]
                mm(pb[0:64, (ct * 2) * 128:(ct * 2 + 1) * 128], BKT[hp, ct, c, 0:64], ART[hp, ct, c, :], ["BKT", "ART"], [PSN[4 + hh]])
                mm(pb[0:64, (ct * 2 + 1) * 128:(ct * 2 + 2) * 128], BKT[hp, ct, c, 64:128], ART[hp, ct, c, :], ["BKT", "ART"], [PSN[4 + hh]])
                mm(psb[6 + hh][0:64, ci * 128 + ct * 64:ci * 128 + (ct + 1) * 64], ART[hp, ct, c, 0:64], BKT[hp, ct, c, 0:64], ["BKT", "ART"], [PSN[6 + hh]])
            for hh in range(2):
                tt(Ms2[0:64, ci, 2 * hh:2 * hh + 2, :, :].rearrange("p h m n -> p (h m) n"),
                   psb[4 + hh][0:64, :].rearrange("p (a n) -> p a n", a=4),
                   mst[:, sw, :].unsqueeze(1).to_broadcast([64, 4, 128]), ALU.mult, [PSN[4 + hh], "mst"], ["Ms"])
          for hh in range(2):
            tt(M1s2[0:64, :, 2 * hh:2 * hh + 2, :], psb[6 + hh][0:64, 0:256].rearrange("p (c a n) -> p c a n", c=2, a=2),
               m1m[:, sw, :].unsqueeze(1).unsqueeze(1).to_broadcast([64, 2, 2, 64]), ALU.mult, [PSN[6 + hh], "m1m"], ["M1s"])
          identb = ident[0:64, 0:64].unsqueeze(1).unsqueeze(1).to_broadcast([64, 2, 4, 64])
          tt(Qs[0][0:64], Ms2[0:64, :, :, 0, 0:64], identb, ALU.add, ["Ms", "ident"], ["Qs0"])
          pN = [psb[6 + ci][0:64, :].rearrange("p (h m n) -> p h m n", h=4, m=2) for ci in range(2)]
          pQ = psb[5][0:64, :].rearrange("p (c h n) -> p c h n", c=2, h=4)
          for k in range(1, 6):
            cur, prv = k % 2, (k - 1) % 2
            for ci in range(2):
                for h in range(4):
                    if k == 1:
                        Pp, Ppp = Ms2[0:64, ci, h, 0, 0:64], M1s2[0:64, ci, h, :]
                        rn_ = ["Ms", "M1s"]
                    else:
                        Pp, Ppp = Pa[prv][0:64, ci, h, 0, :], Pa[prv][0:64, ci, h, 1, :]
                        rn_ = ["Pa%d" % prv]
                    mm(pN[ci][:, h, 0, :], Ppp, Pp, rn_, [PSN[6 + ci]])
                    mm(pN[ci][:, h, 1, :], Pp, Ppp, rn_, [PSN[6 + ci]])
            for ci in range(2):
                cp(Pa[cur][0:64, ci].rearrange("p h m n -> p (h m n)"), psb[6 + ci][0:64, :], [PSN[6 + ci]], ["Pa%d" % cur],
                   eng=("act" if ci == 0 else "dve"))
            for ci in range(2):
                for h in range(4):
                    mm(pQ[:, ci, h, :], Pa[cur][0:64, ci, h, 1, :], Qs[prv][0:64, ci, h, :], ["Pa%d" % cur, "Qs%d" % prv], ["psb5"])
            tt(Qs[cur][0:64], pQ, Qs[prv][0:64], ALU.add, ["psb5", "Qs%d" % prv], ["Qs%d" % cur])
            rstep()
          for ci, c in enumerate(pair):
            Ms = Ms2[:, ci]
            TT = Qs[1][:, ci]
            pW = psb[2][0:64, 0:256].rearrange("p (k n) -> p k n", k=2)
            for ct in range(2):
                mm(pW[:, ct, :], ART[:, ct, c, 0:64], Hb[:, ct, :], ["ART", "Hb"], ["psb2"], start=True, stop=False)
                for hh in range(2):
                    mm(pW[:, ct, :], Ms[0:64, hh * 2 + ct, 1, 0:64], BKV[0:64, c, 2, ct, hh, :], ["Ms", "BKV"], ["psb2"],
                       start=False, stop=(hh == 1))
            cp(Wsb[0:64, :, :], pW, ["psb2"], ["Wsb"])
            rstep()
            pU = psb[3][0:64, 0:256].rearrange("p (k n) -> p k n", k=2)
            for ct in range(2):
                for hh in range(2):
                    mm(pU[:, ct, hh * 64:(hh + 1) * 64], TT[0:64, hh * 2 + ct, :], Wsb[0:64, ct, hh * 64:(hh + 1) * 64],
                       ["Qs1", "Wsb"], ["psb3"])
            for hh in range(2):
                cp(Usb[0:64, :, hh, hh * 64:(hh + 1) * 64], pU[:, :, hh * 64:(hh + 1) * 64], ["psb3"], ["Usb"],
                   eng=("act" if hh == 0 else "dve"))
            if ro:
                pY = psb[0][:, 0:128].rearrange("p (k n) -> p k n", k=2)
                for ct in range(2):
                    mm(pY[:, ct, :], Hb[:, ct, :], ART[:, ct, c, 64:128], ["Hb", "ART"], ["psb0"], start=True, stop=False)
                    for hh in range(2):
                        mm(pY[:, ct, :], Usb[0:64, ct, hh, :], Ms[0:64, hh * 2 + ct, 0, 64:128], ["Usb", "Ms"], ["psb0"],
                           start=False, stop=False)
                        mm(pY[:, ct, :], BKV[0:64, c, 2, ct, hh, :], Ms[0:64, hh * 2 + ct, 1, 64:128], ["BKV", "Ms"], ["psb0"],
                           start=False, stop=(hh == 1))
                cp(yT[:, :, c * 64:(c + 1) * 64], pY, ["psb0"], ["yT"])
            pH = psb[1][:, 0:256].rearrange("p (k n) -> p k n", k=2)
            for ct in range(2):
                for hh in range(2):
                    mm(pH[:, ct, :], BKV[0:64, c, 0, ct, hh, :], Usb[0:64, ct, hh, :], ["BKV", "Usb"], ["psb1"],
                       start=(hh == 0), stop=False)
                    mm(pH[:, ct, :], BKV[0:64, c, 1, ct, hh, :], BKV[0:64, c, 2, ct, hh, :], ["BKV"], ["psb1"],
                       start=False, stop=(hh == 1))
                gcol = Ep[:, ct, c * 64 + cend: c * 64 + cend + 1]
                stt(Hst[:, ct, :], Hst[:, ct, :], gcol, pH[:, ct, :], ALU.mult, ALU.add, ["Hst", "Ep", "psb1"], ["Hst"])
            cp(Hb[:], Hst[:], ["Hst"], ["Hb"])
            rstep()
        if dbg and stop_after == 2 and seq == "l" and bi == 0 and sw == 0:
            return "dbg2"
        if dbg and stop_after == 2.5:
            return "dbg2"
        for _ in rg_:
            pass
        if dbg and stop_after == 2.6:
            return "dbg2"
        if not lat:
            return None
        ysl = yspill[:, :, cols].rearrange("k p n -> p k n")
        if sw == 0:
            P.dma("sp", ysl[:, 0:2, :], yT[:], "yst0", reads=["yT"], writes=["yspill%d" % bi])
            P.dma("sp", ysl[:, 2:4, :], yrT[:], "yst1", reads=["yrT"], writes=["yspillr%d" % bi])
            return None
        load(ysp[:], ysl, "yld", ["ysp"], R=["yspill%d" % bi, "yspillr%d" % bi])
        act(gsb[:, 0, :], rwT[:, 8, :], AF.Sigmoid, ["rwT"], ["gsb"])
        act(gsb[0:32, 1, :], rwT[0:32, 9, :], AF.Sigmoid, ["rwT"], ["gsb"])
        od = slice(0, 64)
        for ct in range(2):
            cs_ = slice(ct * 128, (ct + 1) * 128)
            ps = psb[2 + ct]
            pn = PSN[2 + ct]
            y = cumT[:, ct, :]
            sq = cumpT[:, ct, :]
            bn = Ep[:, ct, :]
            tt(y, yT[:, ct, :], ysp[:, ct, :], ALU.add, ["yT", "ysp"], ["cumT"])
            mm(ps[:, 0:256], bones[:, :], y, ["bones", "cumT"], [pn])
            stt(y, ps[:, 0:256], -1.0 / 64.0, y, ALU.mult, ALU.add, [pn, "cumT"], ["cumT"])
            tt(sq, y, y, ALU.mult, ["cumT"], ["cumpT"])
            mm(ps[:, 256:512], bones[:, :], sq, ["bones", "cumpT"], [pn])
            rsq(sq, ps[:, 256:512], 1.0 / 64.0, LNX_EPS, [pn], ["cumpT"])
            tt(y, y, sq, ALU.mult, ["cumT", "cumpT"], ["cumT"])
            ts(y, y, colvs[:, ct, 3:4], colvs[:, ct, 4:5], ALU.mult, ALU.add, ["cumT", "colvs"], ["cumT"])
            mm(ps[:, 0:256], aup_s[od, cs_], adb[od, :], ["aup", "adb"], [pn])
            act(bn, ps[:, 0:256], AF.Sigmoid, [pn, "colvs"], ["Ep"], bias=colvs[:, ct, 5:6])
            ts(bn, bn, colvs[:, ct, 1:2], colvs[:, ct, 7:8], ALU.mult, ALU.add, ["Ep", "colvs"], ["Ep"])
            tt(bn, rwT[:, ct, :], bn, ALU.mult, ["rwT", "Ep"], ["Ep"])
            tt(bn, bn, kdT[:, ct, :], ALU.add, ["Ep", "kdT"], ["Ep"])
            stt(bn, rwT[:, 4 + ct, :], colvs[:, ct, 2:3], bn, ALU.mult, ALU.mult, ["rwT", "colvs", "Ep"], ["Ep"])
            mm(ps[:, 256:512], bones[:, :], bn, ["bones", "Ep"], [pn])
            tt(bn, ps[:, 256:512], rwT[:, 2 + ct, :], ALU.mult, [pn, "rwT"], ["Ep"])
            tt(y, y, bn, ALU.add, ["cumT", "Ep"], ["cumT"])
            mm(ps[:, 0:256], gup_a[:, cs_], gsb[:, 0, :], ["gupa", "gsb"], [pn], start=True, stop=False)
            mm(ps[:, 0:256], gup_b[0:32, cs_], gsb[0:32, 1, :], ["gupb", "gsb"], [pn], start=False, stop=True)
            tt(oT[:, ct, :], y, ps[:, 0:256], ALU.mult, ["cumT", pn], ["BhT"])
        for hr in range(2):
            ps = psb[2 + hr]
            pn = PSN[2 + hr]
            y = Em[:, hr, :]
            sq = Epv[:, hr, :]
            tt(y, yrT[:, hr, :], ysp[:, 2 + hr, :], ALU.add, ["yrT", "ysp"], ["Em"])
            tt(sq, y, y, ALU.mult, ["Em"], ["Epv"])
            mm(ps[:, 0:256], ones[:, :], sq, ["ones", "Epv"], [pn])
            rsq(sq, ps[:, 0:256], 1.0 / 128.0, EPS, [pn], ["Epv"])
            tt(y, y, sq, ALU.mult, ["Em", "Epv"], ["Em"])
            act(sq, retT[:, 6 + hr, :], AF.Silu, ["retT"], ["Epv"])
            tt(oT[:, 2 + hr, :], y, sq, ALU.mult, ["Em", "Epv"], ["BhT"])
        if dbg and stop_after == 3:
            return "dbg3"
        P.dma("sp", osend.ap()[:, cols].rearrange("(k p) n -> p k n", p=128), oT[:], "ost", reads=["BhT"], writes=["osend"])
        return None

    ts(colvs[:, :, 7], colvs[:, :, 1], -1.0, 1.0, ALU.mult, ALU.add, ["colvs"], ["colvs"])

    def run_sweep(sw):
        memset(Hst[:], 0.0, ["Hst"])
        memset(Hb[:], 0.0, ["Hb"])
        memset(Sst[:], 0.0, ["Sst"])
        memset(Sbf[:], 0.0, ["Sbf"])
        order = [("c", 0, 1)] + [("l", bi, 16) for bi in (range(16) if sw == 0 else range(15, -1, -1))]
        for (seq, bi, nb) in order:
            r = block(sw, seq, bi, nb)
            if r is not None:
                return r
        return None

    r = run_sweep(0)
    if r == "dbg2":
        dtile = arena[:, 0:4096]
        memset(dtile[:], 0.0, ["dtile", "winb"])
        cp(dtile[:, 0:512], yT[:].rearrange("p k n -> p (k n)"), ["yT"], ["dtile", "winb"], eng="dve")
        cp(dtile[:, 512:768], Hst[:].rearrange("p k n -> p (k n)"), ["Hst"], ["dtile", "winb"], eng="dve")
        cp(dtile[:, 1024:1536], kkT[:].rearrange("p k n -> p (k n)"), ["kkT"], ["dtile", "winb"], eng="dve")
        cp(dtile[:, 1536:2048], Ep[:].rearrange("p k n -> p (k n)"), ["Ep"], ["dtile", "winb"], eng="dve")
        cp(dtile[:, 2048:2560], aT[:].rearrange("p k n -> p (k n)"), ["aT"], ["dtile", "winb"], eng="dve")
        cp(dtile[:, 2560:3072], kdT[:].rearrange("p k n -> p (k n)"), ["kdT"], ["dtile", "winb"], eng="dve")
        P.dma("sp", dbg_d, dtile[:], "dbgout", reads=["dtile"], writes=["dbg_d"])
        P.emit(es, {"sp": [("dbgout", 16)]})
        es.close()
        return nc
    if r is None:
        r = run_sweep(1)
    if r == "dbg3":
        dtile = arena[:, 0:4096]
        memset(dtile[:], 0.0, ["dtile", "winb"])
        cp(dtile[:, 0:1024], oT[:].rearrange("p k n -> p (k n)"), ["BhT"], ["dtile", "winb"], eng="dve")
        cp(dtile[:, 1024:1536], yT[:].rearrange("p k n -> p (k n)"), ["yT"], ["dtile", "winb"], eng="dve")
        cp(dtile[:, 1536:2048], yrT[:].rearrange("p k n -> p (k n)"), ["yrT"], ["dtile", "winb"], eng="dve")
        cp(dtile[:, 2048:3072], ysp[:].rearrange("p k n -> p (k n)"), ["ysp"], ["dtile", "winb"], eng="dve")
        P.dma("sp", dbg_d, dtile[:], "dbgout", reads=["dtile"], writes=["dbg_d"])
        P.emit(es, {"sp": [("dbgout", 16)]})
        es.close()
        return nc
    if r == "dbg1":
        dtile = arena[:, 0:4096]
        cp(dtile[:, 0:2560], rwT[:].rearrange("p k n -> p (k n)"), ["rwT"], ["dtile", "winb"], eng="dve")
        cp(dtile[:, 2560:4096], retT[:, 0:6, :].rearrange("p k n -> p (k n)"), ["retT"], ["dtile", "winb"], eng="dve")
        P.dma("sp", dbg_d, dtile[:], "dbgout", reads=["dtile"], writes=["dbg_d"])
        P.emit(es, {"sp": [("dbgout", 16)]})
        es.close()
        return nc

    if stop_after == "noB":
        P.emit(es, {})
        es.close()
        return nc
    if TWO_A:
        P.dma("sp", modcol_o, modcol[:].rearrange("p a k -> p (a k)"), "mco", reads=["modcol"], writes=["modcol_o"])
        P.emit(es, {"sp": [("ost", P.dma_count["ost"]), ("mco", 16), ("a12o", 16)]})
        es.close()
        return nc
    if TWO_B:
        P = Prog(nc)
        DBG["P"] = P
        memset(epsc[:, 0:1], EPS, ["epsc"])
        load(ident[:], c_ident, CK, ["ident"])
        load(hmask_s[:], hmask, CK, ["hmask"])
        load(fconv_s[:], fconv, CK, ["fconv"])
        load(modcol[:].rearrange("p a k -> p (a k)"), modcol_i, CK, ["modcol"])
        a12sp = a12_i
    if stop_after != "nocc" and not TWO_B:
      P.custom("pool", lambda e: e.collective_compute("AllGather", ALU.bypass, replica_groups=[[0, 1, 2, 3, 4, 5, 6, 7]],
                                                    ins=[osend.ap().opt()], outs=[ogath.ap().opt()]),
               "cc0", reads=["osend"], writes=["ogath"])
    og = ogath.ap() if stop_after != "nocc" else osend.ap()

    prevA = P.all_res()
    HT_W = 9216
    hT = arena[:, 0:HT_W].bitcast(BF16).rearrange("p (k n) -> p k n", k=16)
    B2 = Arena()
    B2.off = HT_W
    oTb = B2.bf16(16 * 1152).rearrange("p (k n) -> p k n", k=16)
    cand = [B2.bf16(8 * 1152).rearrange("p (k n) -> p k n", k=8)]
    woutb = B2.bf16(16 * D).rearrange("p (k n) -> p k n", k=16)
    mixrow = B2.f32(D)
    xtb = B2.f32(D)
    A1row = B2.f32(D)
    sqb = B2.bf16(D)
    statb = B2.f32(16)
    for nm in ["hT", "oTb", "cand0", "woutb", "mixrow", "xtb", "A1row", "sqb", "statb"]:
        P.alias(nm, prevA)
    P.dma("pool", woutb, wout, "woutk", writes=["woutb"])
    load(A1row, a12sp[:, 0:D].partition_broadcast(128), "a1k", ["A1row"], R=["a12sp"])
    NB_ = 2 if stop_after != "nocc" else 1
    if TWO_B:
        load(oTb, oin, "oink", ["oTb"])
    if NB_ == 1:
        memset(cand[0][:, 4:8, :], 0.0, ["cand0"])
    for bq in range(NB_):
        memset(cand[0][:, bq * 4 + 0, 0:64], 0.0, ["cand0"])
        memset(cand[0][:, bq * 4 + 3, 1088:1152], 0.0, ["cand0"])
    for kt in (range(16) if not TWO_B else []):
        cb = cand[0]
        cn = "cand0"
        for bq in range(NB_):
            r0 = (bq * 4 + kt // 4) * 512 + (kt % 4) * 128
            if stop_after == "nocc":
                r0 = (kt % 4) * 128
            load(cb[:, bq * 4 + 0, 64:1152], og[r0:r0 + 128, 0:1088], "candk0", [cn], R=["ogath", "osend"])
            load(cb[:, bq * 4 + 1, :], og[r0:r0 + 128, 960:2112], "candk0", [cn], R=["ogath", "osend"])
            load(cb[:, bq * 4 + 2, :], og[r0:r0 + 128, 1984:3136], "candk0", [cn], R=["ogath", "osend"])
            load(cb[:, bq * 4 + 3, 0:1088], og[r0:r0 + 128, 3008:4096], "candk0", [cn], R=["ogath", "osend"])
        ts(oTb[:, kt, :], cb[:, 0, :], tsel_s[:, 0:1], None, ALU.mult, None, [cn, "tsel"], ["oTb"])
        for g_ in range(1, 8):
            stt(oTb[:, kt, :], cb[:, g_, :], tsel_s[:, g_:g_ + 1], oTb[:, kt, :], ALU.mult, ALU.add, [cn, "tsel", "oTb"], ["oTb"])
    for t_ in range(9):
        tk = slice(t_ * 128, (t_ + 1) * 128)
        load(xtb, xb[tk, :], "xbk", ["xtb"])
        for cch in range(4):
            ps = psb[cch % 2]
            pn = PSN[cch % 2]
            for kt in range(16):
                mm(ps[:, :], oTb[:, kt, tk], woutb[:, kt, cch * 512:(cch + 1) * 512], ["oTb", "woutb"], [pn], start=(kt == 0), stop=(kt == 15))
            cp(mixrow[:, cch * 512:(cch + 1) * 512], ps[:, :], [pn], ["mixrow"])
        act(sqb, mixrow, AF.Square, ["mixrow"], ["sqb", "statb"], accum=statb[:, 0:1])
        rsq(statb[:, 1:2], statb[:, 0:1], 1.0 / D, EPS, ["statb"], ["statb"])
        stt(mixrow, mixrow, statb[:, 1:2], A1row, ALU.mult, ALU.mult, ["mixrow", "statb", "A1row"], ["mixrow"])
        tt(xtb, xtb, mixrow, ALU.add, ["xtb", "mixrow"], ["xtb"])
        if t_ == 0:
            P.dma("sp", x1sp[0:64, :], xtb[64:128, :], "x1st", reads=["xtb"], writes=["x1sp"])
        elif t_ == 8:
            P.dma("sp", x1sp[960:1024, :], xtb[0:64, :], "x1st", reads=["xtb"], writes=["x1sp"])
        else:
            P.dma("sp", x1sp[t_ * 128 - 64:t_ * 128 + 64, :], xtb, "x1st", reads=["xtb"], writes=["x1sp"])
        act(sqb, xtb, AF.Square, ["xtb"], ["sqb", "statb"], accum=statb[:, 2:3])
        rsq(statb[:, 3:4], statb[:, 2:3], 1.0 / D, EPS, ["statb"], ["statb"])
        ts(mixrow, xtb, statb[:, 3:4], None, ALU.mult, None, ["xtb", "statb"], ["mixrow"])
        for k4 in range(4):
            ps = psb[2 + k4 % 2]
            pn = PSN[2 + k4 % 2]
            for kk_ in range(4):
                kt = k4 * 4 + kk_
                trp(ps[:, kk_ * 128:(kk_ + 1) * 128], mixrow[:, kt * 128:(kt + 1) * 128], ident[:, :], ["mixrow", "ident"], [pn])
            for kk_ in range(4):
                kt = k4 * 4 + kk_
                if kk_ % 2 == 0:
                    act(hT[:, kt, tk], ps[:, kk_ * 128:(kk_ + 1) * 128], AF.Identity, [pn, "modcol"], ["hT"],
                        bias=modcol[:, 2, kt:kt + 1], scale=modcol[:, 3, kt:kt + 1])
                else:
                    ts(hT[:, kt, tk], ps[:, kk_ * 128:(kk_ + 1) * 128], modcol[:, 3, kt:kt + 1], modcol[:, 2, kt:kt + 1],
                       ALU.mult, ALU.add, [pn, "modcol"], ["hT"])
    B2res = ["oTb", "cand0", "woutb", "mixrow", "xtb", "A1row", "sqb", "statb"]
    B3 = Arena()
    B3.off = HT_W
    wgb = [B3.bf16(16 * 128).rearrange("p (k n) -> p k n", k=16) for _ in range(2)]
    wub = [B3.bf16(16 * 128).rearrange("p (k n) -> p k n", k=16) for _ in range(2)]
    gts = B3.f32(1152)
    cacc = B3.f32(1024)
    HID_OFF = ARENA_W - 22528
    assert B3.off <= HID_OFF
    hid = arena[:, HID_OFF:ARENA_W].bitcast(BF16).rearrange("p (k n) -> p k n", k=NFT)
    for nm in ["wgb0", "wgb1", "wub0", "wub1", "gts", "cacc", "hid"]:
        P.alias(nm, B2res)
    g3 = gts.rearrange("p (r c) -> p r c", c=64)
    a3 = cacc.rearrange("p (r c) -> p r c", c=64)
    for ft in range(NFT):
        wg, wu = wgb[ft % 2], wub[ft % 2]
        P.dma("pool", wg, wgate[ft], "wgk%d" % (ft % 2), writes=["wgb%d" % (ft % 2)])
        P.dma("pool", wu, wupf[ft], "wuk%d" % (ft % 2), writes=["wub%d" % (ft % 2)])
        for ch in range(3):
            ps = psb[ch]
            for kt in range(16):
                mm(ps[:, 0:384], wg[:, kt, :], hT[:, kt, ch * 384:(ch + 1) * 384], ["wgb%d" % (ft % 2), "hT"], [PSN[ch]],
                   start=(kt == 0), stop=(kt == 15))
            cp(gts[:, ch * 384:(ch + 1) * 384], ps[:, 0:384], [PSN[ch]], ["gts"])
        ts(gts[:, 0:64], gts[:, 0:64], hmask_s[:, 0:1], None, ALU.mult, None, ["gts", "hmask"], ["gts"])
        ts(gts[:, 1088:1152], gts[:, 1088:1152], hmask_s[:, 1:2], None, ALU.mult, None, ["gts", "hmask"], ["gts"])
        ts(a3, g3[:, 1:17, :], fconv_s[:, ft, 4:5], fconv_s[:, ft, 9:10], ALU.mult, ALU.add, ["gts", "fconv"], ["cacc"])
        for dr in range(3):
            for dc in range(3):
                if dr == 1 and dc == 1:
                    continue
                wcol = fconv_s[:, ft, dr * 3 + dc:dr * 3 + dc + 1]
                if dc == 1:
                    o_, i_ = a3, g3[:, dr:dr + 16, :]
                elif dc == 0:
                    o_, i_ = a3[:, :, 1:64], g3[:, dr:dr + 16, 0:63]
                else:
                    o_, i_ = a3[:, :, 0:63], g3[:, dr:dr + 16, 1:64]
                stt(o_, i_, wcol, o_, ALU.mult, ALU.add, ["gts", "fconv", "cacc"], ["cacc"])
        act(cacc, cacc, AF.Silu, ["cacc"], ["cacc"])
        for ch in range(2):
            ps = psb[3 + ch]
            for kt in range(16):
                mm(ps[:, :], wu[:, kt, :], hT[:, kt, 64 + ch * 512:64 + (ch + 1) * 512], ["wub%d" % (ft % 2), "hT"], [PSN[3 + ch]],
                   start=(kt == 0), stop=(kt == 15))
            tt(hid[:, ft, ch * 512:(ch + 1) * 512], ps[:, :], cacc[:, ch * 512:(ch + 1) * 512], ALU.mult, [PSN[3 + ch], "cacc"], ["hid"])
    yTf = arena[:, 0:16384].rearrange("p (k n) -> p k n", k=16)
    B4 = Arena()
    B4.off = 16384
    wdb = [B4.bf16(NFT * 128).rearrange("p (k n) -> p k n", k=NFT) for _ in range(2)]
    assert B4.off <= HID_OFF
    B3res = ["hT", "wgb0", "wgb1", "wub0", "wub1", "gts", "cacc"]
    for nm in ["yTf", "wdb0", "wdb1"]:
        P.alias(nm, B3res)
    for mt in range(16):
        wd = wdb[mt % 2]
        P.dma("pool", wd, wdown[mt], "wdk%d" % (mt % 2), writes=["wdb%d" % (mt % 2)])
        for ch in range(2):
            ps = psb[(mt * 2 + ch) % 4]
            pn = PSN[(mt * 2 + ch) % 4]
            for ft in range(NFT):
                mm(ps[:, :], wd[:, ft, :], hid[:, ft, ch * 512:(ch + 1) * 512], ["wdb%d" % (mt % 2), "hid"], [pn],
                   start=(ft == 0), stop=(ft == NFT - 1))
            cp(yTf[:, mt, ch * 512:(ch + 1) * 512], ps[:, :], [pn], ["yTf"])
    B5 = Arena()
    B5.off = 16384
    ytm = B5.f32(D)
    x1t = [B5.f32(D) for _ in range(2)]
    A2row = B5.f32(D)
    sq5 = B5.bf16(D)
    st5 = B5.f32(8)
    assert B5.off <= HID_OFF
    for nm in ["ytm", "x1t0", "x1t1", "A2row", "sq5", "st5"]:
        P.alias(nm, ["wdb0", "wdb1"])
    load(A2row, a12sp[:, D:2 * D].partition_broadcast(128), "a2k", ["A2row"], R=["a12sp"])
    for t_ in range(8):
        tk = slice(t_ * 128, (t_ + 1) * 128)
        xt_ = x1t[t_ % 2]
        xn = "x1t%d" % (t_ % 2)
        load(xt_, x1sp[tk, :], "x1ld%d" % (t_ % 2), [xn], R=["x1sp"])
        for k4 in range(4):
            ps = psb[4 + k4 % 2]
            pn = PSN[4 + k4 % 2]
            for kk_ in range(4):
                mt = k4 * 4 + kk_
                trp(ps[:, kk_ * 128:(kk_ + 1) * 128], yTf[:, mt, tk], ident[:, :], ["yTf", "ident"], [pn])
            cp(ytm[:, k4 * 512:(k4 + 1) * 512], ps[:, :], [pn], ["ytm"], eng=("act" if k4 % 2 == 0 else "dve"))
        act(sq5, ytm, AF.Square, ["ytm"], ["sq5", "st5"], accum=st5[:, 0:1])
        rsq(st5[:, 1:2], st5[:, 0:1], 1.0 / D, EPS, ["st5"], ["st5"])
        stt(ytm, ytm, st5[:, 1:2], A2row, ALU.mult, ALU.mult, ["ytm", "st5", "A2row"], ["ytm"])
        tt(xt_, xt_, ytm, ALU.add, [xn, "ytm"], [xn])
        P.dma("sp", out_d[tk, :], xt_, "outk%d" % (t_ % 2), reads=[xn], writes=["out_d"])
    P.emit(es, {"sp": [("outk0", P.dma_count["outk0"]), ("outk1", P.dma_count["outk1"])]})
    es.close()
    return nc


def _prep(inp):
    f = lambda a: np.ascontiguousarray(np.asarray(a, dtype=np.float32))
    x, c, ctx, c_ctx = f(inp["x"]), f(inp["c"]), f(inp["ctx"]), f(inp["c_ctx"])
    w_in = f(inp["w_in"][0])
    RW = 1024
    shared = {}
    shared["wada"] = np.ascontiguousarray(
        f(inp["w_ada"][0]).reshape(16, 128, 24, 512).transpose(2, 1, 0, 3))
    shared["bada"] = f(inp["b_ada"][0]).reshape(1, 12288)
    shared["nrm_col"] = np.ascontiguousarray(
        np.stack([_col(f(inp["norm_pre_mix"][0])), _col(f(inp["norm_pre_ffn"][0]))], axis=1))
    shared["nrm_row"] = np.stack([f(inp["norm_post_mix"][0]), f(inp["norm_post_ffn"][0])], axis=0)
    shared["wgate"] = np.ascontiguousarray(
        f(inp["ffn_w_gate"][0]).reshape(16, 128, NFT, 128).transpose(2, 1, 0, 3))
    shared["wupf"] = np.ascontiguousarray(
        f(inp["ffn_w_up"][0]).reshape(16, 128, NFT, 128).transpose(2, 1, 0, 3))
    shared["wdown"] = np.ascontiguousarray(
        f(inp["ffn_w_down"][0]).reshape(NFT, 128, 16, 128).transpose(2, 1, 0, 3))
    fc = np.concatenate([f(inp["ffn_conv"][0]).reshape(9, DFF), f(inp["ffn_conv_b"][0]).reshape(1, DFF)], axis=0)
    shared["fconv"] = np.ascontiguousarray(fc.reshape(10, NFT, 128).transpose(2, 1, 0))
    w_out = f(inp["w_out"][0])
    maps = []
    for core in range(8):
        b, g = core // 4, core % 4
        m = dict(shared)
        xa = np.zeros((SEQ + CTX + 4, D), np.float32)
        xa[1:257] = ctx[b]
        xa[259:259 + SEQ] = x[b]
        m["xa"] = xa
        xbm = np.zeros((1152, D), np.float32)
        lo, hi = g * 1024 - 64, g * 1024 + 1088
        slo, shi = max(lo, 0), min(hi, SEQ)
        xbm[slo - lo: shi - lo] = x[b, slo:shi]
        m["xb"] = xbm
        cv = np.stack([c[b], c_ctx], axis=1)
        m["cvt"] = np.ascontiguousarray(cv.reshape(16, 128, 2).transpose(1, 0, 2))
        cs = slice(g * 256, (g + 1) * 256)
        cols = np.concatenate([
            np.arange(0, RW)[cs], np.arange(RW, 2 * RW)[cs],
            np.arange(2304, 2304 + RW)[cs],
            np.arange(2048, 2304),
            np.arange(3328, 3488),
            3488 + np.arange(0, RW)[cs], 3488 + np.arange(RW, 2 * RW)[cs],
            3488 + np.arange(2 * RW, 3 * RW)[cs], 3488 + np.arange(3 * RW, 4 * RW)[cs]])
        assert cols.size == NCOLS
        m["win"] = _ktile(w_in[:, cols])
        rc = f(inp["rw_conv"][0])[:, cols[:1184]]
        rcp = np.zeros((3, 1280), np.float32)
        rcp[:, :1184] = rc
        m["rwconv"] = np.ascontiguousarray(rcp.reshape(3, 10, 128).transpose(2, 1, 0))
        cvs = np.zeros((128, 2, 9), np.float32)
        hs = slice(g * 256, (g + 1) * 256)
        vecs = [f(inp["rw_k_k"][0])[hs], f(inp["rw_k_a"][0])[hs], f(inp["rw_r_k"][0]).reshape(-1)[hs],
                f(inp["rw_lnx_w"][0])[hs], f(inp["rw_lnx_b"][0])[hs],
                f(inp["rw_a0"][0])[0, hs], f(inp["rw_a0"][0])[1, hs]]
        for vi, v in enumerate(vecs):
            cvs[:, :, vi] = v.reshape(2, 128).T
        m["colv"] = cvs
        m["w0row"] = np.concatenate([f(inp["rw_w0"][0])[0, hs], f(inp["rw_w0"][0])[1, hs]]).reshape(1, 512)
        m["wup"] = np.concatenate([f(inp["rw_w_up"][0])[0][:, hs], f(inp["rw_w_up"][0])[1][:, hs]], axis=0)
        m["aup"] = np.concatenate([f(inp["rw_a_up"][0])[0][:, hs], f(inp["rw_a_up"][0])[1][:, hs]], axis=0)
        m["gup"] = np.ascontiguousarray(f(inp["rw_g_up"][0])[:, hs])
        rows = np.concatenate([np.concatenate([np.arange(gp * 256, (gp + 1) * 256),
                                               1024 + np.arange(gp * 256, (gp + 1) * 256)]) for gp in range(4)])
        m["wout"] = _ktile(w_out[rows])
        hm = np.zeros((128, 2), np.float32)
        hm[:, 0] = 1.0 if g > 0 else 0.0
        hm[:, 1] = 1.0 if g < 3 else 0.0
        m["hmask"] = hm
        tsl = np.zeros((128, 8), np.float32)
        tsl[:, b * 4 + g] = 1.0
        m["tsel"] = tsl
        for k, v in _consts(g).items():
            m["c_" + k] = v
        maps.append(m)
    return maps


_NC_CACHE = {}
FUSED = False


def kernel(**inputs):
    import ml_dtypes
    maps = _prep(inputs)
    out = np.zeros((2, SEQ, D), np.float32)
    if FUSED:
        if "nc" not in _NC_CACHE:
            _NC_CACHE["nc"] = build()
        res = run_bass_kernel_spmd(_NC_CACHE["nc"], maps, core_ids=list(range(8)))
        for core in range(8):
            b, g = core // 4, core % 4
            out[b, g * 1024:(g + 1) * 1024] = np.asarray(res.results[core]["out"], dtype=np.float32)
        return out
    if "ncA" not in _NC_CACHE:
        _NC_CACHE["ncA"] = build(stop_after="A2L")
        _NC_CACHE["ncB"] = build(stop_after="B2L")
    dummy = {"oin": np.zeros((128, 16, 1152), ml_dtypes.bfloat16), "modcol_i": np.zeros((128, 96), np.float32),
             "a12_i": np.zeros((1, 2 * D), np.float32)}
    resA = run_bass_kernel_spmd(_NC_CACHE["ncA"], maps, core_ids=list(range(8)))
    osend = [np.asarray(resA.results[c]["osend"]) for c in range(8)]
    mapsB = []
    for core in range(8):
        b, g = core // 4, core % 4
        m = dict(maps[core])
        oin = np.zeros((128, 16, 1152), osend[0].dtype)
        lo, hi = g * 1024 - 64, g * 1024 + 1088
        slo, shi = max(lo, 0), min(hi, SEQ)
        for kt in range(16):
            src = osend[4 * b + kt // 4]
            j = kt % 4
            oin[:, kt, slo - lo:shi - lo] = src[j * 128:(j + 1) * 128, slo:shi]
        m["oin"] = oin
        m["modcol_i"] = np.asarray(resA.results[core]["modcol_o"])
        m["a12_i"] = np.asarray(resA.results[core]["a12_o"])
        mapsB.append(m)
    resB = run_bass_kernel_spmd(_NC_CACHE["ncB"], mapsB, core_ids=list(range(8)))
    for core in range(8):
        b, g = core // 4, core % 4
        out[b, g * 1024:(g + 1) * 1024] = np.asarray(resB.results[core]["out"], dtype=np.float32)
    return out
```

```python
import numpy as np
from contextlib import ExitStack
import concourse.bass as bass
import concourse.mybir as mybir
from concourse.bass_utils import run_bass_kernel_spmd

F32 = mybir.dt.float32
BF16 = mybir.dt.bfloat16
AF = mybir.ActivationFunctionType
ALU = mybir.AluOpType
AX = mybir.AxisListType

D = 2048
SEQ = 4096
CTX = 256
DFF = 5632
NFT = 44
EPS = 1e-6
LNX_EPS = 64e-5
BLK = 256
NWIN = 258
C = 64
NCOLS = 2208
DECAY_C = float(np.exp(-0.5))

DBG = {}


class Ev:
    __slots__ = ("kind", "eng", "idx", "key", "val", "needed")

    def __init__(self, kind, eng=None, idx=0, key=None, val=0):
        self.kind, self.eng, self.idx, self.key, self.val, self.needed = kind, eng, idx, key, val, False


class Prog:
    ENGS = ["pe", "act", "dve", "pool", "sp"]

    def __init__(self, nc):
        self.nc = nc
        self.stream = {e: [] for e in self.ENGS}
        self.last_write = {}
        self.readers = {}
        self.waited = {e: {} for e in self.ENGS}
        self.dma_count = {}
        self.dma_keys = []
        self.last_ev = {}
        self.trace_lines = False
        self.lines = {}
        self.imap = {}
        self.sfx = ""
        self.shared = set()

    def _m(self, names):
        if not self.sfx:
            return list(names)
        return [n if n in self.shared else n + self.sfx for n in names]

    def _deps(self, eng, reads, writes):
        evs = []
        for r in reads:
            w = self.last_write.get(r)
            if w is not None:
                evs.append(w)
        for r in writes:
            w = self.last_write.get(r)
            if w is not None:
                evs.append(w)
            evs.extend(self.readers.get(r, ()))
        best = {}
        for ev in evs:
            if ev.kind == "eng":
                if ev.eng == eng and eng == "pe":
                    continue
                k = ("eng", ev.eng)
                if k not in best or best[k].idx < ev.idx:
                    best[k] = ev
            else:
                k = ("dma", ev.key)
                if k not in best or best[k].val < ev.val:
                    best[k] = ev
        out = []
        for k, ev in best.items():
            cur = self.waited[eng].get(k, -1)
            v = ev.idx if ev.kind == "eng" else ev.val
            if cur >= v:
                continue
            self.waited[eng][k] = v
            ev.needed = True
            out.append(ev)
        return out

    def _commit(self, ev, reads, writes):
        for r in reads:
            self.readers.setdefault(r, []).append(ev)
        for r in writes:
            self.last_write[r] = ev
            self.readers[r] = []

    def op(self, eng, fn, reads=(), writes=()):
        reads, writes = self._m(reads), self._m(writes)
        waits = self._deps(eng, reads, writes)
        ev = Ev("eng", eng=eng, idx=len(self.stream[eng]))
        if self.trace_lines:
            import sys as _s
            fr = _s._getframe(2)
            self.lines[(eng, len(self.stream[eng]))] = (fr.f_lineno, fr.f_back.f_lineno if fr.f_back else 0)
        self.stream[eng].append((fn, waits, ev))
        self._commit(ev, reads, writes)
        self.last_ev[eng] = ev
        return ev

    def dma(self, eng, out, in_, key, reads=(), writes=(), **kw):
        reads, writes = self._m(reads), self._m(writes)
        key = key if (not self.sfx or key in ("const", "constp")) else key + self.sfx
        waits = self._deps(eng, reads, writes)
        if key not in self.dma_count:
            self.dma_count[key] = 0
            self.dma_keys.append(key)
        self.dma_count[key] += 16
        ev = Ev("dma", eng=eng, idx=len(self.stream[eng]), key=key, val=self.dma_count[key])
        self.stream[eng].append((lambda e: e.dma_start(out=out, in_=in_, **kw), waits, ev))
        self._commit(ev, reads, writes)
        return ev

    def custom(self, eng, fn, key, reads=(), writes=(), inc=1):
        waits = self._deps(eng, reads, writes)
        if key not in self.dma_count:
            self.dma_count[key] = 0
            self.dma_keys.append(key)
        self.dma_count[key] += inc
        ev = Ev("dma", eng=eng, idx=len(self.stream[eng]), key=key, val=self.dma_count[key])
        self.stream[eng].append((fn, waits, ev))
        self._commit(ev, reads, writes)
        return ev

    def alias(self, new, olds):
        new = self._m([new])[0]
        evs = []
        for o in olds:
            w = self.last_write.get(o)
            if w is not None:
                evs.append(w)
            evs.extend(self.readers.get(o, ()))
        self.readers.setdefault(new, []).extend(evs)

    def all_res(self):
        return list(set(list(self.last_write.keys()) + list(self.readers.keys())))

    def emit(self, es, final_waits):
        nc = self.nc
        LIM = 30000
        engobj = {"pe": nc.tensor, "act": nc.scalar, "dve": nc.vector, "pool": nc.gpsimd, "sp": nc.sync}
        esems = {}
        for e in self.ENGS:
            cnt = 0
            for (fn, waits, ev) in self.stream[e]:
                if ev.kind == "eng" and ev.needed:
                    cnt += 1
                    ev.val = cnt
            nsem = cnt // LIM + 1
            import os as _os
            if _os.environ.get("KCNT"):
                print("SEMCNT", e, cnt, "ninstr", len(self.stream[e]), "nwaits", sum(len(w) for (_, w, _) in self.stream[e]))
            esems[e] = [es.enter_context(nc.semaphore("s_%s_%d" % (e, i))) for i in range(nsem)]
        dsems = {k: es.enter_context(nc.semaphore("d_%s" % str(k))) for k in self.dma_keys}

        def semval(ev):
            if ev.kind == "eng":
                i = (ev.val - 1) // LIM
                return esems[ev.eng][i], ev.val - i * LIM
            if ev.key in ("const", "constp"):
                return dsems[ev.key], self.dma_count[ev.key]
            return dsems[ev.key], ev.val

        block = es.enter_context(nc.Block())
        streams = self.stream

        def run(e, eng):
            for ii, (fn, waits, ev) in enumerate(streams[e]):
                for w in waits:
                    s, v = semval(w)
                    eng.wait_ge(s, v)
                ins = fn(eng)
                if self.trace_lines:
                    try:
                        self.imap[str(ins.ins.name)] = self.lines.get((e, ii))
                    except Exception:
                        pass
                if ev.kind == "dma":
                    if ev.eng is not None and ev.key is not None:
                        inc = 16 if not str(ev.key).startswith("cc") else 1
                        ins.then_inc(dsems[ev.key], inc)
                elif ev.needed:
                    s, v = semval(ev)
                    ins.then_inc(s, 1)
            for (k, v) in final_waits.get(e, []):
                eng.wait_ge(dsems[k], v)

        @block.tensor
        def _(eng):
            run("pe", eng)

        @block.scalar
        def _(eng):
            run("act", eng)

        @block.vector
        def _(eng):
            run("dve", eng)

        @block.gpsimd
        def _(eng):
            run("pool", eng)

        @block.sync
        def _(eng):
            run("sp", eng)


def _consts(g):
    cst = {}
    cst["ident"] = np.eye(128, dtype=np.float32)
    bo = np.zeros((128, 128), np.float32)
    bo[:64, :64] = 1.0
    bo[64:, 64:] = 1.0
    cst["bones"] = bo
    cst["ones"] = np.ones((128, 128), np.float32)
    s = np.arange(128)[:, None]
    t = np.arange(128)[None, :]
    same = (s // 64) == (t // 64)
    tri = np.zeros((128, 4, 128), np.float32)
    tri[:, 0, :] = same & (s <= t)
    tri[:, 1, :] = same & (s < t)
    tri[:, 2, :] = same & (s >= t)
    tri[:, 3, :] = same & (s > t)
    cst["tri"] = tri
    s = np.arange(64)[:, None]
    t = np.arange(64)[None, :]
    mst = np.zeros((64, 2, 128), np.float32)
    mst[:, 0, :64] = s < t
    mst[:, 0, 64:] = s <= t
    mst[:, 1, :64] = s > t
    mst[:, 1, 64:] = s >= t
    cst["mst"] = mst
    m1 = np.zeros((64, 2, 64), np.float32)
    m1[:, 0, :] = (t < s).T.T
    tt = np.arange(64)[:, None]
    ss = np.arange(64)[None, :]
    m1[:, 0, :] = ss < tt
    m1[:, 1, :] = ss > tt
    cst["m1"] = m1
    j = np.arange(128)[:, None].astype(np.float64)
    i = np.arange(128)[None, :].astype(np.float64)
    dmask = np.zeros((128, 4, 128), np.float32)
    qsc = np.zeros((128, 4, 128), np.float32)
    ksc = np.zeros((128, 4), np.float32)
    gc = np.zeros((128, 2), np.float32)
    sc = 128.0 ** -0.5
    for hr in range(2):
        gam = 1.0 - 2.0 ** (-5.0 - (2 * g + hr))
        lg = np.log(gam)
        dmask[:, hr * 2 + 0, :] = np.where(j <= i, np.exp((i - j) * lg), 0.0) * sc
        dmask[:, hr * 2 + 1, :] = np.where(j > i, np.exp((j - i) * lg), 0.0) * sc
        qsc[:, hr * 2 + 0, :] = np.exp((i + 1.0) * lg)
        qsc[:, hr * 2 + 1, :] = np.exp((128.0 - i) * lg)
        ksc[:, hr * 2 + 0] = (np.exp((127.0 - j) * lg) * sc)[:, 0]
        ksc[:, hr * 2 + 1] = (np.exp(j * lg) * sc)[:, 0]
        gc[:, hr] = np.exp(128.0 * lg)
    cst["dmask"], cst["qsc"], cst["ksc"], cst["gc"] = dmask, qsc, ksc, gc
    n = 32
    inv = 10000.0 ** (-np.arange(n, dtype=np.float64) / n)
    tpos = np.arange(SEQ)
    ang = np.zeros((128, SEQ), np.float64)
    for d in range(128):
        if d < 64:
            ang[d] = (tpos // 64) * inv[d % 32]
        else:
            ang[d] = (tpos % 64) * inv[d % 32]
    cst["ropec"] = np.cos(ang).astype(np.float32)
    cst["ropes"] = np.sin(ang).astype(np.float32)
    pm = np.zeros((128, 128), np.float32)
    for dp in range(128):
        if (dp % 64) < 32:
            pm[dp + 32, dp] = -1.0
        else:
            pm[dp - 32, dp] = 1.0
    cst["pm"] = pm
    sel = np.zeros((2, 130), np.float32)
    sel[0, 0] = 1.0
    sel[1, 1] = 1.0
    sel[0, 2:] = 1.0
    cst["sel"] = sel
    return cst


def _ktile(w):
    K, N = w.shape
    return np.ascontiguousarray(w.reshape(K // 128, 128, N).transpose(1, 0, 2))


def _col(v):
    return np.ascontiguousarray(v.reshape(-1, 128).T)


def build(stop_after=None, dbg=False):
    nc = bass.Bass("TRN2", target_bir_lowering=False)
    P = Prog(nc)
    P.trace_lines = dbg
    DBG["P"] = P
    es = ExitStack()

    def din(name, shape, dt=F32):
        return nc.dram_tensor(name, list(shape), dt, kind="ExternalInput").ap()

    xa = din("xa", [SEQ + CTX + 4, D])
    xb = din("xb", [1152, D])
    cvt = din("cvt", [128, 16, 2])
    wada = din("wada", [24, 128, 16, 512])
    bada = din("bada", [1, 12288])
    nrm_col = din("nrm_col", [128, 2, 16])
    nrm_row = din("nrm_row", [2, D])
    win = din("win", [128, 16, NCOLS])
    rwconv = din("rwconv", [128, 10, 3])
    colv = din("colv", [128, 2, 9])
    w0row = din("w0row", [1, 2 * 256])
    wup = din("wup", [128, 256])
    aup = din("aup", [128, 256])
    gup = din("gup", [160, 256])
    wout = din("wout", [128, 16, D])
    wgate = din("wgate", [NFT, 128, 16, 128])
    wupf = din("wupf", [NFT, 128, 16, 128])
    wdown = din("wdown", [16, 128, NFT, 128])
    fconv = din("fconv", [128, NFT, 10])
    hmask = din("hmask", [128, 2])
    tsel = din("tsel", [128, 8])
    c_ident = din("c_ident", [128, 128])
    c_bones = din("c_bones", [128, 128])
    c_ones = din("c_ones", [128, 128])
    c_tri = din("c_tri", [128, 4, 128])
    c_mst = din("c_mst", [64, 2, 128])
    c_m1 = din("c_m1", [64, 2, 64])
    c_dmask = din("c_dmask", [128, 4, 128])
    c_qsc = din("c_qsc", [128, 4, 128])
    c_ksc = din("c_ksc", [128, 4])
    c_gc = din("c_gc", [128, 2])
    c_ropec = din("c_ropec", [128, SEQ])
    c_ropes = din("c_ropes", [128, SEQ])
    c_pm = din("c_pm", [128, 128])
    c_sel = din("c_sel", [2, 130])
    out_d = None
    yspill = nc.dram_tensor("yspill", [4, 128, SEQ], F32).ap()
    pspill = nc.dram_tensor("pspill", [17, 18, 128, BLK], F32).ap()
    TWO_A = stop_after == "A2L"
    TWO_B = stop_after == "B2L"
    osend = nc.dram_tensor("osend", [512, SEQ], BF16, kind=("ExternalOutput" if TWO_A else "Internal"))
    if not TWO_A:
        out_d = nc.dram_tensor("out", [1024, D], F32, kind="ExternalOutput").ap()
    if TWO_A:
        modcol_o = nc.dram_tensor("modcol_o", [128, 96], F32, kind="ExternalOutput").ap()
        a12_o = nc.dram_tensor("a12_o", [1, 2 * D], F32, kind="ExternalOutput").ap()
    if TWO_B:
        oin = din("oin", [128, 16, 1152], BF16)
        modcol_i = din("modcol_i", [128, 96])
        a12_i = din("a12_i", [1, 2 * D])
    ogath = nc.dram_tensor("ogath", [8 * 512, SEQ], BF16)
    x1sp = nc.dram_tensor("x1sp", [1024, D], F32).ap()
    dbg_d = None
    if dbg:
        dbg_d = nc.dram_tensor("dbg", [128, 4096], F32, kind="ExternalOutput").ap()

    def sb(name, shape, dt=F32):
        return es.enter_context(nc.sbuf_tensor(name, list(shape), dt))

    ident = sb("ident", [128, 128])
    bones = sb("bones", [128, 128])
    ones = sb("ones", [128, 128])
    tri = sb("tri", [128, 4, 128])
    mst = sb("mst", [64, 2, 128])
    m1m = sb("m1m", [64, 2, 64])
    dmask = sb("dmask", [128, 4, 128])
    qsc = sb("qsc", [128, 4, 128])
    ksc = sb("ksc", [128, 4])
    gcs = sb("gcs", [128, 2])
    pm = sb("pm", [128, 128])
    sel = sb("sel", [2, 130])
    nrmc = sb("nrmc", [128, 2, 16])
    rwcv = sb("rwcv", [128, 10, 3])
    colvs = sb("colvs", [128, 2, 9])
    w0bc = sb("w0bc", [128, 512])
    wup_s = sb("wup_s", [128, 256], BF16)
    aup_s = sb("aup_s", [128, 256], BF16)
    gup_a = sb("gup_a", [128, 256], BF16)
    gup_b = sb("gup_b", [32, 256], BF16)
    hmask_s = sb("hmask_s", [128, 2])
    tsel_s = sb("tsel_s", [128, 8])
    fconv_s = sb("fconv_s", [128, NFT, 10])
    modcol = sb("modcol", [128, 6, 16])
    ARENA_W = 48800
    arena = sb("arena", [128, ARENA_W])
    psb = [es.enter_context(nc.psum_tensor("psb%d" % i, [128, 512], F32)) for i in range(8)]

    class Arena:
        def __init__(self):
            self.off = 0

        def f32(self, n):
            o = self.off
            self.off += n
            assert self.off <= ARENA_W, self.off
            return arena[:, o:o + n]

        def bf16(self, n):
            w = (n + 1) // 2
            o = self.off
            self.off += w
            assert self.off <= ARENA_W, self.off
            return arena[:, o:o + w].bitcast(BF16)

    V, S, T, G, PE = "dve", "act", "pe", "pool", "pe"

    def tt(out, a, b, op, R, W, eng="dve"):
        P.op(eng, lambda e: e.tensor_tensor(out=out, in0=a, in1=b, op=op), R, W)

    def ts(out, a, s1, s2, op0, op1, R, W, eng="dve"):
        if s2 is None:
            P.op(eng, lambda e: e.tensor_scalar(out=out, in0=a, scalar1=s1, scalar2=None, op0=op0), R, W)
        else:
            P.op(eng, lambda e: e.tensor_scalar(out=out, in0=a, scalar1=s1, scalar2=s2, op0=op0, op1=op1), R, W)

    def stt(out, a, s, b, op0, op1, R, W, eng="dve"):
        P.op(eng, lambda e: e.scalar_tensor_tensor(out=out, in0=a, scalar=s, in1=b, op0=op0, op1=op1), R, W)

    def act(out, a, func, R, W, bias=None, scale=None, accum=None):
        kw = {}
        if bias is not None:
            kw["bias"] = bias
        if scale is not None:
            kw["scale"] = scale
        if accum is not None:
            kw["accum_out"] = accum
        P.op("act", lambda e: e.activation(out=out, in_=a, func=func, **kw), R, W)

    epsc = sb("epsc", [128, 4])
    identh = sb("identh", [128, 128], BF16)

    def rsq(out, a, scale, biasv, R, W):
        bi = {EPS: 0, LNX_EPS: 1, 1e-12: 2}[biasv]
        P.op("act", lambda e: e.activation(out=out, in_=a, func=AF.Sqrt, bias=epsc[0:out.shape[0], bi:bi + 1], scale=scale), list(R) + ["epsc"], W)
        P.op("dve", lambda e: e.reciprocal(out=out, in_=out), W, W)

    def cp(out, a, R, W, eng="act"):
        if eng == "act":
            P.op("act", lambda e: e.copy(out=out, in_=a), R, W)
        else:
            P.op(eng, lambda e: e.tensor_copy(out=out, in_=a), R, W)

    def mm(out, lhsT, rhs, R, W, start=True, stop=True):
        P.op("pe", lambda e: e.matmul(out, lhsT, rhs, start=start, stop=stop), R, W)

    def trp(out, in_, idn, R, W):
        P.op("pe", lambda e: e.transpose(out, in_, idn), R, W)

    def memset(ap, val, W, eng="dve"):
        P.op(eng, lambda e: e.memset(ap, val), (), W)

    dq = ["sp", "act"]
    dqi = [0]

    dq_default = ["sp"]

    def load(out, in_, key, W, R=(), eng=None):
        if eng is None:
            eng = dq_default[0]
        return P.dma(eng, out, in_, key, reads=R, writes=W)

    memset(epsc[:, 0:1], EPS, ["epsc"])
    memset(epsc[:, 1:2], LNX_EPS, ["epsc"])
    memset(epsc[:, 2:3], 1e-12, ["epsc"])
    CK = "const"
    for (dst, src, nm) in [(ident, c_ident, "ident"), (bones, c_bones, "bones"), (ones, c_ones, "ones"),
                           (tri, c_tri, "tri"), (mst, c_mst, "mst"), (m1m, c_m1, "m1m"), (dmask, c_dmask, "dmask"),
                           (qsc, c_qsc, "qsc"), (ksc, c_ksc, "ksc"), (gcs, c_gc, "gcs"), (pm, c_pm, "pm"),
                           (sel, c_sel, "sel"), (nrmc, nrm_col, "nrmc"), (rwcv, rwconv, "rwcv"),
                           (colvs, colv, "colvs"), (hmask_s, hmask, "hmask"), (tsel_s, tsel, "tsel"),
                           (fconv_s, fconv, "fconv")]:
        load(dst[:], src, CK, [nm])
    load(w0bc[:], w0row.partition_broadcast(128), CK, ["w0bc"])
    P.op("act", lambda e: e.copy(out=identh[:], in_=ident[:]), ["ident"], ["identh"])
    P.dma("pool", wup_s[:], wup, "constp", writes=["wup"])
    P.dma("pool", aup_s[:], aup, "constp", writes=["aup"])
    P.dma("pool", gup_a[:], gup[0:128, :], "constp", writes=["gupa"])
    P.dma("pool", gup_b[:], gup[128:160, :], "constp", writes=["gupb"])

    A0 = Arena()
    wa_buf = [A0.f32(16 * 512).rearrange("p (k n) -> p k n", k=16) for _ in range(2)]
    modrow = A0.f32(12288)
    badab = [A0.f32(512) for _ in range(2)]
    npost = A0.f32(2 * D).rearrange("p (a n) -> p a n", a=2)
    A12 = A0.f32(2 * D).rearrange("p (a n) -> p a n", a=2)
    a12sp = nc.dram_tensor("a12sp", [1, 2 * D], F32).ap()
    scT = sb("scT", [128, 16, 2])
    load(scT[:], cvt, CK, ["scT"])
    load(npost[:], nrm_row.rearrange("a n -> (a n)").partition_broadcast(128).rearrange("p (a n) -> p a n", a=2)
         if False else nrm_row.unsqueeze(0).to_broadcast([128, 2, D]), CK, ["npost"])
    act(scT[:], scT[:], AF.Silu, ["scT"], ["scT"])
    for n in range(24):
        wb = wa_buf[n % 2]
        load(wb, wada[n], "wada%d" % (n % 2), ["wab%d" % (n % 2)])
        load(badab[n % 2][0:2, :], bada[:, n * 512:(n + 1) * 512].partition_broadcast(2), "wada%d" % (n % 2), ["wab%d" % (n % 2)])
        ps = psb[n % 2]
        for kt in range(16):
            mm(ps[0:2, :], scT[:, kt, :], wb[:, kt, :], ["scT", "wab%d" % (n % 2)], ["psb%d" % (n % 2)],
               start=(kt == 0), stop=(kt == 15))
        tt(modrow[0:2, n * 512:(n + 1) * 512], ps[0:2, :], badab[n % 2][0:2, :], ALU.add,
           ["psb%d" % (n % 2), "wab%d" % (n % 2)], ["modrow"])
    segs = [(0, 0), (0, 1), (0, 3), (0, 4), (1, 0), (1, 1)]
    psc = psb[2]
    for i, (r, sg) in enumerate(segs):
        for kt in range(16):
            mm(psc[:, i * 16 + kt:i * 16 + kt + 1], modrow[0:2, sg * D + kt * 128: sg * D + (kt + 1) * 128],
               sel[0:2, r:r + 1], ["modrow", "sel"], ["psb2"])
    cp(modcol[:].rearrange("p a k -> p (a k)"), psc[:, 0:96], ["psb2"], ["modcol"], eng="dve")
    for (i, nidx) in [(1, 0), (3, 1), (5, 0)]:
        stt(modcol[:, i, :], modcol[:, i, :], 1.0, nrmc[:, nidx, :], ALU.add, ALU.mult, ["modcol", "nrmc"], ["modcol"])
    for a, sg in enumerate([2, 5]):
        for j in range(4):
            ps = psb[3 + (j % 2)]
            mm(ps[:, :], sel[0:2, 2:130], modrow[0:2, sg * D + j * 512: sg * D + (j + 1) * 512], ["modrow", "sel"],
               ["psb%d" % (3 + j % 2)])
            tt(A12[:, a, j * 512:(j + 1) * 512], ps[:, :], npost[:, a, j * 512:(j + 1) * 512], ALU.mult,
               ["psb%d" % (3 + j % 2), "npost"], ["A12"])
    P.dma("sp", a12sp, A12[0:1].rearrange("p a n -> p (a n)"), "a12st", reads=["A12"], writes=["a12sp"])
    if TWO_A:
        P.dma("sp", a12_o, A12[0:1].rearrange("p a n -> p (a n)"), "a12o", reads=["A12"], writes=["a12_o"])
    ph0_res = ["wab0", "wab1", "modrow", "badab", "npost", "A12"]

    if dbg and stop_after == 0:
        dtile = A0.f32(4096)
        memset(dtile[:], 0.0, ["dtile"])
        cp(dtile[:, 0:96], modcol[:].rearrange("p a k -> p (a k)"), ["modcol"], ["dtile"], eng="dve")
        cp(dtile[:, 128:128 + 2048], A12[:, 0, :], ["A12"], ["dtile"], eng="dve")
        ev = P.dma("sp", dbg_d, dtile[:], "dbgout", reads=["dtile"], writes=["dbg_d"])
        P.emit(es, {"sp": [("dbgout", 16)]})
        es.close()
        return nc

    PMAP = {0: 0, 1: 1, 2: 0, 3: 1, 4: 2, 5: 3, 6: 2, 7: 3}
    yspill_f = nc.dram_tensor("yspill_f", [4, 128, SEQ], F32).ap()
    yspill_b = nc.dram_tensor("yspill_b", [4, 128, SEQ], F32).ap()

    def make_thread(tid, TA):
        def PB(i):
            return psb[4 * tid + PMAP[i]]

        def PBN(i):
            return "PS%d" % (4 * tid + PMAP[i])

        vtm = TA.bf16(2 * 2 * 128).rearrange("p (c h n) -> p c h n", c=2, h=2)
        rwT = TA.f32(10 * BLK).rearrange("p (k n) -> p k n", k=10)
        retT = TA.f32(8 * BLK).rearrange("p (k n) -> p k n", k=8)
        stat = TA.f32(8)

        def t2(dt=F32):
            if dt == F32:
                return TA.f32(2 * BLK).rearrange("p (k n) -> p k n", k=2)
            return TA.bf16(2 * BLK).rearrange("p (k n) -> p k n", k=2)

        thb = TA.bf16(BLK)
        adb = TA.bf16(BLK)
        gsb = TA.bf16(2 * BLK).rearrange("p (k n) -> p k n", k=2)
        sigtm = t2()
        t2base = TA.off
        cumT, cumpT, aT, kkT, tmpA, tmpB, Ep, Em, Epv, kdT, BhT, KhT, yT = [t2() for _ in range(13)]
        rwT2 = arena[:, t2base:t2base + 10 * BLK].rearrange("p (k n) -> p k n", k=10)
        retT2 = arena[:, t2base + 10 * BLK:t2base + 18 * BLK].rearrange("p (k n) -> p k n", k=8)
        aT2, kd2T = tmpB, BhT
        ART = TA.bf16(2 * 4 * 128).rearrange("p (k c n) -> p k c n", k=2, c=4)
        BKT = TA.bf16(2 * 4 * 128).rearrange("p (k c n) -> p k c n", k=2, c=4)
        BKV = TA.bf16(4 * 3 * 2 * 2 * 128).rearrange("p (c m k h n) -> p c m k h n", c=4, m=3, k=2, h=2)
        Ms2 = TA.bf16(2 * 4 * 2 * 128).rearrange("p (c h m n) -> p c h m n", c=2, h=4, m=2)
        M1s2 = TA.bf16(2 * 4 * 64).rearrange("p (c h n) -> p c h n", c=2, h=4)
        Pa = [TA.bf16(2 * 4 * 2 * 64).rearrange("p (c h m n) -> p c h m n", c=2, h=4, m=2) for _ in range(2)]
        Qs = [TA.bf16(2 * 4 * 64).rearrange("p (c h n) -> p c h n", c=2, h=4) for _ in range(2)]
        Wsb = TA.bf16(2 * 128).rearrange("p (k n) -> p k n", k=2)
        Usb = TA.bf16(2 * 2 * 128).rearrange("p (k h n) -> p k h n", k=2, h=2)
        Hst = TA.f32(2 * 128).rearrange("p (k n) -> p k n", k=2)
        Hb = TA.bf16(2 * 128).rearrange("p (k n) -> p k n", k=2)
        qr = t2(BF16)
        kr = t2(BF16)
        qtl = t2(BF16)
        rc_t = TA.f32(BLK)
        rs_t = TA.f32(BLK)
        ktm = TA.bf16(2 * 2 * 128).rearrange("p (c h n) -> p c h n", c=2, h=2)
        Ssb = TA.bf16(2 * 128).rearrange("p (h n) -> p h n", h=2)
        Sst = TA.f32(2 * 128).rearrange("p (h n) -> p h n", h=2)
        Sbf = TA.bf16(2 * 128).rearrange("p (h n) -> p h n", h=2)
        yrT = t2()
        vrb = t2(BF16)
        ysp_flat = TA.f32(4 * BLK)
        ysp = ysp_flat.rearrange("p (k n) -> p k n", k=4)
        sqj = ysp_flat.bitcast(BF16)
        print("TA.off", TA.off)
        oT = BhT.rearrange("p k n -> p (k n)").bitcast(BF16).rearrange("p (k n) -> p k n", k=4)

        tsize = TA.off

        def init():
            memset(BKV[0:64], 0.0, ["BKV"])
            memset(Usb[0:64], 0.0, ["Usb"])
            memset(Hst[:], 0.0, ["Hst"])
            memset(Hb[:], 0.0, ["Hb"])
            memset(Sst[:], 0.0, ["Sst"])
            memset(Sbf[:], 0.0, ["Sbf"])

        def front(seq, bi, nblk, alt=False):
            sw = 0
            rwT_, retT_, rn, tn = (rwT2, retT2, "rwTalt", "retTalt") if alt else (rwT, retT, "rwT", "retT")
            lat = seq == "l"
            ro = lat
            row0 = (0 if not lat else 258) + bi * BLK
            s1i, shi_ = (5, 4) if not lat else (1, 0)
            blk_i = 0 if not lat else bi + 1
            nrw = 9 if ro else 8
            nrt = 8 if ro else 4
            for j in range(3):
                nr = 128 if j < 2 else 2
                xt = xts[j % 2]
                xn = "xt%d" % (j % 2)
                load(xt[0:nr, :], xa[row0 + j * 128: row0 + j * 128 + nr, :], "xk%d" % (j % 2), [xn])
                act(sqj[0:nr, :], xt[0:nr, :], AF.Square, [xn], ["ysp", "stat"], accum=stat[0:nr, 0:1])
                rsq(stat[0:nr, 1:2], stat[0:nr, 0:1], 1.0 / D, EPS, ["stat"], ["stat"])
                ts(xt[0:nr, :], xt[0:nr, :], stat[0:nr, 1:2], None, ALU.mult, None, [xn, "stat"], [xn])
                for k4 in range(4):
                    ps = PB(k4 % 2)
                    pn = PBN(k4 % 2)
                    for kk_ in range(4):
                        kt = k4 * 4 + kk_
                        trp(ps[:, kk_ * 128: kk_ * 128 + nr], xt[0:nr, kt * 128:(kt + 1) * 128], ident[0:nr, 0:nr], [xn, "ident"], [pn])
                    for kk_ in range(4):
                        kt = k4 * 4 + kk_
                        if kk_ % 2 == 0:
                            act(xmT[:, kt, j * 128: j * 128 + nr], ps[:, kk_ * 128: kk_ * 128 + nr], AF.Identity, [pn, "modcol"], ["xmT"],
                                bias=modcol[:, shi_, kt:kt + 1], scale=modcol[:, s1i, kt:kt + 1])
                        else:
                            ts(xmT[:, kt, j * 128: j * 128 + nr], ps[:, kk_ * 128: kk_ * 128 + nr], modcol[:, s1i, kt:kt + 1],
                               modcol[:, shi_, kt:kt + 1], ALU.mult, ALU.add, [pn, "modcol"], ["xmT"])
            if bi == 0:
                memset(xmT[:, :, 0:1], 0.0, ["xmT"])
            if bi == nblk - 1:
                memset(xmT[:, :, 257:258], 0.0, ["xmT"])
            tiles = [0, 1, 2, 3, 4, 5, 6, 7, 8, 9, 10, 11, 12, 13, 14, 15, 16, 17] if ro else [0, 1, 2, 3, 4, 5, 6, 7, 10, 11, 12, 13]
            for ti, mt in enumerate(tiles):
                ps = PB(ti % 2)
                pn = PBN(ti % 2)
                if mt < 9:
                    c0, mw = mt * 128, 128
                elif mt == 9:
                    c0, mw = 1152, 32
                else:
                    c0, mw = 1184 + (mt - 10) * 128, 128
                for kt in range(16):
                    mm(ps[0:mw, 0:NWIN], winb[:, kt, c0:c0 + mw], xmT[:, kt, :], ["winb", "xmT"], [pn], start=(kt == 0), stop=(kt == 15))
                if mt < 10:
                    ts(rwT_[0:mw, mt, :], ps[0:mw, 1:257], rwcv[0:mw, mt, 1:2], None, ALU.mult, None, [pn, "rwcv"], [rn])
                    stt(rwT_[0:mw, mt, :], ps[0:mw, 0:256], rwcv[0:mw, mt, 0:1], rwT_[0:mw, mt, :], ALU.mult, ALU.add, [pn, "rwcv", rn], [rn])
                    stt(rwT_[0:mw, mt, :], ps[0:mw, 2:258], rwcv[0:mw, mt, 2:3], rwT_[0:mw, mt, :], ALU.mult, ALU.add, [pn, "rwcv", rn], [rn])
                else:
                    cp(retT_[:, mt - 10, :], ps[:, 1:257], [pn], [tn])

            P.dma("sp", pspill[blk_i, 0:nrw].rearrange("k p n -> p k n"), rwT_[:, 0:nrw, :], ("psa" + str(int(alt))), reads=[rn], writes=["pspA%d" % blk_i])
            if ro:
                P.dma("sp", pspill[blk_i, 9, 0:32, :], rwT_[0:32, 9, :], ("psb_" + str(int(alt))), reads=[rn], writes=["pspB%d" % blk_i])
            P.dma("sp", pspill[blk_i, 10:10 + nrt].rearrange("k p n -> p k n"), retT_[:, 0:nrt, :], ("psc" + str(int(alt))), reads=[tn], writes=["pspC%d" % blk_i])


        def block(sw, seq, bi, nblk):
            lat = seq == "l"
            ro = lat
            row0 = (0 if not lat else 258) + bi * BLK
            s1i, shi_ = (5, 4) if not lat else (1, 0)
            blk_i = 0 if not lat else bi + 1
            nrw = 9 if ro else 8
            nrt = 8 if ro else 4
            load(rwT[:, 0:nrw, :], pspill[blk_i, 0:nrw].rearrange("k p n -> p k n"), "pla", ["rwT"], R=["pspA%d" % blk_i])
            if ro:
                load(rwT[0:32, 9, :], pspill[blk_i, 9, 0:32, :], "plb", ["rwT"], R=["pspB%d" % blk_i])
            load(retT[:, 0:nrt, :], pspill[blk_i, 10:10 + nrt].rearrange("k p n -> p k n"), "plc", ["retT"], R=["pspC%d" % blk_i])
            yield
            dp = slice(0, 64) if sw == 0 else slice(64, 128)
            v4 = lambda ap: ap.rearrange("p (c n) -> p c n", c=4)
            cend = 63 if sw == 0 else 0
            act(thb[dp, :], rwT[dp, 6, :], AF.Tanh, ["rwT"], ["thb"])
            cp(adb[:, :], rwT[:, 7, :], ["rwT"], ["adb"])
            for t_ in range(2):
                ps = PB(2 + t_)
                pn = PBN(2 + t_)
                mm(ps[:, 0:256], thb[dp, t_ * 128:(t_ + 1) * 128], wup_s[dp, :], ["thb", "wup"], [pn])
                tt(sigtm[:, t_, :], ps[:, 0:256], w0bc[:, sw * 256:(sw + 1) * 256], ALU.add, [pn, "w0bc"], ["sigtm"])
                yield
            act(sigtm[:], sigtm[:], AF.Sigmoid, ["sigtm"], ["sigtm"])
            for which, dst, dn in [(0, cumT, "cumT"), (1, cumpT, "cumpT")]:
                ps = PB(2 + which)
                pn = PBN(2 + which)
                for ct in range(2):
                    for t_ in range(2):
                        mm(ps[:, ct * 256 + t_ * 128: ct * 256 + (t_ + 1) * 128], sigtm[:, t_, ct * 128:(ct + 1) * 128],
                           tri[:, 2 * sw + which, :], ["sigtm", "tri"], [pn])
                cp(dst[:].rearrange("p k n -> p (k n)"), ps[:, :], [pn], [dn])
                yield
            act(Ep[:], cumT[:], AF.Exp, ["cumT"], ["Ep"], scale=-DECAY_C)
            act(Em[:], cumT[:], AF.Exp, ["cumT"], ["Em"], scale=DECAY_C)
            act(Epv[:], cumpT[:], AF.Exp, ["cumpT"], ["Epv"], scale=-DECAY_C)
            yield
            for ct in range(2):
                cs_ = slice(ct * 128, (ct + 1) * 128)
                ps = PB(2 + ct)
                pn = PBN(2 + ct)
                mm(ps[:, 0:256], aup_s[dp, cs_], adb[dp, :], ["aup", "adb"], [pn])
                act(aT[:, ct, :], ps[:, 0:256], AF.Sigmoid, [pn, "colvs"], ["aT"], bias=colvs[:, ct, 5 + sw:6 + sw])
                yield
                ts(tmpA[:, ct, :], rwT[:, ct, :], colvs[:, ct, 0:1], None, ALU.mult, None, ["rwT", "colvs"], ["tmpA"])
                tt(tmpB[:, ct, :], tmpA[:, ct, :], tmpA[:, ct, :], ALU.mult, ["tmpA"], ["tmpB"])
                mm(ps[:, 256:512], bones[:, :], tmpB[:, ct, :], ["bones", "tmpB"], [pn])
                rsq(tmpB[:, ct, :], ps[:, 256:512], 1.0, 1e-12, [pn], ["tmpB"])
                yield
                tt(kkT[:, ct, :], tmpA[:, ct, :], tmpB[:, ct, :], ALU.mult, ["tmpA", "tmpB"], ["kkT"])
                stt(ART[:, ct, :, 0:64], v4(kkT[:, ct, :]), -1.0, v4(Epv[:, ct, :]), ALU.mult, ALU.mult, ["kkT", "Epv"], ["ART"])
                tt(ART[:, ct, :, 64:128], v4(rwT[:, 4 + ct, :]), v4(Ep[:, ct, :]), ALU.mult, ["rwT", "Ep"], ["ART"])
                gcb = v4(Ep[:, ct, :])[:, :, cend:cend + 1].to_broadcast([128, 4, 64])
                tt(tmpB[:, ct, :], kkT[:, ct, :], aT[:, ct, :], ALU.mult, ["kkT", "aT"], ["tmpB"])
                tt(BhT[:, ct, :], tmpB[:, ct, :], Em[:, ct, :], ALU.mult, ["tmpB", "Em"], ["BhT"])
                cp(BKT[:, ct, :, 0:64], v4(BhT[:, ct, :]), ["BhT"], ["BKT"])
                tt(v4(BhT[:, ct, :]), v4(BhT[:, ct, :]), gcb, ALU.mult, ["BhT", "Ep"], ["BhT"])
                ts(tmpA[:, ct, :], aT[:, ct, :], colvs[:, ct, 1:2], colvs[:, ct, 7:8], ALU.mult, ALU.add, ["aT", "colvs"], ["tmpA"])
                tt(kdT[:, ct, :], rwT[:, ct, :], tmpA[:, ct, :], ALU.mult, ["rwT", "tmpA"], ["kdT"])
                tt(KhT[:, ct, :], kdT[:, ct, :], Em[:, ct, :], ALU.mult, ["kdT", "Em"], ["KhT"])
                cp(BKT[:, ct, :, 64:128], v4(KhT[:, ct, :]), ["KhT"], ["BKT"])
                tt(v4(KhT[:, ct, :]), v4(KhT[:, ct, :]), gcb, ALU.mult, ["KhT", "Ep"], ["KhT"])
                yield
            if dbg and stop_after == 2.1:
                return
            for c in range(4):
                for ct in range(2):
                    ps = PB(2 + ct)
                    pn = PBN(2 + ct)
                    for m_, (src, sn) in enumerate([(BhT[:, ct, :], "BhT"), (KhT[:, ct, :], "KhT"), (rwT[:, 2 + ct, :], "rwT")]):
                        trp(ps[0:64, m_ * 128:(m_ + 1) * 128], src[:, c * 64:(c + 1) * 64], ident[:, :], [sn, "ident"], [pn])
                    pv = ps[0:64, 0:384].rearrange("p (m n) -> p m n", m=3)
                    for hh in range(2):
                        cp(BKV[0:64, c, :, ct, hh, hh * 64:(hh + 1) * 64], pv[:, :, hh * 64:(hh + 1) * 64], [pn], ["BKV"],
                           eng=("act" if hh == 0 else "dve"))
                yield
            if dbg and stop_after == 2.2:
                return
            cols = slice(bi * BLK, (bi + 1) * BLK)

            def ret_steps():
                cols = slice(bi * BLK, (bi + 1) * BLK)
                if lat:
                    load(rc_t, c_ropec[:, cols], "ropek", ["rope"])
                    load(rs_t, c_ropes[:, cols], "ropek", ["rope"])
                    for hr in range(2):
                        for isk, src_t in [(True, retT[:, hr, :]), (False, retT[:, 4 + hr, :])]:
                            ps = PB(2 + (0 if isk else 1))
                            pn = PBN(2 + (0 if isk else 1))
                            mm(ps[:, 0:256], pm[:, :], src_t, ["pm", "retT"], [pn])
                            tt(tmpA[:, 0, :], ps[:, 0:256], rs_t, ALU.mult, [pn, "rope"], ["tmpA"])
                            tt(tmpB[:, 0, :], src_t, rc_t, ALU.mult, ["retT", "rope"], ["tmpB"])
                            if isk:
                                tt(kr[:, hr, :], tmpA[:, 0, :], tmpB[:, 0, :], ALU.add, ["tmpA", "tmpB"], ["kr"])
                            else:
                                tt(qr[:, hr, :], tmpA[:, 0, :], tmpB[:, 0, :], ALU.add, ["tmpA", "tmpB"], ["qr"])
                        yield
                else:
                    for hr in range(2):
                        cp(kr[:, hr, :], retT[:, hr, :], ["retT"], ["kr"])
                if dbg and stop_after == 2.61:
                    return
                for c2 in range(2):
                    cc = slice(c2 * 128, (c2 + 1) * 128)
                    for hr in range(2):
                        ps = PB(2 + hr)
                        pn = PBN(2 + hr)
                        cp(vrb[:, hr, cc], retT[:, 2 + hr, cc], ["retT"], ["vrb"])
                        mm(ps[:, 0:128], vrb[:, hr, cc], identh[:, :], ["vrb", "identh"], [pn])
                        mm(ps[:, 128:256], kr[:, hr, cc], identh[:, :], ["kr", "identh"], [pn])
                        import os as _os
                        cp(vtm[:, c2, hr, :], ps[:, 0:128], [pn], ["vtm"], eng="dve")
                        if True:
                            ts(ktm[:, c2, hr, :], ps[:, 128:256], ksc[:, hr * 2 + sw:hr * 2 + sw + 1], None, ALU.mult, None, [pn, "ksc"], ["ktm"])
                        yield
                if dbg and stop_after == 2.62:
                    return
                for c2 in (range(2) if sw == 0 else (1, 0)):
                    cc = slice(c2 * 128, (c2 + 1) * 128)
                    for hr in range(2):
                        hc = slice(hr * 128, (hr + 1) * 128)
                        if ro:
                            mm(PB(2)[:, hc], kr[:, hr, cc], qr[:, hr, cc], ["kr", "qr"], [PBN(2)])
                            tt(Ssb[:, hr, :], PB(2)[:, hc], dmask[:, hr * 2 + sw, :], ALU.mult, [PBN(2), "dmask"], ["Ssb"])
                            tt(qtl[:, hr, cc], qr[:, hr, cc], qsc[:, hr * 2 + sw, :], ALU.mult, ["qr", "qsc"], ["qtl"])
                            yield
                            mm(PB(3)[:, hc], vtm[:, c2, hr, :], Ssb[:, hr, :], ["vtm", "Ssb"], [PBN(3)], start=True, stop=False)
                            mm(PB(3)[:, hc], Sbf[:, hr, :], qtl[:, hr, cc], ["Sbf", "qtl"], [PBN(3)], start=False, stop=True)
                            cp(yrT[:, hr, cc], PB(3)[:, hc], [PBN(3)], ["yrT"])
                            yield
                        mm(PB(0)[:, hc], ktm[:, c2, hr, :], vtm[:, c2, hr, :], ["ktm", "vtm"], [PBN(0)])
                        stt(Sst[:, hr, :], Sst[:, hr, :], gcs[:, hr:hr + 1], PB(0)[:, hc], ALU.mult, ALU.add, ["Sst", "gcs", PBN(0)], ["Sst"])
                        cp(Sbf[:, hr, :], Sst[:, hr, :], ["Sst"], ["Sbf"])
                        yield
                yield

            rg_ = ret_steps()

            def rstep():
                try:
                    next(rg_)
                except StopIteration:
                    pass
            chunk_order = list(range(4)) if sw == 0 else [3, 2, 1, 0]
            for pr in range(2):
              pair = chunk_order[2 * pr:2 * pr + 2]
              for ci, c in enumerate(pair):
                for h in range(4):
                    ct, hh = h // 2, h % 2
                    hp = slice(hh * 64, hh * 64 + 64)
                    pb = PB(4 + hh)
                    mm(pb[0:64, (ct * 2) * 128:(ct * 2 + 1) * 128], BKT[hp, ct, c, 0:64], ART[hp, ct, c, :], ["BKT", "ART"], [PBN(4 + hh)])
                    mm(pb[0:64, (ct * 2 + 1) * 128:(ct * 2 + 2) * 128], BKT[hp, ct, c, 64:128], ART[hp, ct, c, :], ["BKT", "ART"], [PBN(4 + hh)])
                    mm(PB(2 + hh)[0:64, 256 + ci * 128 + ct * 64:256 + ci * 128 + (ct + 1) * 64], ART[hp, ct, c, 0:64], BKT[hp, ct, c, 0:64], ["BKT", "ART"], [PBN(2 + hh)])
                for hh in range(2):
                    tt(Ms2[0:64, ci, 2 * hh:2 * hh + 2, :, :].rearrange("p h m n -> p (h m) n"),
                       PB(4 + hh)[0:64, :].rearrange("p (a n) -> p a n", a=4),
                       mst[:, sw, :].unsqueeze(1).to_broadcast([64, 4, 128]), ALU.mult, [PBN(4 + hh), "mst"], ["Ms"])
                yield
              for hh in range(2):
                tt(M1s2[0:64, :, 2 * hh:2 * hh + 2, :], PB(2 + hh)[0:64, 256:512].rearrange("p (c a n) -> p c a n", c=2, a=2),
                   m1m[:, sw, :].unsqueeze(1).unsqueeze(1).to_broadcast([64, 2, 2, 64]), ALU.mult, [PBN(2 + hh), "m1m"], ["M1s"])
                yield
              identb = ident[0:64, 0:64].unsqueeze(1).unsqueeze(1).to_broadcast([64, 2, 4, 64])
              tt(Qs[0][0:64], Ms2[0:64, :, :, 0, 0:64], identb, ALU.add, ["Ms", "ident"], ["Qs0"])
              pN = [PB(6 + ci)[0:64, :].rearrange("p (h m n) -> p h m n", h=4, m=2) for ci in range(2)]
              pQ = PB(3)[0:64, :].rearrange("p (c h n) -> p c h n", c=2, h=4)
              for k in range(1, 6):
                cur, prv = k % 2, (k - 1) % 2
                for ci in range(2):
                    for h in range(4):
                        if k == 1:
                            Pp, Ppp = Ms2[0:64, ci, h, 0, 0:64], M1s2[0:64, ci, h, :]
                            rn_ = ["Ms", "M1s"]
                        else:
                            Pp, Ppp = Pa[prv][0:64, ci, h, 0, :], Pa[prv][0:64, ci, h, 1, :]
                            rn_ = ["Pa%d" % prv]
                        mm(pN[ci][:, h, 0, :], Ppp, Pp, rn_, [PBN(6 + ci)])
                        mm(pN[ci][:, h, 1, :], Pp, Ppp, rn_, [PBN(6 + ci)])
                for ci in range(2):
                    cp(Pa[cur][0:64, ci].rearrange("p h m n -> p (h m n)"), PB(6 + ci)[0:64, :], [PBN(6 + ci)], ["Pa%d" % cur],
                       eng=("act" if ci == 0 else "dve"))
                    yield
                for ci in range(2):
                    for h in range(4):
                        mm(pQ[:, ci, h, :], Pa[cur][0:64, ci, h, 1, :], Qs[prv][0:64, ci, h, :], ["Pa%d" % cur, "Qs%d" % prv], [PBN(3)])
                tt(Qs[cur][0:64], pQ, Qs[prv][0:64], ALU.add, [PBN(3), "Qs%d" % prv], ["Qs%d" % cur])
                yield
                rstep()
                yield
              for ci, c in enumerate(pair):
                Ms = Ms2[:, ci]
                TT = Qs[1][:, ci]
                pW = PB(2)[0:64, 0:256].rearrange("p (k n) -> p k n", k=2)
                for ct in range(2):
                    mm(pW[:, ct, :], ART[:, ct, c, 0:64], Hb[:, ct, :], ["ART", "Hb"], [PBN(2)], start=True, stop=False)
                    for hh in range(2):
                        mm(pW[:, ct, :], Ms[0:64, hh * 2 + ct, 1, 0:64], BKV[0:64, c, 2, ct, hh, :], ["Ms", "BKV"], [PBN(2)],
                           start=False, stop=(hh == 1))
                cp(Wsb[0:64, :, :], pW, [PBN(2)], ["Wsb"])
                yield
                rstep()
                yield
                pU = PB(3)[0:64, 0:256].rearrange("p (k n) -> p k n", k=2)
                for ct in range(2):
                    for hh in range(2):
                        mm(pU[:, ct, hh * 64:(hh + 1) * 64], TT[0:64, hh * 2 + ct, :], Wsb[0:64, ct, hh * 64:(hh + 1) * 64],
                           ["Qs1", "Wsb"], [PBN(3)])
                for hh in range(2):
                    cp(Usb[0:64, :, hh, hh * 64:(hh + 1) * 64], pU[:, :, hh * 64:(hh + 1) * 64], [PBN(3)], ["Usb"],
                       eng=("act" if hh == 0 else "dve"))
                    yield
                if ro:
                    pY = PB(0)[:, 0:128].rearrange("p (k n) -> p k n", k=2)
                    for ct in range(2):
                        mm(pY[:, ct, :], Hb[:, ct, :], ART[:, ct, c, 64:128], ["Hb", "ART"], [PBN(0)], start=True, stop=False)
                        for hh in range(2):
                            mm(pY[:, ct, :], Usb[0:64, ct, hh, :], Ms[0:64, hh * 2 + ct, 0, 64:128], ["Usb", "Ms"], [PBN(0)],
                               start=False, stop=False)
                            mm(pY[:, ct, :], BKV[0:64, c, 2, ct, hh, :], Ms[0:64, hh * 2 + ct, 1, 64:128], ["BKV", "Ms"], [PBN(0)],
                               start=False, stop=(hh == 1))
                    cp(yT[:, :, c * 64:(c + 1) * 64], pY, [PBN(0)], ["yT"])
                    yield
                pH = PB(1)[:, 0:256].rearrange("p (k n) -> p k n", k=2)
                for ct in range(2):
                    for hh in range(2):
                        mm(pH[:, ct, :], BKV[0:64, c, 0, ct, hh, :], Usb[0:64, ct, hh, :], ["BKV", "Usb"], [PBN(1)],
                           start=(hh == 0), stop=False)
                        mm(pH[:, ct, :], BKV[0:64, c, 1, ct, hh, :], BKV[0:64, c, 2, ct, hh, :], ["BKV"], [PBN(1)],
                           start=False, stop=(hh == 1))
                    gcol = Ep[:, ct, c * 64 + cend: c * 64 + cend + 1]
                    stt(Hst[:, ct, :], Hst[:, ct, :], gcol, pH[:, ct, :], ALU.mult, ALU.add, ["Hst", "Ep", PBN(1)], ["Hst"])
                    yield
                cp(Hb[:], Hst[:], ["Hst"], ["Hb"])
                rstep()
                yield
            if dbg and stop_after == 2 and seq == "l" and bi == 0 and sw == 0:
                return
            if dbg and stop_after == 2.5:
                return
            for _ in rg_:
                yield
            if dbg and stop_after == 2.6:
                return
            if not lat:
                return
            do_post = (bi >= 8) if sw == 0 else (bi <= 7)
            ys_own = (yspill_f if sw == 0 else yspill_b)[:, :, cols].rearrange("k p n -> p k n")
            ys_oth = (yspill_b if sw == 0 else yspill_f)[:, :, cols].rearrange("k p n -> p k n")
            own_n = ("YSf%d" if sw == 0 else "YSb%d") % bi
            oth_n = ("YSb%d" if sw == 0 else "YSf%d") % bi
            if not do_post:
                P.dma(dq_default[0], ys_own[:, 0:2, :], yT[:], "yst0", reads=["yT"], writes=[own_n + "a"])
                P.dma(dq_default[0], ys_own[:, 2:4, :], yrT[:], "yst1", reads=["yrT"], writes=[own_n + "b"])
                return
            assert (oth_n + "a") in P.last_write and (oth_n + "b") in P.last_write, oth_n
            load(ysp[:], ys_oth, "yld", ["ysp"], R=[oth_n + "a", oth_n + "b"])
            act(gsb[:, 0, :], rwT[:, 8, :], AF.Sigmoid, ["rwT"], ["gsb"])
            act(gsb[0:32, 1, :], rwT[0:32, 9, :], AF.Sigmoid, ["rwT"], ["gsb"])
            od = slice(64, 128) if sw == 0 else slice(0, 64)
            osw = 1 - sw
            for ct in range(2):
                cs_ = slice(ct * 128, (ct + 1) * 128)
                ps = PB(2 + ct)
                pn = PBN(2 + ct)
                y = cumT[:, ct, :]
                sq = cumpT[:, ct, :]
                bn = Ep[:, ct, :]
                tt(y, yT[:, ct, :], ysp[:, ct, :], ALU.add, ["yT", "ysp"], ["cumT"])
                mm(ps[:, 0:256], bones[:, :], y, ["bones", "cumT"], [pn])
                stt(y, ps[:, 0:256], -1.0 / 64.0, y, ALU.mult, ALU.add, [pn, "cumT"], ["cumT"])
                tt(sq, y, y, ALU.mult, ["cumT"], ["cumpT"])
                mm(ps[:, 256:512], bones[:, :], sq, ["bones", "cumpT"], [pn])
                rsq(sq, ps[:, 256:512], 1.0 / 64.0, LNX_EPS, [pn], ["cumpT"])
                tt(y, y, sq, ALU.mult, ["cumT", "cumpT"], ["cumT"])
                ts(y, y, colvs[:, ct, 3:4], colvs[:, ct, 4:5], ALU.mult, ALU.add, ["cumT", "colvs"], ["cumT"])
                mm(ps[:, 0:256], aup_s[od, cs_], adb[od, :], ["aup", "adb"], [pn])
                act(bn, ps[:, 0:256], AF.Sigmoid, [pn, "colvs"], ["Ep"], bias=colvs[:, ct, 5 + osw:6 + osw])
                ts(bn, bn, colvs[:, ct, 1:2], colvs[:, ct, 7:8], ALU.mult, ALU.add, ["Ep", "colvs"], ["Ep"])
                tt(bn, rwT[:, ct, :], bn, ALU.mult, ["rwT", "Ep"], ["Ep"])
                tt(bn, bn, kdT[:, ct, :], ALU.add, ["Ep", "kdT"], ["Ep"])
                stt(bn, rwT[:, 4 + ct, :], colvs[:, ct, 2:3], bn, ALU.mult, ALU.mult, ["rwT", "colvs", "Ep"], ["Ep"])
                mm(ps[:, 256:512], bones[:, :], bn, ["bones", "Ep"], [pn])
                tt(bn, ps[:, 256:512], rwT[:, 2 + ct, :], ALU.mult, [pn, "rwT"], ["Ep"])
                tt(y, y, bn, ALU.add, ["cumT", "Ep"], ["cumT"])
                mm(ps[:, 0:256], gup_a[:, cs_], gsb[:, 0, :], ["gupa", "gsb"], [pn], start=True, stop=False)
                mm(ps[:, 0:256], gup_b[0:32, cs_], gsb[0:32, 1, :], ["gupb", "gsb"], [pn], start=False, stop=True)
                tt(oT[:, ct, :], y, ps[:, 0:256], ALU.mult, ["cumT", pn], ["BhT"])
            for hr in range(2):
                ps = PB(2 + hr)
                pn = PBN(2 + hr)
                y = Em[:, hr, :]
                sq = Epv[:, hr, :]
                tt(y, yrT[:, hr, :], ysp[:, 2 + hr, :], ALU.add, ["yrT", "ysp"], ["Em"])
                tt(sq, y, y, ALU.mult, ["Em"], ["Epv"])
                mm(ps[:, 0:256], ones[:, :], sq, ["ones", "Epv"], [pn])
                rsq(sq, ps[:, 0:256], 1.0 / 128.0, EPS, [pn], ["Epv"])
                tt(y, y, sq, ALU.mult, ["Em", "Epv"], ["Em"])
                act(sq, retT[:, 6 + hr, :], AF.Silu, ["retT"], ["Epv"])
                tt(oT[:, 2 + hr, :], y, sq, ALU.mult, ["Em", "Epv"], ["BhT"])
            if dbg and stop_after == 3:
                return
            P.dma(dq_default[0], osend.ap()[:, cols].rearrange("(k p) n -> p k n", p=128), oT[:], "ost", reads=["BhT"], writes=["osend"])
            return


        return front, block, init, tsize

    TNAMES = ['rwT', 'retT', 'stat', 'thb', 'adb', 'gsb', 'sigtm', 'cumT', 'cumpT', 'aT', 'kkT', 'tmpA', 'tmpB', 'Ep', 'Em', 'Epv', 'kdT', 'BhT', 'KhT', 'yT', 'ART', 'BKT', 'BKV', 'Ms', 'M1s', 'Pa0', 'Pa1', 'Qs0', 'Qs1', 'Wsb', 'Usb', 'Hst', 'Hb', 'qr', 'kr', 'qtl', 'rope', 'vtm', 'vrb', 'ktm', 'Ssb', 'Sst', 'Sbf', 'yrT', 'ysp']
    FA = Arena()
    winb = FA.bf16(16 * NCOLS).rearrange("p (k n) -> p k n", k=16)
    xts = [FA.f32(D) for _ in range(2)]
    xmT = FA.bf16(16 * NWIN).rearrange("p (k n) -> p k n", k=16)
    TA1 = Arena()
    f1, b1, i1, tsz = make_thread(1, TA1)
    TA0 = Arena()
    TA0.off = max(FA.off, tsz)
    f0, b0, i0, _ = make_thread(0, TA0)
    print("arena use", FA.off, tsz, TA0.off)
    for nm in ["winb", "xt0", "xt1", "xmT"]:
        P.alias(nm, ph0_res)
    P.dma("pool", winb, win, "constp", writes=["winb"])
    ts(colvs[:, :, 7], colvs[:, :, 1], -1.0, 1.0, ALU.mult, ALU.add, ["colvs"], ["colvs"])
    P.shared = set(P.all_res()) | {"winb", "osend", "colvs"} | {"PS%d" % i for i in range(8)}
    for i_ in range(17):
        P.shared |= {"pspA%d" % i_, "pspB%d" % i_, "pspC%d" % i_}
    for i_ in range(16):
        P.shared |= {"YSf%da" % i_, "YSf%db" % i_, "YSb%da" % i_, "YSb%db" % i_}
    front_local = ["xt0", "xt1", "xmT"]
    P.sfx = "_t0"
    for nm in TNAMES + front_local:
        P.alias(nm, ph0_res)
    i0()
    f0("c", 0, 1, alt=False)
    for bi_ in range(16):
        f0("l", bi_, 16, alt=(bi_ % 2 == 0))
    for nm in ["cumT", "cumpT", "aT", "kkT", "tmpA", "tmpB", "Ep", "Em", "Epv", "kdT"]:
        P.alias(nm, ["rwTalt_t0", "retTalt_t0"])
    P.sfx = "_t1"
    for nm in TNAMES:
        P.alias(nm, ph0_res + ["winb", "xt0_t0", "xt1_t0", "xmT_t0"])
    i1()

    def sweep_gen(sw, blk):
        order = [("c", 0, 1)] + [("l", bi, 16) for bi in (range(16) if sw == 0 else range(15, -1, -1))]
        for (seq, bi, nb) in order:
            yield from blk(sw, seq, bi, nb)

    gens = [("_t0", sweep_gen(0, b0)), ("_t1", sweep_gen(1, b1))]
    while gens:
        for item in list(gens):
            P.sfx = item[0]
            dq_default[0] = "sp" if item[0] == "_t0" else "pool"
            try:
                next(item[1])
            except StopIteration:
                gens.remove(item)
    P.sfx = ""
    dq_default[0] = "sp"
    if dbg and stop_after == 3:
        dt16 = arena[:, 0:512].bitcast(BF16).rearrange("p (k n) -> p k n", k=4)
        dtile = arena[:, 1024:1024 + 4096]
        allr = P.all_res()
        load(dt16, osend.ap()[:, 15 * BLK:16 * BLK].rearrange("(k p) n -> p k n", p=128), "dbgin", ["dt16"] + allr, R=["osend"])
        memset(dtile[:], 0.0, ["dtile"] + allr)
        cp(dtile[:, 0:1024], dt16.rearrange("p k n -> p (k n)"), ["dt16"], ["dtile"], eng="dve")
        P.dma("sp", dbg_d, dtile[:], "dbgout", reads=["dtile"], writes=["dbg_d"])
        P.emit(es, {"sp": [("dbgout", 16)]})
        es.close()
        return nc

    if stop_after == "noB":
        P.emit(es, {})
        es.close()
        return nc
    if TWO_A:
        P.dma("sp", modcol_o, modcol[:].rearrange("p a k -> p (a k)"), "mco", reads=["modcol"], writes=["modcol_o"])
        P.emit(es, {"sp": [(k_, P.dma_count[k_]) for k_ in P.dma_keys if str(k_).startswith("ost")] + [("mco", 16), ("a12o", 16)]})
        es.close()
        return nc
    if TWO_B:
        P = Prog(nc)
        DBG["P"] = P
        memset(epsc[:, 0:1], EPS, ["epsc"])
        load(ident[:], c_ident, CK, ["ident"])
        load(hmask_s[:], hmask, CK, ["hmask"])
        load(fconv_s[:], fconv, CK, ["fconv"])
        load(modcol[:].rearrange("p a k -> p (a k)"), modcol_i, CK, ["modcol"])
        a12sp = a12_i
    if stop_after != "nocc" and not TWO_B:
      P.custom("pool", lambda e: e.collective_compute("AllGather", ALU.bypass, replica_groups=[[0, 1, 2, 3, 4, 5, 6, 7]],
                                                    ins=[osend.ap().opt()], outs=[ogath.ap().opt()]),
               "cc0", reads=["osend"], writes=["ogath"])
    og = ogath.ap() if stop_after != "nocc" else osend.ap()

    prevA = P.all_res()
    PSN = ["psb%d" % i for i in range(8)]
    for i_ in range(8):
        P.alias("psb%d" % i_, ["PS%d" % i_, "psb%d" % i_])
    HT_W = 9216
    hT = arena[:, 0:HT_W].bitcast(BF16).rearrange("p (k n) -> p k n", k=16)
    B2 = Arena()
    B2.off = HT_W
    oTb = B2.bf16(16 * 1152).rearrange("p (k n) -> p k n", k=16)
    cand = [B2.bf16(8 * 1152).rearrange("p (k n) -> p k n", k=8)]
    woutb = B2.bf16(16 * D).rearrange("p (k n) -> p k n", k=16)
    mixrow = B2.f32(D)
    xtb = B2.f32(D)
    A1row = B2.f32(D)
    sqb = B2.bf16(D)
    statb = B2.f32(16)
    for nm in ["hT", "oTb", "cand0", "woutb", "mixrow", "xtb", "A1row", "sqb", "statb"]:
        P.alias(nm, prevA)
    P.dma("pool", woutb, wout, "woutk", writes=["woutb"])
    load(A1row, a12sp[:, 0:D].partition_broadcast(128), "a1k", ["A1row"], R=["a12sp"])
    NB_ = 2 if stop_after != "nocc" else 1
    if TWO_B:
        load(oTb, oin, "oink", ["oTb"])
    if NB_ == 1:
        memset(cand[0][:, 4:8, :], 0.0, ["cand0"])
    for bq in range(NB_):
        memset(cand[0][:, bq * 4 + 0, 0:64], 0.0, ["cand0"])
        memset(cand[0][:, bq * 4 + 3, 1088:1152], 0.0, ["cand0"])
    for kt in (range(16) if not TWO_B else []):
        cb = cand[0]
        cn = "cand0"
        for bq in range(NB_):
            r0 = (bq * 4 + kt // 4) * 512 + (kt % 4) * 128
            if stop_after == "nocc":
                r0 = (kt % 4) * 128
            load(cb[:, bq * 4 + 0, 64:1152], og[r0:r0 + 128, 0:1088], "candk0", [cn], R=["ogath", "osend"])
            load(cb[:, bq * 4 + 1, :], og[r0:r0 + 128, 960:2112], "candk0", [cn], R=["ogath", "osend"])
            load(cb[:, bq * 4 + 2, :], og[r0:r0 + 128, 1984:3136], "candk0", [cn], R=["ogath", "osend"])
            load(cb[:, bq * 4 + 3, 0:1088], og[r0:r0 + 128, 3008:4096], "candk0", [cn], R=["ogath", "osend"])
        ts(oTb[:, kt, :], cb[:, 0, :], tsel_s[:, 0:1], None, ALU.mult, None, [cn, "tsel"], ["oTb"])
        for g_ in range(1, 8):
            stt(oTb[:, kt, :], cb[:, g_, :], tsel_s[:, g_:g_ + 1], oTb[:, kt, :], ALU.mult, ALU.add, [cn, "tsel", "oTb"], ["oTb"])
    for t_ in range(9):
        tk = slice(t_ * 128, (t_ + 1) * 128)
        load(xtb, xb[tk, :], "xbk", ["xtb"])
        for cch in range(4):
            ps = psb[cch % 2]
            pn = PSN[cch % 2]
            for kt in range(16):
                mm(ps[:, :], oTb[:, kt, tk], woutb[:, kt, cch * 512:(cch + 1) * 512], ["oTb", "woutb"], [pn], start=(kt == 0), stop=(kt == 15))
            cp(mixrow[:, cch * 512:(cch + 1) * 512], ps[:, :], [pn], ["mixrow"])
        act(sqb, mixrow, AF.Square, ["mixrow"], ["sqb", "statb"], accum=statb[:, 0:1])
        rsq(statb[:, 1:2], statb[:, 0:1], 1.0 / D, EPS, ["statb"], ["statb"])
        stt(mixrow, mixrow, statb[:, 1:2], A1row, ALU.mult, ALU.mult, ["mixrow", "statb", "A1row"], ["mixrow"])
        tt(xtb, xtb, mixrow, ALU.add, ["xtb", "mixrow"], ["xtb"])
        if t_ == 0:
            P.dma("sp", x1sp[0:64, :], xtb[64:128, :], "x1st", reads=["xtb"], writes=["x1sp"])
        elif t_ == 8:
            P.dma("sp", x1sp[960:1024, :], xtb[0:64, :], "x1st", reads=["xtb"], writes=["x1sp"])
        else:
            P.dma("sp", x1sp[t_ * 128 - 64:t_ * 128 + 64, :], xtb, "x1st", reads=["xtb"], writes=["x1sp"])
        act(sqb, xtb, AF.Square, ["xtb"], ["sqb", "statb"], accum=statb[:, 2:3])
        rsq(statb[:, 3:4], statb[:, 2:3], 1.0 / D, EPS, ["statb"], ["statb"])
        ts(mixrow, xtb, statb[:, 3:4], None, ALU.mult, None, ["xtb", "statb"], ["mixrow"])
        for k4 in range(4):
            ps = psb[2 + k4 % 2]
            pn = PSN[2 + k4 % 2]
            for kk_ in range(4):
                kt = k4 * 4 + kk_
                trp(ps[:, kk_ * 128:(kk_ + 1) * 128], mixrow[:, kt * 128:(kt + 1) * 128], ident[:, :], ["mixrow", "ident"], [pn])
            for kk_ in range(4):
                kt = k4 * 4 + kk_
                if kk_ % 2 == 0:
                    act(hT[:, kt, tk], ps[:, kk_ * 128:(kk_ + 1) * 128], AF.Identity, [pn, "modcol"], ["hT"],
                        bias=modcol[:, 2, kt:kt + 1], scale=modcol[:, 3, kt:kt + 1])
                else:
                    ts(hT[:, kt, tk], ps[:, kk_ * 128:(kk_ + 1) * 128], modcol[:, 3, kt:kt + 1], modcol[:, 2, kt:kt + 1],
                       ALU.mult, ALU.add, [pn, "modcol"], ["hT"])
    B2res = ["oTb", "cand0", "woutb", "mixrow", "xtb", "A1row", "sqb", "statb"]
    B3 = Arena()
    B3.off = HT_W
    wgb = [B3.bf16(16 * 128).rearrange("p (k n) -> p k n", k=16) for _ in range(2)]
    wub = [B3.bf16(16 * 128).rearrange("p (k n) -> p k n", k=16) for _ in range(2)]
    gts = B3.f32(1152)
    cacc = B3.f32(1024)
    HID_OFF = ARENA_W - 22528
    assert B3.off <= HID_OFF
    hid = arena[:, HID_OFF:ARENA_W].bitcast(BF16).rearrange("p (k n) -> p k n", k=NFT)
    for nm in ["wgb0", "wgb1", "wub0", "wub1", "gts", "cacc", "hid"]:
        P.alias(nm, B2res)
    g3 = gts.rearrange("p (r c) -> p r c", c=64)
    a3 = cacc.rearrange("p (r c) -> p r c", c=64)
    for ft in range(NFT):
        wg, wu = wgb[ft % 2], wub[ft % 2]
        P.dma("pool", wg, wgate[ft], "wgk%d" % (ft % 2), writes=["wgb%d" % (ft % 2)])
        P.dma("pool", wu, wupf[ft], "wuk%d" % (ft % 2), writes=["wub%d" % (ft % 2)])
        for ch in range(3):
            ps = psb[ch]
            for kt in range(16):
                mm(ps[:, 0:384], wg[:, kt, :], hT[:, kt, ch * 384:(ch + 1) * 384], ["wgb%d" % (ft % 2), "hT"], [PSN[ch]],
                   start=(kt == 0), stop=(kt == 15))
            cp(gts[:, ch * 384:(ch + 1) * 384], ps[:, 0:384], [PSN[ch]], ["gts"])
        ts(gts[:, 0:64], gts[:, 0:64], hmask_s[:, 0:1], None, ALU.mult, None, ["gts", "hmask"], ["gts"])
        ts(gts[:, 1088:1152], gts[:, 1088:1152], hmask_s[:, 1:2], None, ALU.mult, None, ["gts", "hmask"], ["gts"])
        ts(a3, g3[:, 1:17, :], fconv_s[:, ft, 4:5], fconv_s[:, ft, 9:10], ALU.mult, ALU.add, ["gts", "fconv"], ["cacc"])
        for dr in range(3):
            for dc in range(3):
                if dr == 1 and dc == 1:
                    continue
                wcol = fconv_s[:, ft, dr * 3 + dc:dr * 3 + dc + 1]
                if dc == 1:
                    o_, i_ = a3, g3[:, dr:dr + 16, :]
                elif dc == 0:
                    o_, i_ = a3[:, :, 1:64], g3[:, dr:dr + 16, 0:63]
                else:
                    o_, i_ = a3[:, :, 0:63], g3[:, dr:dr + 16, 1:64]
                stt(o_, i_, wcol, o_, ALU.mult, ALU.add, ["gts", "fconv", "cacc"], ["cacc"])
        act(cacc, cacc, AF.Silu, ["cacc"], ["cacc"])
        for ch in range(2):
            ps = psb[3 + ch]
            for kt in range(16):
                mm(ps[:, :], wu[:, kt, :], hT[:, kt, 64 + ch * 512:64 + (ch + 1) * 512], ["wub%d" % (ft % 2), "hT"], [PSN[3 + ch]],
                   start=(kt == 0), stop=(kt == 15))
            tt(hid[:, ft, ch * 512:(ch + 1) * 512], ps[:, :], cacc[:, ch * 512:(ch + 1) * 512], ALU.mult, [PSN[3 + ch], "cacc"], ["hid"])
    yTf = arena[:, 0:16384].rearrange("p (k n) -> p k n", k=16)
    B4 = Arena()
    B4.off = 16384
    wdb = [B4.bf16(NFT * 128).rearrange("p (k n) -> p k n", k=NFT) for _ in range(2)]
    assert B4.off <= HID_OFF
    B3res = ["hT", "wgb0", "wgb1", "wub0", "wub1", "gts", "cacc"]
    for nm in ["yTf", "wdb0", "wdb1"]:
        P.alias(nm, B3res)
    for mt in range(16):
        wd = wdb[mt % 2]
        P.dma("pool", wd, wdown[mt], "wdk%d" % (mt % 2), writes=["wdb%d" % (mt % 2)])
        for ch in range(2):
            ps = psb[(mt * 2 + ch) % 4]
            pn = PSN[(mt * 2 + ch) % 4]
            for ft in range(NFT):
                mm(ps[:, :], wd[:, ft, :], hid[:, ft, ch * 512:(ch + 1) * 512], ["wdb%d" % (mt % 2), "hid"], [pn],
                   start=(ft == 0), stop=(ft == NFT - 1))
            cp(yTf[:, mt, ch * 512:(ch + 1) * 512], ps[:, :], [pn], ["yTf"])
    B5 = Arena()
    B5.off = 16384
    ytm = B5.f32(D)
    x1t = [B5.f32(D) for _ in range(2)]
    A2row = B5.f32(D)
    sq5 = B5.bf16(D)
    st5 = B5.f32(8)
    assert B5.off <= HID_OFF
    for nm in ["ytm", "x1t0", "x1t1", "A2row", "sq5", "st5"]:
        P.alias(nm, ["wdb0", "wdb1"])
    load(A2row, a12sp[:, D:2 * D].partition_broadcast(128), "a2k", ["A2row"], R=["a12sp"])
    for t_ in range(8):
        tk = slice(t_ * 128, (t_ + 1) * 128)
        xt_ = x1t[t_ % 2]
        xn = "x1t%d" % (t_ % 2)
        load(xt_, x1sp[tk, :], "x1ld%d" % (t_ % 2), [xn], R=["x1sp"])
        for k4 in range(4):
            ps = psb[4 + k4 % 2]
            pn = PSN[4 + k4 % 2]
            for kk_ in range(4):
                mt = k4 * 4 + kk_
                trp(ps[:, kk_ * 128:(kk_ + 1) * 128], yTf[:, mt, tk], ident[:, :], ["yTf", "ident"], [pn])
            cp(ytm[:, k4 * 512:(k4 + 1) * 512], ps[:, :], [pn], ["ytm"], eng=("act" if k4 % 2 == 0 else "dve"))
        act(sq5, ytm, AF.Square, ["ytm"], ["sq5", "st5"], accum=st5[:, 0:1])
        rsq(st5[:, 1:2], st5[:, 0:1], 1.0 / D, EPS, ["st5"], ["st5"])
        stt(ytm, ytm, st5[:, 1:2], A2row, ALU.mult, ALU.mult, ["ytm", "st5", "A2row"], ["ytm"])
        tt(xt_, xt_, ytm, ALU.add, [xn, "ytm"], [xn])
        P.dma("sp", out_d[tk, :], xt_, "outk%d" % (t_ % 2), reads=[xn], writes=["out_d"])
    P.emit(es, {"sp": [("outk0", P.dma_count["outk0"]), ("outk1", P.dma_count["outk1"])]})
    es.close()
    return nc


def _prep(inp):
    f = lambda a: np.ascontiguousarray(np.asarray(a, dtype=np.float32))
    x, c, ctx, c_ctx = f(inp["x"]), f(inp["c"]), f(inp["ctx"]), f(inp["c_ctx"])
    w_in = f(inp["w_in"][0])
    RW = 1024
    shared = {}
    shared["wada"] = np.ascontiguousarray(
        f(inp["w_ada"][0]).reshape(16, 128, 24, 512).transpose(2, 1, 0, 3))
    shared["bada"] = f(inp["b_ada"][0]).reshape(1, 12288)
    shared["nrm_col"] = np.ascontiguousarray(
        np.stack([_col(f(inp["norm_pre_mix"][0])), _col(f(inp["norm_pre_ffn"][0]))], axis=1))
    shared["nrm_row"] = np.stack([f(inp["norm_post_mix"][0]), f(inp["norm_post_ffn"][0])], axis=0)
    shared["wgate"] = np.ascontiguousarray(
        f(inp["ffn_w_gate"][0]).reshape(16, 128, NFT, 128).transpose(2, 1, 0, 3))
    shared["wupf"] = np.ascontiguousarray(
        f(inp["ffn_w_up"][0]).reshape(16, 128, NFT, 128).transpose(2, 1, 0, 3))
    shared["wdown"] = np.ascontiguousarray(
        f(inp["ffn_w_down"][0]).reshape(NFT, 128, 16, 128).transpose(2, 1, 0, 3))
    fc = np.concatenate([f(inp["ffn_conv"][0]).reshape(9, DFF), f(inp["ffn_conv_b"][0]).reshape(1, DFF)], axis=0)
    shared["fconv"] = np.ascontiguousarray(fc.reshape(10, NFT, 128).transpose(2, 1, 0))
    w_out = f(inp["w_out"][0])
    maps = []
    for core in range(8):
        b, g = core // 4, core % 4
        m = dict(shared)
        xa = np.zeros((SEQ + CTX + 4, D), np.float32)
        xa[1:257] = ctx[b]
        xa[259:259 + SEQ] = x[b]
        m["xa"] = xa
        xbm = np.zeros((1152, D), np.float32)
        lo, hi = g * 1024 - 64, g * 1024 + 1088
        slo, shi = max(lo, 0), min(hi, SEQ)
        xbm[slo - lo: shi - lo] = x[b, slo:shi]
        m["xb"] = xbm
        cv = np.stack([c[b], c_ctx], axis=1)
        m["cvt"] = np.ascontiguousarray(cv.reshape(16, 128, 2).transpose(1, 0, 2))
        cs = slice(g * 256, (g + 1) * 256)
        cols = np.concatenate([
            np.arange(0, RW)[cs], np.arange(RW, 2 * RW)[cs],
            np.arange(2304, 2304 + RW)[cs],
            np.arange(2048, 2304),
            np.arange(3328, 3488),
            3488 + np.arange(0, RW)[cs], 3488 + np.arange(RW, 2 * RW)[cs],
            3488 + np.arange(2 * RW, 3 * RW)[cs], 3488 + np.arange(3 * RW, 4 * RW)[cs]])
        assert cols.size == NCOLS
        m["win"] = _ktile(w_in[:, cols])
        rc = f(inp["rw_conv"][0])[:, cols[:1184]]
        rcp = np.zeros((3, 1280), np.float32)
        rcp[:, :1184] = rc
        m["rwconv"] = np.ascontiguousarray(rcp.reshape(3, 10, 128).transpose(2, 1, 0))
        cvs = np.zeros((128, 2, 9), np.float32)
        hs = slice(g * 256, (g + 1) * 256)
        vecs = [f(inp["rw_k_k"][0])[hs], f(inp["rw_k_a"][0])[hs], f(inp["rw_r_k"][0]).reshape(-1)[hs],
                f(inp["rw_lnx_w"][0])[hs], f(inp["rw_lnx_b"][0])[hs],
                f(inp["rw_a0"][0])[0, hs], f(inp["rw_a0"][0])[1, hs]]
        for vi, v in enumerate(vecs):
            cvs[:, :, vi] = v.reshape(2, 128).T
        m["colv"] = cvs
        m["w0row"] = np.concatenate([f(inp["rw_w0"][0])[0, hs], f(inp["rw_w0"][0])[1, hs]]).reshape(1, 512)
        m["wup"] = np.concatenate([f(inp["rw_w_up"][0])[0][:, hs], f(inp["rw_w_up"][0])[1][:, hs]], axis=0)
        m["aup"] = np.concatenate([f(inp["rw_a_up"][0])[0][:, hs], f(inp["rw_a_up"][0])[1][:, hs]], axis=0)
        m["gup"] = np.ascontiguousarray(f(inp["rw_g_up"][0])[:, hs])
        rows = np.concatenate([np.concatenate([np.arange(gp * 256, (gp + 1) * 256),
                                               1024 + np.arange(gp * 256, (gp + 1) * 256)]) for gp in range(4)])
        m["wout"] = _ktile(w_out[rows])
        hm = np.zeros((128, 2), np.float32)
        hm[:, 0] = 1.0 if g > 0 else 0.0
        hm[:, 1] = 1.0 if g < 3 else 0.0
        m["hmask"] = hm
        tsl = np.zeros((128, 8), np.float32)
        tsl[:, b * 4 + g] = 1.0
        m["tsel"] = tsl
        for k, v in _consts(g).items():
            m["c_" + k] = v
        maps.append(m)
    return maps


_NC_CACHE = {}
FUSED = False


def kernel(**inputs):
    import ml_dtypes
    maps = _prep(inputs)
    out = np.zeros((2, SEQ, D), np.float32)
    if FUSED:
        if "nc" not in _NC_CACHE:
            _NC_CACHE["nc"] = build()
        res = run_bass_kernel_spmd(_NC_CACHE["nc"], maps, core_ids=list(range(8)))
        for core in range(8):
            b, g = core // 4, core % 4
            out[b, g * 1024:(g + 1) * 1024] = np.asarray(res.results[core]["out"], dtype=np.float32)
        return out
    if "ncA" not in _NC_CACHE:
        _NC_CACHE["ncA"] = build(stop_after="A2L")
        _NC_CACHE["ncB"] = build(stop_after="B2L")
    dummy = {"oin": np.zeros((128, 16, 1152), ml_dtypes.bfloat16), "modcol_i": np.zeros((128, 96), np.float32),
             "a12_i": np.zeros((1, 2 * D), np.float32)}
    resA = run_bass_kernel_spmd(_NC_CACHE["ncA"], maps, core_ids=list(range(8)))
    osend = [np.asarray(resA.results[c]["osend"]) for c in range(8)]
    mapsB = []
    for core in range(8):
        b, g = core // 4, core % 4
        m = dict(maps[core])
        oin = np.zeros((128, 16, 1152), osend[0].dtype)
        lo, hi = g * 1024 - 64, g * 1024 + 1088
        slo, shi = max(lo, 0), min(hi, SEQ)
        for kt in range(16):
            src = osend[4 * b + kt // 4]
            j = kt % 4
            oin[:, kt, slo - lo:shi - lo] = src[j * 128:(j + 1) * 128, slo:shi]
        m["oin"] = oin
        m["modcol_i"] = np.asarray(resA.results[core]["modcol_o"])
        m["a12_i"] = np.asarray(resA.results[core]["a12_o"])
        mapsB.append(m)
    resB = run_bass_kernel_spmd(_NC_CACHE["ncB"], mapsB, core_ids=list(range(8)))
    for core in range(8):
        b, g = core // 4, core % 4
        out[b, g * 1024:(g + 1) * 1024] = np.asarray(resB.results[core]["out"], dtype=np.float32)
    return out
```

```python
import numpy as np
from contextlib import ExitStack
import concourse.bass as bass
import concourse.mybir as mybir
from concourse.bass_utils import run_bass_kernel_spmd

F32 = mybir.dt.float32
BF16 = mybir.dt.bfloat16
AF = mybir.ActivationFunctionType
ALU = mybir.AluOpType
AX = mybir.AxisListType

D = 2048
SEQ = 4096
CTX = 256
DFF = 5632
NFT = 44
EPS = 1e-6
LNX_EPS = 64e-5
BLK = 256
NWIN = 258
C = 64
NCOLS = 2208
DECAY_C = float(np.exp(-0.5))

DBG = {}


class Ev:
    __slots__ = ("kind", "eng", "idx", "key", "val", "needed")

    def __init__(self, kind, eng=None, idx=0, key=None, val=0):
        self.kind, self.eng, self.idx, self.key, self.val, self.needed = kind, eng, idx, key, val, False


class Prog:
    ENGS = ["pe", "act", "dve", "pool", "sp"]

    def __init__(self, nc):
        self.nc = nc
        self.stream = {e: [] for e in self.ENGS}
        self.last_write = {}
        self.readers = {}
        self.waited = {e: {} for e in self.ENGS}
        self.dma_count = {}
        self.dma_keys = []
        self.last_ev = {}
        self.trace_lines = False
        self.lines = {}
        self.imap = {}
        self.sfx = ""
        self.shared = set()

    def _m(self, names):
        if not self.sfx:
            return list(names)
        return [n if n in self.shared else n + self.sfx for n in names]

    def _deps(self, eng, reads, writes):
        evs = []
        for r in reads:
            w = self.last_write.get(r)
            if w is not None:
                evs.append(w)
        for r in writes:
            w = self.last_write.get(r)
            if w is not None:
                evs.append(w)
            evs.extend(self.readers.get(r, ()))
        best = {}
        for ev in evs:
            if ev.kind == "eng":
                if ev.eng == eng and eng == "pe":
                    continue
                k = ("eng", ev.eng)
                if k not in best or best[k].idx < ev.idx:
                    best[k] = ev
            else:
                k = ("dma", ev.key)
                if k not in best or best[k].val < ev.val:
                    best[k] = ev
        out = []
        for k, ev in best.items():
            cur = self.waited[eng].get(k, -1)
            v = ev.idx if ev.kind == "eng" else ev.val
            if cur >= v:
                continue
            self.waited[eng][k] = v
            ev.needed = True
            out.append(ev)
        return out

    def _commit(self, ev, reads, writes):
        for r in reads:
            self.readers.setdefault(r, []).append(ev)
        for r in writes:
            self.last_write[r] = ev
            self.readers[r] = []

    def op(self, eng, fn, reads=(), writes=()):
        reads, writes = self._m(reads), self._m(writes)
        waits = self._deps(eng, reads, writes)
        ev = Ev("eng", eng=eng, idx=len(self.stream[eng]))
        if self.trace_lines:
            import sys as _s
            fr = _s._getframe(2)
            self.lines[(eng, len(self.stream[eng]))] = (fr.f_lineno, fr.f_back.f_lineno if fr.f_back else 0)
        self.stream[eng].append((fn, waits, ev))
        self._commit(ev, reads, writes)
        self.last_ev[eng] = ev
        return ev

    def dma(self, eng, out, in_, key, reads=(), writes=(), **kw):
        reads, writes = self._m(reads), self._m(writes)
        key = key if (not self.sfx or key in ("const", "constp")) else key + self.sfx
        waits = self._deps(eng, reads, writes)
        if key not in self.dma_count:
            self.dma_count[key] = 0
            self.dma_keys.append(key)
        self.dma_count[key] += 16
        ev = Ev("dma", eng=eng, idx=len(self.stream[eng]), key=key, val=self.dma_count[key])
        self.stream[eng].append((lambda e: e.dma_start(out=out, in_=in_, **kw), waits, ev))
        self._commit(ev, reads, writes)
        return ev

    def custom(self, eng, fn, key, reads=(), writes=(), inc=1):
        waits = self._deps(eng, reads, writes)
        if key not in self.dma_count:
            self.dma_count[key] = 0
            self.dma_keys.append(key)
        self.dma_count[key] += inc
        ev = Ev("dma", eng=eng, idx=len(self.stream[eng]), key=key, val=self.dma_count[key])
        self.stream[eng].append((fn, waits, ev))
        self._commit(ev, reads, writes)
        return ev

    def alias(self, new, olds):
        new = self._m([new])[0]
        evs = []
        for o in olds:
            w = self.last_write.get(o)
            if w is not None:
                evs.append(w)
            evs.extend(self.readers.get(o, ()))
        self.readers.setdefault(new, []).extend(evs)

    def all_res(self):
        return list(set(list(self.last_write.keys()) + list(self.readers.keys())))

    def emit(self, es, final_waits):
        nc = self.nc
        LIM = 30000
        engobj = {"pe": nc.tensor, "act": nc.scalar, "dve": nc.vector, "pool": nc.gpsimd, "sp": nc.sync}
        esems = {}
        for e in self.ENGS:
            cnt = 0
            for (fn, waits, ev) in self.stream[e]:
                if ev.kind == "eng" and ev.needed:
                    cnt += 1
                    ev.val = cnt
            nsem = cnt // LIM + 1
            import os as _os
            if _os.environ.get("KCNT"):
                print("SEMCNT", e, cnt, "ninstr", len(self.stream[e]), "nwaits", sum(len(w) for (_, w, _) in self.stream[e]))
            esems[e] = [es.enter_context(nc.semaphore("s_%s_%d" % (e, i))) for i in range(nsem)]
        dsems = {k: es.enter_context(nc.semaphore("d_%s" % str(k))) for k in self.dma_keys}

        def semval(ev):
            if ev.kind == "eng":
                i = (ev.val - 1) // LIM
                return esems[ev.eng][i], ev.val - i * LIM
            if ev.key in ("const", "constp"):
                return dsems[ev.key], self.dma_count[ev.key]
            return dsems[ev.key], ev.val

        block = es.enter_context(nc.Block())
        streams = self.stream

        def run(e, eng):
            for ii, (fn, waits, ev) in enumerate(streams[e]):
                for w in waits:
                    s, v = semval(w)
                    eng.wait_ge(s, v)
                ins = fn(eng)
                if self.trace_lines:
                    try:
                        self.imap[str(ins.ins.name)] = self.lines.get((e, ii))
                    except Exception:
                        pass
                if ev.kind == "dma":
                    if ev.eng is not None and ev.key is not None:
                        inc = 16 if not str(ev.key).startswith("cc") else 1
                        ins.then_inc(dsems[ev.key], inc)
                elif ev.needed:
                    s, v = semval(ev)
                    ins.then_inc(s, 1)
            for (k, v) in final_waits.get(e, []):
                eng.wait_ge(dsems[k], v)

        @block.tensor
        def _(eng):
            run("pe", eng)

        @block.scalar
        def _(eng):
            run("act", eng)

        @block.vector
        def _(eng):
            run("dve", eng)

        @block.gpsimd
        def _(eng):
            run("pool", eng)

        @block.sync
        def _(eng):
            run("sp", eng)


def _consts(g):
    cst = {}
    cst["ident"] = np.eye(128, dtype=np.float32)
    bo = np.zeros((128, 128), np.float32)
    bo[:64, :64] = 1.0
    bo[64:, 64:] = 1.0
    cst["bones"] = bo
    cst["ones"] = np.ones((128, 128), np.float32)
    s = np.arange(128)[:, None]
    t = np.arange(128)[None, :]
    same = (s // 64) == (t // 64)
    tri = np.zeros((128, 4, 128), np.float32)
    tri[:, 0, :] = same & (s <= t)
    tri[:, 1, :] = same & (s < t)
    tri[:, 2, :] = same & (s >= t)
    tri[:, 3, :] = same & (s > t)
    cst["tri"] = tri
    s = np.arange(64)[:, None]
    t = np.arange(64)[None, :]
    mst = np.zeros((64, 2, 128), np.float32)
    mst[:, 0, :64] = s < t
    mst[:, 0, 64:] = s <= t
    mst[:, 1, :64] = s > t
    mst[:, 1, 64:] = s >= t
    cst["mst"] = mst
    m1 = np.zeros((64, 2, 64), np.float32)
    m1[:, 0, :] = (t < s).T.T
    tt = np.arange(64)[:, None]
    ss = np.arange(64)[None, :]
    m1[:, 0, :] = ss < tt
    m1[:, 1, :] = ss > tt
    cst["m1"] = m1
    j = np.arange(128)[:, None].astype(np.float64)
    i = np.arange(128)[None, :].astype(np.float64)
    dmask = np.zeros((128, 4, 128), np.float32)
    qsc = np.zeros((128, 4, 128), np.float32)
    ksc = np.zeros((128, 4), np.float32)
    gc = np.zeros((128, 2), np.float32)
    sc = 128.0 ** -0.5
    for hr in range(2):
        gam = 1.0 - 2.0 ** (-5.0 - (2 * g + hr))
        lg = np.log(gam)
        dmask[:, hr * 2 + 0, :] = np.where(j <= i, np.exp((i - j) * lg), 0.0) * sc
        dmask[:, hr * 2 + 1, :] = np.where(j > i, np.exp((j - i) * lg), 0.0) * sc
        qsc[:, hr * 2 + 0, :] = np.exp((i + 1.0) * lg)
        qsc[:, hr * 2 + 1, :] = np.exp((128.0 - i) * lg)
        ksc[:, hr * 2 + 0] = (np.exp((127.0 - j) * lg) * sc)[:, 0]
        ksc[:, hr * 2 + 1] = (np.exp(j * lg) * sc)[:, 0]
        gc[:, hr] = np.exp(128.0 * lg)
    cst["dmask"], cst["qsc"], cst["ksc"], cst["gc"] = dmask, qsc, ksc, gc
    n = 32
    inv = 10000.0 ** (-np.arange(n, dtype=np.float64) / n)
    tpos = np.arange(SEQ)
    ang = np.zeros((128, SEQ), np.float64)
    for d in range(128):
        if d < 64:
            ang[d] = (tpos // 64) * inv[d % 32]
        else:
            ang[d] = (tpos % 64) * inv[d % 32]
    cst["ropec"] = np.cos(ang).astype(np.float32)
    cst["ropes"] = np.sin(ang).astype(np.float32)
    pm = np.zeros((128, 128), np.float32)
    for dp in range(128):
        if (dp % 64) < 32:
            pm[dp + 32, dp] = -1.0
        else:
            pm[dp - 32, dp] = 1.0
    cst["pm"] = pm
    sel = np.zeros((2, 130), np.float32)
    sel[0, 0] = 1.0
    sel[1, 1] = 1.0
    sel[0, 2:] = 1.0
    cst["sel"] = sel
    return cst


def _ktile(w):
    K, N = w.shape
    return np.ascontiguousarray(w.reshape(K // 128, 128, N).transpose(1, 0, 2))


def _col(v):
    return np.ascontiguousarray(v.reshape(-1, 128).T)


def build(stop_after=None, dbg=False):
    nc = bass.Bass("TRN2", target_bir_lowering=False)
    P = Prog(nc)
    P.trace_lines = dbg
    DBG["P"] = P
    es = ExitStack()

    def din(name, shape, dt=F32):
        return nc.dram_tensor(name, list(shape), dt, kind="ExternalInput").ap()

    xa = din("xa", [SEQ + CTX + 4, D])
    xb = din("xb", [1152, D])
    cvt = din("cvt", [128, 16, 2])
    wada = din("wada", [24, 128, 16, 512])
    bada = din("bada", [1, 12288])
    nrm_col = din("nrm_col", [128, 2, 16])
    nrm_row = din("nrm_row", [2, D])
    win = din("win", [128, 16, NCOLS])
    rwconv = din("rwconv", [128, 10, 3])
    colv = din("colv", [128, 2, 9])
    w0row = din("w0row", [1, 2 * 256])
    wup = din("wup", [128, 256])
    aup = din("aup", [128, 256])
    gup = din("gup", [160, 256])
    wout = din("wout", [128, 16, D])
    wgate = din("wgate", [NFT, 128, 16, 128])
    wupf = din("wupf", [NFT, 128, 16, 128])
    wdown = din("wdown", [16, 128, NFT, 128])
    fconv = din("fconv", [128, NFT, 10])
    hmask = din("hmask", [128, 2])
    tsel = din("tsel", [128, 8])
    c_ident = din("c_ident", [128, 128])
    c_bones = din("c_bones", [128, 128])
    c_ones = din("c_ones", [128, 128])
    c_tri = din("c_tri", [128, 4, 128])
    c_mst = din("c_mst", [64, 2, 128])
    c_m1 = din("c_m1", [64, 2, 64])
    c_dmask = din("c_dmask", [128, 4, 128])
    c_qsc = din("c_qsc", [128, 4, 128])
    c_ksc = din("c_ksc", [128, 4])
    c_gc = din("c_gc", [128, 2])
    c_ropec = din("c_ropec", [128, SEQ])
    c_ropes = din("c_ropes", [128, SEQ])
    c_pm = din("c_pm", [128, 128])
    c_sel = din("c_sel", [2, 130])
    out_d = None
    yspill = nc.dram_tensor("yspill", [4, 128, SEQ], F32).ap()
    pspill = nc.dram_tensor("pspill", [17, 18, 128, BLK], F32).ap()
    TWO_A = stop_after == "A2L"
    TWO_B = stop_after == "B2L"
    osend = nc.dram_tensor("osend", [512, SEQ], BF16, kind=("ExternalOutput" if TWO_A else "Internal"))
    if not TWO_A:
        out_d = nc.dram_tensor("out", [1024, D], F32, kind="ExternalOutput").ap()
    if TWO_A:
        modcol_o = nc.dram_tensor("modcol_o", [128, 96], F32, kind="ExternalOutput").ap()
        a12_o = nc.dram_tensor("a12_o", [1, 2 * D], F32, kind="ExternalOutput").ap()
    if TWO_B:
        oin = din("oin", [128, 16, 1152], BF16)
        modcol_i = din("modcol_i", [128, 96])
        a12_i = din("a12_i", [1, 2 * D])
    ogath = nc.dram_tensor("ogath", [8 * 512, SEQ], BF16)
    x1sp = nc.dram_tensor("x1sp", [1024, D], F32).ap()
    dbg_d = None
    if dbg:
        dbg_d = nc.dram_tensor("dbg", [128, 4096], F32, kind="ExternalOutput").ap()

    def sb(name, shape, dt=F32):
        return es.enter_context(nc.sbuf_tensor(name, list(shape), dt))

    ident = sb("ident", [128, 128])
    bones = sb("bones", [128, 128])
    ones = sb("ones", [128, 128])
    tri = sb("tri", [128, 4, 128])
    mst = sb("mst", [64, 2, 128])
    m1m = sb("m1m", [64, 2, 64])
    dmask = sb("dmask", [128, 4, 128])
    qsc = sb("qsc", [128, 4, 128])
    ksc = sb("ksc", [128, 4])
    gcs = sb("gcs", [128, 2])
    pm = sb("pm", [128, 128])
    sel = sb("sel", [2, 130])
    nrmc = sb("nrmc", [128, 2, 16])
    rwcv = sb("rwcv", [128, 10, 3])
    colvs = sb("colvs", [128, 2, 9])
    w0bc = sb("w0bc", [128, 512])
    wup_s = sb("wup_s", [128, 256], BF16)
    aup_s = sb("aup_s", [128, 256], BF16)
    gup_a = sb("gup_a", [128, 256], BF16)
    gup_b = sb("gup_b", [32, 256], BF16)
    hmask_s = sb("hmask_s", [128, 2])
    tsel_s = sb("tsel_s", [128, 8])
    fconv_s = sb("fconv_s", [128, NFT, 10])
    modcol = sb("modcol", [128, 6, 16])
    ARENA_W = 48800
    arena = sb("arena", [128, ARENA_W])
    psb = [es.enter_context(nc.psum_tensor("psb%d" % i, [128, 512], F32)) for i in range(8)]

    class Arena:
        def __init__(self):
            self.off = 0

        def f32(self, n):
            o = self.off
            self.off += n
            assert self.off <= ARENA_W, self.off
            return arena[:, o:o + n]

        def bf16(self, n):
            w = (n + 1) // 2
            o = self.off
            self.off += w
            assert self.off <= ARENA_W, self.off
            return arena[:, o:o + w].bitcast(BF16)

    V, S, T, G, PE = "dve", "act", "pe", "pool", "pe"

    def tt(out, a, b, op, R, W, eng="dve"):
        P.op(eng, lambda e: e.tensor_tensor(out=out, in0=a, in1=b, op=op), R, W)

    def ts(out, a, s1, s2, op0, op1, R, W, eng="dve"):
        if s2 is None:
            P.op(eng, lambda e: e.tensor_scalar(out=out, in0=a, scalar1=s1, scalar2=None, op0=op0), R, W)
        else:
            P.op(eng, lambda e: e.tensor_scalar(out=out, in0=a, scalar1=s1, scalar2=s2, op0=op0, op1=op1), R, W)

    def stt(out, a, s, b, op0, op1, R, W, eng="dve"):
        P.op(eng, lambda e: e.scalar_tensor_tensor(out=out, in0=a, scalar=s, in1=b, op0=op0, op1=op1), R, W)

    def act(out, a, func, R, W, bias=None, scale=None, accum=None):
        kw = {}
        if bias is not None:
            kw["bias"] = bias
        if scale is not None:
            kw["scale"] = scale
        if accum is not None:
            kw["accum_out"] = accum
        P.op("act", lambda e: e.activation(out=out, in_=a, func=func, **kw), R, W)

    epsc = sb("epsc", [128, 4])
    identh = sb("identh", [128, 128], BF16)

    def rsq(out, a, scale, biasv, R, W):
        bi = {EPS: 0, LNX_EPS: 1, 1e-12: 2}[biasv]
        P.op("act", lambda e: e.activation(out=out, in_=a, func=AF.Sqrt, bias=epsc[0:out.shape[0], bi:bi + 1], scale=scale), list(R) + ["epsc"], W)
        P.op("dve", lambda e: e.reciprocal(out=out, in_=out), W, W)

    def cp(out, a, R, W, eng="act"):
        if eng == "act":
            P.op("act", lambda e: e.copy(out=out, in_=a), R, W)
        else:
            P.op(eng, lambda e: e.tensor_copy(out=out, in_=a), R, W)

    def mm(out, lhsT, rhs, R, W, start=True, stop=True):
        P.op("pe", lambda e: e.matmul(out, lhsT, rhs, start=start, stop=stop), R, W)

    def trp(out, in_, idn, R, W):
        P.op("pe", lambda e: e.transpose(out, in_, idn), R, W)

    def memset(ap, val, W, eng="dve"):
        P.op(eng, lambda e: e.memset(ap, val), (), W)

    dq = ["sp", "act"]
    dqi = [0]

    dq_default = ["sp"]

    def load(out, in_, key, W, R=(), eng=None):
        if eng is None:
            eng = dq_default[0]
        return P.dma(eng, out, in_, key, reads=R, writes=W)

    memset(epsc[:, 0:1], EPS, ["epsc"])
    memset(epsc[:, 1:2], LNX_EPS, ["epsc"])
    memset(epsc[:, 2:3], 1e-12, ["epsc"])
    CK = "const"
    for (dst, src, nm) in [(ident, c_ident, "ident"), (bones, c_bones, "bones"), (ones, c_ones, "ones"),
                           (tri, c_tri, "tri"), (mst, c_mst, "mst"), (m1m, c_m1, "m1m"), (dmask, c_dmask, "dmask"),
                           (qsc, c_qsc, "qsc"), (ksc, c_ksc, "ksc"), (gcs, c_gc, "gcs"), (pm, c_pm, "pm"),
                           (sel, c_sel, "sel"), (nrmc, nrm_col, "nrmc"), (rwcv, rwconv, "rwcv"),
                           (colvs, colv, "colvs"), (hmask_s, hmask, "hmask"), (tsel_s, tsel, "tsel"),
                           (fconv_s, fconv, "fconv")]:
        load(dst[:], src, CK, [nm])
    load(w0bc[:], w0row.partition_broadcast(128), CK, ["w0bc"])
    P.op("act", lambda e: e.copy(out=identh[:], in_=ident[:]), ["ident"], ["identh"])
    P.dma("pool", wup_s[:], wup, "constp", writes=["wup"])
    P.dma("pool", aup_s[:], aup, "constp", writes=["aup"])
    P.dma("pool", gup_a[:], gup[0:128, :], "constp", writes=["gupa"])
    P.dma("pool", gup_b[:], gup[128:160, :], "constp", writes=["gupb"])

    A0 = Arena()
    wa_buf = [A0.f32(16 * 512).rearrange("p (k n) -> p k n", k=16) for _ in range(2)]
    modrow = A0.f32(12288)
    badab = [A0.f32(512) for _ in range(2)]
    npost = A0.f32(2 * D).rearrange("p (a n) -> p a n", a=2)
    A12 = A0.f32(2 * D).rearrange("p (a n) -> p a n", a=2)
    a12sp = nc.dram_tensor("a12sp", [1, 2 * D], F32).ap()
    scT = sb("scT", [128, 16, 2])
    load(scT[:], cvt, CK, ["scT"])
    load(npost[:], nrm_row.rearrange("a n -> (a n)").partition_broadcast(128).rearrange("p (a n) -> p a n", a=2)
         if False else nrm_row.unsqueeze(0).to_broadcast([128, 2, D]), CK, ["npost"])
    act(scT[:], scT[:], AF.Silu, ["scT"], ["scT"])
    for n in range(24):
        wb = wa_buf[n % 2]
        load(wb, wada[n], "wada%d" % (n % 2), ["wab%d" % (n % 2)])
        load(badab[n % 2][0:2, :], bada[:, n * 512:(n + 1) * 512].partition_broadcast(2), "wada%d" % (n % 2), ["wab%d" % (n % 2)])
        ps = psb[n % 2]
        for kt in range(16):
            mm(ps[0:2, :], scT[:, kt, :], wb[:, kt, :], ["scT", "wab%d" % (n % 2)], ["psb%d" % (n % 2)],
               start=(kt == 0), stop=(kt == 15))
        tt(modrow[0:2, n * 512:(n + 1) * 512], ps[0:2, :], badab[n % 2][0:2, :], ALU.add,
           ["psb%d" % (n % 2), "wab%d" % (n % 2)], ["modrow"])
    segs = [(0, 0), (0, 1), (0, 3), (0, 4), (1, 0), (1, 1)]
    psc = psb[2]
    for i, (r, sg) in enumerate(segs):
        for kt in range(16):
            mm(psc[:, i * 16 + kt:i * 16 + kt + 1], modrow[0:2, sg * D + kt * 128: sg * D + (kt + 1) * 128],
               sel[0:2, r:r + 1], ["modrow", "sel"], ["psb2"])
    cp(modcol[:].rearrange("p a k -> p (a k)"), psc[:, 0:96], ["psb2"], ["modcol"], eng="dve")
    for (i, nidx) in [(1, 0), (3, 1), (5, 0)]:
        stt(modcol[:, i, :], modcol[:, i, :], 1.0, nrmc[:, nidx, :], ALU.add, ALU.mult, ["modcol", "nrmc"], ["modcol"])
    for a, sg in enumerate([2, 5]):
        for j in range(4):
            ps = psb[3 + (j % 2)]
            mm(ps[:, :], sel[0:2, 2:130], modrow[0:2, sg * D + j * 512: sg * D + (j + 1) * 512], ["modrow", "sel"],
               ["psb%d" % (3 + j % 2)])
            tt(A12[:, a, j * 512:(j + 1) * 512], ps[:, :], npost[:, a, j * 512:(j + 1) * 512], ALU.mult,
               ["psb%d" % (3 + j % 2), "npost"], ["A12"])
    P.dma("sp", a12sp, A12[0:1].rearrange("p a n -> p (a n)"), "a12st", reads=["A12"], writes=["a12sp"])
    if TWO_A:
        P.dma("sp", a12_o, A12[0:1].rearrange("p a n -> p (a n)"), "a12o", reads=["A12"], writes=["a12_o"])
    ph0_res = ["wab0", "wab1", "modrow", "badab", "npost", "A12"]

    if dbg and stop_after == 0:
        dtile = A0.f32(4096)
        memset(dtile[:], 0.0, ["dtile"])
        cp(dtile[:, 0:96], modcol[:].rearrange("p a k -> p (a k)"), ["modcol"], ["dtile"], eng="dve")
        cp(dtile[:, 128:128 + 2048], A12[:, 0, :], ["A12"], ["dtile"], eng="dve")
        ev = P.dma("sp", dbg_d, dtile[:], "dbgout", reads=["dtile"], writes=["dbg_d"])
        P.emit(es, {"sp": [("dbgout", 16)]})
        es.close()
        return nc

    PMAP = {0: 0, 1: 1, 2: 0, 3: 1, 4: 2, 5: 3, 6: 2, 7: 3}
    yspill_f = nc.dram_tensor("yspill_f", [4, 128, SEQ], F32).ap()
    yspill_b = nc.dram_tensor("yspill_b", [4, 128, SEQ], F32).ap()

    def make_thread(tid, TA):
        def PB(i):
            return psb[4 * tid + PMAP[i]]

        def PBN(i):
            return "PS%d" % (4 * tid + PMAP[i])

        vtm = TA.bf16(2 * 2 * 128).rearrange("p (c h n) -> p c h n", c=2, h=2)
        rwT = TA.f32(10 * BLK).rearrange("p (k n) -> p k n", k=10)
        retT = TA.f32(8 * BLK).rearrange("p (k n) -> p k n", k=8)
        stat = TA.f32(8)

        def t2(dt=F32):
            if dt == F32:
                return TA.f32(2 * BLK).rearrange("p (k n) -> p k n", k=2)
            return TA.bf16(2 * BLK).rearrange("p (k n) -> p k n", k=2)

        thb = TA.bf16(BLK)
        adb = TA.bf16(BLK)
        gsb = TA.bf16(2 * BLK).rearrange("p (k n) -> p k n", k=2)
        sigtm = t2()
        t2base = TA.off
        cumT, cumpT, aT, kkT, tmpA, tmpB, Ep, Em, Epv, kdT, BhT, KhT, yT = [t2() for _ in range(13)]
        rwT2 = arena[:, t2base:t2base + 10 * BLK].rearrange("p (k n) -> p k n", k=10)
        retT2 = arena[:, t2base + 10 * BLK:t2base + 18 * BLK].rearrange("p (k n) -> p k n", k=8)
        aT2, kd2T = tmpB, BhT
        ART = TA.bf16(2 * 4 * 128).rearrange("p (k c n) -> p k c n", k=2, c=4)
        BKT = TA.bf16(2 * 4 * 128).rearrange("p (k c n) -> p k c n", k=2, c=4)
        BKV = TA.bf16(4 * 3 * 2 * 2 * 128).rearrange("p (c m k h n) -> p c m k h n", c=4, m=3, k=2, h=2)
        Ms2 = TA.bf16(2 * 4 * 2 * 128).rearrange("p (c h m n) -> p c h m n", c=2, h=4, m=2)
        M1s2 = TA.bf16(2 * 4 * 64).rearrange("p (c h n) -> p c h n", c=2, h=4)
        Pa = [TA.bf16(2 * 4 * 2 * 64).rearrange("p (c h m n) -> p c h m n", c=2, h=4, m=2) for _ in range(2)]
        Qs = [TA.bf16(2 * 4 * 64).rearrange("p (c h n) -> p c h n", c=2, h=4) for _ in range(2)]
        Wsb = TA.bf16(2 * 128).rearrange("p (k n) -> p k n", k=2)
        Usb = TA.bf16(2 * 2 * 128).rearrange("p (k h n) -> p k h n", k=2, h=2)
        Hst = TA.f32(2 * 128).rearrange("p (k n) -> p k n", k=2)
        Hb = TA.bf16(2 * 128).rearrange("p (k n) -> p k n", k=2)
        qr = t2(BF16)
        kr = t2(BF16)
        qtl = t2(BF16)
        rc_t = TA.f32(BLK)
        rs_t = TA.f32(BLK)
        ktm = TA.bf16(2 * 2 * 128).rearrange("p (c h n) -> p c h n", c=2, h=2)
        Ssb = TA.bf16(2 * 128).rearrange("p (h n) -> p h n", h=2)
        Sst = TA.f32(2 * 128).rearrange("p (h n) -> p h n", h=2)
        Sbf = TA.bf16(2 * 128).rearrange("p (h n) -> p h n", h=2)
        yrT = t2()
        vrb = t2(BF16)
        ysp_flat = TA.f32(4 * BLK)
        ysp = ysp_flat.rearrange("p (k n) -> p k n", k=4)
        sqj = ysp_flat.bitcast(BF16)
        print("TA.off", TA.off)
        oT = BhT.rearrange("p k n -> p (k n)").bitcast(BF16).rearrange("p (k n) -> p k n", k=4)

        tsize = TA.off

        def init():
            memset(BKV[0:64], 0.0, ["BKV"])
            memset(Usb[0:64], 0.0, ["Usb"])
            memset(Hst[:], 0.0, ["Hst"])
            memset(Hb[:], 0.0, ["Hb"])
            memset(Sst[:], 0.0, ["Sst"])
            memset(Sbf[:], 0.0, ["Sbf"])

        def front(seq, bi, nblk, alt=False):
            sw = 0
            rwT_, retT_, rn, tn = (rwT2, retT2, "rwTalt", "retTalt") if alt else (rwT, retT, "rwT", "retT")
            lat = seq == "l"
            ro = lat
            row0 = (0 if not lat else 258) + bi * BLK
            s1i, shi_ = (5, 4) if not lat else (1, 0)
            blk_i = 0 if not lat else bi + 1
            nrw = 9 if ro else 8
            nrt = 8 if ro else 4
            for j in range(3):
                nr = 128 if j < 2 else 2
                xt = xts[j % 2]
                xn = "xt%d" % (j % 2)
                load(xt[0:nr, :], xa[row0 + j * 128: row0 + j * 128 + nr, :], "xk%d" % (j % 2), [xn])
                act(sqj[0:nr, :], xt[0:nr, :], AF.Square, [xn], ["ysp", "stat"], accum=stat[0:nr, 0:1])
                rsq(stat[0:nr, 1:2], stat[0:nr, 0:1], 1.0 / D, EPS, ["stat"], ["stat"])
                ts(xt[0:nr, :], xt[0:nr, :], stat[0:nr, 1:2], None, ALU.mult, None, [xn, "stat"], [xn])
                for k4 in range(4):
                    ps = psb[(j * 4 + k4) % 8]
                    pn = "PS%d" % ((j * 4 + k4) % 8)
                    for kk_ in range(4):
                        kt = k4 * 4 + kk_
                        trp(ps[:, kk_ * 128: kk_ * 128 + nr], xt[0:nr, kt * 128:(kt + 1) * 128], ident[0:nr, 0:nr], [xn, "ident"], [pn])
                    for kk_ in range(4):
                        kt = k4 * 4 + kk_
                        if kk_ % 2 == 0:
                            act(xmT[:, kt, j * 128: j * 128 + nr], ps[:, kk_ * 128: kk_ * 128 + nr], AF.Identity, [pn, "modcol"], ["xmT"],
                                bias=modcol[:, shi_, kt:kt + 1], scale=modcol[:, s1i, kt:kt + 1])
                        else:
                            ts(xmT[:, kt, j * 128: j * 128 + nr], ps[:, kk_ * 128: kk_ * 128 + nr], modcol[:, s1i, kt:kt + 1],
                               modcol[:, shi_, kt:kt + 1], ALU.mult, ALU.add, [pn, "modcol"], ["xmT"])
            if bi == 0:
                memset(xmT[:, :, 0:1], 0.0, ["xmT"])
            if bi == nblk - 1:
                memset(xmT[:, :, 257:258], 0.0, ["xmT"])
            tiles = [0, 1, 2, 3, 4, 5, 6, 7, 8, 9, 10, 11, 12, 13, 14, 15, 16, 17] if ro else [0, 1, 2, 3, 4, 5, 6, 7, 10, 11, 12, 13]
            for ti, mt in enumerate(tiles):
                ps = psb[ti % 8]
                pn = "PS%d" % (ti % 8)
                if mt < 9:
                    c0, mw = mt * 128, 128
                elif mt == 9:
                    c0, mw = 1152, 32
                else:
                    c0, mw = 1184 + (mt - 10) * 128, 128
                for kt in range(16):
                    mm(ps[0:mw, 0:NWIN], winb[:, kt, c0:c0 + mw], xmT[:, kt, :], ["winb", "xmT"], [pn], start=(kt == 0), stop=(kt == 15))
                if mt < 10:
                    ts(rwT_[0:mw, mt, :], ps[0:mw, 1:257], rwcv[0:mw, mt, 1:2], None, ALU.mult, None, [pn, "rwcv"], [rn])
                    stt(rwT_[0:mw, mt, :], ps[0:mw, 0:256], rwcv[0:mw, mt, 0:1], rwT_[0:mw, mt, :], ALU.mult, ALU.add, [pn, "rwcv", rn], [rn])
                    stt(rwT_[0:mw, mt, :], ps[0:mw, 2:258], rwcv[0:mw, mt, 2:3], rwT_[0:mw, mt, :], ALU.mult, ALU.add, [pn, "rwcv", rn], [rn])
                else:
                    cp(retT_[:, mt - 10, :], ps[:, 1:257], [pn], [tn])

            P.dma("sp", pspill[blk_i, 0:nrw].rearrange("k p n -> p k n"), rwT_[:, 0:nrw, :], ("psa" + str(int(alt))), reads=[rn], writes=["pspA%d" % blk_i])
            if ro:
                P.dma("sp", pspill[blk_i, 9, 0:32, :], rwT_[0:32, 9, :], ("psb_" + str(int(alt))), reads=[rn], writes=["pspB%d" % blk_i])
            P.dma("sp", pspill[blk_i, 10:10 + nrt].rearrange("k p n -> p k n"), retT_[:, 0:nrt, :], ("psc" + str(int(alt))), reads=[tn], writes=["pspC%d" % blk_i])


        def block(sw, seq, bi, nblk):
            lat = seq == "l"
            ro = lat
            row0 = (0 if not lat else 258) + bi * BLK
            s1i, shi_ = (5, 4) if not lat else (1, 0)
            blk_i = 0 if not lat else bi + 1
            nrw = 9 if ro else 8
            nrt = 8 if ro else 4
            load(rwT[:, 0:nrw, :], pspill[blk_i, 0:nrw].rearrange("k p n -> p k n"), "pla", ["rwT"], R=["pspA%d" % blk_i])
            if ro:
                load(rwT[0:32, 9, :], pspill[blk_i, 9, 0:32, :], "plb", ["rwT"], R=["pspB%d" % blk_i])
            load(retT[:, 0:nrt, :], pspill[blk_i, 10:10 + nrt].rearrange("k p n -> p k n"), "plc", ["retT"], R=["pspC%d" % blk_i])
            yield
            dp = slice(0, 64) if sw == 0 else slice(64, 128)
            v4 = lambda ap: ap.rearrange("p (c n) -> p c n", c=4)
            cend = 63 if sw == 0 else 0
            act(thb[dp, :], rwT[dp, 6, :], AF.Tanh, ["rwT"], ["thb"])
            cp(adb[:, :], rwT[:, 7, :], ["rwT"], ["adb"])
            for t_ in range(2):
                ps = PB(2 + t_)
                pn = PBN(2 + t_)
                mm(ps[:, 0:256], thb[dp, t_ * 128:(t_ + 1) * 128], wup_s[dp, :], ["thb", "wup"], [pn])
                tt(sigtm[:, t_, :], ps[:, 0:256], w0bc[:, sw * 256:(sw + 1) * 256], ALU.add, [pn, "w0bc"], ["sigtm"])
                yield
            act(sigtm[:], sigtm[:], AF.Sigmoid, ["sigtm"], ["sigtm"])
            for which, dst, dn in [(0, cumT, "cumT"), (1, cumpT, "cumpT")]:
                ps = PB(2 + which)
                pn = PBN(2 + which)
                for ct in range(2):
                    for t_ in range(2):
                        mm(ps[:, ct * 256 + t_ * 128: ct * 256 + (t_ + 1) * 128], sigtm[:, t_, ct * 128:(ct + 1) * 128],
                           tri[:, 2 * sw + which, :], ["sigtm", "tri"], [pn])
                cp(dst[:].rearrange("p k n -> p (k n)"), ps[:, :], [pn], [dn])
                yield
            act(Ep[:], cumT[:], AF.Exp, ["cumT"], ["Ep"], scale=-DECAY_C)
            act(Em[:], cumT[:], AF.Exp, ["cumT"], ["Em"], scale=DECAY_C)
            act(Epv[:], cumpT[:], AF.Exp, ["cumpT"], ["Epv"], scale=-DECAY_C)
            yield
            for ct in range(2):
                cs_ = slice(ct * 128, (ct + 1) * 128)
                ps = PB(2 + ct)
                pn = PBN(2 + ct)
                mm(ps[:, 0:256], aup_s[dp, cs_], adb[dp, :], ["aup", "adb"], [pn])
                act(aT[:, ct, :], ps[:, 0:256], AF.Sigmoid, [pn, "colvs"], ["aT"], bias=colvs[:, ct, 5 + sw:6 + sw])
                yield
                ts(tmpA[:, ct, :], rwT[:, ct, :], colvs[:, ct, 0:1], None, ALU.mult, None, ["rwT", "colvs"], ["tmpA"])
                tt(tmpB[:, ct, :], tmpA[:, ct, :], tmpA[:, ct, :], ALU.mult, ["tmpA"], ["tmpB"])
                mm(ps[:, 256:512], bones[:, :], tmpB[:, ct, :], ["bones", "tmpB"], [pn])
                rsq(tmpB[:, ct, :], ps[:, 256:512], 1.0, 1e-12, [pn], ["tmpB"])
                yield
                tt(kkT[:, ct, :], tmpA[:, ct, :], tmpB[:, ct, :], ALU.mult, ["tmpA", "tmpB"], ["kkT"])
                stt(ART[:, ct, :, 0:64], v4(kkT[:, ct, :]), -1.0, v4(Epv[:, ct, :]), ALU.mult, ALU.mult, ["kkT", "Epv"], ["ART"])
                tt(ART[:, ct, :, 64:128], v4(rwT[:, 4 + ct, :]), v4(Ep[:, ct, :]), ALU.mult, ["rwT", "Ep"], ["ART"])
                gcb = v4(Ep[:, ct, :])[:, :, cend:cend + 1].to_broadcast([128, 4, 64])
                tt(tmpB[:, ct, :], kkT[:, ct, :], aT[:, ct, :], ALU.mult, ["kkT", "aT"], ["tmpB"])
                tt(BhT[:, ct, :], tmpB[:, ct, :], Em[:, ct, :], ALU.mult, ["tmpB", "Em"], ["BhT"])
                cp(BKT[:, ct, :, 0:64], v4(BhT[:, ct, :]), ["BhT"], ["BKT"])
                tt(v4(BhT[:, ct, :]), v4(BhT[:, ct, :]), gcb, ALU.mult, ["BhT", "Ep"], ["BhT"])
                ts(tmpA[:, ct, :], aT[:, ct, :], colvs[:, ct, 1:2], colvs[:, ct, 7:8], ALU.mult, ALU.add, ["aT", "colvs"], ["tmpA"])
                tt(kdT[:, ct, :], rwT[:, ct, :], tmpA[:, ct, :], ALU.mult, ["rwT", "tmpA"], ["kdT"])
                tt(KhT[:, ct, :], kdT[:, ct, :], Em[:, ct, :], ALU.mult, ["kdT", "Em"], ["KhT"])
                cp(BKT[:, ct, :, 64:128], v4(KhT[:, ct, :]), ["KhT"], ["BKT"])
                tt(v4(KhT[:, ct, :]), v4(KhT[:, ct, :]), gcb, ALU.mult, ["KhT", "Ep"], ["KhT"])
                yield
            if dbg and stop_after == 2.1:
                return
            for c in range(4):
                for ct in range(2):
                    ps = PB(2 + ct)
                    pn = PBN(2 + ct)
                    for m_, (src, sn) in enumerate([(BhT[:, ct, :], "BhT"), (KhT[:, ct, :], "KhT"), (rwT[:, 2 + ct, :], "rwT")]):
                        trp(ps[0:64, m_ * 128:(m_ + 1) * 128], src[:, c * 64:(c + 1) * 64], ident[:, :], [sn, "ident"], [pn])
                    pv = ps[0:64, 0:384].rearrange("p (m n) -> p m n", m=3)
                    for hh in range(2):
                        cp(BKV[0:64, c, :, ct, hh, hh * 64:(hh + 1) * 64], pv[:, :, hh * 64:(hh + 1) * 64], [pn], ["BKV"],
                           eng=("act" if hh == 0 else "dve"))
                yield
            if dbg and stop_after == 2.2:
                return
            cols = slice(bi * BLK, (bi + 1) * BLK)

            def ret_steps():
                cols = slice(bi * BLK, (bi + 1) * BLK)
                if lat:
                    load(rc_t, c_ropec[:, cols], "ropek", ["rope"])
                    load(rs_t, c_ropes[:, cols], "ropek", ["rope"])
                    for hr in range(2):
                        for isk, src_t in [(True, retT[:, hr, :]), (False, retT[:, 4 + hr, :])]:
                            ps = PB(2 + (0 if isk else 1))
                            pn = PBN(2 + (0 if isk else 1))
                            mm(ps[:, 0:256], pm[:, :], src_t, ["pm", "retT"], [pn])
                            tt(tmpA[:, 0, :], ps[:, 0:256], rs_t, ALU.mult, [pn, "rope"], ["tmpA"])
                            tt(tmpB[:, 0, :], src_t, rc_t, ALU.mult, ["retT", "rope"], ["tmpB"])
                            if isk:
                                tt(kr[:, hr, :], tmpA[:, 0, :], tmpB[:, 0, :], ALU.add, ["tmpA", "tmpB"], ["kr"])
                            else:
                                tt(qr[:, hr, :], tmpA[:, 0, :], tmpB[:, 0, :], ALU.add, ["tmpA", "tmpB"], ["qr"])
                        yield
                else:
                    for hr in range(2):
                        cp(kr[:, hr, :], retT[:, hr, :], ["retT"], ["kr"])
                if dbg and stop_after == 2.61:
                    return
                for c2 in range(2):
                    cc = slice(c2 * 128, (c2 + 1) * 128)
                    for hr in range(2):
                        ps = PB(2 + hr)
                        pn = PBN(2 + hr)
                        cp(vrb[:, hr, cc], retT[:, 2 + hr, cc], ["retT"], ["vrb"])
                        mm(ps[:, 0:128], vrb[:, hr, cc], identh[:, :], ["vrb", "identh"], [pn])
                        mm(ps[:, 128:256], kr[:, hr, cc], identh[:, :], ["kr", "identh"], [pn])
                        import os as _os
                        cp(vtm[:, c2, hr, :], ps[:, 0:128], [pn], ["vtm"], eng="dve")
                        if True:
                            ts(ktm[:, c2, hr, :], ps[:, 128:256], ksc[:, hr * 2 + sw:hr * 2 + sw + 1], None, ALU.mult, None, [pn, "ksc"], ["ktm"])
                        yield
                if dbg and stop_after == 2.62:
                    return
                for c2 in (range(2) if sw == 0 else (1, 0)):
                    cc = slice(c2 * 128, (c2 + 1) * 128)
                    for hr in range(2):
                        hc = slice(hr * 128, (hr + 1) * 128)
                        if ro:
                            mm(PB(2)[:, hc], kr[:, hr, cc], qr[:, hr, cc], ["kr", "qr"], [PBN(2)])
                            tt(Ssb[:, hr, :], PB(2)[:, hc], dmask[:, hr * 2 + sw, :], ALU.mult, [PBN(2), "dmask"], ["Ssb"])
                            tt(qtl[:, hr, cc], qr[:, hr, cc], qsc[:, hr * 2 + sw, :], ALU.mult, ["qr", "qsc"], ["qtl"])
                            yield
                            mm(PB(3)[:, hc], vtm[:, c2, hr, :], Ssb[:, hr, :], ["vtm", "Ssb"], [PBN(3)], start=True, stop=False)
                            mm(PB(3)[:, hc], Sbf[:, hr, :], qtl[:, hr, cc], ["Sbf", "qtl"], [PBN(3)], start=False, stop=True)
                            cp(yrT[:, hr, cc], PB(3)[:, hc], [PBN(3)], ["yrT"])
                            yield
                        mm(PB(0)[:, hc], ktm[:, c2, hr, :], vtm[:, c2, hr, :], ["ktm", "vtm"], [PBN(0)])
                        stt(Sst[:, hr, :], Sst[:, hr, :], gcs[:, hr:hr + 1], PB(0)[:, hc], ALU.mult, ALU.add, ["Sst", "gcs", PBN(0)], ["Sst"])
                        cp(Sbf[:, hr, :], Sst[:, hr, :], ["Sst"], ["Sbf"])
                        yield
                yield

            rg_ = ret_steps()

            def rstep():
                try:
                    next(rg_)
                except StopIteration:
                    pass
            chunk_order = list(range(4)) if sw == 0 else [3, 2, 1, 0]
            for pr in range(2):
              pair = chunk_order[2 * pr:2 * pr + 2]
              for ci, c in enumerate(pair):
                for h in range(4):
                    ct, hh = h // 2, h % 2
                    hp = slice(hh * 64, hh * 64 + 64)
                    pb = PB(4 + hh)
                    mm(pb[0:64, (ct * 2) * 128:(ct * 2 + 1) * 128], BKT[hp, ct, c, 0:64], ART[hp, ct, c, :], ["BKT", "ART"], [PBN(4 + hh)])
                    mm(pb[0:64, (ct * 2 + 1) * 128:(ct * 2 + 2) * 128], BKT[hp, ct, c, 64:128], ART[hp, ct, c, :], ["BKT", "ART"], [PBN(4 + hh)])
                    mm(PB(2 + hh)[0:64, 256 + ci * 128 + ct * 64:256 + ci * 128 + (ct + 1) * 64], ART[hp, ct, c, 0:64], BKT[hp, ct, c, 0:64], ["BKT", "ART"], [PBN(2 + hh)])
                for hh in range(2):
                    tt(Ms2[0:64, ci, 2 * hh:2 * hh + 2, :, :].rearrange("p h m n -> p (h m) n"),
                       PB(4 + hh)[0:64, :].rearrange("p (a n) -> p a n", a=4),
                       mst[:, sw, :].unsqueeze(1).to_broadcast([64, 4, 128]), ALU.mult, [PBN(4 + hh), "mst"], ["Ms"])
                yield
              for hh in range(2):
                tt(M1s2[0:64, :, 2 * hh:2 * hh + 2, :], PB(2 + hh)[0:64, 256:512].rearrange("p (c a n) -> p c a n", c=2, a=2),
                   m1m[:, sw, :].unsqueeze(1).unsqueeze(1).to_broadcast([64, 2, 2, 64]), ALU.mult, [PBN(2 + hh), "m1m"], ["M1s"])
                yield
              identb = ident[0:64, 0:64].unsqueeze(1).unsqueeze(1).to_broadcast([64, 2, 4, 64])
              tt(Qs[0][0:64], Ms2[0:64, :, :, 0, 0:64], identb, ALU.add, ["Ms", "ident"], ["Qs0"])
              pN = [PB(6 + ci)[0:64, :].rearrange("p (h m n) -> p h m n", h=4, m=2) for ci in range(2)]
              pQ = PB(3)[0:64, :].rearrange("p (c h n) -> p c h n", c=2, h=4)
              for k in range(1, 6):
                cur, prv = k % 2, (k - 1) % 2
                for ci in range(2):
                    for h in range(4):
                        if k == 1:
                            Pp, Ppp = Ms2[0:64, ci, h, 0, 0:64], M1s2[0:64, ci, h, :]
                            rn_ = ["Ms", "M1s"]
                        else:
                            Pp, Ppp = Pa[prv][0:64, ci, h, 0, :], Pa[prv][0:64, ci, h, 1, :]
                            rn_ = ["Pa%d" % prv]
                        mm(pN[ci][:, h, 0, :], Ppp, Pp, rn_, [PBN(6 + ci)])
                        mm(pN[ci][:, h, 1, :], Pp, Ppp, rn_, [PBN(6 + ci)])
                for ci in range(2):
                    cp(Pa[cur][0:64, ci].rearrange("p h m n -> p (h m n)"), PB(6 + ci)[0:64, :], [PBN(6 + ci)], ["Pa%d" % cur],
                       eng=("act" if ci == 0 else "dve"))
                    yield
                for ci in range(2):
                    for h in range(4):
                        mm(pQ[:, ci, h, :], Pa[cur][0:64, ci, h, 1, :], Qs[prv][0:64, ci, h, :], ["Pa%d" % cur, "Qs%d" % prv], [PBN(3)])
                tt(Qs[cur][0:64], pQ, Qs[prv][0:64], ALU.add, [PBN(3), "Qs%d" % prv], ["Qs%d" % cur])
                yield
                rstep()
                yield
              for ci, c in enumerate(pair):
                Ms = Ms2[:, ci]
                TT = Qs[1][:, ci]
                pW = PB(2)[0:64, 0:256].rearrange("p (k n) -> p k n", k=2)
                for ct in range(2):
                    mm(pW[:, ct, :], ART[:, ct, c, 0:64], Hb[:, ct, :], ["ART", "Hb"], [PBN(2)], start=True, stop=False)
                    for hh in range(2):
                        mm(pW[:, ct, :], Ms[0:64, hh * 2 + ct, 1, 0:64], BKV[0:64, c, 2, ct, hh, :], ["Ms", "BKV"], [PBN(2)],
                           start=False, stop=(hh == 1))
                cp(Wsb[0:64, :, :], pW, [PBN(2)], ["Wsb"])
                yield
                rstep()
                yield
                pU = PB(3)[0:64, 0:256].rearrange("p (k n) -> p k n", k=2)
                for ct in range(2):
                    for hh in range(2):
                        mm(pU[:, ct, hh * 64:(hh + 1) * 64], TT[0:64, hh * 2 + ct, :], Wsb[0:64, ct, hh * 64:(hh + 1) * 64],
                           ["Qs1", "Wsb"], [PBN(3)])
                for hh in range(2):
                    cp(Usb[0:64, :, hh, hh * 64:(hh + 1) * 64], pU[:, :, hh * 64:(hh + 1) * 64], [PBN(3)], ["Usb"],
                       eng=("act" if hh == 0 else "dve"))
                    yield
                if ro:
                    pY = PB(0)[:, 0:128].rearrange("p (k n) -> p k n", k=2)
                    for ct in range(2):
                        mm(pY[:, ct, :], Hb[:, ct, :], ART[:, ct, c, 64:128], ["Hb", "ART"], [PBN(0)], start=True, stop=False)
                        for hh in range(2):
                            mm(pY[:, ct, :], Usb[0:64, ct, hh, :], Ms[0:64, hh * 2 + ct, 0, 64:128], ["Usb", "Ms"], [PBN(0)],
                               start=False, stop=False)
                            mm(pY[:, ct, :], BKV[0:64, c, 2, ct, hh, :], Ms[0:64, hh * 2 + ct, 1, 64:128], ["BKV", "Ms"], [PBN(0)],
                               start=False, stop=(hh == 1))
                    cp(yT[:, :, c * 64:(c + 1) * 64], pY, [PBN(0)], ["yT"])
                    yield
                pH = PB(1)[:, 0:256].rearrange("p (k n) -> p k n", k=2)
                for ct in range(2):
                    for hh in range(2):
                        mm(pH[:, ct, :], BKV[0:64, c, 0, ct, hh, :], Usb[0:64, ct, hh, :], ["BKV", "Usb"], [PBN(1)],
                           start=(hh == 0), stop=False)
                        mm(pH[:, ct, :], BKV[0:64, c, 1, ct, hh, :], BKV[0:64, c, 2, ct, hh, :], ["BKV"], [PBN(1)],
                           start=False, stop=(hh == 1))
                    gcol = Ep[:, ct, c * 64 + cend: c * 64 + cend + 1]
                    stt(Hst[:, ct, :], Hst[:, ct, :], gcol, pH[:, ct, :], ALU.mult, ALU.add, ["Hst", "Ep", PBN(1)], ["Hst"])
                    yield
                cp(Hb[:], Hst[:], ["Hst"], ["Hb"])
                rstep()
                yield
            if dbg and stop_after == 2 and seq == "l" and bi == 0 and sw == 0:
                return
            if dbg and stop_after == 2.5:
                return
            for _ in rg_:
                yield
            if dbg and stop_after == 2.6:
                return
            if not lat:
                return
            do_post = (bi >= 8) if sw == 0 else (bi <= 7)
            ys_own = (yspill_f if sw == 0 else yspill_b)[:, :, cols].rearrange("k p n -> p k n")
            ys_oth = (yspill_b if sw == 0 else yspill_f)[:, :, cols].rearrange("k p n -> p k n")
            own_n = ("YSf%d" if sw == 0 else "YSb%d") % bi
            oth_n = ("YSb%d" if sw == 0 else "YSf%d") % bi
            if not do_post:
                P.dma(dq_default[0], ys_own[:, 0:2, :], yT[:], "yst0", reads=["yT"], writes=[own_n + "a"])
                P.dma(dq_default[0], ys_own[:, 2:4, :], yrT[:], "yst1", reads=["yrT"], writes=[own_n + "b"])
                return
            assert (oth_n + "a") in P.last_write and (oth_n + "b") in P.last_write, oth_n
            load(ysp[:], ys_oth, "yld", ["ysp"], R=[oth_n + "a", oth_n + "b"])
            act(gsb[:, 0, :], rwT[:, 8, :], AF.Sigmoid, ["rwT"], ["gsb"])
            act(gsb[0:32, 1, :], rwT[0:32, 9, :], AF.Sigmoid, ["rwT"], ["gsb"])
            od = slice(64, 128) if sw == 0 else slice(0, 64)
            osw = 1 - sw
            for ct in range(2):
                cs_ = slice(ct * 128, (ct + 1) * 128)
                ps = PB(2 + ct)
                pn = PBN(2 + ct)
                y = cumT[:, ct, :]
                sq = cumpT[:, ct, :]
                bn = Ep[:, ct, :]
                tt(y, yT[:, ct, :], ysp[:, ct, :], ALU.add, ["yT", "ysp"], ["cumT"])
                mm(ps[:, 0:256], bones[:, :], y, ["bones", "cumT"], [pn])
                stt(y, ps[:, 0:256], -1.0 / 64.0, y, ALU.mult, ALU.add, [pn, "cumT"], ["cumT"])
                yield
                tt(sq, y, y, ALU.mult, ["cumT"], ["cumpT"])
                mm(ps[:, 256:512], bones[:, :], sq, ["bones", "cumpT"], [pn])
                rsq(sq, ps[:, 256:512], 1.0 / 64.0, LNX_EPS, [pn], ["cumpT"])
                yield
                tt(y, y, sq, ALU.mult, ["cumT", "cumpT"], ["cumT"])
                ts(y, y, colvs[:, ct, 3:4], colvs[:, ct, 4:5], ALU.mult, ALU.add, ["cumT", "colvs"], ["cumT"])
                mm(ps[:, 0:256], aup_s[od, cs_], adb[od, :], ["aup", "adb"], [pn])
                act(bn, ps[:, 0:256], AF.Sigmoid, [pn, "colvs"], ["Ep"], bias=colvs[:, ct, 5 + osw:6 + osw])
                yield
                ts(bn, bn, colvs[:, ct, 1:2], colvs[:, ct, 7:8], ALU.mult, ALU.add, ["Ep", "colvs"], ["Ep"])
                tt(bn, rwT[:, ct, :], bn, ALU.mult, ["rwT", "Ep"], ["Ep"])
                tt(bn, bn, kdT[:, ct, :], ALU.add, ["Ep", "kdT"], ["Ep"])
                stt(bn, rwT[:, 4 + ct, :], colvs[:, ct, 2:3], bn, ALU.mult, ALU.mult, ["rwT", "colvs", "Ep"], ["Ep"])
                mm(ps[:, 256:512], bones[:, :], bn, ["bones", "Ep"], [pn])
                tt(bn, ps[:, 256:512], rwT[:, 2 + ct, :], ALU.mult, [pn, "rwT"], ["Ep"])
                yield
                tt(y, y, bn, ALU.add, ["cumT", "Ep"], ["cumT"])
                mm(ps[:, 0:256], gup_a[:, cs_], gsb[:, 0, :], ["gupa", "gsb"], [pn], start=True, stop=False)
                mm(ps[:, 0:256], gup_b[0:32, cs_], gsb[0:32, 1, :], ["gupb", "gsb"], [pn], start=False, stop=True)
                tt(oT[:, ct, :], y, ps[:, 0:256], ALU.mult, ["cumT", pn], ["BhT"])
                yield
            for hr in range(2):
                ps = PB(2 + hr)
                pn = PBN(2 + hr)
                y = Em[:, hr, :]
                sq = Epv[:, hr, :]
                tt(y, yrT[:, hr, :], ysp[:, 2 + hr, :], ALU.add, ["yrT", "ysp"], ["Em"])
                tt(sq, y, y, ALU.mult, ["Em"], ["Epv"])
                mm(ps[:, 0:256], ones[:, :], sq, ["ones", "Epv"], [pn])
                rsq(sq, ps[:, 0:256], 1.0 / 128.0, EPS, [pn], ["Epv"])
                yield
                tt(y, y, sq, ALU.mult, ["Em", "Epv"], ["Em"])
                act(sq, retT[:, 6 + hr, :], AF.Silu, ["retT"], ["Epv"])
                tt(oT[:, 2 + hr, :], y, sq, ALU.mult, ["Em", "Epv"], ["BhT"])
            if dbg and stop_after == 3:
                return
            P.dma(dq_default[0], osend.ap()[:, cols].rearrange("(k p) n -> p k n", p=128), oT[:], "ost", reads=["BhT"], writes=["osend"])
            return


        return front, block, init, tsize

    TNAMES = ['rwT', 'retT', 'stat', 'thb', 'adb', 'gsb', 'sigtm', 'cumT', 'cumpT', 'aT', 'kkT', 'tmpA', 'tmpB', 'Ep', 'Em', 'Epv', 'kdT', 'BhT', 'KhT', 'yT', 'ART', 'BKT', 'BKV', 'Ms', 'M1s', 'Pa0', 'Pa1', 'Qs0', 'Qs1', 'Wsb', 'Usb', 'Hst', 'Hb', 'qr', 'kr', 'qtl', 'rope', 'vtm', 'vrb', 'ktm', 'Ssb', 'Sst', 'Sbf', 'yrT', 'ysp']
    FA = Arena()
    winb = FA.bf16(16 * NCOLS).rearrange("p (k n) -> p k n", k=16)
    xts = [FA.f32(D) for _ in range(2)]
    xmT = FA.bf16(16 * NWIN).rearrange("p (k n) -> p k n", k=16)
    TA1 = Arena()
    f1, b1, i1, tsz = make_thread(1, TA1)
    TA0 = Arena()
    TA0.off = max(FA.off, tsz)
    f0, b0, i0, _ = make_thread(0, TA0)
    print("arena use", FA.off, tsz, TA0.off)
    for nm in ["winb", "xt0", "xt1", "xmT"]:
        P.alias(nm, ph0_res)
    P.dma("pool", winb, win, "constp", writes=["winb"])
    ts(colvs[:, :, 7], colvs[:, :, 1], -1.0, 1.0, ALU.mult, ALU.add, ["colvs"], ["colvs"])
    P.shared = set(P.all_res()) | {"winb", "osend", "colvs"} | {"PS%d" % i for i in range(8)}
    for i_ in range(17):
        P.shared |= {"pspA%d" % i_, "pspB%d" % i_, "pspC%d" % i_}
    for i_ in range(16):
        P.shared |= {"YSf%da" % i_, "YSf%db" % i_, "YSb%da" % i_, "YSb%db" % i_}
    front_local = ["xt0", "xt1", "xmT"]
    P.sfx = "_t0"
    for nm in TNAMES + front_local:
        P.alias(nm, ph0_res)
    i0()
    f0("c", 0, 1, alt=False)
    for bi_ in range(16):
        f0("l", bi_, 16, alt=(bi_ % 2 == 0))
    for nm in ["cumT", "cumpT", "aT", "kkT", "tmpA", "tmpB", "Ep", "Em", "Epv", "kdT"]:
        P.alias(nm, ["rwTalt_t0", "retTalt_t0"])
    P.sfx = "_t1"
    for nm in TNAMES:
        P.alias(nm, ph0_res + ["winb", "xt0_t0", "xt1_t0", "xmT_t0"])
    i1()

    def sweep_gen(sw, blk):
        order = [("c", 0, 1)] + [("l", bi, 16) for bi in (range(16) if sw == 0 else range(15, -1, -1))]
        for (seq, bi, nb) in order:
            yield from blk(sw, seq, bi, nb)

    gens = [("_t0", sweep_gen(0, b0)), ("_t1", sweep_gen(1, b1))]
    while gens:
        for item in list(gens):
            P.sfx = item[0]
            dq_default[0] = "sp" if item[0] == "_t0" else "pool"
            try:
                next(item[1])
            except StopIteration:
                gens.remove(item)
    P.sfx = ""
    dq_default[0] = "sp"
    if dbg and stop_after == 3:
        dt16 = arena[:, 0:512].bitcast(BF16).rearrange("p (k n) -> p k n", k=4)
        dtile = arena[:, 1024:1024 + 4096]
        allr = P.all_res()
        load(dt16, osend.ap()[:, 15 * BLK:16 * BLK].rearrange("(k p) n -> p k n", p=128), "dbgin", ["dt16"] + allr, R=["osend"])
        memset(dtile[:], 0.0, ["dtile"] + allr)
        cp(dtile[:, 0:1024], dt16.rearrange("p k n -> p (k n)"), ["dt16"], ["dtile"], eng="dve")
        P.dma("sp", dbg_d, dtile[:], "dbgout", reads=["dtile"], writes=["dbg_d"])
        P.emit(es, {"sp": [("dbgout", 16)]})
        es.close()
        return nc

    if stop_after == "noB":
        P.emit(es, {})
        es.close()
        return nc
    if TWO_A:
        P.dma("sp", modcol_o, modcol[:].rearrange("p a k -> p (a k)"), "mco", reads=["modcol"], writes=["modcol_o"])
        P.emit(es, {"sp": [(k_, P.dma_count[k_]) for k_ in P.dma_keys if str(k_).startswith("ost")] + [("mco", 16), ("a12o", 16)]})
        es.close()
        return nc
    if TWO_B:
        P = Prog(nc)
        DBG["P"] = P
        memset(epsc[:, 0:1], EPS, ["epsc"])
        load(ident[:], c_ident, CK, ["ident"])
        load(hmask_s[:], hmask, CK, ["hmask"])
        load(fconv_s[:], fconv, CK, ["fconv"])
        load(modcol[:].rearrange("p a k -> p (a k)"), modcol_i, CK, ["modcol"])
        a12sp = a12_i
    if stop_after != "nocc" and not TWO_B:
      P.custom("pool", lambda e: e.collective_compute("AllGather", ALU.bypass, replica_groups=[[0, 1, 2, 3, 4, 5, 6, 7]],
                                                    ins=[osend.ap().opt()], outs=[ogath.ap().opt()]),
               "cc0", reads=["osend"], writes=["ogath"])
    og = ogath.ap() if stop_after != "nocc" else osend.ap()

    prevA = P.all_res()
    PSN = ["psb%d" % i for i in range(8)]
    for i_ in range(8):
        P.alias("psb%d" % i_, ["PS%d" % i_, "psb%d" % i_])
    HT_W = 9216
    hT = arena[:, 0:HT_W].bitcast(BF16).rearrange("p (k n) -> p k n", k=16)
    B2 = Arena()
    B2.off = HT_W
    oTb = B2.bf16(16 * 1152).rearrange("p (k n) -> p k n", k=16)
    cand = [B2.bf16(8 * 1152).rearrange("p (k n) -> p k n", k=8)]
    woutb = B2.bf16(16 * D).rearrange("p (k n) -> p k n", k=16)
    mixrow = B2.f32(D)
    xtb = B2.f32(D)
    A1row = B2.f32(D)
    sqb = B2.bf16(D)
    statb = B2.f32(16)
    for nm in ["hT", "oTb", "cand0", "woutb", "mixrow", "xtb", "A1row", "sqb", "statb"]:
        P.alias(nm, prevA)
    P.dma("pool", woutb, wout, "woutk", writes=["woutb"])
    load(A1row, a12sp[:, 0:D].partition_broadcast(128), "a1k", ["A1row"], R=["a12sp"])
    NB_ = 2 if stop_after != "nocc" else 1
    if TWO_B:
        load(oTb, oin, "oink", ["oTb"])
    if NB_ == 1:
        memset(cand[0][:, 4:8, :], 0.0, ["cand0"])
    for bq in range(NB_):
        memset(cand[0][:, bq * 4 + 0, 0:64], 0.0, ["cand0"])
        memset(cand[0][:, bq * 4 + 3, 1088:1152], 0.0, ["cand0"])
    for kt in (range(16) if not TWO_B else []):
        cb = cand[0]
        cn = "cand0"
        for bq in range(NB_):
            r0 = (bq * 4 + kt // 4) * 512 + (kt % 4) * 128
            if stop_after == "nocc":
                r0 = (kt % 4) * 128
            load(cb[:, bq * 4 + 0, 64:1152], og[r0:r0 + 128, 0:1088], "candk0", [cn], R=["ogath", "osend"])
            load(cb[:, bq * 4 + 1, :], og[r0:r0 + 128, 960:2112], "candk0", [cn], R=["ogath", "osend"])
            load(cb[:, bq * 4 + 2, :], og[r0:r0 + 128, 1984:3136], "candk0", [cn], R=["ogath", "osend"])
            load(cb[:, bq * 4 + 3, 0:1088], og[r0:r0 + 128, 3008:4096], "candk0", [cn], R=["ogath", "osend"])
        ts(oTb[:, kt, :], cb[:, 0, :], tsel_s[:, 0:1], None, ALU.mult, None, [cn, "tsel"], ["oTb"])
        for g_ in range(1, 8):
            stt(oTb[:, kt, :], cb[:, g_, :], tsel_s[:, g_:g_ + 1], oTb[:, kt, :], ALU.mult, ALU.add, [cn, "tsel", "oTb"], ["oTb"])
    for t_ in range(9):
        tk = slice(t_ * 128, (t_ + 1) * 128)
        load(xtb, xb[tk, :], "xbk", ["xtb"])
        for cch in range(4):
            ps = psb[cch % 2]
            pn = PSN[cch % 2]
            for kt in range(16):
                mm(ps[:, :], oTb[:, kt, tk], woutb[:, kt, cch * 512:(cch + 1) * 512], ["oTb", "woutb"], [pn], start=(kt == 0), stop=(kt == 15))
            cp(mixrow[:, cch * 512:(cch + 1) * 512], ps[:, :], [pn], ["mixrow"])
        act(sqb, mixrow, AF.Square, ["mixrow"], ["sqb", "statb"], accum=statb[:, 0:1])
        rsq(statb[:, 1:2], statb[:, 0:1], 1.0 / D, EPS, ["statb"], ["statb"])
        stt(mixrow, mixrow, statb[:, 1:2], A1row, ALU.mult, ALU.mult, ["mixrow", "statb", "A1row"], ["mixrow"])
        tt(xtb, xtb, mixrow, ALU.add, ["xtb", "mixrow"], ["xtb"])
        if t_ == 0:
            P.dma("sp", x1sp[0:64, :], xtb[64:128, :], "x1st", reads=["xtb"], writes=["x1sp"])
        elif t_ == 8:
            P.dma("sp", x1sp[960:1024, :], xtb[0:64, :], "x1st", reads=["xtb"], writes=["x1sp"])
        else:
            P.dma("sp", x1sp[t_ * 128 - 64:t_ * 128 + 64, :], xtb, "x1st", reads=["xtb"], writes=["x1sp"])
        act(sqb, xtb, AF.Square, ["xtb"], ["sqb", "statb"], accum=statb[:, 2:3])
        rsq(statb[:, 3:4], statb[:, 2:3], 1.0 / D, EPS, ["statb"], ["statb"])
        ts(mixrow, xtb, statb[:, 3:4], None, ALU.mult, None, ["xtb", "statb"], ["mixrow"])
        for k4 in range(4):
            ps = psb[2 + k4 % 2]
            pn = PSN[2 + k4 % 2]
            for kk_ in range(4):
                kt = k4 * 4 + kk_
                trp(ps[:, kk_ * 128:(kk_ + 1) * 128], mixrow[:, kt * 128:(kt + 1) * 128], ident[:, :], ["mixrow", "ident"], [pn])
            for kk_ in range(4):
                kt = k4 * 4 + kk_
                if kk_ % 2 == 0:
                    act(hT[:, kt, tk], ps[:, kk_ * 128:(kk_ + 1) * 128], AF.Identity, [pn, "modcol"], ["hT"],
                        bias=modcol[:, 2, kt:kt + 1], scale=modcol[:, 3, kt:kt + 1])
                else:
                    ts(hT[:, kt, tk], ps[:, kk_ * 128:(kk_ + 1) * 128], modcol[:, 3, kt:kt + 1], modcol[:, 2, kt:kt + 1],
                       ALU.mult, ALU.add, [pn, "modcol"], ["hT"])
    B2res = ["oTb", "cand0", "woutb", "mixrow", "xtb", "A1row", "sqb", "statb"]
    B3 = Arena()
    B3.off = HT_W
    wgb = [B3.bf16(16 * 128).rearrange("p (k n) -> p k n", k=16) for _ in range(2)]
    wub = [B3.bf16(16 * 128).rearrange("p (k n) -> p k n", k=16) for _ in range(2)]
    gts = B3.f32(1152)
    cacc = B3.f32(1024)
    HID_OFF = ARENA_W - 22528
    assert B3.off <= HID_OFF
    hid = arena[:, HID_OFF:ARENA_W].bitcast(BF16).rearrange("p (k n) -> p k n", k=NFT)
    for nm in ["wgb0", "wgb1", "wub0", "wub1", "gts", "cacc", "hid"]:
        P.alias(nm, B2res)
    g3 = gts.rearrange("p (r c) -> p r c", c=64)
    a3 = cacc.rearrange("p (r c) -> p r c", c=64)
    for ft in range(NFT):
        wg, wu = wgb[ft % 2], wub[ft % 2]
        P.dma("pool", wg, wgate[ft], "wgk%d" % (ft % 2), writes=["wgb%d" % (ft % 2)])
        P.dma("pool", wu, wupf[ft], "wuk%d" % (ft % 2), writes=["wub%d" % (ft % 2)])
        for ch in range(3):
            ps = psb[ch]
            for kt in range(16):
                mm(ps[:, 0:384], wg[:, kt, :], hT[:, kt, ch * 384:(ch + 1) * 384], ["wgb%d" % (ft % 2), "hT"], [PSN[ch]],
                   start=(kt == 0), stop=(kt == 15))
            cp(gts[:, ch * 384:(ch + 1) * 384], ps[:, 0:384], [PSN[ch]], ["gts"])
        ts(gts[:, 0:64], gts[:, 0:64], hmask_s[:, 0:1], None, ALU.mult, None, ["gts", "hmask"], ["gts"])
        ts(gts[:, 1088:1152], gts[:, 1088:1152], hmask_s[:, 1:2], None, ALU.mult, None, ["gts", "hmask"], ["gts"])
        ts(a3, g3[:, 1:17, :], fconv_s[:, ft, 4:5], fconv_s[:, ft, 9:10], ALU.mult, ALU.add, ["gts", "fconv"], ["cacc"])
        for dr in range(3):
            for dc in range(3):
                if dr == 1 and dc == 1:
                    continue
                wcol = fconv_s[:, ft, dr * 3 + dc:dr * 3 + dc + 1]
                if dc == 1:
                    o_, i_ = a3, g3[:, dr:dr + 16, :]
                elif dc == 0:
                    o_, i_ = a3[:, :, 1:64], g3[:, dr:dr + 16, 0:63]
                else:
                    o_, i_ = a3[:, :, 0:63], g3[:, dr:dr + 16, 1:64]
                stt(o_, i_, wcol, o_, ALU.mult, ALU.add, ["gts", "fconv", "cacc"], ["cacc"])
        act(cacc, cacc, AF.Silu, ["cacc"], ["cacc"])
        for ch in range(2):
            ps = psb[3 + ch]
            for kt in range(16):
                mm(ps[:, :], wu[:, kt, :], hT[:, kt, 64 + ch * 512:64 + (ch + 1) * 512], ["wub%d" % (ft % 2), "hT"], [PSN[3 + ch]],
                   start=(kt == 0), stop=(kt == 15))
            tt(hid[:, ft, ch * 512:(ch + 1) * 512], ps[:, :], cacc[:, ch * 512:(ch + 1) * 512], ALU.mult, [PSN[3 + ch], "cacc"], ["hid"])
    yTf = arena[:, 0:16384].rearrange("p (k n) -> p k n", k=16)
    B4 = Arena()
    B4.off = 16384
    wdb = [B4.bf16(NFT * 128).rearrange("p (k n) -> p k n", k=NFT) for _ in range(2)]
    assert B4.off <= HID_OFF
    B3res = ["hT", "wgb0", "wgb1", "wub0", "wub1", "gts", "cacc"]
    for nm in ["yTf", "wdb0", "wdb1"]:
        P.alias(nm, B3res)
    for mt in range(16):
        wd = wdb[mt % 2]
        P.dma("pool", wd, wdown[mt], "wdk%d" % (mt % 2), writes=["wdb%d" % (mt % 2)])
        for ch in range(2):
            ps = psb[(mt * 2 + ch) % 4]
            pn = PSN[(mt * 2 + ch) % 4]
            for ft in range(NFT):
                mm(ps[:, :], wd[:, ft, :], hid[:, ft, ch * 512:(ch + 1) * 512], ["wdb%d" % (mt % 2), "hid"], [pn],
                   start=(ft == 0), stop=(ft == NFT - 1))
            cp(yTf[:, mt, ch * 512:(ch + 1) * 512], ps[:, :], [pn], ["yTf"])
    B5 = Arena()
    B5.off = 16384
    ytm = B5.f32(D)
    x1t = [B5.f32(D) for _ in range(2)]
    A2row = B5.f32(D)
    sq5 = B5.bf16(D)
    st5 = B5.f32(8)
    assert B5.off <= HID_OFF
    for nm in ["ytm", "x1t0", "x1t1", "A2row", "sq5", "st5"]:
        P.alias(nm, ["wdb0", "wdb1"])
    load(A2row, a12sp[:, D:2 * D].partition_broadcast(128), "a2k", ["A2row"], R=["a12sp"])
    for t_ in range(8):
        tk = slice(t_ * 128, (t_ + 1) * 128)
        xt_ = x1t[t_ % 2]
        xn = "x1t%d" % (t_ % 2)
        load(xt_, x1sp[tk, :], "x1ld%d" % (t_ % 2), [xn], R=["x1sp"])
        for k4 in range(4):
            ps = psb[4 + k4 % 2]
            pn = PSN[4 + k4 % 2]
            for kk_ in range(4):
                mt = k4 * 4 + kk_
                trp(ps[:, kk_ * 128:(kk_ + 1) * 128], yTf[:, mt, tk], ident[:, :], ["yTf", "ident"], [pn])
            cp(ytm[:, k4 * 512:(k4 + 1) * 512], ps[:, :], [pn], ["ytm"], eng=("act" if k4 % 2 == 0 else "dve"))
        act(sq5, ytm, AF.Square, ["ytm"], ["sq5", "st5"], accum=st5[:, 0:1])
        rsq(st5[:, 1:2], st5[:, 0:1], 1.0 / D, EPS, ["st5"], ["st5"])
        stt(ytm, ytm, st5[:, 1:2], A2row, ALU.mult, ALU.mult, ["ytm", "st5", "A2row"], ["ytm"])
        tt(xt_, xt_, ytm, ALU.add, [xn, "ytm"], [xn])
        P.dma("sp", out_d[tk, :], xt_, "outk%d" % (t_ % 2), reads=[xn], writes=["out_d"])
    P.emit(es, {"sp": [("outk0", P.dma_count["outk0"]), ("outk1", P.dma_count["outk1"])]})
    es.close()
    return nc


def _prep(inp):
    f = lambda a: np.ascontiguousarray(np.asarray(a, dtype=np.float32))
    x, c, ctx, c_ctx = f(inp["x"]), f(inp["c"]), f(inp["ctx"]), f(inp["c_ctx"])
    w_in = f(inp["w_in"][0])
    RW = 1024
    shared = {}
    shared["wada"] = np.ascontiguousarray(
        f(inp["w_ada"][0]).reshape(16, 128, 24, 512).transpose(2, 1, 0, 3))
    shared["bada"] = f(inp["b_ada"][0]).reshape(1, 12288)
    shared["nrm_col"] = np.ascontiguousarray(
        np.stack([_col(f(inp["norm_pre_mix"][0])), _col(f(inp["norm_pre_ffn"][0]))], axis=1))
    shared["nrm_row"] = np.stack([f(inp["norm_post_mix"][0]), f(inp["norm_post_ffn"][0])], axis=0)
    shared["wgate"] = np.ascontiguousarray(
        f(inp["ffn_w_gate"][0]).reshape(16, 128, NFT, 128).transpose(2, 1, 0, 3))
    shared["wupf"] = np.ascontiguousarray(
        f(inp["ffn_w_up"][0]).reshape(16, 128, NFT, 128).transpose(2, 1, 0, 3))
    shared["wdown"] = np.ascontiguousarray(
        f(inp["ffn_w_down"][0]).reshape(NFT, 128, 16, 128).transpose(2, 1, 0, 3))
    fc = np.concatenate([f(inp["ffn_conv"][0]).reshape(9, DFF), f(inp["ffn_conv_b"][0]).reshape(1, DFF)], axis=0)
    shared["fconv"] = np.ascontiguousarray(fc.reshape(10, NFT, 128).transpose(2, 1, 0))
    w_out = f(inp["w_out"][0])
    maps = []
    for core in range(8):
        b, g = core // 4, core % 4
        m = dict(shared)
        xa = np.zeros((SEQ + CTX + 4, D), np.float32)
        xa[1:257] = ctx[b]
        xa[259:259 + SEQ] = x[b]
        m["xa"] = xa
        xbm = np.zeros((1152, D), np.float32)
        lo, hi = g * 1024 - 64, g * 1024 + 1088
        slo, shi = max(lo, 0), min(hi, SEQ)
        xbm[slo - lo: shi - lo] = x[b, slo:shi]
        m["xb"] = xbm
        cv = np.stack([c[b], c_ctx], axis=1)
        m["cvt"] = np.ascontiguousarray(cv.reshape(16, 128, 2).transpose(1, 0, 2))
        cs = slice(g * 256, (g + 1) * 256)
        cols = np.concatenate([
            np.arange(0, RW)[cs], np.arange(RW, 2 * RW)[cs],
            np.arange(2304, 2304 + RW)[cs],
            np.arange(2048, 2304),
            np.arange(3328, 3488),
            3488 + np.arange(0, RW)[cs], 3488 + np.arange(RW, 2 * RW)[cs],
            3488 + np.arange(2 * RW, 3 * RW)[cs], 3488 + np.arange(3 * RW, 4 * RW)[cs]])
        assert cols.size == NCOLS
        m["win"] = _ktile(w_in[:, cols])
        rc = f(inp["rw_conv"][0])[:, cols[:1184]]
        rcp = np.zeros((3, 1280), np.float32)
        rcp[:, :1184] = rc
        m["rwconv"] = np.ascontiguousarray(rcp.reshape(3, 10, 128).transpose(2, 1, 0))
        cvs = np.zeros((128, 2, 9), np.float32)
        hs = slice(g * 256, (g + 1) * 256)
        vecs = [f(inp["rw_k_k"][0])[hs], f(inp["rw_k_a"][0])[hs], f(inp["rw_r_k"][0]).reshape(-1)[hs],
                f(inp["rw_lnx_w"][0])[hs], f(inp["rw_lnx_b"][0])[hs],
                f(inp["rw_a0"][0])[0, hs], f(inp["rw_a0"][0])[1, hs]]
        for vi, v in enumerate(vecs):
            cvs[:, :, vi] = v.reshape(2, 128).T
        m["colv"] = cvs
        m["w0row"] = np.concatenate([f(inp["rw_w0"][0])[0, hs], f(inp["rw_w0"][0])[1, hs]]).reshape(1, 512)
        m["wup"] = np.concatenate([f(inp["rw_w_up"][0])[0][:, hs], f(inp["rw_w_up"][0])[1][:, hs]], axis=0)
        m["aup"] = np.concatenate([f(inp["rw_a_up"][0])[0][:, hs], f(inp["rw_a_up"][0])[1][:, hs]], axis=0)
        m["gup"] = np.ascontiguousarray(f(inp["rw_g_up"][0])[:, hs])
        rows = np.concatenate([np.concatenate([np.arange(gp * 256, (gp + 1) * 256),
                                               1024 + np.arange(gp * 256, (gp + 1) * 256)]) for gp in range(4)])
        m["wout"] = _ktile(w_out[rows])
        hm = np.zeros((128, 2), np.float32)
        hm[:, 0] = 1.0 if g > 0 else 0.0
        hm[:, 1] = 1.0 if g < 3 else 0.0
        m["hmask"] = hm
        tsl = np.zeros((128, 8), np.float32)
        tsl[:, b * 4 + g] = 1.0
        m["tsel"] = tsl
        for k, v in _consts(g).items():
            m["c_" + k] = v
        maps.append(m)
    return maps


_NC_CACHE = {}
FUSED = False


def kernel(**inputs):
    import ml_dtypes
    maps = _prep(inputs)
    out = np.zeros((2, SEQ, D), np.float32)
    if FUSED:
        if "nc" not in _NC_CACHE:
            _NC_CACHE["nc"] = build()
        res = run_bass_kernel_spmd(_NC_CACHE["nc"], maps, core_ids=list(range(8)))
        for core in range(8):
            b, g = core // 4, core % 4
            out[b, g * 1024:(g + 1) * 1024] = np.asarray(res.results[core]["out"], dtype=np.float32)
        return out
    if "ncA" not in _NC_CACHE:
        _NC_CACHE["ncA"] = build(stop_after="A2L")
        _NC_CACHE["ncB"] = build(stop_after="B2L")
    dummy = {"oin": np.zeros((128, 16, 1152), ml_dtypes.bfloat16), "modcol_i": np.zeros((128, 96), np.float32),
             "a12_i": np.zeros((1, 2 * D), np.float32)}
    resA = run_bass_kernel_spmd(_NC_CACHE["ncA"], maps, core_ids=list(range(8)))
    osend = [np.asarray(resA.results[c]["osend"]) for c in range(8)]
    mapsB = []
    for core in range(8):
        b, g = core // 4, core % 4
        m = dict(maps[core])
        oin = np.zeros((128, 16, 1152), osend[0].dtype)
        lo, hi = g * 1024 - 64, g * 1024 + 1088
        slo, shi = max(lo, 0), min(hi, SEQ)
        for kt in range(16):
            src = osend[4 * b + kt // 4]
            j = kt % 4
            oin[:, kt, slo - lo:shi - lo] = src[j * 128:(j + 1) * 128, slo:shi]
        m["oin"] = oin
        m["modcol_i"] = np.asarray(resA.results[core]["modcol_o"])
        m["a12_i"] = np.asarray(resA.results[core]["a12_o"])
        mapsB.append(m)
    resB = run_bass_kernel_spmd(_NC_CACHE["ncB"], mapsB, core_ids=list(range(8)))
    for core in range(8):
        b, g = core // 4, core % 4
        out[b, g * 1024:(g + 1) * 1024] = np.asarray(resB.results[core]["out"], dtype=np.float32)
    return out
```

```python
import numpy as np
from contextlib import ExitStack
import concourse.bass as bass
import concourse.mybir as mybir
from concourse.bass_utils import run_bass_kernel_spmd

F32 = mybir.dt.float32
BF16 = mybir.dt.bfloat16
AF = mybir.ActivationFunctionType
ALU = mybir.AluOpType
AX = mybir.AxisListType

D = 2048
SEQ = 4096
CTX = 256
DFF = 5632
NFT = 44
EPS = 1e-6
LNX_EPS = 64e-5
BLK = 256
NWIN = 258
C = 64
NCOLS = 2208
DECAY_C = float(np.exp(-0.5))

DBG = {}


class Ev:
    __slots__ = ("kind", "eng", "idx", "key", "val", "needed")

    def __init__(self, kind, eng=None, idx=0, key=None, val=0):
        self.kind, self.eng, self.idx, self.key, self.val, self.needed = kind, eng, idx, key, val, False


class Prog:
    ENGS = ["pe", "act", "dve", "pool", "sp"]

    def __init__(self, nc):
        self.nc = nc
        self.stream = {e: [] for e in self.ENGS}
        self.last_write = {}
        self.readers = {}
        self.waited = {e: {} for e in self.ENGS}
        self.dma_count = {}
        self.dma_keys = []
        self.last_ev = {}
        self.trace_lines = False
        self.lines = {}
        self.imap = {}
        self.sfx = ""
        self.shared = set()

    def _m(self, names):
        if not self.sfx:
            return list(names)
        return [n if n in self.shared else n + self.sfx for n in names]

    def _deps(self, eng, reads, writes):
        evs = []
        for r in reads:
            w = self.last_write.get(r)
            if w is not None:
                evs.append(w)
        for r in writes:
            w = self.last_write.get(r)
            if w is not None:
                evs.append(w)
            evs.extend(self.readers.get(r, ()))
        best = {}
        for ev in evs:
            if ev.kind == "eng":
                if ev.eng == eng and eng == "pe":
                    continue
                k = ("eng", ev.eng)
                if k not in best or best[k].idx < ev.idx:
                    best[k] = ev
            else:
                k = ("dma", ev.key)
                if k not in best or best[k].val < ev.val:
                    best[k] = ev
        out = []
        for k, ev in best.items():
            cur = self.waited[eng].get(k, -1)
            v = ev.idx if ev.kind == "eng" else ev.val
            if cur >= v:
                continue
            self.waited[eng][k] = v
            ev.needed = True
            out.append(ev)
        return out

    def _commit(self, ev, reads, writes):
        for r in reads:
            self.readers.setdefault(r, []).append(ev)
        for r in writes:
            self.last_write[r] = ev
            self.readers[r] = []

    def op(self, eng, fn, reads=(), writes=()):
        reads, writes = self._m(reads), self._m(writes)
        waits = self._deps(eng, reads, writes)
        ev = Ev("eng", eng=eng, idx=len(self.stream[eng]))
        if self.trace_lines:
            import sys as _s
            fr = _s._getframe(2)
            self.lines[(eng, len(self.stream[eng]))] = (fr.f_lineno, fr.f_back.f_lineno if fr.f_back else 0)
        self.stream[eng].append((fn, waits, ev))
        self._commit(ev, reads, writes)
        self.last_ev[eng] = ev
        return ev

    def dma(self, eng, out, in_, key, reads=(), writes=(), **kw):
        reads, writes = self._m(reads), self._m(writes)
        key = key if (not self.sfx or key in ("const", "constp")) else key + self.sfx
        waits = self._deps(eng, reads, writes)
        if key not in self.dma_count:
            self.dma_count[key] = 0
            self.dma_keys.append(key)
        self.dma_count[key] += 16
        ev = Ev("dma", eng=eng, idx=len(self.stream[eng]), key=key, val=self.dma_count[key])
        self.stream[eng].append((lambda e: e.dma_start(out=out, in_=in_, **kw), waits, ev))
        self._commit(ev, reads, writes)
        return ev

    def custom(self, eng, fn, key, reads=(), writes=(), inc=1):
        waits = self._deps(eng, reads, writes)
        if key not in self.dma_count:
            self.dma_count[key] = 0
            self.dma_keys.append(key)
        self.dma_count[key] += inc
        ev = Ev("dma", eng=eng, idx=len(self.stream[eng]), key=key, val=self.dma_count[key])
        self.stream[eng].append((fn, waits, ev))
        self._commit(ev, reads, writes)
        return ev

    def alias(self, new, olds):
        new = self._m([new])[0]
        evs = []
        for o in olds:
            w = self.last_write.get(o)
            if w is not None:
                evs.append(w)
            evs.extend(self.readers.get(o, ()))
        self.readers.setdefault(new, []).extend(evs)

    def all_res(self):
        return list(set(list(self.last_write.keys()) + list(self.readers.keys())))

    def emit(self, es, final_waits):
        nc = self.nc
        LIM = 30000
        engobj = {"pe": nc.tensor, "act": nc.scalar, "dve": nc.vector, "pool": nc.gpsimd, "sp": nc.sync}
        esems = {}
        for e in self.ENGS:
            cnt = 0
            for (fn, waits, ev) in self.stream[e]:
                if ev.kind == "eng" and ev.needed:
                    cnt += 1
                    ev.val = cnt
            nsem = cnt // LIM + 1
            import os as _os
            if _os.environ.get("KCNT"):
                print("SEMCNT", e, cnt, "ninstr", len(self.stream[e]), "nwaits", sum(len(w) for (_, w, _) in self.stream[e]))
            esems[e] = [es.enter_context(nc.semaphore("s_%s_%d" % (e, i))) for i in range(nsem)]
        dsems = {k: es.enter_context(nc.semaphore("d_%s" % str(k))) for k in self.dma_keys}

        def semval(ev):
            if ev.kind == "eng":
                i = (ev.val - 1) // LIM
                return esems[ev.eng][i], ev.val - i * LIM
            if ev.key in ("const", "constp"):
                return dsems[ev.key], self.dma_count[ev.key]
            return dsems[ev.key], ev.val

        block = es.enter_context(nc.Block())
        streams = self.stream

        def run(e, eng):
            for ii, (fn, waits, ev) in enumerate(streams[e]):
                for w in waits:
                    s, v = semval(w)
                    eng.wait_ge(s, v)
                ins = fn(eng)
                if self.trace_lines:
                    try:
                        self.imap[str(ins.ins.name)] = self.lines.get((e, ii))
                    except Exception:
                        pass
                if ev.kind == "dma":
                    if ev.eng is not None and ev.key is not None:
                        inc = 16 if not str(ev.key).startswith("cc") else 1
                        ins.then_inc(dsems[ev.key], inc)
                elif ev.needed:
                    s, v = semval(ev)
                    ins.then_inc(s, 1)
            for (k, v) in final_waits.get(e, []):
                eng.wait_ge(dsems[k], v)

        @block.tensor
        def _(eng):
            run("pe", eng)

        @block.scalar
        def _(eng):
            run("act", eng)

        @block.vector
        def _(eng):
            run("dve", eng)

        @block.gpsimd
        def _(eng):
            run("pool", eng)

        @block.sync
        def _(eng):
            run("sp", eng)


def _consts(g):
    cst = {}
    cst["ident"] = np.eye(128, dtype=np.float32)
    bo = np.zeros((128, 128), np.float32)
    bo[:64, :64] = 1.0
    bo[64:, 64:] = 1.0
    cst["bones"] = bo
    cst["ones"] = np.ones((128, 128), np.float32)
    s = np.arange(128)[:, None]
    t = np.arange(128)[None, :]
    same = (s // 64) == (t // 64)
    tri = np.zeros((128, 4, 128), np.float32)
    tri[:, 0, :] = same & (s <= t)
    tri[:, 1, :] = same & (s < t)
    tri[:, 2, :] = same & (s >= t)
    tri[:, 3, :] = same & (s > t)
    cst["tri"] = tri
    s = np.arange(64)[:, None]
    t = np.arange(64)[None, :]
    mst = np.zeros((64, 2, 128), np.float32)
    mst[:, 0, :64] = s < t
    mst[:, 0, 64:] = s <= t
    mst[:, 1, :64] = s > t
    mst[:, 1, 64:] = s >= t
    cst["mst"] = mst
    m1 = np.zeros((64, 2, 64), np.float32)
    m1[:, 0, :] = (t < s).T.T
    tt = np.arange(64)[:, None]
    ss = np.arange(64)[None, :]
    m1[:, 0, :] = ss < tt
    m1[:, 1, :] = ss > tt
    cst["m1"] = m1
    j = np.arange(128)[:, None].astype(np.float64)
    i = np.arange(128)[None, :].astype(np.float64)
    dmask = np.zeros((128, 4, 128), np.float32)
    qsc = np.zeros((128, 4, 128), np.float32)
    ksc = np.zeros((128, 4), np.float32)
    gc = np.zeros((128, 2), np.float32)
    sc = 128.0 ** -0.5
    for hr in range(2):
        gam = 1.0 - 2.0 ** (-5.0 - (2 * g + hr))
        lg = np.log(gam)
        dmask[:, hr * 2 + 0, :] = np.where(j <= i, np.exp((i - j) * lg), 0.0) * sc
        dmask[:, hr * 2 + 1, :] = np.where(j > i, np.exp((j - i) * lg), 0.0) * sc
        qsc[:, hr * 2 + 0, :] = np.exp((i + 1.0) * lg)
        qsc[:, hr * 2 + 1, :] = np.exp((128.0 - i) * lg)
        ksc[:, hr * 2 + 0] = (np.exp((127.0 - j) * lg) * sc)[:, 0]
        ksc[:, hr * 2 + 1] = (np.exp(j * lg) * sc)[:, 0]
        gc[:, hr] = np.exp(128.0 * lg)
    cst["dmask"], cst["qsc"], cst["ksc"], cst["gc"] = dmask, qsc, ksc, gc
    n = 32
    inv = 10000.0 ** (-np.arange(n, dtype=np.float64) / n)
    tpos = np.arange(SEQ)
    ang = np.zeros((128, SEQ), np.float64)
    for d in range(128):
        if d < 64:
            ang[d] = (tpos // 64) * inv[d % 32]
        else:
            ang[d] = (tpos % 64) * inv[d % 32]
    cst["ropec"] = np.cos(ang).astype(np.float32)
    cst["ropes"] = np.sin(ang).astype(np.float32)
    pm = np.zeros((128, 128), np.float32)
    for dp in range(128):
        if (dp % 64) < 32:
            pm[dp + 32, dp] = -1.0
        else:
            pm[dp - 32, dp] = 1.0
    cst["pm"] = pm
    sel = np.zeros((2, 130), np.float32)
    sel[0, 0] = 1.0
    sel[1, 1] = 1.0
    sel[0, 2:] = 1.0
    cst["sel"] = sel
    return cst


def _ktile(w):
    K, N = w.shape
    return np.ascontiguousarray(w.reshape(K // 128, 128, N).transpose(1, 0, 2))


def _col(v):
    return np.ascontiguousarray(v.reshape(-1, 128).T)


def build(stop_after=None, dbg=False):
    nc = bass.Bass("TRN2", target_bir_lowering=False)
    P = Prog(nc)
    P.trace_lines = dbg
    DBG["P"] = P
    es = ExitStack()

    def din(name, shape, dt=F32):
        return nc.dram_tensor(name, list(shape), dt, kind="ExternalInput").ap()

    xa = din("xa", [SEQ + CTX + 4, D])
    xb = din("xb", [1152, D])
    cvt = din("cvt", [128, 16, 2])
    wada = din("wada", [24, 128, 16, 512])
    bada = din("bada", [1, 12288])
    nrm_col = din("nrm_col", [128, 2, 16])
    nrm_row = din("nrm_row", [2, D])
    win = din("win", [128, 16, NCOLS])
    rwconv = din("rwconv", [128, 10, 3])
    colv = din("colv", [128, 2, 9])
    w0row = din("w0row", [1, 2 * 256])
    wup = din("wup", [128, 256])
    aup = din("aup", [128, 256])
    gup = din("gup", [160, 256])
    wout = din("wout", [128, 16, D])
    wgate = din("wgate", [NFT, 128, 16, 128])
    wupf = din("wupf", [NFT, 128, 16, 128])
    wdown = din("wdown", [16, 128, NFT, 128])
    fconv = din("fconv", [128, NFT, 10])
    hmask = din("hmask", [128, 2])
    tsel = din("tsel", [128, 8])
    c_ident = din("c_ident", [128, 128])
    c_bones = din("c_bones", [128, 128])
    c_ones = din("c_ones", [128, 128])
    c_tri = din("c_tri", [128, 4, 128])
    c_mst = din("c_mst", [64, 2, 128])
    c_m1 = din("c_m1", [64, 2, 64])
    c_dmask = din("c_dmask", [128, 4, 128])
    c_qsc = din("c_qsc", [128, 4, 128])
    c_ksc = din("c_ksc", [128, 4])
    c_gc = din("c_gc", [128, 2])
    c_ropec = din("c_ropec", [128, SEQ])
    c_ropes = din("c_ropes", [128, SEQ])
    c_pm = din("c_pm", [128, 128])
    c_sel = din("c_sel", [2, 130])
    out_d = None
    yspill = nc.dram_tensor("yspill", [4, 128, SEQ], F32).ap()
    pspill = nc.dram_tensor("pspill", [17, 18, 128, BLK], F32).ap()
    TWO_A = stop_after == "A2L"
    TWO_B = stop_after == "B2L"
    osend = nc.dram_tensor("osend", [512, SEQ], BF16, kind=("ExternalOutput" if TWO_A else "Internal"))
    if not TWO_A:
        out_d = nc.dram_tensor("out", [1024, D], F32, kind="ExternalOutput").ap()
    if TWO_A:
        modcol_o = nc.dram_tensor("modcol_o", [128, 96], F32, kind="ExternalOutput").ap()
        a12_o = nc.dram_tensor("a12_o", [1, 2 * D], F32, kind="ExternalOutput").ap()
    if TWO_B:
        oin = din("oin", [128, 16, 1152], BF16)
        modcol_i = din("modcol_i", [128, 96])
        a12_i = din("a12_i", [1, 2 * D])
    ogath = nc.dram_tensor("ogath", [8 * 512, SEQ], BF16)
    x1sp = nc.dram_tensor("x1sp", [1024, D], F32).ap()
    dbg_d = None
    if dbg:
        dbg_d = nc.dram_tensor("dbg", [128, 4096], F32, kind="ExternalOutput").ap()

    def sb(name, shape, dt=F32):
        return es.enter_context(nc.sbuf_tensor(name, list(shape), dt))

    ident = sb("ident", [128, 128])
    bones = sb("bones", [128, 128])
    ones = sb("ones", [128, 128])
    tri = sb("tri", [128, 4, 128])
    mst = sb("mst", [64, 2, 128])
    m1m = sb("m1m", [64, 2, 64])
    dmask = sb("dmask", [128, 4, 128])
    qsc = sb("qsc", [128, 4, 128])
    ksc = sb("ksc", [128, 4])
    gcs = sb("gcs", [128, 2])
    pm = sb("pm", [128, 128])
    sel = sb("sel", [2, 130])
    nrmc = sb("nrmc", [128, 2, 16])
    rwcv = sb("rwcv", [128, 10, 3])
    colvs = sb("colvs", [128, 2, 9])
    w0bc = sb("w0bc", [128, 512])
    wup_s = sb("wup_s", [128, 256], BF16)
    aup_s = sb("aup_s", [128, 256], BF16)
    gup_a = sb("gup_a", [128, 256], BF16)
    gup_b = sb("gup_b", [32, 256], BF16)
    hmask_s = sb("hmask_s", [128, 2])
    tsel_s = sb("tsel_s", [128, 8])
    fconv_s = sb("fconv_s", [128, NFT, 10])
    modcol = sb("modcol", [128, 6, 16])
    ARENA_W = 48800
    arena = sb("arena", [128, ARENA_W])
    psb = [es.enter_context(nc.psum_tensor("psb%d" % i, [128, 512], F32)) for i in range(8)]

    class Arena:
        def __init__(self):
            self.off = 0

        def f32(self, n):
            o = self.off
            self.off += n
            assert self.off <= ARENA_W, self.off
            return arena[:, o:o + n]

        def bf16(self, n):
            w = (n + 1) // 2
            o = self.off
            self.off += w
            assert self.off <= ARENA_W, self.off
            return arena[:, o:o + w].bitcast(BF16)

    V, S, T, G, PE = "dve", "act", "pe", "pool", "pe"

    def tt(out, a, b, op, R, W, eng="dve"):
        P.op(eng, lambda e: e.tensor_tensor(out=out, in0=a, in1=b, op=op), R, W)

    def ts(out, a, s1, s2, op0, op1, R, W, eng="dve"):
        if s2 is None:
            P.op(eng, lambda e: e.tensor_scalar(out=out, in0=a, scalar1=s1, scalar2=None, op0=op0), R, W)
        else:
            P.op(eng, lambda e: e.tensor_scalar(out=out, in0=a, scalar1=s1, scalar2=s2, op0=op0, op1=op1), R, W)

    def stt(out, a, s, b, op0, op1, R, W, eng="dve"):
        P.op(eng, lambda e: e.scalar_tensor_tensor(out=out, in0=a, scalar=s, in1=b, op0=op0, op1=op1), R, W)

    def act(out, a, func, R, W, bias=None, scale=None, accum=None):
        kw = {}
        if bias is not None:
            kw["bias"] = bias
        if scale is not None:
            kw["scale"] = scale
        if accum is not None:
            kw["accum_out"] = accum
        P.op("act", lambda e: e.activation(out=out, in_=a, func=func, **kw), R, W)

    epsc = sb("epsc", [128, 4])
    identh = sb("identh", [128, 128], BF16)

    def rsq(out, a, scale, biasv, R, W):
        bi = {EPS: 0, LNX_EPS: 1, 1e-12: 2}[biasv]
        P.op("act", lambda e: e.activation(out=out, in_=a, func=AF.Sqrt, bias=epsc[0:out.shape[0], bi:bi + 1], scale=scale), list(R) + ["epsc"], W)
        P.op("dve", lambda e: e.reciprocal(out=out, in_=out), W, W)

    def cp(out, a, R, W, eng="act"):
        if eng == "act":
            P.op("act", lambda e: e.copy(out=out, in_=a), R, W)
        else:
            P.op(eng, lambda e: e.tensor_copy(out=out, in_=a), R, W)

    def mm(out, lhsT, rhs, R, W, start=True, stop=True):
        P.op("pe", lambda e: e.matmul(out, lhsT, rhs, start=start, stop=stop), R, W)

    def trp(out, in_, idn, R, W):
        P.op("pe", lambda e: e.transpose(out, in_, idn), R, W)

    def memset(ap, val, W, eng="dve"):
        P.op(eng, lambda e: e.memset(ap, val), (), W)

    dq = ["sp", "act"]
    dqi = [0]

    dq_default = ["sp"]

    def load(out, in_, key, W, R=(), eng=None):
        if eng is None:
            eng = dq_default[0]
        return P.dma(eng, out, in_, key, reads=R, writes=W)

    memset(epsc[:, 0:1], EPS, ["epsc"])
    memset(epsc[:, 1:2], LNX_EPS, ["epsc"])
    memset(epsc[:, 2:3], 1e-12, ["epsc"])
    CK = "const"
    for (dst, src, nm) in [(ident, c_ident, "ident"), (bones, c_bones, "bones"), (ones, c_ones, "ones"),
                           (tri, c_tri, "tri"), (mst, c_mst, "mst"), (m1m, c_m1, "m1m"), (dmask, c_dmask, "dmask"),
                           (qsc, c_qsc, "qsc"), (ksc, c_ksc, "ksc"), (gcs, c_gc, "gcs"), (pm, c_pm, "pm"),
                           (sel, c_sel, "sel"), (nrmc, nrm_col, "nrmc"), (rwcv, rwconv, "rwcv"),
                           (colvs, colv, "colvs"), (hmask_s, hmask, "hmask"), (tsel_s, tsel, "tsel"),
                           (fconv_s, fconv, "fconv")]:
        load(dst[:], src, CK, [nm])
    load(w0bc[:], w0row.partition_broadcast(128), CK, ["w0bc"])
    P.op("act", lambda e: e.copy(out=identh[:], in_=ident[:]), ["ident"], ["identh"])
    P.dma("pool", wup_s[:], wup, "constp", writes=["wup"])
    P.dma("pool", aup_s[:], aup, "constp", writes=["aup"])
    P.dma("pool", gup_a[:], gup[0:128, :], "constp", writes=["gupa"])
    P.dma("pool", gup_b[:], gup[128:160, :], "constp", writes=["gupb"])

    A0 = Arena()
    wa_buf = [A0.f32(16 * 512).rearrange("p (k n) -> p k n", k=16) for _ in range(2)]
    modrow = A0.f32(12288)
    badab = [A0.f32(512) for _ in range(2)]
    npost = A0.f32(2 * D).rearrange("p (a n) -> p a n", a=2)
    A12 = A0.f32(2 * D).rearrange("p (a n) -> p a n", a=2)
    a12sp = nc.dram_tensor("a12sp", [1, 2 * D], F32).ap()
    scT = sb("scT", [128, 16, 2])
    load(scT[:], cvt, CK, ["scT"])
    load(npost[:], nrm_row.rearrange("a n -> (a n)").partition_broadcast(128).rearrange("p (a n) -> p a n", a=2)
         if False else nrm_row.unsqueeze(0).to_broadcast([128, 2, D]), CK, ["npost"])
    act(scT[:], scT[:], AF.Silu, ["scT"], ["scT"])
    for n in range(24):
        wb = wa_buf[n % 2]
        load(wb, wada[n], "wada%d" % (n % 2), ["wab%d" % (n % 2)])
        load(badab[n % 2][0:2, :], bada[:, n * 512:(n + 1) * 512].partition_broadcast(2), "wada%d" % (n % 2), ["wab%d" % (n % 2)])
        ps = psb[n % 2]
        for kt in range(16):
            mm(ps[0:2, :], scT[:, kt, :], wb[:, kt, :], ["scT", "wab%d" % (n % 2)], ["psb%d" % (n % 2)],
               start=(kt == 0), stop=(kt == 15))
        tt(modrow[0:2, n * 512:(n + 1) * 512], ps[0:2, :], badab[n % 2][0:2, :], ALU.add,
           ["psb%d" % (n % 2), "wab%d" % (n % 2)], ["modrow"])
    segs = [(0, 0), (0, 1), (0, 3), (0, 4), (1, 0), (1, 1)]
    psc = psb[2]
    for i, (r, sg) in enumerate(segs):
        for kt in range(16):
            mm(psc[:, i * 16 + kt:i * 16 + kt + 1], modrow[0:2, sg * D + kt * 128: sg * D + (kt + 1) * 128],
               sel[0:2, r:r + 1], ["modrow", "sel"], ["psb2"])
    cp(modcol[:].rearrange("p a k -> p (a k)"), psc[:, 0:96], ["psb2"], ["modcol"], eng="dve")
    for (i, nidx) in [(1, 0), (3, 1), (5, 0)]:
        stt(modcol[:, i, :], modcol[:, i, :], 1.0, nrmc[:, nidx, :], ALU.add, ALU.mult, ["modcol", "nrmc"], ["modcol"])
    for a, sg in enumerate([2, 5]):
        for j in range(4):
            ps = psb[3 + (j % 2)]
            mm(ps[:, :], sel[0:2, 2:130], modrow[0:2, sg * D + j * 512: sg * D + (j + 1) * 512], ["modrow", "sel"],
               ["psb%d" % (3 + j % 2)])
            tt(A12[:, a, j * 512:(j + 1) * 512], ps[:, :], npost[:, a, j * 512:(j + 1) * 512], ALU.mult,
               ["psb%d" % (3 + j % 2), "npost"], ["A12"])
    P.dma("sp", a12sp, A12[0:1].rearrange("p a n -> p (a n)"), "a12st", reads=["A12"], writes=["a12sp"])
    if TWO_A:
        P.dma("sp", a12_o, A12[0:1].rearrange("p a n -> p (a n)"), "a12o", reads=["A12"], writes=["a12_o"])
    ph0_res = ["wab0", "wab1", "modrow", "badab", "npost", "A12"]

    if dbg and stop_after == 0:
        dtile = A0.f32(4096)
        memset(dtile[:], 0.0, ["dtile"])
        cp(dtile[:, 0:96], modcol[:].rearrange("p a k -> p (a k)"), ["modcol"], ["dtile"], eng="dve")
        cp(dtile[:, 128:128 + 2048], A12[:, 0, :], ["A12"], ["dtile"], eng="dve")
        ev = P.dma("sp", dbg_d, dtile[:], "dbgout", reads=["dtile"], writes=["dbg_d"])
        P.emit(es, {"sp": [("dbgout", 16)]})
        es.close()
        return nc

    PMAP = {0: 0, 1: 1, 2: 0, 3: 1, 4: 2, 5: 3, 6: 2, 7: 3}
    yspill_f = nc.dram_tensor("yspill_f", [4, 128, SEQ], F32).ap()
    yspill_b = nc.dram_tensor("yspill_b", [4, 128, SEQ], F32).ap()

    def make_thread(tid, TA):
        def PB(i):
            return psb[4 * tid + PMAP[i]]

        def PBN(i):
            return "PS%d" % (4 * tid + PMAP[i])

        vtm = TA.bf16(2 * 2 * 128).rearrange("p (c h n) -> p c h n", c=2, h=2)
        rwT = TA.f32(10 * BLK).rearrange("p (k n) -> p k n", k=10)
        retT = TA.f32(8 * BLK).rearrange("p (k n) -> p k n", k=8)
        stat = TA.f32(8)

        def t2(dt=F32):
            if dt == F32:
                return TA.f32(2 * BLK).rearrange("p (k n) -> p k n", k=2)
            return TA.bf16(2 * BLK).rearrange("p (k n) -> p k n", k=2)

        thb = TA.bf16(BLK)
        adb = TA.bf16(BLK)
        gsb = TA.bf16(2 * BLK).rearrange("p (k n) -> p k n", k=2)
        sigtm = t2()
        t2base = TA.off
        cumT, cumpT, aT, kkT, tmpA, tmpB, Ep, Em, Epv, kdT, BhT, KhT, yT = [t2() for _ in range(13)]
        rwT2 = arena[:, t2base:t2base + 10 * BLK].rearrange("p (k n) -> p k n", k=10)
        retT2 = arena[:, t2base + 10 * BLK:t2base + 18 * BLK].rearrange("p (k n) -> p k n", k=8)
        xts3 = [xts[0], xts[1], arena[:, t2base + 18 * BLK:t2base + 18 * BLK + D]]
        aT2, kd2T = tmpB, BhT
        ART = TA.bf16(2 * 4 * 128).rearrange("p (k c n) -> p k c n", k=2, c=4)
        BKT = TA.bf16(2 * 4 * 128).rearrange("p (k c n) -> p k c n", k=2, c=4)
        BKV = TA.bf16(4 * 3 * 2 * 2 * 128).rearrange("p (c m k h n) -> p c m k h n", c=4, m=3, k=2, h=2)
        Ms2 = TA.bf16(2 * 4 * 2 * 128).rearrange("p (c h m n) -> p c h m n", c=2, h=4, m=2)
        M1s2 = TA.bf16(2 * 4 * 64).rearrange("p (c h n) -> p c h n", c=2, h=4)
        Pa = [TA.bf16(2 * 4 * 2 * 64).rearrange("p (c h m n) -> p c h m n", c=2, h=4, m=2) for _ in range(2)]
        Qs = [TA.bf16(2 * 4 * 64).rearrange("p (c h n) -> p c h n", c=2, h=4) for _ in range(2)]
        Wsb = TA.bf16(2 * 128).rearrange("p (k n) -> p k n", k=2)
        Usb = TA.bf16(2 * 2 * 128).rearrange("p (k h n) -> p k h n", k=2, h=2)
        Hst = TA.f32(2 * 128).rearrange("p (k n) -> p k n", k=2)
        Hb = TA.bf16(2 * 128).rearrange("p (k n) -> p k n", k=2)
        qr = t2(BF16)
        kr = t2(BF16)
        qtl = t2(BF16)
        rc_t = TA.f32(BLK)
        rs_t = TA.f32(BLK)
        ktm = TA.bf16(2 * 2 * 128).rearrange("p (c h n) -> p c h n", c=2, h=2)
        Ssb = TA.bf16(2 * 128).rearrange("p (h n) -> p h n", h=2)
        Sst = TA.f32(2 * 128).rearrange("p (h n) -> p h n", h=2)
        Sbf = TA.bf16(2 * 128).rearrange("p (h n) -> p h n", h=2)
        yrT = t2()
        vrb = t2(BF16)
        ysp_flat = TA.f32(4 * BLK)
        ysp = ysp_flat.rearrange("p (k n) -> p k n", k=4)
        sqj = ysp_flat.bitcast(BF16)
        print("TA.off", TA.off)
        oT = BhT.rearrange("p k n -> p (k n)").bitcast(BF16).rearrange("p (k n) -> p k n", k=4)

        tsize = TA.off

        def init():
            memset(BKV[0:64], 0.0, ["BKV"])
            memset(Usb[0:64], 0.0, ["Usb"])
            memset(Hst[:], 0.0, ["Hst"])
            memset(Hb[:], 0.0, ["Hb"])
            memset(Sst[:], 0.0, ["Sst"])
            memset(Sbf[:], 0.0, ["Sbf"])

        def front(seq, bi, nblk, alt=False):
            sw = 0
            rwT_, retT_, rn, tn = (rwT2, retT2, "rwTalt", "retTalt") if alt else (rwT, retT, "rwT", "retT")
            lat = seq == "l"
            ro = lat
            row0 = (0 if not lat else 258) + bi * BLK
            s1i, shi_ = (5, 4) if not lat else (1, 0)
            blk_i = 0 if not lat else bi + 1
            nrw = 9 if ro else 8
            nrt = 8 if ro else 4
            for j in range(3):
                nr = 128 if j < 2 else 2
                xt = xts3[j]
                xn = "xt%d" % j
                load(xt[0:nr, :], xa[row0 + j * 128: row0 + j * 128 + nr, :], "xk%d" % j, [xn])
                act(sqj[0:nr, :], xt[0:nr, :], AF.Square, [xn], ["ysp", "stat"], accum=stat[0:nr, 0:1])
                rsq(stat[0:nr, 1:2], stat[0:nr, 0:1], 1.0 / D, EPS, ["stat"], ["stat"])
                ts(xt[0:nr, :], xt[0:nr, :], stat[0:nr, 1:2], None, ALU.mult, None, [xn, "stat"], [xn])
                for k4 in range(4):
                    ps = psb[(j * 4 + k4) % 8]
                    pn = "PS%d" % ((j * 4 + k4) % 8)
                    for kk_ in range(4):
                        kt = k4 * 4 + kk_
                        trp(ps[:, kk_ * 128: kk_ * 128 + nr], xt[0:nr, kt * 128:(kt + 1) * 128], ident[0:nr, 0:nr], [xn, "ident"], [pn])
                    for kk_ in range(4):
                        kt = k4 * 4 + kk_
                        if kk_ % 2 == 0:
                            act(xmT[:, kt, j * 128: j * 128 + nr], ps[:, kk_ * 128: kk_ * 128 + nr], AF.Identity, [pn, "modcol"], ["xmT"],
                                bias=modcol[:, shi_, kt:kt + 1], scale=modcol[:, s1i, kt:kt + 1])
                        else:
                            ts(xmT[:, kt, j * 128: j * 128 + nr], ps[:, kk_ * 128: kk_ * 128 + nr], modcol[:, s1i, kt:kt + 1],
                               modcol[:, shi_, kt:kt + 1], ALU.mult, ALU.add, [pn, "modcol"], ["xmT"])
            if bi == 0:
                memset(xmT[:, :, 0:1], 0.0, ["xmT"])
            if bi == nblk - 1:
                memset(xmT[:, :, 257:258], 0.0, ["xmT"])
            tiles = [0, 1, 2, 3, 4, 5, 6, 7, 8, 9, 10, 11, 12, 13, 14, 15, 16, 17] if ro else [0, 1, 2, 3, 4, 5, 6, 7, 10, 11, 12, 13]
            for ti, mt in enumerate(tiles):
                ps = psb[ti % 8]
                pn = "PS%d" % (ti % 8)
                if mt < 9:
                    c0, mw = mt * 128, 128
                elif mt == 9:
                    c0, mw = 1152, 32
                else:
                    c0, mw = 1184 + (mt - 10) * 128, 128
                for kt in range(16):
                    mm(ps[0:mw, 0:NWIN], winb[:, kt, c0:c0 + mw], xmT[:, kt, :], ["winb", "xmT"], [pn], start=(kt == 0), stop=(kt == 15))
                if mt < 10:
                    ts(rwT_[0:mw, mt, :], ps[0:mw, 1:257], rwcv[0:mw, mt, 1:2], None, ALU.mult, None, [pn, "rwcv"], [rn])
                    stt(rwT_[0:mw, mt, :], ps[0:mw, 0:256], rwcv[0:mw, mt, 0:1], rwT_[0:mw, mt, :], ALU.mult, ALU.add, [pn, "rwcv", rn], [rn])
                    stt(rwT_[0:mw, mt, :], ps[0:mw, 2:258], rwcv[0:mw, mt, 2:3], rwT_[0:mw, mt, :], ALU.mult, ALU.add, [pn, "rwcv", rn], [rn])
                else:
                    cp(retT_[:, mt - 10, :], ps[:, 1:257], [pn], [tn])

            P.dma("pool", pspill[blk_i, 0:nrw].rearrange("k p n -> p k n"), rwT_[:, 0:nrw, :], ("psa" + str(int(alt))), reads=[rn], writes=["pspA%d" % blk_i])
            if ro:
                P.dma("pool", pspill[blk_i, 9, 0:32, :], rwT_[0:32, 9, :], ("psb_" + str(int(alt))), reads=[rn], writes=["pspB%d" % blk_i])
            P.dma("pool", pspill[blk_i, 10:10 + nrt].rearrange("k p n -> p k n"), retT_[:, 0:nrt, :], ("psc" + str(int(alt))), reads=[tn], writes=["pspC%d" % blk_i])


        def block(sw, seq, bi, nblk):
            lat = seq == "l"
            ro = lat
            row0 = (0 if not lat else 258) + bi * BLK
            s1i, shi_ = (5, 4) if not lat else (1, 0)
            blk_i = 0 if not lat else bi + 1
            nrw = 9 if ro else 8
            nrt = 8 if ro else 4
            load(rwT[:, 0:nrw, :], pspill[blk_i, 0:nrw].rearrange("k p n -> p k n"), "pla", ["rwT"], R=["pspA%d" % blk_i])
            if ro:
                load(rwT[0:32, 9, :], pspill[blk_i, 9, 0:32, :], "plb", ["rwT"], R=["pspB%d" % blk_i])
            load(retT[:, 0:nrt, :], pspill[blk_i, 10:10 + nrt].rearrange("k p n -> p k n"), "plc", ["retT"], R=["pspC%d" % blk_i])
            yield
            dp = slice(0, 64) if sw == 0 else slice(64, 128)
            v4 = lambda ap: ap.rearrange("p (c n) -> p c n", c=4)
            cend = 63 if sw == 0 else 0
            act(thb[dp, :], rwT[dp, 6, :], AF.Tanh, ["rwT"], ["thb"])
            cp(adb[:, :], rwT[:, 7, :], ["rwT"], ["adb"])
            for t_ in range(2):
                ps = PB(2 + t_)
                pn = PBN(2 + t_)
                mm(ps[:, 0:256], thb[dp, t_ * 128:(t_ + 1) * 128], wup_s[dp, :], ["thb", "wup"], [pn])
                tt(sigtm[:, t_, :], ps[:, 0:256], w0bc[:, sw * 256:(sw + 1) * 256], ALU.add, [pn, "w0bc"], ["sigtm"])
                yield
            act(sigtm[:], sigtm[:], AF.Sigmoid, ["sigtm"], ["sigtm"])
            for which, dst, dn in [(0, cumT, "cumT"), (1, cumpT, "cumpT")]:
                ps = PB(2 + which)
                pn = PBN(2 + which)
                for ct in range(2):
                    for t_ in range(2):
                        mm(ps[:, ct * 256 + t_ * 128: ct * 256 + (t_ + 1) * 128], sigtm[:, t_, ct * 128:(ct + 1) * 128],
                           tri[:, 2 * sw + which, :], ["sigtm", "tri"], [pn])
                cp(dst[:].rearrange("p k n -> p (k n)"), ps[:, :], [pn], [dn])
                yield
            act(Ep[:], cumT[:], AF.Exp, ["cumT"], ["Ep"], scale=-DECAY_C)
            act(Em[:], cumT[:], AF.Exp, ["cumT"], ["Em"], scale=DECAY_C)
            act(Epv[:], cumpT[:], AF.Exp, ["cumpT"], ["Epv"], scale=-DECAY_C)
            yield
            for ct in range(2):
                cs_ = slice(ct * 128, (ct + 1) * 128)
                ps = PB(2 + ct)
                pn = PBN(2 + ct)
                mm(ps[:, 0:256], aup_s[dp, cs_], adb[dp, :], ["aup", "adb"], [pn])
                act(aT[:, ct, :], ps[:, 0:256], AF.Sigmoid, [pn, "colvs"], ["aT"], bias=colvs[:, ct, 5 + sw:6 + sw])
                yield
                ts(tmpA[:, ct, :], rwT[:, ct, :], colvs[:, ct, 0:1], None, ALU.mult, None, ["rwT", "colvs"], ["tmpA"])
                tt(tmpB[:, ct, :], tmpA[:, ct, :], tmpA[:, ct, :], ALU.mult, ["tmpA"], ["tmpB"])
                mm(ps[:, 256:512], bones[:, :], tmpB[:, ct, :], ["bones", "tmpB"], [pn])
                rsq(tmpB[:, ct, :], ps[:, 256:512], 1.0, 1e-12, [pn], ["tmpB"])
                yield
                tt(kkT[:, ct, :], tmpA[:, ct, :], tmpB[:, ct, :], ALU.mult, ["tmpA", "tmpB"], ["kkT"])
                stt(ART[:, ct, :, 0:64], v4(kkT[:, ct, :]), -1.0, v4(Epv[:, ct, :]), ALU.mult, ALU.mult, ["kkT", "Epv"], ["ART"])
                tt(ART[:, ct, :, 64:128], v4(rwT[:, 4 + ct, :]), v4(Ep[:, ct, :]), ALU.mult, ["rwT", "Ep"], ["ART"])
                gcb = v4(Ep[:, ct, :])[:, :, cend:cend + 1].to_broadcast([128, 4, 64])
                tt(tmpB[:, ct, :], kkT[:, ct, :], aT[:, ct, :], ALU.mult, ["kkT", "aT"], ["tmpB"])
                tt(BhT[:, ct, :], tmpB[:, ct, :], Em[:, ct, :], ALU.mult, ["tmpB", "Em"], ["BhT"])
                cp(BKT[:, ct, :, 0:64], v4(BhT[:, ct, :]), ["BhT"], ["BKT"])
                tt(v4(BhT[:, ct, :]), v4(BhT[:, ct, :]), gcb, ALU.mult, ["BhT", "Ep"], ["BhT"])
                ts(tmpA[:, ct, :], aT[:, ct, :], colvs[:, ct, 1:2], colvs[:, ct, 7:8], ALU.mult, ALU.add, ["aT", "colvs"], ["tmpA"])
                tt(kdT[:, ct, :], rwT[:, ct, :], tmpA[:, ct, :], ALU.mult, ["rwT", "tmpA"], ["kdT"])
                tt(KhT[:, ct, :], kdT[:, ct, :], Em[:, ct, :], ALU.mult, ["kdT", "Em"], ["KhT"])
                cp(BKT[:, ct, :, 64:128], v4(KhT[:, ct, :]), ["KhT"], ["BKT"])
                tt(v4(KhT[:, ct, :]), v4(KhT[:, ct, :]), gcb, ALU.mult, ["KhT", "Ep"], ["KhT"])
                yield
            if dbg and stop_after == 2.1:
                return
            for c in range(4):
                for ct in range(2):
                    ps = PB(2 + ct)
                    pn = PBN(2 + ct)
                    for m_, (src, sn) in enumerate([(BhT[:, ct, :], "BhT"), (KhT[:, ct, :], "KhT"), (rwT[:, 2 + ct, :], "rwT")]):
                        trp(ps[0:64, m_ * 128:(m_ + 1) * 128], src[:, c * 64:(c + 1) * 64], ident[:, :], [sn, "ident"], [pn])
                    pv = ps[0:64, 0:384].rearrange("p (m n) -> p m n", m=3)
                    for hh in range(2):
                        cp(BKV[0:64, c, :, ct, hh, hh * 64:(hh + 1) * 64], pv[:, :, hh * 64:(hh + 1) * 64], [pn], ["BKV"],
                           eng=("act" if hh == 0 else "dve"))
                yield
            if dbg and stop_after == 2.2:
                return
            cols = slice(bi * BLK, (bi + 1) * BLK)

            def ret_steps():
                cols = slice(bi * BLK, (bi + 1) * BLK)
                if lat:
                    load(rc_t, c_ropec[:, cols], "ropek", ["rope"])
                    load(rs_t, c_ropes[:, cols], "ropek", ["rope"])
                    for hr in range(2):
                        for isk, src_t in [(True, retT[:, hr, :]), (False, retT[:, 4 + hr, :])]:
                            ps = PB(2 + (0 if isk else 1))
                            pn = PBN(2 + (0 if isk else 1))
                            mm(ps[:, 0:256], pm[:, :], src_t, ["pm", "retT"], [pn])
                            tt(tmpA[:, 0, :], ps[:, 0:256], rs_t, ALU.mult, [pn, "rope"], ["tmpA"])
                            tt(tmpB[:, 0, :], src_t, rc_t, ALU.mult, ["retT", "rope"], ["tmpB"])
                            if isk:
                                tt(kr[:, hr, :], tmpA[:, 0, :], tmpB[:, 0, :], ALU.add, ["tmpA", "tmpB"], ["kr"])
                            else:
                                tt(qr[:, hr, :], tmpA[:, 0, :], tmpB[:, 0, :], ALU.add, ["tmpA", "tmpB"], ["qr"])
                        yield
                else:
                    for hr in range(2):
                        cp(kr[:, hr, :], retT[:, hr, :], ["retT"], ["kr"])
                if dbg and stop_after == 2.61:
                    return
                for c2 in range(2):
                    cc = slice(c2 * 128, (c2 + 1) * 128)
                    for hr in range(2):
                        ps = PB(2 + hr)
                        pn = PBN(2 + hr)
                        cp(vrb[:, hr, cc], retT[:, 2 + hr, cc], ["retT"], ["vrb"])
                        mm(ps[:, 0:128], vrb[:, hr, cc], identh[:, :], ["vrb", "identh"], [pn])
                        mm(ps[:, 128:256], kr[:, hr, cc], identh[:, :], ["kr", "identh"], [pn])
                        import os as _os
                        cp(vtm[:, c2, hr, :], ps[:, 0:128], [pn], ["vtm"], eng="dve")
                        if True:
                            ts(ktm[:, c2, hr, :], ps[:, 128:256], ksc[:, hr * 2 + sw:hr * 2 + sw + 1], None, ALU.mult, None, [pn, "ksc"], ["ktm"])
                        yield
                if dbg and stop_after == 2.62:
                    return
                for c2 in (range(2) if sw == 0 else (1, 0)):
                    cc = slice(c2 * 128, (c2 + 1) * 128)
                    for hr in range(2):
                        hc = slice(hr * 128, (hr + 1) * 128)
                        if ro:
                            mm(PB(2)[:, hc], kr[:, hr, cc], qr[:, hr, cc], ["kr", "qr"], [PBN(2)])
                            tt(Ssb[:, hr, :], PB(2)[:, hc], dmask[:, hr * 2 + sw, :], ALU.mult, [PBN(2), "dmask"], ["Ssb"])
                            tt(qtl[:, hr, cc], qr[:, hr, cc], qsc[:, hr * 2 + sw, :], ALU.mult, ["qr", "qsc"], ["qtl"])
                            yield
                            mm(PB(3)[:, hc], vtm[:, c2, hr, :], Ssb[:, hr, :], ["vtm", "Ssb"], [PBN(3)], start=True, stop=False)
                            mm(PB(3)[:, hc], Sbf[:, hr, :], qtl[:, hr, cc], ["Sbf", "qtl"], [PBN(3)], start=False, stop=True)
                            cp(yrT[:, hr, cc], PB(3)[:, hc], [PBN(3)], ["yrT"])
                            yield
                        mm(PB(0)[:, hc], ktm[:, c2, hr, :], vtm[:, c2, hr, :], ["ktm", "vtm"], [PBN(0)])
                        stt(Sst[:, hr, :], Sst[:, hr, :], gcs[:, hr:hr + 1], PB(0)[:, hc], ALU.mult, ALU.add, ["Sst", "gcs", PBN(0)], ["Sst"])
                        cp(Sbf[:, hr, :], Sst[:, hr, :], ["Sst"], ["Sbf"])
                        yield
                yield

            rg_ = ret_steps()

            def rstep():
                try:
                    next(rg_)
                except StopIteration:
                    pass
            chunk_order = list(range(4)) if sw == 0 else [3, 2, 1, 0]
            for pr in range(2):
              pair = chunk_order[2 * pr:2 * pr + 2]
              for ci, c in enumerate(pair):
                for h in range(4):
                    ct, hh = h // 2, h % 2
                    hp = slice(hh * 64, hh * 64 + 64)
                    pb = PB(4 + hh)
                    mm(pb[0:64, (ct * 2) * 128:(ct * 2 + 1) * 128], BKT[hp, ct, c, 0:64], ART[hp, ct, c, :], ["BKT", "ART"], [PBN(4 + hh)])
                    mm(pb[0:64, (ct * 2 + 1) * 128:(ct * 2 + 2) * 128], BKT[hp, ct, c, 64:128], ART[hp, ct, c, :], ["BKT", "ART"], [PBN(4 + hh)])
                    mm(PB(2 + hh)[0:64, 256 + ci * 128 + ct * 64:256 + ci * 128 + (ct + 1) * 64], ART[hp, ct, c, 0:64], BKT[hp, ct, c, 0:64], ["BKT", "ART"], [PBN(2 + hh)])
                for hh in range(2):
                    tt(Ms2[0:64, ci, 2 * hh:2 * hh + 2, :, :].rearrange("p h m n -> p (h m) n"),
                       PB(4 + hh)[0:64, :].rearrange("p (a n) -> p a n", a=4),
                       mst[:, sw, :].unsqueeze(1).to_broadcast([64, 4, 128]), ALU.mult, [PBN(4 + hh), "mst"], ["Ms"])
                yield
              for hh in range(2):
                tt(M1s2[0:64, :, 2 * hh:2 * hh + 2, :], PB(2 + hh)[0:64, 256:512].rearrange("p (c a n) -> p c a n", c=2, a=2),
                   m1m[:, sw, :].unsqueeze(1).unsqueeze(1).to_broadcast([64, 2, 2, 64]), ALU.mult, [PBN(2 + hh), "m1m"], ["M1s"])
                yield
              identb = ident[0:64, 0:64].unsqueeze(1).unsqueeze(1).to_broadcast([64, 2, 4, 64])
              tt(Qs[0][0:64], Ms2[0:64, :, :, 0, 0:64], identb, ALU.add, ["Ms", "ident"], ["Qs0"])
              pN = [PB(6 + ci)[0:64, :].rearrange("p (h m n) -> p h m n", h=4, m=2) for ci in range(2)]
              pQ = PB(3)[0:64, :].rearrange("p (c h n) -> p c h n", c=2, h=4)
              for k in range(1, 6):
                cur, prv = k % 2, (k - 1) % 2
                for ci in range(2):
                    for h in range(4):
                        if k == 1:
                            Pp, Ppp = Ms2[0:64, ci, h, 0, 0:64], M1s2[0:64, ci, h, :]
                            rn_ = ["Ms", "M1s"]
                        else:
                            Pp, Ppp = Pa[prv][0:64, ci, h, 0, :], Pa[prv][0:64, ci, h, 1, :]
                            rn_ = ["Pa%d" % prv]
                        mm(pN[ci][:, h, 0, :], Ppp, Pp, rn_, [PBN(6 + ci)])
                        mm(pN[ci][:, h, 1, :], Pp, Ppp, rn_, [PBN(6 + ci)])
                for ci in range(2):
                    cp(Pa[cur][0:64, ci].rearrange("p h m n -> p (h m n)"), PB(6 + ci)[0:64, :], [PBN(6 + ci)], ["Pa%d" % cur],
                       eng=("act" if ci == 0 else "dve"))
                    yield
                for ci in range(2):
                    for h in range(4):
                        mm(pQ[:, ci, h, :], Pa[cur][0:64, ci, h, 1, :], Qs[prv][0:64, ci, h, :], ["Pa%d" % cur, "Qs%d" % prv], [PBN(3)])
                tt(Qs[cur][0:64], pQ, Qs[prv][0:64], ALU.add, [PBN(3), "Qs%d" % prv], ["Qs%d" % cur])
                yield
                rstep()
                yield
              for ci, c in enumerate(pair):
                Ms = Ms2[:, ci]
                TT = Qs[1][:, ci]
                pW = PB(2)[0:64, 0:256].rearrange("p (k n) -> p k n", k=2)
                for ct in range(2):
                    mm(pW[:, ct, :], ART[:, ct, c, 0:64], Hb[:, ct, :], ["ART", "Hb"], [PBN(2)], start=True, stop=False)
                    for hh in range(2):
                        mm(pW[:, ct, :], Ms[0:64, hh * 2 + ct, 1, 0:64], BKV[0:64, c, 2, ct, hh, :], ["Ms", "BKV"], [PBN(2)],
                           start=False, stop=(hh == 1))
                cp(Wsb[0:64, :, :], pW, [PBN(2)], ["Wsb"])
                yield
                rstep()
                yield
                pU = PB(3)[0:64, 0:256].rearrange("p (k n) -> p k n", k=2)
                for ct in range(2):
                    for hh in range(2):
                        mm(pU[:, ct, hh * 64:(hh + 1) * 64], TT[0:64, hh * 2 + ct, :], Wsb[0:64, ct, hh * 64:(hh + 1) * 64],
                           ["Qs1", "Wsb"], [PBN(3)])
                for hh in range(2):
                    cp(Usb[0:64, :, hh, hh * 64:(hh + 1) * 64], pU[:, :, hh * 64:(hh + 1) * 64], [PBN(3)], ["Usb"],
                       eng=("act" if hh == 0 else "dve"))
                    yield
                if ro:
                    pY = PB(0)[:, 0:128].rearrange("p (k n) -> p k n", k=2)
                    for ct in range(2):
                        mm(pY[:, ct, :], Hb[:, ct, :], ART[:, ct, c, 64:128], ["Hb", "ART"], [PBN(0)], start=True, stop=False)
                        for hh in range(2):
                            mm(pY[:, ct, :], Usb[0:64, ct, hh, :], Ms[0:64, hh * 2 + ct, 0, 64:128], ["Usb", "Ms"], [PBN(0)],
                               start=False, stop=False)
                            mm(pY[:, ct, :], BKV[0:64, c, 2, ct, hh, :], Ms[0:64, hh * 2 + ct, 1, 64:128], ["BKV", "Ms"], [PBN(0)],
                               start=False, stop=(hh == 1))
                    cp(yT[:, :, c * 64:(c + 1) * 64], pY, [PBN(0)], ["yT"])
                    yield
                pH = PB(1)[:, 0:256].rearrange("p (k n) -> p k n", k=2)
                for ct in range(2):
                    for hh in range(2):
                        mm(pH[:, ct, :], BKV[0:64, c, 0, ct, hh, :], Usb[0:64, ct, hh, :], ["BKV", "Usb"], [PBN(1)],
                           start=(hh == 0), stop=False)
                        mm(pH[:, ct, :], BKV[0:64, c, 1, ct, hh, :], BKV[0:64, c, 2, ct, hh, :], ["BKV"], [PBN(1)],
                           start=False, stop=(hh == 1))
                    gcol = Ep[:, ct, c * 64 + cend: c * 64 + cend + 1]
                    stt(Hst[:, ct, :], Hst[:, ct, :], gcol, pH[:, ct, :], ALU.mult, ALU.add, ["Hst", "Ep", PBN(1)], ["Hst"])
                    yield
                cp(Hb[:], Hst[:], ["Hst"], ["Hb"])
                rstep()
                yield
            if dbg and stop_after == 2 and seq == "l" and bi == 0 and sw == 0:
                return
            if dbg and stop_after == 2.5:
                return
            for _ in rg_:
                yield
            if dbg and stop_after == 2.6:
                return
            if not lat:
                return
            do_post = (bi >= 8) if sw == 0 else (bi <= 7)
            ys_own = (yspill_f if sw == 0 else yspill_b)[:, :, cols].rearrange("k p n -> p k n")
            ys_oth = (yspill_b if sw == 0 else yspill_f)[:, :, cols].rearrange("k p n -> p k n")
            own_n = ("YSf%d" if sw == 0 else "YSb%d") % bi
            oth_n = ("YSb%d" if sw == 0 else "YSf%d") % bi
            if not do_post:
                P.dma(dq_default[0], ys_own[:, 0:2, :], yT[:], "yst0", reads=["yT"], writes=[own_n + "a"])
                P.dma(dq_default[0], ys_own[:, 2:4, :], yrT[:], "yst1", reads=["yrT"], writes=[own_n + "b"])
                return
            assert (oth_n + "a") in P.last_write and (oth_n + "b") in P.last_write, oth_n
            load(ysp[:], ys_oth, "yld", ["ysp"], R=[oth_n + "a", oth_n + "b"])
            act(gsb[:, 0, :], rwT[:, 8, :], AF.Sigmoid, ["rwT"], ["gsb"])
            act(gsb[0:32, 1, :], rwT[0:32, 9, :], AF.Sigmoid, ["rwT"], ["gsb"])
            od = slice(64, 128) if sw == 0 else slice(0, 64)
            osw = 1 - sw
            for ct in range(2):
                cs_ = slice(ct * 128, (ct + 1) * 128)
                ps = PB(2 + ct)
                pn = PBN(2 + ct)
                y = cumT[:, ct, :]
                sq = cumpT[:, ct, :]
                bn = Ep[:, ct, :]
                tt(y, yT[:, ct, :], ysp[:, ct, :], ALU.add, ["yT", "ysp"], ["cumT"])
                mm(ps[:, 0:256], bones[:, :], y, ["bones", "cumT"], [pn])
                stt(y, ps[:, 0:256], -1.0 / 64.0, y, ALU.mult, ALU.add, [pn, "cumT"], ["cumT"])
                yield
                tt(sq, y, y, ALU.mult, ["cumT"], ["cumpT"])
                mm(ps[:, 256:512], bones[:, :], sq, ["bones", "cumpT"], [pn])
                rsq(sq, ps[:, 256:512], 1.0 / 64.0, LNX_EPS, [pn], ["cumpT"])
                yield
                tt(y, y, sq, ALU.mult, ["cumT", "cumpT"], ["cumT"])
                ts(y, y, colvs[:, ct, 3:4], colvs[:, ct, 4:5], ALU.mult, ALU.add, ["cumT", "colvs"], ["cumT"])
                mm(ps[:, 0:256], aup_s[od, cs_], adb[od, :], ["aup", "adb"], [pn])
                act(bn, ps[:, 0:256], AF.Sigmoid, [pn, "colvs"], ["Ep"], bias=colvs[:, ct, 5 + osw:6 + osw])
                yield
                ts(bn, bn, colvs[:, ct, 1:2], colvs[:, ct, 7:8], ALU.mult, ALU.add, ["Ep", "colvs"], ["Ep"])
                tt(bn, rwT[:, ct, :], bn, ALU.mult, ["rwT", "Ep"], ["Ep"])
                tt(bn, bn, kdT[:, ct, :], ALU.add, ["Ep", "kdT"], ["Ep"])
                stt(bn, rwT[:, 4 + ct, :], colvs[:, ct, 2:3], bn, ALU.mult, ALU.mult, ["rwT", "colvs", "Ep"], ["Ep"])
                mm(ps[:, 256:512], bones[:, :], bn, ["bones", "Ep"], [pn])
                tt(bn, ps[:, 256:512], rwT[:, 2 + ct, :], ALU.mult, [pn, "rwT"], ["Ep"])
                yield
                tt(y, y, bn, ALU.add, ["cumT", "Ep"], ["cumT"])
                mm(ps[:, 0:256], gup_a[:, cs_], gsb[:, 0, :], ["gupa", "gsb"], [pn], start=True, stop=False)
                mm(ps[:, 0:256], gup_b[0:32, cs_], gsb[0:32, 1, :], ["gupb", "gsb"], [pn], start=False, stop=True)
                tt(oT[:, ct, :], y, ps[:, 0:256], ALU.mult, ["cumT", pn], ["BhT"])
                yield
            for hr in range(2):
                ps = PB(2 + hr)
                pn = PBN(2 + hr)
                y = Em[:, hr, :]
                sq = Epv[:, hr, :]
                tt(y, yrT[:, hr, :], ysp[:, 2 + hr, :], ALU.add, ["yrT", "ysp"], ["Em"])
                tt(sq, y, y, ALU.mult, ["Em"], ["Epv"])
                mm(ps[:, 0:256], ones[:, :], sq, ["ones", "Epv"], [pn])
                rsq(sq, ps[:, 0:256], 1.0 / 128.0, EPS, [pn], ["Epv"])
                yield
                tt(y, y, sq, ALU.mult, ["Em", "Epv"], ["Em"])
                act(sq, retT[:, 6 + hr, :], AF.Silu, ["retT"], ["Epv"])
                tt(oT[:, 2 + hr, :], y, sq, ALU.mult, ["Em", "Epv"], ["BhT"])
            if dbg and stop_after == 3:
                return
            P.dma(dq_default[0], osend.ap()[:, cols].rearrange("(k p) n -> p k n", p=128), oT[:], "ost", reads=["BhT"], writes=["osend"])
            return


        return front, block, init, tsize

    TNAMES = ['rwT', 'retT', 'stat', 'thb', 'adb', 'gsb', 'sigtm', 'cumT', 'cumpT', 'aT', 'kkT', 'tmpA', 'tmpB', 'Ep', 'Em', 'Epv', 'kdT', 'BhT', 'KhT', 'yT', 'ART', 'BKT', 'BKV', 'Ms', 'M1s', 'Pa0', 'Pa1', 'Qs0', 'Qs1', 'Wsb', 'Usb', 'Hst', 'Hb', 'qr', 'kr', 'qtl', 'rope', 'vtm', 'vrb', 'ktm', 'Ssb', 'Sst', 'Sbf', 'yrT', 'ysp']
    FA = Arena()
    winb = FA.bf16(16 * NCOLS).rearrange("p (k n) -> p k n", k=16)
    xts = [FA.f32(D) for _ in range(2)]
    xmT = FA.bf16(16 * NWIN).rearrange("p (k n) -> p k n", k=16)
    TA1 = Arena()
    f1, b1, i1, tsz = make_thread(1, TA1)
    TA0 = Arena()
    TA0.off = max(FA.off, tsz)
    f0, b0, i0, _ = make_thread(0, TA0)
    print("arena use", FA.off, tsz, TA0.off)
    for nm in ["winb", "xt0", "xt1", "xmT"]:
        P.alias(nm, ph0_res)
    P.dma("pool", winb, win, "constp", writes=["winb"])
    ts(colvs[:, :, 7], colvs[:, :, 1], -1.0, 1.0, ALU.mult, ALU.add, ["colvs"], ["colvs"])
    P.shared = set(P.all_res()) | {"winb", "osend", "colvs"} | {"PS%d" % i for i in range(8)}
    for i_ in range(17):
        P.shared |= {"pspA%d" % i_, "pspB%d" % i_, "pspC%d" % i_}
    for i_ in range(16):
        P.shared |= {"YSf%da" % i_, "YSf%db" % i_, "YSb%da" % i_, "YSb%db" % i_}
    front_local = ["xt0", "xt1", "xt2", "xmT"]
    P.sfx = "_t0"
    for nm in TNAMES + front_local:
        P.alias(nm, ph0_res)
    i0()
    f0("c", 0, 1, alt=False)
    for bi_ in range(16):
        f0("l", bi_, 16, alt=(bi_ % 2 == 0))
    for nm in ["cumT", "cumpT", "aT", "kkT", "tmpA", "tmpB", "Ep", "Em", "Epv", "kdT"]:
        P.alias(nm, ["rwTalt_t0", "retTalt_t0"])
    for nm in ["kdT", "BhT", "KhT", "yT"]:
        P.alias(nm, ["xt2_t0"])
    P.sfx = "_t1"
    for nm in TNAMES:
        P.alias(nm, ph0_res + ["winb", "xt0_t0", "xt1_t0", "xmT_t0"])
    i1()

    def sweep_gen(sw, blk):
        order = [("c", 0, 1)] + [("l", bi, 16) for bi in (range(16) if sw == 0 else range(15, -1, -1))]
        for (seq, bi, nb) in order:
            yield from blk(sw, seq, bi, nb)

    gens = [("_t0", sweep_gen(0, b0)), ("_t1", sweep_gen(1, b1))]
    while gens:
        for item in list(gens):
            P.sfx = item[0]
            dq_default[0] = "sp" if item[0] == "_t0" else "pool"
            try:
                next(item[1])
            except StopIteration:
                gens.remove(item)
    P.sfx = ""
    dq_default[0] = "sp"
    if dbg and stop_after == 3:
        dt16 = arena[:, 0:512].bitcast(BF16).rearrange("p (k n) -> p k n", k=4)
        dtile = arena[:, 1024:1024 + 4096]
        allr = P.all_res()
        load(dt16, osend.ap()[:, 15 * BLK:16 * BLK].rearrange("(k p) n -> p k n", p=128), "dbgin", ["dt16"] + allr, R=["osend"])
        memset(dtile[:], 0.0, ["dtile"] + allr)
        cp(dtile[:, 0:1024], dt16.rearrange("p k n -> p (k n)"), ["dt16"], ["dtile"], eng="dve")
        P.dma("sp", dbg_d, dtile[:], "dbgout", reads=["dtile"], writes=["dbg_d"])
        P.emit(es, {"sp": [("dbgout", 16)]})
        es.close()
        return nc

    if stop_after == "noB":
        P.emit(es, {})
        es.close()
        return nc
    if TWO_A:
        P.dma("sp", modcol_o, modcol[:].rearrange("p a k -> p (a k)"), "mco", reads=["modcol"], writes=["modcol_o"])
        P.emit(es, {"sp": [(k_, P.dma_count[k_]) for k_ in P.dma_keys if str(k_).startswith("ost")] + [("mco", 16), ("a12o", 16)]})
        es.close()
        return nc
    if TWO_B:
        P = Prog(nc)
        DBG["P"] = P
        memset(epsc[:, 0:1], EPS, ["epsc"])
        load(ident[:], c_ident, CK, ["ident"])
        load(hmask_s[:], hmask, CK, ["hmask"])
        load(fconv_s[:], fconv, CK, ["fconv"])
        load(modcol[:].rearrange("p a k -> p (a k)"), modcol_i, CK, ["modcol"])
        a12sp = a12_i
    if stop_after != "nocc" and not TWO_B:
      P.custom("pool", lambda e: e.collective_compute("AllGather", ALU.bypass, replica_groups=[[0, 1, 2, 3, 4, 5, 6, 7]],
                                                    ins=[osend.ap().opt()], outs=[ogath.ap().opt()]),
               "cc0", reads=["osend"], writes=["ogath"])
    og = ogath.ap() if stop_after != "nocc" else osend.ap()

    prevA = P.all_res()
    PSN = ["psb%d" % i for i in range(8)]
    for i_ in range(8):
        P.alias("psb%d" % i_, ["PS%d" % i_, "psb%d" % i_])
    HT_W = 9216
    hT = arena[:, 0:HT_W].bitcast(BF16).rearrange("p (k n) -> p k n", k=16)
    B2 = Arena()
    B2.off = HT_W
    oTb = B2.bf16(16 * 1152).rearrange("p (k n) -> p k n", k=16)
    cand = [B2.bf16(8 * 1152).rearrange("p (k n) -> p k n", k=8)]
    woutb = B2.bf16(16 * D).rearrange("p (k n) -> p k n", k=16)
    mixrow = B2.f32(D)
    xtb = B2.f32(D)
    A1row = B2.f32(D)
    sqb = B2.bf16(D)
    statb = B2.f32(16)
    for nm in ["hT", "oTb", "cand0", "woutb", "mixrow", "xtb", "A1row", "sqb", "statb"]:
        P.alias(nm, prevA)
    P.dma("pool", woutb, wout, "woutk", writes=["woutb"])
    load(A1row, a12sp[:, 0:D].partition_broadcast(128), "a1k", ["A1row"], R=["a12sp"])
    NB_ = 2 if stop_after != "nocc" else 1
    if TWO_B:
        load(oTb, oin, "oink", ["oTb"])
    if NB_ == 1:
        memset(cand[0][:, 4:8, :], 0.0, ["cand0"])
    for bq in range(NB_):
        memset(cand[0][:, bq * 4 + 0, 0:64], 0.0, ["cand0"])
        memset(cand[0][:, bq * 4 + 3, 1088:1152], 0.0, ["cand0"])
    for kt in (range(16) if not TWO_B else []):
        cb = cand[0]
        cn = "cand0"
        for bq in range(NB_):
            r0 = (bq * 4 + kt // 4) * 512 + (kt % 4) * 128
            if stop_after == "nocc":
                r0 = (kt % 4) * 128
            load(cb[:, bq * 4 + 0, 64:1152], og[r0:r0 + 128, 0:1088], "candk0", [cn], R=["ogath", "osend"])
            load(cb[:, bq * 4 + 1, :], og[r0:r0 + 128, 960:2112], "candk0", [cn], R=["ogath", "osend"])
            load(cb[:, bq * 4 + 2, :], og[r0:r0 + 128, 1984:3136], "candk0", [cn], R=["ogath", "osend"])
            load(cb[:, bq * 4 + 3, 0:1088], og[r0:r0 + 128, 3008:4096], "candk0", [cn], R=["ogath", "osend"])
        ts(oTb[:, kt, :], cb[:, 0, :], tsel_s[:, 0:1], None, ALU.mult, None, [cn, "tsel"], ["oTb"])
        for g_ in range(1, 8):
            stt(oTb[:, kt, :], cb[:, g_, :], tsel_s[:, g_:g_ + 1], oTb[:, kt, :], ALU.mult, ALU.add, [cn, "tsel", "oTb"], ["oTb"])
    for t_ in range(9):
        tk = slice(t_ * 128, (t_ + 1) * 128)
        load(xtb, xb[tk, :], "xbk", ["xtb"])
        for cch in range(4):
            ps = psb[cch % 2]
            pn = PSN[cch % 2]
            for kt in range(16):
                mm(ps[:, :], oTb[:, kt, tk], woutb[:, kt, cch * 512:(cch + 1) * 512], ["oTb", "woutb"], [pn], start=(kt == 0), stop=(kt == 15))
            cp(mixrow[:, cch * 512:(cch + 1) * 512], ps[:, :], [pn], ["mixrow"])
        act(sqb, mixrow, AF.Square, ["mixrow"], ["sqb", "statb"], accum=statb[:, 0:1])
        rsq(statb[:, 1:2], statb[:, 0:1], 1.0 / D, EPS, ["statb"], ["statb"])
        stt(mixrow, mixrow, statb[:, 1:2], A1row, ALU.mult, ALU.mult, ["mixrow", "statb", "A1row"], ["mixrow"])
        tt(xtb, xtb, mixrow, ALU.add, ["xtb", "mixrow"], ["xtb"])
        if t_ == 0:
            P.dma("sp", x1sp[0:64, :], xtb[64:128, :], "x1st", reads=["xtb"], writes=["x1sp"])
        elif t_ == 8:
            P.dma("sp", x1sp[960:1024, :], xtb[0:64, :], "x1st", reads=["xtb"], writes=["x1sp"])
        else:
            P.dma("sp", x1sp[t_ * 128 - 64:t_ * 128 + 64, :], xtb, "x1st", reads=["xtb"], writes=["x1sp"])
        act(sqb, xtb, AF.Square, ["xtb"], ["sqb", "statb"], accum=statb[:, 2:3])
        rsq(statb[:, 3:4], statb[:, 2:3], 1.0 / D, EPS, ["statb"], ["statb"])
        ts(mixrow, xtb, statb[:, 3:4], None, ALU.mult, None, ["xtb", "statb"], ["mixrow"])
        for k4 in range(4):
            ps = psb[2 + k4 % 2]
            pn = PSN[2 + k4 % 2]
            for kk_ in range(4):
                kt = k4 * 4 + kk_
                trp(ps[:, kk_ * 128:(kk_ + 1) * 128], mixrow[:, kt * 128:(kt + 1) * 128], ident[:, :], ["mixrow", "ident"], [pn])
            for kk_ in range(4):
                kt = k4 * 4 + kk_
                if kk_ % 2 == 0:
                    act(hT[:, kt, tk], ps[:, kk_ * 128:(kk_ + 1) * 128], AF.Identity, [pn, "modcol"], ["hT"],
                        bias=modcol[:, 2, kt:kt + 1], scale=modcol[:, 3, kt:kt + 1])
                else:
                    ts(hT[:, kt, tk], ps[:, kk_ * 128:(kk_ + 1) * 128], modcol[:, 3, kt:kt + 1], modcol[:, 2, kt:kt + 1],
                       ALU.mult, ALU.add, [pn, "modcol"], ["hT"])
    B2res = ["oTb", "cand0", "woutb", "mixrow", "xtb", "A1row", "sqb", "statb"]
    B3 = Arena()
    B3.off = HT_W
    wgb = [B3.bf16(16 * 128).rearrange("p (k n) -> p k n", k=16) for _ in range(2)]
    wub = [B3.bf16(16 * 128).rearrange("p (k n) -> p k n", k=16) for _ in range(2)]
    gts = B3.f32(1152)
    cacc = B3.f32(1024)
    HID_OFF = ARENA_W - 22528
    assert B3.off <= HID_OFF
    hid = arena[:, HID_OFF:ARENA_W].bitcast(BF16).rearrange("p (k n) -> p k n", k=NFT)
    for nm in ["wgb0", "wgb1", "wub0", "wub1", "gts", "cacc", "hid"]:
        P.alias(nm, B2res)
    g3 = gts.rearrange("p (r c) -> p r c", c=64)
    a3 = cacc.rearrange("p (r c) -> p r c", c=64)
    for ft in range(NFT):
        wg, wu = wgb[ft % 2], wub[ft % 2]
        P.dma("pool", wg, wgate[ft], "wgk%d" % (ft % 2), writes=["wgb%d" % (ft % 2)])
        P.dma("pool", wu, wupf[ft], "wuk%d" % (ft % 2), writes=["wub%d" % (ft % 2)])
        for ch in range(3):
            ps = psb[ch]
            for kt in range(16):
                mm(ps[:, 0:384], wg[:, kt, :], hT[:, kt, ch * 384:(ch + 1) * 384], ["wgb%d" % (ft % 2), "hT"], [PSN[ch]],
                   start=(kt == 0), stop=(kt == 15))
            cp(gts[:, ch * 384:(ch + 1) * 384], ps[:, 0:384], [PSN[ch]], ["gts"])
        ts(gts[:, 0:64], gts[:, 0:64], hmask_s[:, 0:1], None, ALU.mult, None, ["gts", "hmask"], ["gts"])
        ts(gts[:, 1088:1152], gts[:, 1088:1152], hmask_s[:, 1:2], None, ALU.mult, None, ["gts", "hmask"], ["gts"])
        ts(a3, g3[:, 1:17, :], fconv_s[:, ft, 4:5], fconv_s[:, ft, 9:10], ALU.mult, ALU.add, ["gts", "fconv"], ["cacc"])
        for dr in range(3):
            for dc in range(3):
                if dr == 1 and dc == 1:
                    continue
                wcol = fconv_s[:, ft, dr * 3 + dc:dr * 3 + dc + 1]
                if dc == 1:
                    o_, i_ = a3, g3[:, dr:dr + 16, :]
                elif dc == 0:
                    o_, i_ = a3[:, :, 1:64], g3[:, dr:dr + 16, 0:63]
                else:
                    o_, i_ = a3[:, :, 0:63], g3[:, dr:dr + 16, 1:64]
                stt(o_, i_, wcol, o_, ALU.mult, ALU.add, ["gts", "fconv", "cacc"], ["cacc"])
        act(cacc, cacc, AF.Silu, ["cacc"], ["cacc"])
        for ch in range(2):
            ps = psb[3 + ch]
            for kt in range(16):
                mm(ps[:, :], wu[:, kt, :], hT[:, kt, 64 + ch * 512:64 + (ch + 1) * 512], ["wub%d" % (ft % 2), "hT"], [PSN[3 + ch]],
                   start=(kt == 0), stop=(kt == 15))
            tt(hid[:, ft, ch * 512:(ch + 1) * 512], ps[:, :], cacc[:, ch * 512:(ch + 1) * 512], ALU.mult, [PSN[3 + ch], "cacc"], ["hid"])
    yTf = arena[:, 0:16384].rearrange("p (k n) -> p k n", k=16)
    B4 = Arena()
    B4.off = 16384
    wdb = [B4.bf16(NFT * 128).rearrange("p (k n) -> p k n", k=NFT) for _ in range(2)]
    assert B4.off <= HID_OFF
    B3res = ["hT", "wgb0", "wgb1", "wub0", "wub1", "gts", "cacc"]
    for nm in ["yTf", "wdb0", "wdb1"]:
        P.alias(nm, B3res)
    for mt in range(16):
        wd = wdb[mt % 2]
        P.dma("pool", wd, wdown[mt], "wdk%d" % (mt % 2), writes=["wdb%d" % (mt % 2)])
        for ch in range(2):
            ps = psb[(mt * 2 + ch) % 4]
            pn = PSN[(mt * 2 + ch) % 4]
            for ft in range(NFT):
                mm(ps[:, :], wd[:, ft, :], hid[:, ft, ch * 512:(ch + 1) * 512], ["wdb%d" % (mt % 2), "hid"], [pn],
                   start=(ft == 0), stop=(ft == NFT - 1))
            cp(yTf[:, mt, ch * 512:(ch + 1) * 512], ps[:, :], [pn], ["yTf"])
    B5 = Arena()
    B5.off = 16384
    ytm = B5.f32(D)
    x1t = [B5.f32(D) for _ in range(2)]
    A2row = B5.f32(D)
    sq5 = B5.bf16(D)
    st5 = B5.f32(8)
    assert B5.off <= HID_OFF
    for nm in ["ytm", "x1t0", "x1t1", "A2row", "sq5", "st5"]:
        P.alias(nm, ["wdb0", "wdb1"])
    load(A2row, a12sp[:, D:2 * D].partition_broadcast(128), "a2k", ["A2row"], R=["a12sp"])
    for t_ in range(8):
        tk = slice(t_ * 128, (t_ + 1) * 128)
        xt_ = x1t[t_ % 2]
        xn = "x1t%d" % (t_ % 2)
        load(xt_, x1sp[tk, :], "x1ld%d" % (t_ % 2), [xn], R=["x1sp"])
        for k4 in range(4):
            ps = psb[4 + k4 % 2]
            pn = PSN[4 + k4 % 2]
            for kk_ in range(4):
                mt = k4 * 4 + kk_
                trp(ps[:, kk_ * 128:(kk_ + 1) * 128], yTf[:, mt, tk], ident[:, :], ["yTf", "ident"], [pn])
            cp(ytm[:, k4 * 512:(k4 + 1) * 512], ps[:, :], [pn], ["ytm"], eng=("act" if k4 % 2 == 0 else "dve"))
        act(sq5, ytm, AF.Square, ["ytm"], ["sq5", "st5"], accum=st5[:, 0:1])
        rsq(st5[:, 1:2], st5[:, 0:1], 1.0 / D, EPS, ["st5"], ["st5"])
        stt(ytm, ytm, st5[:, 1:2], A2row, ALU.mult, ALU.mult, ["ytm", "st5", "A2row"], ["ytm"])
        tt(xt_, xt_, ytm, ALU.add, [xn, "ytm"], [xn])
        P.dma("sp", out_d[tk, :], xt_, "outk%d" % (t_ % 2), reads=[xn], writes=["out_d"])
    P.emit(es, {"sp": [("outk0", P.dma_count["outk0"]), ("outk1", P.dma_count["outk1"])]})
    es.close()
    return nc


def _prep(inp):
    f = lambda a: np.ascontiguousarray(np.asarray(a, dtype=np.float32))
    x, c, ctx, c_ctx = f(inp["x"]), f(inp["c"]), f(inp["ctx"]), f(inp["c_ctx"])
    w_in = f(inp["w_in"][0])
    RW = 1024
    shared = {}
    shared["wada"] = np.ascontiguousarray(
        f(inp["w_ada"][0]).reshape(16, 128, 24, 512).transpose(2, 1, 0, 3))
    shared["bada"] = f(inp["b_ada"][0]).reshape(1, 12288)
    shared["nrm_col"] = np.ascontiguousarray(
        np.stack([_col(f(inp["norm_pre_mix"][0])), _col(f(inp["norm_pre_ffn"][0]))], axis=1))
    shared["nrm_row"] = np.stack([f(inp["norm_post_mix"][0]), f(inp["norm_post_ffn"][0])], axis=0)
    shared["wgate"] = np.ascontiguousarray(
        f(inp["ffn_w_gate"][0]).reshape(16, 128, NFT, 128).transpose(2, 1, 0, 3))
    shared["wupf"] = np.ascontiguousarray(
        f(inp["ffn_w_up"][0]).reshape(16, 128, NFT, 128).transpose(2, 1, 0, 3))
    shared["wdown"] = np.ascontiguousarray(
        f(inp["ffn_w_down"][0]).reshape(NFT, 128, 16, 128).transpose(2, 1, 0, 3))
    fc = np.concatenate([f(inp["ffn_conv"][0]).reshape(9, DFF), f(inp["ffn_conv_b"][0]).reshape(1, DFF)], axis=0)
    shared["fconv"] = np.ascontiguousarray(fc.reshape(10, NFT, 128).transpose(2, 1, 0))
    w_out = f(inp["w_out"][0])
    maps = []
    for core in range(8):
        b, g = core // 4, core % 4
        m = dict(shared)
        xa = np.zeros((SEQ + CTX + 4, D), np.float32)
        xa[1:257] = ctx[b]
        xa[259:259 + SEQ] = x[b]
        m["xa"] = xa
        xbm = np.zeros((1152, D), np.float32)
        lo, hi = g * 1024 - 64, g * 1024 + 1088
        slo, shi = max(lo, 0), min(hi, SEQ)
        xbm[slo - lo: shi - lo] = x[b, slo:shi]
        m["xb"] = xbm
        cv = np.stack([c[b], c_ctx], axis=1)
        m["cvt"] = np.ascontiguousarray(cv.reshape(16, 128, 2).transpose(1, 0, 2))
        cs = slice(g * 256, (g + 1) * 256)
        cols = np.concatenate([
            np.arange(0, RW)[cs], np.arange(RW, 2 * RW)[cs],
            np.arange(2304, 2304 + RW)[cs],
            np.arange(2048, 2304),
            np.arange(3328, 3488),
            3488 + np.arange(0, RW)[cs], 3488 + np.arange(RW, 2 * RW)[cs],
            3488 + np.arange(2 * RW, 3 * RW)[cs], 3488 + np.arange(3 * RW, 4 * RW)[cs]])
        assert cols.size == NCOLS
        m["win"] = _ktile(w_in[:, cols])
        rc = f(inp["rw_conv"][0])[:, cols[:1184]]
        rcp = np.zeros((3, 1280), np.float32)
        rcp[:, :1184] = rc
        m["rwconv"] = np.ascontiguousarray(rcp.reshape(3, 10, 128).transpose(2, 1, 0))
        cvs = np.zeros((128, 2, 9), np.float32)
        hs = slice(g * 256, (g + 1) * 256)
        vecs = [f(inp["rw_k_k"][0])[hs], f(inp["rw_k_a"][0])[hs], f(inp["rw_r_k"][0]).reshape(-1)[hs],
                f(inp["rw_lnx_w"][0])[hs], f(inp["rw_lnx_b"][0])[hs],
                f(inp["rw_a0"][0])[0, hs], f(inp["rw_a0"][0])[1, hs]]
        for vi, v in enumerate(vecs):
            cvs[:, :, vi] = v.reshape(2, 128).T
        m["colv"] = cvs
        m["w0row"] = np.concatenate([f(inp["rw_w0"][0])[0, hs], f(inp["rw_w0"][0])[1, hs]]).reshape(1, 512)
        m["wup"] = np.concatenate([f(inp["rw_w_up"][0])[0][:, hs], f(inp["rw_w_up"][0])[1][:, hs]], axis=0)
        m["aup"] = np.concatenate([f(inp["rw_a_up"][0])[0][:, hs], f(inp["rw_a_up"][0])[1][:, hs]], axis=0)
        m["gup"] = np.ascontiguousarray(f(inp["rw_g_up"][0])[:, hs])
        rows = np.concatenate([np.concatenate([np.arange(gp * 256, (gp + 1) * 256),
                                               1024 + np.arange(gp * 256, (gp + 1) * 256)]) for gp in range(4)])
        m["wout"] = _ktile(w_out[rows])
        hm = np.zeros((128, 2), np.float32)
        hm[:, 0] = 1.0 if g > 0 else 0.0
        hm[:, 1] = 1.0 if g < 3 else 0.0
        m["hmask"] = hm
        tsl = np.zeros((128, 8), np.float32)
        tsl[:, b * 4 + g] = 1.0
        m["tsel"] = tsl
        for k, v in _consts(g).items():
            m["c_" + k] = v
        maps.append(m)
    return maps


_NC_CACHE = {}
FUSED = False


def kernel(**inputs):
    import ml_dtypes
    maps = _prep(inputs)
    out = np.zeros((2, SEQ, D), np.float32)
    if FUSED:
        if "nc" not in _NC_CACHE:
            _NC_CACHE["nc"] = build()
        res = run_bass_kernel_spmd(_NC_CACHE["nc"], maps, core_ids=list(range(8)))
        for core in range(8):
            b, g = core // 4, core % 4
            out[b, g * 1024:(g + 1) * 1024] = np.asarray(res.results[core]["out"], dtype=np.float32)
        return out
    if "ncA" not in _NC_CACHE:
        _NC_CACHE["ncA"] = build(stop_after="A2L")
        _NC_CACHE["ncB"] = build(stop_after="B2L")
    dummy = {"oin": np.zeros((128, 16, 1152), ml_dtypes.bfloat16), "modcol_i": np.zeros((128, 96), np.float32),
             "a12_i": np.zeros((1, 2 * D), np.float32)}
    resA = run_bass_kernel_spmd(_NC_CACHE["ncA"], maps, core_ids=list(range(8)))
    osend = [np.asarray(resA.results[c]["osend"]) for c in range(8)]
    mapsB = []
    for core in range(8):
        b, g = core // 4, core % 4
        m = dict(maps[core])
        oin = np.zeros((128, 16, 1152), osend[0].dtype)
        lo, hi = g * 1024 - 64, g * 1024 + 1088
        slo, shi = max(lo, 0), min(hi, SEQ)
        for kt in range(16):
            src = osend[4 * b + kt // 4]
            j = kt % 4
            oin[:, kt, slo - lo:shi - lo] = src[j * 128:(j + 1) * 128, slo:shi]
        m["oin"] = oin
        m["modcol_i"] = np.asarray(resA.results[core]["modcol_o"])
        m["a12_i"] = np.asarray(resA.results[core]["a12_o"])
        mapsB.append(m)
    resB = run_bass_kernel_spmd(_NC_CACHE["ncB"], mapsB, core_ids=list(range(8)))
    for core in range(8):
        b, g = core // 4, core % 4
        out[b, g * 1024:(g + 1) * 1024] = np.asarray(resB.results[core]["out"], dtype=np.float32)
    return out
```
